# Optimizing a Trainium2 kernel written in Bass

```python
import math
import jax, jax.numpy as jnp
from jax import lax
import numpy as np

D_MODEL = 1024
BATCH = 2
SEQ = 16384
DEPTH = 1
DEC_BATCH = 4
DEC_SEQ = 4096
PAST_LEN = 128

HEAD_DIM = 64
N_GQA_HEADS = 8
N_GQA_KV = 2
GQA_GROUP = N_GQA_HEADS // N_GQA_KV
N_DIFF_HEADS = 4
DIFF_V_DIM = 2 * HEAD_DIM
GQA_WIDTH = N_GQA_HEADS * HEAD_DIM
DIFF_WIDTH = N_DIFF_HEADS * DIFF_V_DIM
MIX_WIDTH = GQA_WIDTH + DIFF_WIDTH
GQA_Q_COLS = N_GQA_HEADS * HEAD_DIM
GQA_KV_COLS = N_GQA_KV * HEAD_DIM
DIFF_QK_COLS = N_DIFF_HEADS * 2 * HEAD_DIM
DIFF_V_COLS = N_DIFF_HEADS * DIFF_V_DIM
IN_WIDTH = GQA_Q_COLS + 2 * GQA_KV_COLS + 2 * DIFF_QK_COLS + DIFF_V_COLS
D_FF = -(-8 * D_MODEL // (3 * 256)) * 256
GRID_W = 64
Q_BLOCK = 128
NUM_BUCKETS = 32
MAX_DISTANCE = 128
ROPE_THETA = 10000.0
EPS = 1e-6
ATTN_SCALE = 1.0 / math.sqrt(HEAD_DIM)

kernel_name = 'hybrid_gqa_axialrope_diffattn_encoder'


def rmsnorm(x, g):
    xf = x.astype(jnp.float32)
    y = xf * lax.rsqrt(jnp.mean(xf * xf, axis=-1, keepdims=True) + EPS)
    return (y * g.astype(jnp.float32)).astype(x.dtype)


def axial_rope_tables(n):
    rows = n // GRID_W
    row = jnp.repeat(jnp.arange(rows), GRID_W).astype(jnp.float32)
    col = jnp.tile(jnp.arange(GRID_W), rows).astype(jnp.float32)
    half = HEAD_DIM // 2
    inv = ROPE_THETA ** (-jnp.arange(0, half, 2, dtype=jnp.float32) / half)
    ang_r = row[:, None] * inv[None, :]
    ang_c = col[:, None] * inv[None, :]
    ang = jnp.concatenate([ang_r, ang_r, ang_c, ang_c], axis=-1)
    return jnp.cos(ang), jnp.sin(ang)


def apply_rope(x, cos, sin):
    xf = x.astype(jnp.float32)
    xs = xf.reshape(*xf.shape[:-1], 2, 2, HEAD_DIM // 4)
    rot = jnp.stack([-xs[..., 1, :], xs[..., 0, :]], axis=-2).reshape(xf.shape)
    bshape = (1, cos.shape[0]) + (1,) * (x.ndim - 3) + (HEAD_DIM,)
    return (xf * cos.reshape(bshape) + rot * sin.reshape(bshape)).astype(x.dtype)


def rel_bucket(rel):
    half = NUM_BUCKETS // 2
    max_exact = half // 2
    n = jnp.abs(rel)
    nf = jnp.maximum(n, max_exact).astype(jnp.float32)
    large = max_exact + (jnp.log(nf / max_exact) / math.log(MAX_DISTANCE / max_exact)
                         * (half - max_exact)).astype(jnp.int32)
    large = jnp.minimum(large, half - 1)
    return jnp.where(rel > 0, half, 0) + jnp.where(n < max_exact, n, large)


def gqa_attention(q, k, v):
    b, n = q.shape[0], q.shape[1]
    nblk = n // Q_BLOCK
    qb = jnp.moveaxis(q.reshape(b, nblk, Q_BLOCK, *q.shape[2:]), 1, 0)

    def one(qblk):
        s = jnp.einsum('bqkgd,bskd->bkgqs', qblk, k, preferred_element_type=jnp.float32) * ATTN_SCALE
        p = jax.nn.softmax(s, axis=-1).astype(v.dtype)
        return jnp.einsum('bkgqs,bskd->bqkgd', p, v)

    out = lax.map(one, qb)
    return jnp.moveaxis(out, 0, 1).reshape(b, n, GQA_WIDTH)


def diff_attention(q, k, v, lam, rel_bias):
    b, n = q.shape[0], q.shape[1]
    nblk = n // Q_BLOCK
    qb = jnp.moveaxis(q.reshape(b, nblk, Q_BLOCK, *q.shape[2:]), 1, 0)
    starts = jnp.arange(nblk) * Q_BLOCK
    kpos = jnp.arange(n)

    def one(args):
        qblk, start = args
        qpos = start + jnp.arange(Q_BLOCK)
        bucket = rel_bucket(kpos[None, :] - qpos[:, None])
        bias = jnp.moveaxis(rel_bias[bucket], -1, 0).astype(jnp.float32)
        s = jnp.einsum('bqhjd,bshjd->bhjqs', qblk, k, preferred_element_type=jnp.float32) * ATTN_SCALE
        p = jax.nn.softmax(s + bias[None, :, None], axis=-1)
        a = p[:, :, 0] - lam * p[:, :, 1]
        return jnp.einsum('bhqs,bshe->bqhe', a.astype(v.dtype), v)

    out = lax.map(one, (qb, starts))
    return jnp.moveaxis(out, 0, 1).reshape(b, n, N_DIFF_HEADS, DIFF_V_DIM)


def encoder_layer(x, c, layer_idx, rel_bias, w_ada, b_ada, g_pre_mix, w_in, g_q, g_k,
                  lam_q1, lam_k1, lam_q2, lam_k2, g_subln, w_out, g_post_mix,
                  g_pre_ffn, w_gu, w_down, g_post_ffn):
    b, n, _ = x.shape
    mod = jax.nn.silu(c) @ w_ada + b_ada
    sh1, sc1, gt1, sh2, sc2, gt2 = jnp.split(mod[:, None, :], 6, axis=-1)

    h = rmsnorm(x, g_pre_mix) * (1 + sc1) + sh1
    proj = h @ w_in
    o1 = GQA_Q_COLS
    o2 = o1 + GQA_KV_COLS
    o3 = o2 + GQA_KV_COLS
    o4 = o3 + DIFF_QK_COLS
    o5 = o4 + DIFF_QK_COLS
    qa = proj[..., :o1].reshape(b, n, N_GQA_KV, GQA_GROUP, HEAD_DIM)
    ka = proj[..., o1:o2].reshape(b, n, N_GQA_KV, HEAD_DIM)
    va = proj[..., o2:o3].reshape(b, n, N_GQA_KV, HEAD_DIM)
    qd = proj[..., o3:o4].reshape(b, n, N_DIFF_HEADS, 2, HEAD_DIM)
    kd = proj[..., o4:o5].reshape(b, n, N_DIFF_HEADS, 2, HEAD_DIM)
    vd = proj[..., o5:].reshape(b, n, N_DIFF_HEADS, DIFF_V_DIM)

    cos, sin = axial_rope_tables(n)
    qa = apply_rope(rmsnorm(qa, g_q), cos, sin)
    ka = apply_rope(rmsnorm(ka, g_k), cos, sin)
    out_a = gqa_attention(qa, ka, va)

    lam_init = 0.8 - 0.6 * math.exp(-0.3 * layer_idx)
    lam = (jnp.exp(jnp.sum(lam_q1.astype(jnp.float32) * lam_k1.astype(jnp.float32)))
           - jnp.exp(jnp.sum(lam_q2.astype(jnp.float32) * lam_k2.astype(jnp.float32))) + lam_init)
    out_d = diff_attention(qd, kd, vd, lam, rel_bias)
    out_d = (rmsnorm(out_d, g_subln) * (1.0 - lam_init)).reshape(b, n, DIFF_WIDTH)

    mix = jnp.concatenate([out_a, out_d], axis=-1) @ w_out
    x = x + gt1 * rmsnorm(mix, g_post_mix)

    h = rmsnorm(x, g_pre_ffn) * (1 + sc2) + sh2
    gate, up = jnp.split(h @ w_gu, 2, axis=-1)
    f = (jax.nn.silu(gate) * up) @ w_down
    return x + gt2 * rmsnorm(f, g_post_ffn)


def trunk(x, c, rel_bias, w_ada, b_ada, g_pre_mix, w_in, g_q, g_k, lam_q1, lam_k1,
          lam_q2, lam_k2, g_subln, w_out, g_post_mix, g_pre_ffn, w_gu, w_down, g_post_ffn):
    for l in range(DEPTH):
        x = encoder_layer(x, c, l, rel_bias, w_ada[l], b_ada[l], g_pre_mix[l], w_in[l],
                          g_q[l], g_k[l], lam_q1[l], lam_k1[l], lam_q2[l], lam_k2[l],
                          g_subln[l], w_out[l], g_post_mix[l], g_pre_ffn[l], w_gu[l],
                          w_down[l], g_post_ffn[l])
    return x


def setup_inputs(seed: int = 0) -> dict:
    key = jax.random.key(seed)
    ks = jax.random.split(key, 24)
    f32 = jnp.float32

    def nrm(k, shape, scale):
        return jax.random.normal(k, shape, f32) * scale

    def gain(k, shape):
        return 1.0 + 0.05 * jax.random.normal(k, shape, f32)

    return {
        'x_prompt': nrm(ks[0], (BATCH, SEQ, D_MODEL), 1.0),
        'x_sample': nrm(ks[1], (DEC_BATCH, DEC_SEQ, D_MODEL), 1.0),
        'c_prompt': nrm(ks[2], (BATCH, D_MODEL), 1.0),
        'c_sample': nrm(ks[3], (DEC_BATCH, D_MODEL), 1.0),
        'rel_bias': nrm(ks[4], (NUM_BUCKETS, N_DIFF_HEADS), 0.5),
        'w_ada': nrm(ks[5], (DEPTH, D_MODEL, 6 * D_MODEL), 0.5 * D_MODEL ** -0.5),
        'b_ada': nrm(ks[6], (DEPTH, 6 * D_MODEL), 0.01),
        'g_pre_mix': gain(ks[7], (DEPTH, D_MODEL)),
        'w_in': nrm(ks[8], (DEPTH, D_MODEL, IN_WIDTH), D_MODEL ** -0.5),
        'g_q': gain(ks[9], (DEPTH, HEAD_DIM)),
        'g_k': gain(ks[10], (DEPTH, HEAD_DIM)),
        'lam_q1': nrm(ks[11], (DEPTH, HEAD_DIM), 0.1),
        'lam_k1': nrm(ks[12], (DEPTH, HEAD_DIM), 0.1),
        'lam_q2': nrm(ks[13], (DEPTH, HEAD_DIM), 0.1),
        'lam_k2': nrm(ks[14], (DEPTH, HEAD_DIM), 0.1),
        'g_subln': gain(ks[15], (DEPTH, DIFF_V_DIM)),
        'w_out': nrm(ks[16], (DEPTH, MIX_WIDTH, D_MODEL), MIX_WIDTH ** -0.5),
        'g_post_mix': gain(ks[17], (DEPTH, D_MODEL)),
        'g_pre_ffn': gain(ks[18], (DEPTH, D_MODEL)),
        'w_gu': nrm(ks[19], (DEPTH, D_MODEL, 2 * D_FF), D_MODEL ** -0.5),
        'w_down': nrm(ks[20], (DEPTH, D_FF, D_MODEL), D_FF ** -0.5),
        'g_post_ffn': gain(ks[21], (DEPTH, D_MODEL)),
    }


def reference(x_prompt, x_sample, c_prompt, c_sample, rel_bias, w_ada, b_ada, g_pre_mix,
              w_in, g_q, g_k, lam_q1, lam_k1, lam_q2, lam_k2, g_subln, w_out, g_post_mix,
              g_pre_ffn, w_gu, w_down, g_post_ffn):
    y_prompt = trunk(x_prompt, c_prompt, rel_bias, w_ada, b_ada, g_pre_mix, w_in, g_q, g_k,
                     lam_q1, lam_k1, lam_q2, lam_k2, g_subln, w_out, g_post_mix,
                     g_pre_ffn, w_gu, w_down, g_post_ffn)
    y_sample = trunk(x_sample, c_sample, rel_bias, w_ada, b_ada, g_pre_mix, w_in, g_q, g_k,
                     lam_q1, lam_k1, lam_q2, lam_k2, g_subln, w_out, g_post_mix,
                     g_pre_ffn, w_gu, w_down, g_post_ffn)
    return (y_prompt, y_sample)
```

```python
import contextlib
import math
import numpy as np
import ml_dtypes
import concourse.bass as bass
import concourse.mybir as mybir
from concourse.bass_utils import run_bass_kernel_spmd

F32 = mybir.dt.float32
BF16 = mybir.dt.bfloat16
ALU = mybir.AluOpType
AF = mybir.ActivationFunctionType
AX = mybir.AxisListType

D = 1024
DFF = 2816
HD = 64
EPS = 1e-6
NB = 32
WIN = 2944
LAM_INIT = 0.8 - 0.6 * math.exp(-0.3 * 0)
ULEN = 1279
WLEN = 639

ENGS = ["pe", "act", "dve", "pool", "sp"]
N_DMA_SEMS = 6


class Buf:
    __slots__ = ("name", "w", "r", "excl")

    def __init__(self, name, excl=False):
        self.name = name
        self.excl = excl
        self.w = None
        self.r = []


class Op:
    __slots__ = ("eng", "fn", "waits", "signal", "ev", "is_dma")

    def __init__(self, eng, fn, is_dma):
        self.eng = eng
        self.fn = fn
        self.waits = []
        self.signal = False
        self.ev = None
        self.is_dma = is_dma


class Prog:
    def __init__(self, nc, stack):
        self.nc = nc
        self.sems = {}
        self.cnt = {}
        for e in ENGS:
            self.sems[e] = stack.enter_context(nc.semaphore("s_" + e))
            self.cnt[e] = 0
        self.dma_sems = {}
        for q in ("sp", "act", "pool"):
            lst = []
            for i in range(N_DMA_SEMS):
                nm = "d_%s%d" % (q, i)
                self.sems[nm] = stack.enter_context(nc.semaphore(nm))
                self.cnt[nm] = 0
                lst.append(nm)
            self.dma_sems[q] = lst
        self.dma_rr = {q: 0 for q in self.dma_sems}
        self.waited = {e: {} for e in ENGS}
        self.ops = None
        self.nops = 0

    def begin(self):
        self.ops = {e: [] for e in ENGS}
        self.allops = []
        self.dma_last = {}

    def _dep(self, op, other):
        if other is None or other is op:
            return
        if other.eng == "pe" and op.eng == "pe" and not other.is_dma and not op.is_dma:
            return
        op.waits.append(other)

    def op(self, eng, fn, reads=(), writes=(), dma=False):
        o = Op(eng, fn, dma)
        reads = list(reads)
        writes = list(writes)
        for b in reads:
            if b.excl and b not in writes:
                writes.append(b)
        for b in reads:
            self._dep(o, b.w)
        for b in writes:
            self._dep(o, b.w)
            for r in b.r:
                self._dep(o, r)
        for b in reads:
            b.r.append(o)
        for b in writes:
            b.w = o
            b.r = []
        if dma:
            i = self.dma_rr[eng]
            self.dma_rr[eng] = (i + 1) % N_DMA_SEMS
            nm = self.dma_sems[eng][i]
            prev = self.dma_last.get(nm)
            if prev is not None:
                o.waits.append(prev)
            self.dma_last[nm] = o
            self.cnt[nm] += 16
            o.ev = (nm, self.cnt[nm])
            o.signal = True
        self.ops[eng].append(o)
        self.allops.append(o)
        return o

    def dma(self, q, out, in_, reads=(), writes=()):
        return self.op(q, lambda e: e.dma_start(out=out, in_=in_), reads, writes, dma=True)

    def end(self):
        nc = self.nc
        for o in self.allops:
            for w in o.waits:
                w.signal = True
        for e in ENGS:
            for o in reversed(self.ops[e]):
                if not o.is_dma:
                    o.signal = True
                    break
        for e in ENGS:
            for o in self.ops[e]:
                if not o.is_dma and o.signal:
                    self.cnt[e] += 1
                    o.ev = (e, self.cnt[e])
        final = dict(self.cnt)
        sems = self.sems
        ops = self.ops
        waited_all = self.waited
        self.nops += len(self.allops)

        def emit(ename, eng):
            waited = waited_all[ename]
            for o in ops[ename]:
                need = {}
                for w in o.waits:
                    s, v = w.ev
                    if need.get(s, 0) < v:
                        need[s] = v
                for s, v in need.items():
                    if waited.get(s, 0) >= v:
                        continue
                    waited[s] = v
                    eng.wait_ge(sems[s], v)
                ins = o.fn(eng)
                if o.signal:
                    s, v = o.ev
                    ins.then_inc(sems[s], 16 if o.is_dma else 1)
            for s, v in final.items():
                if v > 0 and waited.get(s, 0) < v:
                    waited[s] = v
                    eng.wait_ge(sems[s], v)

        with nc.Block() as block:
            @block.tensor
            def _(eng):
                emit("pe", eng)

            @block.scalar
            def _(eng):
                emit("act", eng)

            @block.vector
            def _(eng):
                emit("dve", eng)

            @block.gpsimd
            def _(eng):
                emit("pool", eng)

            @block.sync
            def _(eng):
                emit("sp", eng)
        self.ops = None


def _perm64():
    d = np.arange(64)
    return np.where((d % 32) < 16, d + 16, d - 16)


def _rel_bucket_np(rel):
    half = NB // 2
    max_exact = half // 2
    n = np.abs(rel)
    nf = np.maximum(n, max_exact).astype(np.float32)
    large = max_exact + (np.log(nf / np.float32(max_exact)) / np.float32(math.log(128 / max_exact))
                         * (half - max_exact)).astype(np.int32)
    large = np.minimum(large, half - 1)
    return np.where(rel > 0, half, 0) + np.where(n < max_exact, n, large)


def _rope_tables(pos):
    row = (pos // 64).astype(np.float32)
    col = (pos % 64).astype(np.float32)
    half = HD // 2
    inv = (np.float32(10000.0) ** (-np.arange(0, half, 2, dtype=np.float32) / np.float32(half))).astype(np.float32)
    ang_r = row[:, None] * inv[None, :]
    ang_c = col[:, None] * inv[None, :]
    ang = np.concatenate([ang_r, ang_r, ang_c, ang_c], axis=-1).astype(np.float32)
    cos = np.cos(ang).astype(np.float32)
    sin = np.sin(ang).astype(np.float32)
    d = np.arange(64)
    sign = np.where((d % 32) < 16, -1.0, 1.0).astype(np.float32)
    sin_s = sin * sign[None, :]
    cosT = np.ascontiguousarray(np.concatenate([cos.T, cos.T], axis=0))
    sinT = np.ascontiguousarray(np.concatenate([sin_s.T, sin_s.T], axis=0))
    return cosT, sinT


def _onehot(buckets):
    e = np.zeros((NB, len(buckets)), dtype=np.float32)
    e[buckets, np.arange(len(buckets))] = 1.0
    return e


def build_program(NP, NS, debug=False):
    jobs = [dict(name="P", N=NP, nq=NP // 4, b=0), dict(name="S", N=NS, nq=NS // 2, b=1)]
    for jb in jobs:
        assert jb["nq"] % 512 == 0 and jb["N"] % 512 == 0
        jb["NC"] = jb["N"] // 128
    nc = bass.Bass("TRN2", target_bir_lowering=False)

    def din(name, shape, dt=F32):
        return nc.dram_tensor(name, list(shape), dt, kind="ExternalInput").ap()

    def dscr(name, shape, dt):
        if debug and not name.startswith("wb_"):
            return nc.dram_tensor(name, list(shape), dt, kind="ExternalOutput").ap()
        return nc.dram_tensor(name, list(shape), dt).ap()

    x_in = {"P": din("xP", [NP, D]), "S": din("xS", [NS, D])}
    cT_d = din("cT", [128, 8, 2])
    w_ada_d = din("w_ada", [D, 6 * D])
    b_ada2_d = din("b_ada2", [2, 6 * D])
    grow_d = din("grow", [2, 4, D])
    w_in_d = din("w_in_p", [D, WIN])
    w_out_d = din("w_out", [D, D])
    w_gu_d = din("w_gu", [D, 2 * DFF])
    w_down_d = din("w_down", [DFF, D])
    gcols_d = din("gcols", [128, 8])
    lamv_d = din("lamv", [1, 4, 64])
    relb_d = din("relb", [NB, 4])
    rope_d = {jb["name"]: (din("cos" + jb["name"], [128, jb["N"]]), din("sin" + jb["name"], [128, jb["N"]])) for jb in jobs}
    cmat_d = din("cmat", [128, 5, 128], BF16)
    emain_d = din("emain", [NB, ULEN], BF16)
    ewrap_d = {jb["name"]: din("ewrap" + jb["name"], [NB, WLEN], BF16) for jb in jobs}
    ewrap2_d = {jb["name"]: din("ewrap2" + jb["name"], [NB, WLEN], BF16) for jb in jobs}
    efar_d = {jb["name"]: din("efar" + jb["name"], [NB, jb["NC"] + 1], BF16) for jb in jobs}
    y_out = {jb["name"]: nc.dram_tensor("y" + jb["name"], [jb["nq"], D], F32, kind="ExternalOutput").ap() for jb in jobs}

    wb_in = dscr("wb_in", [D, WIN], BF16)
    wb_out = dscr("wb_out", [D, D], BF16)
    wb_gu = dscr("wb_gu", [D, 2 * DFF], BF16)
    wb_down = dscr("wb_down", [DFF, D], BF16)
    rows_d = dscr("rows_d", [2, 6, D], F32)
    lam_d = dscr("lam_d", [1, 2], F32)
    ub_d = dscr("ub_d", [3, 4, ULEN], BF16)
    uw_d = {jb["name"]: dscr("uw_d" + jb["name"], [3, 4, WLEN], BF16) for jb in jobs}
    uw2_d = {jb["name"]: dscr("uw2_d" + jb["name"], [3, 4, WLEN], BF16) for jb in jobs}
    ufar_d = {jb["name"]: dscr("ufar_d" + jb["name"], [1, 4 * (jb["NC"] + 1)], F32) for jb in jobs}
    S = {}
    for jb in jobs:
        n, N, nq = jb["name"], jb["N"], jb["nq"]
        S[n] = dict(
            KTa=dscr("KTa" + n, [128, N], BF16), Va=dscr("Va" + n, [N, 130], BF16),
            KTd=dscr("KTd" + n, [4, 128, N], BF16), Vd=dscr("Vd" + n, [N, 512], BF16),
            QTa=dscr("QTa" + n, [4, 128, nq], BF16), QTd=dscr("QTd" + n, [4, 128, nq], BF16),
            outT=dscr("outT" + n, [D, nq], BF16), zrow=dscr("zrow" + n, [2, 2, 512], F32),
            x1=dscr("x1" + n, [nq, D], F32), h2T=dscr("h2T" + n, [8, 128, nq], BF16),
        )

    with contextlib.ExitStack() as gst:
        pg = Prog(nc, gst)
        def gsb(name, shape, dt):
            return gst.enter_context(nc.sbuf_tensor("g_" + name, list(shape), dt))

        cmat = gsb("cmat", [128, 5, 128], BF16)
        gcols = gsb("gcols", [128, 8], F32)
        negl = gsb("negl", [128, 2], F32)
        nhalf = gsb("nhalf", [128, 512], F32)
        zero1 = gsb("zero1", [128, 1], F32)
        farb = {jb["name"]: gsb("farb" + jb["name"], [128, 4, jb["NC"] + 1], F32) for jb in jobs}
        ps = gst.enter_context(nc.psum_tensor("psum_all", [128, 8, 512], F32))
        ident = cmat[:, 0, :]
        antiid = cmat[:, 1, :]
        onesb = cmat[:, 2, :]
        blk64 = cmat[:, 3, :]
        o128 = cmat[:, 4, :]

        def psb(i):
            return ps[:, i, :]

        def new_ps():
            return [Buf("ps%d" % i, excl=True) for i in range(8)]

        with contextlib.ExitStack() as st:
            def sb(name, shape, dt):
                return st.enter_context(nc.sbuf_tensor("s0_" + name, list(shape), dt))

            PB = new_ps()
            pg.begin()
            Bc = Buf("consts")
            pg.dma("sp", cmat[:], cmat_d, writes=[Bc])
            pg.dma("sp", gcols[:], gcols_d, writes=[Bc])
            pg.op("pool", lambda e: e.memset(nhalf[:], -0.5), writes=[Bc])
            pg.op("pool", lambda e: e.memset(zero1[:], 0.0), writes=[Bc])
            pg.op("dve", lambda e: e.tensor_scalar(out=gcols[:, 4:5], in0=gcols[:, 4:5], scalar1=1.0 - LAM_INIT, scalar2=None, op0=ALU.mult),
                  reads=[Bc], writes=[Bc])

            cT = sb("cT", [128, 8, 2], F32)
            scT = sb("scT", [128, 8, 2], F32)
            scTb = sb("scTb", [128, 8, 2], BF16)
            BcT, BscT = Buf("cT"), Buf("scT")
            pg.dma("sp", cT[:], cT_d, writes=[BcT])
            pg.op("act", lambda e: e.activation(out=scT[:], in_=cT[:], func=AF.Silu), reads=[BcT], writes=[BscT])
            pg.op("dve", lambda e: e.tensor_copy(scTb[:], scT[:]), reads=[BscT], writes=[BscT])

            mod = sb("mod", [2, 6 * D], F32)
            bada = sb("bada", [2, 6 * D], F32)
            Bmod, Bbada = Buf("mod"), Buf("bada")
            pg.dma("sp", bada[:], b_ada2_d, writes=[Bbada])
            waf = [sb("waf%d" % i, [128, 8, 512], F32) for i in range(2)]
            wab = [sb("wab%d" % i, [128, 8, 512], BF16) for i in range(2)]
            Bwaf = [Buf("waf%d" % i) for i in range(2)]
            Bwab = [Buf("wab%d" % i) for i in range(2)]
            for n in range(12):
                i = n % 2
                pg.dma("sp", waf[i][:], w_ada_d[:, n * 512:(n + 1) * 512].rearrange("(k p) n -> p k n", p=128), writes=[Bwaf[i]])
                ce = "dve" if n % 2 == 0 else "pool"
                pg.op(ce, lambda e, i=i: e.tensor_copy(wab[i][:], waf[i][:]), reads=[Bwaf[i]], writes=[Bwab[i]])

                def mm(e, i=i):
                    ins = None
                    for k in range(8):
                        ins = e.matmul(ps[0:2, i, :], lhsT=scTb[:, k, :], rhs=wab[i][:, k, :], start=(k == 0), stop=(k == 7))
                    return ins
                pg.op("pe", mm, reads=[Bwab[i], BscT], writes=[PB[i]])
                pg.op("dve", lambda e, i=i, n=n: e.tensor_tensor(out=mod[:, n * 512:(n + 1) * 512], in0=ps[0:2, i, :],
                                                               in1=bada[:, n * 512:(n + 1) * 512], op=ALU.add),
                      reads=[PB[i], Bbada], writes=[Bmod])
            grow = sb("grow", [2, 4, D], F32)
            rows = sb("rows", [2, 6, D], F32)
            Bgrow, Brows = Buf("grow"), Buf("rows")
            pg.dma("sp", grow[:], grow_d, writes=[Bgrow])
            pg.op("dve", lambda e: e.scalar_tensor_tensor(out=rows[:, 0, :], in0=mod[:, 1024:2048], scalar=1.0, in1=grow[:, 0, :],
                                                          op0=ALU.add, op1=ALU.mult), reads=[Bmod, Bgrow], writes=[Brows])
            pg.op("dve", lambda e: e.tensor_copy(rows[:, 1, :], mod[:, 0:1024]), reads=[Bmod], writes=[Brows])
            pg.op("dve", lambda e: e.tensor_tensor(out=rows[:, 2, :], in0=mod[:, 2048:3072], in1=grow[:, 1, :], op=ALU.mult),
                  reads=[Bmod, Bgrow], writes=[Brows])
            pg.op("dve", lambda e: e.scalar_tensor_tensor(out=rows[:, 3, :], in0=mod[:, 4096:5120], scalar=1.0, in1=grow[:, 2, :],
                                                          op0=ALU.add, op1=ALU.mult), reads=[Bmod, Bgrow], writes=[Brows])
            pg.op("dve", lambda e: e.tensor_copy(rows[:, 4, :], mod[:, 3072:4096]), reads=[Bmod], writes=[Brows])
            pg.op("dve", lambda e: e.tensor_tensor(out=rows[:, 5, :], in0=mod[:, 5120:6144], in1=grow[:, 3, :], op=ALU.mult),
                  reads=[Bmod, Bgrow], writes=[Brows])
            pg.dma("sp", rows_d, rows[:], reads=[Brows])

            lamv = sb("lamv", [1, 4, 64], F32)
            lj = sb("lj", [1, 64], F32)
            ls = sb("ls", [1, 4], F32)
            Blam = Buf("lam")
            pg.dma("sp", lamv[:], lamv_d, writes=[Blam])
            for t in range(2):
                pg.op("dve", lambda e, t=t: e.scalar_tensor_tensor(out=lj[:], in0=lamv[:, 2 * t, :], scalar=1.0, in1=lamv[:, 2 * t + 1, :],
                                                                 op0=ALU.mult, op1=ALU.mult, accum_out=ls[:, t:t + 1]),
                      reads=[Blam], writes=[Blam])
            pg.op("act", lambda e: e.activation(out=ls[:, 2:4], in_=ls[:, 0:2], func=AF.Exp), reads=[Blam], writes=[Blam])
            pg.op("dve", lambda e: e.scalar_tensor_tensor(out=ls[:, 1:2], in0=ls[:, 2:3], scalar=LAM_INIT, in1=ls[:, 3:4],
                                                          op0=ALU.add, op1=ALU.subtract), reads=[Blam], writes=[Blam])
            pg.op("dve", lambda e: e.tensor_scalar(out=ls[:, 0:1], in0=ls[:, 1:2], scalar1=-1.0, scalar2=None, op0=ALU.mult),
                  reads=[Blam], writes=[Blam])
            pg.dma("sp", lam_d, ls[:, 0:2], reads=[Blam])
            Bnegl = Buf("negl")
            pg.dma("sp", negl[:], lam_d.broadcast_to([128, 2]), reads=[Blam], writes=[Bnegl])

            rb = sb("rb", [NB, 4], F32)
            rbr = sb("rbr", [NB, 4], F32)
            rbp = [sb("rbp%d" % i, [NB, 4], BF16) for i in range(3)]
            Brb = Buf("rb")
            pg.dma("sp", rb[:], relb_d, writes=[Brb])
            pg.op("dve", lambda e: e.tensor_copy(rbp[0][:], rb[:]), reads=[Brb], writes=[Brb])
            pg.op("dve", lambda e: e.tensor_tensor(out=rbr[:], in0=rb[:], in1=rbp[0][:], op=ALU.subtract), reads=[Brb], writes=[Brb])
            pg.op("dve", lambda e: e.tensor_copy(rbp[1][:], rbr[:]), reads=[Brb], writes=[Brb])
            pg.op("dve", lambda e: e.tensor_tensor(out=rbr[:], in0=rbr[:], in1=rbp[1][:], op=ALU.subtract), reads=[Brb], writes=[Brb])
            pg.op("dve", lambda e: e.tensor_copy(rbp[2][:], rbr[:]), reads=[Brb], writes=[Brb])
            emat = sb("emat", [NB, ULEN], BF16)
            uout = sb("uout", [4, 3, ULEN], BF16)
            ufo = sb("ufo", [4, 132], F32)
            Bem, Buo = Buf("emat"), Buf("uout")
            specs = [("main", emain_d, ULEN, ub_d)]
            for jb in jobs:
                specs.append(("wrap", ewrap_d[jb["name"]], WLEN, uw_d[jb["name"]]))
                specs.append(("wrap", ewrap2_d[jb["name"]], WLEN, uw2_d[jb["name"]]))
            for jb in jobs:
                specs.append(("far", efar_d[jb["name"]], jb["NC"] + 1, ufar_d[jb["name"]]))
            pbank = 2
            for kind, esrc, L, dst in specs:
                pg.dma("sp", emat[:, 0:L], esrc, writes=[Bem])
                if kind != "far":
                    for part in range(3):
                        for s0 in range(0, L, 512):
                            w = min(512, L - s0)
                            bk = 2 + (pbank % 2)
                            pbank += 1
                            pg.op("pe", lambda e, bk=bk, part=part, s0=s0, w=w: e.matmul(ps[0:4, bk, 0:w], lhsT=rbp[part][:, :],
                                                                                       rhs=emat[:, s0:s0 + w], start=True, stop=True),
                                  reads=[Bem, Brb], writes=[PB[bk]])
                            pg.op("dve", lambda e, bk=bk, part=part, s0=s0, w=w: e.tensor_copy(uout[:, part, s0:s0 + w], ps[0:4, bk, 0:w]),
                                  reads=[PB[bk]], writes=[Buo])
                    pg.dma("sp", dst.rearrange("t h l -> h t l"), uout[:, :, 0:L], reads=[Buo])
                else:
                    bk = 2 + (pbank % 2)
                    pbank += 1

                    def mmf(e, bk=bk, L=L):
                        ins = None
                        for part in range(3):
                            ins = e.matmul(ps[0:4, bk, 0:L], lhsT=rbp[part][:, :], rhs=emat[:, 0:L], start=(part == 0), stop=(part == 2))
                        return ins
                    pg.op("pe", mmf, reads=[Bem, Brb], writes=[PB[bk]])
                    pg.op("dve", lambda e, bk=bk, L=L: e.tensor_copy(ufo[:, 0:L], ps[0:4, bk, 0:L]), reads=[PB[bk]], writes=[Buo])
                    pg.dma("sp", dst.rearrange("o (h l) -> (o h) l", h=4), ufo[:, 0:L], reads=[Buo])
            for jb in jobs:
                n = jb["name"]
                pg.dma("sp", farb[n][:].rearrange("p h l -> p (h l)"), ufar_d[n].broadcast_to([128, 4 * (jb["NC"] + 1)]),
                       reads=[Buo], writes=[Bc])

            pieces = []
            for k in range(8):
                pieces.append((w_in_d[k * 128:(k + 1) * 128, :], wb_in[k * 128:(k + 1) * 128, :], WIN))
            for k in range(8):
                pieces.append((w_out_d[k * 128:(k + 1) * 128, :], wb_out[k * 128:(k + 1) * 128, :], D))
            for k in range(8):
                for hh in range(2):
                    pieces.append((w_gu_d[k * 128:(k + 1) * 128, hh * DFF:(hh + 1) * DFF], wb_gu[k * 128:(k + 1) * 128, hh * DFF:(hh + 1) * DFF], DFF))
            for k in range(22):
                pieces.append((w_down_d[k * 128:(k + 1) * 128, :], wb_down[k * 128:(k + 1) * 128, :], D))
            NBUF = 2
            wcf = [sb("wcf%d" % i, [128, WIN], F32) for i in range(NBUF)]
            wcb = [sb("wcb%d" % i, [128, WIN], BF16) for i in range(NBUF)]
            Bwcf = [Buf("wcf%d" % i) for i in range(NBUF)]
            Bwcb = [Buf("wcb%d" % i) for i in range(NBUF)]
            for n, (src, dst, w) in enumerate(pieces):
                i = n % NBUF
                pg.dma("sp", wcf[i][:, 0:w], src, writes=[Bwcf[i]])
                ce = ["dve", "pool", "act"][n % 3]
                if ce == "act":
                    pg.op(ce, lambda e, i=i, w=w: e.activation(out=wcb[i][:, 0:w], in_=wcf[i][:, 0:w], func=AF.Copy), reads=[Bwcf[i]], writes=[Bwcb[i]])
                else:
                    pg.op(ce, lambda e, i=i, w=w: e.tensor_copy(wcb[i][:, 0:w], wcf[i][:, 0:w]), reads=[Bwcf[i]], writes=[Bwcb[i]])
                pg.dma("pool", dst, wcb[i][:, 0:w], reads=[Bwcb[i]])
            pg.end()

        for jb in jobs:
            name, N, nq = jb["name"], jb["N"], jb["nq"]
            sc = S[name]
            cos_d, sin_d = rope_d[name]
            with contextlib.ExitStack() as st:
                def sb(nm, shape, dt):
                    return st.enter_context(nc.sbuf_tensor("s1_" + nm + name, list(shape), dt))
                PB = new_ps()
                pg.begin()
                win = sb("win", [128, 8, WIN], BF16)
                Bwin = Buf("win")
                for k in range(8):
                    pg.dma("sp", win[:, k, :], wb_in[k * 128:(k + 1) * 128, :], writes=[Bwin])
                A1 = sb("A1", [128, D], F32)
                SH1 = sb("SH1", [128, D], F32)
                Bmodt = Buf("modt")
                pg.dma("sp", A1[:], rows_d[jb["b"], 0:1, :].broadcast_to([128, D]), writes=[Bmodt])
                pg.dma("sp", SH1[:], rows_d[jb["b"], 1:2, :].broadcast_to([128, D]), writes=[Bmodt])
                NXB = 2
                xt = [sb("xt%d" % i, [128, 4, D], F32) for i in range(NXB)]
                Bxt = [Buf("xt%d" % i) for i in range(NXB)]
                rt = [(sb("cos%d" % i, [128, 512], F32), sb("sin%d" % i, [128, 512], F32)) for i in range(2)]
                Brt = [Buf("rt%d" % i) for i in range(2)]
                junk = sb("junk", [128, D], F32)
                ss = sb("ss", [128, 4], F32)
                rstd = sb("rstd", [128, 4], F32)
                tt = [sb("tt%d" % i, [128, D], F32) for i in range(2)]
                hb = [sb("hb%d" % i, [128, D], BF16) for i in range(2)]
                hT = sb("hT", [128, 8, 512], BF16)
                Bjunk, Bss, Brstd = Buf("junk"), Buf("ss"), Buf("rstd")
                Btt = [Buf("tt%d" % i) for i in range(2)]
                Bhb = [Buf("hb%d" % i) for i in range(2)]
                BhT = [Buf("hT%d" % i) for i in range(4)]
                asb = sb("asb", [128, 512], F32)
                sq = sb("sq", [128, 512], F32)
                sqh = sb("sqh", [128, 512], BF16)
                sqm = sb("sqm", [128, 512], BF16)
                rs = sb("rs", [128, 512], F32)
                t1 = sb("t1", [128, 512], F32)
                t2 = sb("t2", [128, 512], F32)
                Basb, Bsq, Bsqh, Bsqm, Brs, Bt1, Bt2 = [Buf(x) for x in ["asb", "sq", "sqh", "sqm", "rs", "t1", "t2"]]
                kta = [sb("kta%d" % i, [128, 512], BF16) for i in range(2)]
                ktd = [sb("ktd%d" % i, [128, 4, 512], BF16) for i in range(2)]
                qta = [sb("qta%d" % i, [128, 4, 512], BF16) for i in range(2)]
                qtd = [sb("qtd%d" % i, [128, 4, 512], BF16) for i in range(2)]
                va = [sb("va%d" % i, [128, 4, 2, 65], BF16) for i in range(2)]
                vd = [sb("vd%d" % i, [128, 4, 512], BF16) for i in range(2)]
                Bkta = [Buf("kta%d" % i) for i in range(2)]
                Bktd = [Buf("ktd%d" % i) for i in range(2)]
                Bqta = [Buf("qta%d" % i) for i in range(2)]
                Bqtd = [Buf("qtd%d" % i) for i in range(2)]
                Bva = [Buf("va%d" % i) for i in range(2)]
                Bvd = [Buf("vd%d" % i) for i in range(2)]
                for i in range(2):
                    pg.op("pool", lambda e, i=i: e.memset(va[i][:], 1.0), writes=[Bva[i]])
                psT = [ps[:, i, :].bitcast(BF16) for i in range(2)]

                def normrope(pa, pb, gi, gpi, outap, scale, ri):
                    cs, sn = rt[ri]
                    pg.op("act", lambda e: e.activation(out=asb[:], in_=psb(pa), func=AF.Copy), reads=[PB[pa]], writes=[Basb])
                    pg.op("dve", lambda e: e.tensor_tensor(out=sq[:], in0=asb[:], in1=asb[:], op=ALU.mult), reads=[Basb], writes=[Bsq])
                    pg.op("dve", lambda e: e.tensor_copy(sqh[:], sq[:]), reads=[Bsq], writes=[Bsqh])
                    pg.op("pool", lambda e: e.tensor_tensor(out=sq[:], in0=sq[:], in1=sqh[:], op=ALU.subtract), reads=[Bsq, Bsqh], writes=[Bsq])
                    pg.op("pool", lambda e: e.tensor_copy(sqm[:], sq[:]), reads=[Bsq], writes=[Bsqm])

                    def mmss(e):
                        e.matmul(psb(7), lhsT=blk64, rhs=sqh[:], start=True, stop=False)
                        return e.matmul(psb(7), lhsT=blk64, rhs=sqm[:], start=False, stop=True)
                    pg.op("pe", mmss, reads=[Bsqh, Bsqm, Bc], writes=[PB[7]])
                    pg.op("dve", lambda e: e.tensor_scalar(out=rs[:], in0=psb(7), scalar1=EPS, scalar2=None, op0=ALU.add), reads=[PB[7]], writes=[Brs])
                    pg.op("pool", lambda e: e.tensor_tensor(out=rs[:], in0=rs[:], in1=nhalf[:], op=ALU.pow), reads=[Brs, Bc], writes=[Brs])
                    pg.op("dve", lambda e: e.scalar_tensor_tensor(out=t1[:], in0=asb[:], scalar=gcols[:, gi:gi + 1], in1=cs[:], op0=ALU.mult, op1=ALU.mult),
                          reads=[Basb, Bc, Brt[ri]], writes=[Bt1])
                    pg.op("dve", lambda e: e.scalar_tensor_tensor(out=t2[:], in0=psb(pb), scalar=gcols[:, gpi:gpi + 1], in1=sn[:], op0=ALU.mult, op1=ALU.mult),
                          reads=[PB[pb], Bc, Brt[ri]], writes=[Bt2])
                    pg.op("pool", lambda e: e.tensor_tensor(out=t1[:], in0=t1[:], in1=t2[:], op=ALU.add), reads=[Bt1, Bt2], writes=[Bt1])
                    return lambda e: e.scalar_tensor_tensor(out=outap, in0=t1[:], scalar=scale, in1=rs[:], op0=ALU.mult, op1=ALU.mult)

                def proj(bank, ch):
                    def f(e):
                        ins = None
                        for k in range(8):
                            ins = e.matmul(psb(bank), lhsT=win[:, k, ch * 128:(ch + 1) * 128], rhs=hT[:, k, :], start=(k == 0), stop=(k == 7))
                        return ins
                    pg.op("pe", f, reads=[Bwin] + BhT, writes=[PB[bank]])

                ntiles = N // 512
                for t in range(ntiles):
                    xi = t % NXB
                    own = (t * 512 < nq)
                    ti = t % 2
                    pg.dma("sp", xt[xi][:], x_in[name][t * 512:(t + 1) * 512, :].rearrange("(s p) d -> p s d", p=128), writes=[Bxt[xi]])
                    pg.dma("sp", rt[ti][0][:], cos_d[:, t * 512:(t + 1) * 512], writes=[Brt[ti]])
                    pg.dma("sp", rt[ti][1][:], sin_d[:, t * 512:(t + 1) * 512], writes=[Brt[ti]])
                    for s in range(4):
                        pg.op("dve", lambda e, s=s, xi=xi: e.scalar_tensor_tensor(out=junk[:], in0=xt[xi][:, s, :], scalar=1.0 / D, in1=xt[xi][:, s, :],
                                                                               op0=ALU.mult, op1=ALU.mult, accum_out=ss[:, s:s + 1]),
                              reads=[Bxt[xi]], writes=[Bjunk, Bss])
                    pg.op("dve", lambda e: e.tensor_scalar(out=rstd[:], in0=ss[:], scalar1=EPS, scalar2=None, op0=ALU.add), reads=[Bss], writes=[Brstd])
                    pg.op("pool", lambda e: e.tensor_tensor(out=rstd[:], in0=rstd[:], in1=nhalf[:, 0:4], op=ALU.pow), reads=[Brstd, Bc], writes=[Brstd])
                    for s in range(4):
                        i = s % 2
                        pg.op("dve", lambda e, s=s, i=i, xi=xi: e.scalar_tensor_tensor(out=tt[i][:], in0=xt[xi][:, s, :], scalar=rstd[:, s:s + 1], in1=A1[:],
                                                                                    op0=ALU.mult, op1=ALU.mult),
                              reads=[Bxt[xi], Brstd, Bmodt], writes=[Btt[i]])
                        pg.op("pool", lambda e, i=i: e.tensor_tensor(out=hb[i][:], in0=tt[i][:], in1=SH1[:], op=ALU.add),
                              reads=[Btt[i], Bmodt], writes=[Bhb[i]])

                        def tr(e, i=i):
                            ins = None
                            for k in range(8):
                                ins = e.transpose(out=psT[i][:, k * 128:(k + 1) * 128], in_=hb[i][:, k * 128:(k + 1) * 128], identity=ident)
                            return ins
                        pg.op("pe", tr, reads=[Bhb[i], Bc], writes=[PB[i]])
                        pg.op("act", lambda e, s=s, i=i: e.activation(out=hT[:, :, s * 128:(s + 1) * 128],
                                                                    in_=psT[i].rearrange("p (k t) -> p k t", k=8), func=AF.Copy),
                              reads=[PB[i]], writes=[BhT[s]])
                    proj(2, 8)
                    proj(3, 9)
                    fin = normrope(2, 3, 2, 3, kta[ti][:], 1.0, ti)
                    pg.op("dve", fin, reads=[Bt1, Brs], writes=[Bkta[ti]])
                    pg.dma("pool", sc["KTa"][:, t * 512:(t + 1) * 512], kta[ti][:], reads=[Bkta[ti]])
                    for h in range(4):
                        bk = 4 + (h % 2)
                        proj(bk, 14 + h)
                        ce = "act" if h % 2 == 0 else "dve"
                        if ce == "act":
                            pg.op("act", lambda e, h=h, bk=bk, ti=ti: e.activation(out=ktd[ti][:, h, :], in_=psb(bk), func=AF.Copy), reads=[PB[bk]], writes=[Bktd[ti]])
                        else:
                            pg.op("dve", lambda e, h=h, bk=bk, ti=ti: e.tensor_copy(ktd[ti][:, h, :], psb(bk)), reads=[PB[bk]], writes=[Bktd[ti]])
                    pg.dma("pool", sc["KTd"][:, :, t * 512:(t + 1) * 512].rearrange("h p t -> p h t"), ktd[ti][:], reads=[Bktd[ti]])
                    for s in range(4):
                        def mmv(e, s=s):
                            ins = None
                            for k in range(8):
                                ins = e.matmul(psb(6), lhsT=hT[:, k, s * 128:(s + 1) * 128], rhs=win[:, k, 2432:2944], start=(k == 0), stop=(k == 7))
                            return ins

                        def mmva(e, s=s):
                            ins = None
                            for k in range(8):
                                ins = e.matmul(ps[:, 5, 0:128], lhsT=hT[:, k, s * 128:(s + 1) * 128], rhs=win[:, k, 2304:2432], start=(k == 0), stop=(k == 7))
                            return ins
                        pg.op("pe", mmv, reads=[Bwin, BhT[s]], writes=[PB[6]])
                        pg.op("act", lambda e, s=s, ti=ti: e.activation(out=vd[ti][:, s, :], in_=psb(6), func=AF.Copy), reads=[PB[6]], writes=[Bvd[ti]])
                        pg.op("pe", mmva, reads=[Bwin, BhT[s]], writes=[PB[5]])
                        pg.op("dve", lambda e, s=s, ti=ti: e.tensor_copy(va[ti][:, s, :, 0:64], ps[:, 5, 0:128].rearrange("p (a b) -> p a b", a=2)),
                              reads=[PB[5]], writes=[Bva[ti]])
                    pg.dma("pool", sc["Vd"][t * 512:(t + 1) * 512, :].rearrange("(s p) c -> p s c", p=128), vd[ti][:], reads=[Bvd[ti]])
                    pg.dma("pool", sc["Va"][t * 512:(t + 1) * 512, :].rearrange("(s p) c -> p s c", p=128),
                           va[ti][:].rearrange("p s a b -> p s (a b)"), reads=[Bva[ti]])
                    if own:
                        for g in range(4):
                            proj(2, g)
                            proj(3, 4 + g)
                            fin = normrope(2, 3, 0, 1, qta[ti][:, g, :], 0.125, ti)
                            pg.op("dve", fin, reads=[Bt1, Brs], writes=[Bqta[ti]])
                        pg.dma("pool", sc["QTa"][:, :, t * 512:(t + 1) * 512].rearrange("g p t -> p g t"), qta[ti][:], reads=[Bqta[ti]])
                        for h in range(4):
                            bk = 4 + (h % 2)
                            proj(bk, 10 + h)
                            if h % 2 == 0:
                                pg.op("act", lambda e, h=h, bk=bk, ti=ti: e.activation(out=qtd[ti][:, h, :], in_=psb(bk), func=AF.Copy, scale=0.125),
                                      reads=[PB[bk]], writes=[Bqtd[ti]])
                            else:
                                pg.op("dve", lambda e, h=h, bk=bk, ti=ti: e.tensor_scalar(out=qtd[ti][:, h, :], in0=psb(bk), scalar1=0.125, scalar2=None, op0=ALU.mult),
                                      reads=[PB[bk]], writes=[Bqtd[ti]])
                        pg.dma("pool", sc["QTd"][:, :, t * 512:(t + 1) * 512].rearrange("h p t -> p h t"), qtd[ti][:], reads=[Bqtd[ti]])
                pg.end()

        for jb in jobs:
            name, N, nq, NC = jb["name"], jb["N"], jb["nq"], jb["NC"]
            sc = S[name]
            with contextlib.ExitStack() as st:
                def sb(nm, shape, dt):
                    return st.enter_context(nc.sbuf_tensor("s2_" + nm + name, list(shape), dt))
                PB = new_ps()
                pg.begin()
                KT = sb("KT", [128, N], BF16)
                VV = sb("VV", [128, NC, 130], BF16)
                QT = sb("QT", [128, 4, nq], BF16)
                NG = 8
                cpg = NC // NG
                BKV = [Buf("kv%d" % i) for i in range(NG)]
                BQ = Buf("QT")
                pT = [sb("pT%d" % i, [128, 1024], BF16) for i in range(2)]
                BpT = [Buf("pT%d" % i) for i in range(2)]
                osb = [sb("osb%d" % i, [128, 512], F32) for i in range(2)]
                Bosb = [Buf("osb%d" % i) for i in range(2)]
                zr = sb("zr", [128, 2, 512], F32)
                Bzr = Buf("zr")
                bcz = sb("bcz", [128, 2, 512], F32)
                Bbcz = Buf("bcz")
                onrm = sb("onrm", [128, 512], BF16)
                Bonrm = Buf("onrm")
                dd = sb("dd", [128, 512], F32)
                dsq = sb("dsq", [128, 512], F32)
                dsqh = sb("dsqh", [128, 512], BF16)
                dsqm = sb("dsqm", [128, 512], BF16)
                drs = sb("drs", [128, 512], F32)
                Bdd, Bdsq, Bdsqh, Bdsqm, Bdrs = [Buf(x) for x in ["dd", "dsq", "dsqh", "dsqm", "drs"]]
                TT = sb("TT", [128, 8, 512], F32)
                BTT = Buf("TT")
                hk = [sb("hk%d" % i, [128, 3, 512], BF16) for i in range(2)]
                Bhk = [Buf("hk%d" % i) for i in range(2)]
                zdram = sc["zrow"]
                Bzd = [Buf("zd0"), Buf("zd1")]

                def load_kv(kt_src, v_src, vw):
                    for gi in range(NG):
                        c0 = gi * cpg
                        pg.dma("sp", KT[:, c0 * 128:(c0 + cpg) * 128], kt_src[:, c0 * 128:(c0 + cpg) * 128], writes=[BKV[gi]])
                        pg.dma("sp", VV[:, c0:c0 + cpg, 0:vw], v_src[c0 * 128:(c0 + cpg) * 128, :].rearrange("(c p) w -> p c w", p=128), writes=[BKV[gi]])

                def attn_tile(units, lhs_v, near, bias_of, epilogue):
                    def qk(c):
                        par = c % 2

                        def f(e):
                            ins = None
                            for u in range(2):
                                ins = e.matmul(psb(2 * par + u), lhsT=KT[64 * u:64 * u + 64, c * 128:(c + 1) * 128], rhs=units[u], start=True, stop=True)
                            return ins
                        pg.op("pe", f, reads=[BKV[c // cpg], BQ], writes=[PB[2 * par], PB[2 * par + 1]])
                        if c in near:
                            ti = near[c]
                            pg.op("dve", lambda e: e.tensor_tensor(out=ps[:, 2 * par:2 * par + 2, :], in0=ps[:, 2 * par:2 * par + 2, :],
                                                                 in1=TT[:, ti:ti + 1, :].to_broadcast([128, 2, 512]), op=ALU.add),
                                  reads=[BTT], writes=[PB[2 * par], PB[2 * par + 1]])

                    def ex(c):
                        par = c % 2
                        b = bias_of(c)
                        pg.op("act", lambda e: e.activation(out=pT[par][:], in_=ps[:, 2 * par:2 * par + 2, :].rearrange("p a b -> p (a b)"),
                                                          func=AF.Exp, bias=(b if b is not None else zero1[:])),
                              reads=[PB[2 * par], PB[2 * par + 1], Bc], writes=[BpT[par]])

                    def pv(c):
                        par = c % 2
                        dm = diff_mode[0]

                        def f(e):
                            ins = None
                            for u in range(2):
                                lv, m = lhs_v(u, c)
                                ins = e.matmul(ps[0:m, 4 + u, :], lhsT=lv, rhs=pT[par][:, u * 512:(u + 1) * 512], start=(c == 0), stop=(c == NC - 1))
                            if dm:
                                for u in range(2):
                                    ins = e.matmul(ps[32 * u:32 * u + 1, 6, :], lhsT=onesb[:, 0:1], rhs=pT[par][:, u * 512:(u + 1) * 512],
                                                   start=(c == 0), stop=(c == NC - 1), skip_group_check=True)
                            return ins
                        w = [PB[4], PB[5]] + ([PB[6]] if diff_mode[0] else [])
                        pg.op("pe", f, reads=[BKV[c // cpg], BpT[par], Bc], writes=w)
                    qk(0)
                    for c in range(NC):
                        if c + 1 < NC:
                            qk(c + 1)
                        ex(c)
                        pv(c)
                    epilogue()

                diff_mode = [False]
                load_kv(sc["KTa"], sc["Va"], 130)
                for g in range(4):
                    pg.dma("sp", QT[:, g, :], sc["QTa"][g], writes=[BQ])
                for qt in range(nq // 128):
                    units = [QT[64 * u:64 * u + 64, :, qt * 128:(qt + 1) * 128] for u in range(2)]

                    def lhs_v(u, c):
                        return VV[:, c, u * 65:(u + 1) * 65], 65

                    def epi(qt=qt):
                        for u in range(2):
                            pg.op("dve", lambda e, u=u: e.tensor_copy(osb[u][0:65, :], ps[0:65, 4 + u, :]), reads=[PB[4 + u]], writes=[Bosb[u]])
                            pg.op("dve", lambda e, u=u: e.reciprocal(out=zr[64:65, u, :], in_=osb[u][64:65, :]), reads=[Bosb[u]], writes=[Bzr])
                            pg.dma("pool", zdram[u, 0:1, :], zr[64:65, u, :], reads=[Bzr], writes=[Bzd[u]])
                            pg.dma("pool", bcz[0:64, u, :], zdram[u, 0:1, :].broadcast_to([64, 512]), reads=[Bzd[u]], writes=[Bbcz])
                            pg.op("dve", lambda e, u=u: e.tensor_tensor(out=onrm[0:64, :], in0=osb[u][0:64, :], in1=bcz[0:64, u, :], op=ALU.mult),
                                  reads=[Bosb[u], Bbcz], writes=[Bonrm])
                            pg.dma("pool", sc["outT"][u * 256:(u + 1) * 256, qt * 128:(qt + 1) * 128].rearrange("(g d) t -> d g t", g=4),
                                   onrm[0:64, :].rearrange("d (g t) -> d g t", g=4), reads=[Bonrm])
                    attn_tile(units, lhs_v, {}, lambda c: None, epi)
                diff_mode[0] = True
                for h in range(4):
                    load_kv(sc["KTd"][h], sc["Vd"][:, h * 128:(h + 1) * 128], 128)
                    pg.dma("sp", QT[:, 0, :], sc["QTd"][h], writes=[BQ])
                    for ti in range(8):
                        i = ti % 2
                        if ti < 6:
                            dofs = (ti - 1) * 128
                            base = 512 - dofs
                            for part in range(3):
                                src = bass.AP(ub_d.tensor, ub_d[part, h, base:base + 1].offset, [[1, 128], [1, 512]])
                                pg.dma("sp", hk[i][:, part, :], src, writes=[Bhk[i]])
                        else:
                            uw = uw_d[name] if ti == 6 else uw2_d[name]
                            for part in range(3):
                                src = bass.AP(uw.tensor, uw[part, h, 0:1].offset, [[1, 128], [1, 512]])
                                pg.dma("sp", hk[i][:, part, :], src, writes=[Bhk[i]])

                        def mmT(e, i=i):
                            ins = None
                            for part in range(3):
                                ins = e.matmul(psb(7), lhsT=antiid, rhs=hk[i][:, part, :], start=(part == 0), stop=(part == 2))
                            return ins
                        pg.op("pe", mmT, reads=[Bhk[i], Bc], writes=[PB[7]])
                        pg.op("dve", lambda e, ti=ti: e.tensor_copy(TT[:, ti, :], psb(7)), reads=[PB[7]], writes=[BTT])
                    for qt in range(nq // 512):
                        units = [QT[64 * u:64 * u + 64, 0, qt * 512:(qt + 1) * 512] for u in range(2)]
                        c0 = qt * 4
                        near = {}
                        for ti in range(6):
                            c = c0 - 1 + ti
                            if 0 <= c < NC:
                                near[c] = ti
                        if qt == 0:
                            near[NC - 1] = 6
                        if qt == nq // 512 - 1:
                            near[nq // 128] = 7
                        fb = farb[name]

                        def bias_of(c, near=near, c0=c0, h=h, fb=fb):
                            if c in near:
                                return None
                            if c < c0:
                                return fb[:, h, NC:NC + 1]
                            return fb[:, h, c:c + 1]

                        def lhs_v(u, c):
                            return VV[:, c, 0:128], 128

                        def epi(qt=qt, h=h):
                            for u in range(2):
                                pg.op("dve", lambda e, u=u: e.tensor_copy(osb[u][:], psb(4 + u)), reads=[PB[4 + u]], writes=[Bosb[u]])
                                pg.op("dve", lambda e, u=u: e.reciprocal(out=zr[32 * u:32 * u + 1, u, :], in_=ps[32 * u:32 * u + 1, 6, :]), reads=[PB[6]], writes=[Bzr])
                                pg.dma("pool", zdram[u, 1:2, :], zr[32 * u:32 * u + 1, u, :], reads=[Bzr], writes=[Bzd[u]])
                                pg.dma("pool", bcz[:, u, :], zdram[u, 1:2, :].broadcast_to([128, 512]), reads=[Bzd[u]], writes=[Bbcz])
                            pg.op("dve", lambda e: e.tensor_tensor(out=osb[0][:], in0=osb[0][:], in1=bcz[:, 0, :], op=ALU.mult), reads=[Bosb[0], Bbcz], writes=[Bosb[0]])
                            pg.op("pool", lambda e: e.tensor_tensor(out=osb[1][:], in0=osb[1][:], in1=bcz[:, 1, :], op=ALU.mult), reads=[Bosb[1], Bbcz], writes=[Bosb[1]])
                            pg.op("dve", lambda e: e.scalar_tensor_tensor(out=dd[:], in0=osb[1][:], scalar=negl[:, 0:1], in1=osb[0][:], op0=ALU.mult, op1=ALU.add),
                                  reads=[Bosb[0], Bosb[1], Bnegl], writes=[Bdd])
                            pg.op("pool", lambda e: e.tensor_tensor(out=dsq[:], in0=dd[:], in1=dd[:], op=ALU.mult), reads=[Bdd], writes=[Bdsq])
                            pg.op("dve", lambda e: e.tensor_copy(dsqh[:], dsq[:]), reads=[Bdsq], writes=[Bdsqh])
                            pg.op("pool", lambda e: e.tensor_tensor(out=dsq[:], in0=dsq[:], in1=dsqh[:], op=ALU.subtract), reads=[Bdsq, Bdsqh], writes=[Bdsq])
                            pg.op("pool", lambda e: e.tensor_copy(dsqm[:], dsq[:]), reads=[Bdsq], writes=[Bdsqm])

                            def mmss(e):
                                e.matmul(psb(7), lhsT=o128, rhs=dsqh[:], start=True, stop=False)
                                return e.matmul(psb(7), lhsT=o128, rhs=dsqm[:], start=False, stop=True)
                            pg.op("pe", mmss, reads=[Bdsqh, Bdsqm, Bc], writes=[PB[7]])
                            pg.op("dve", lambda e: e.tensor_scalar(out=drs[:], in0=psb(7), scalar1=EPS, scalar2=None, op0=ALU.add), reads=[PB[7]], writes=[Bdrs])
                            pg.op("pool", lambda e: e.tensor_tensor(out=drs[:], in0=drs[:], in1=nhalf[:], op=ALU.pow), reads=[Bdrs, Bc], writes=[Bdrs])
                            pg.op("dve", lambda e: e.scalar_tensor_tensor(out=onrm[:], in0=dd[:], scalar=gcols[:, 4:5], in1=drs[:], op0=ALU.mult, op1=ALU.mult),
                                  reads=[Bdd, Bdrs, Bc], writes=[Bonrm])
                            pg.dma("pool", sc["outT"][512 + h * 128:512 + (h + 1) * 128, qt * 512:(qt + 1) * 512], onrm[:], reads=[Bonrm])
                        attn_tile(units, lhs_v, near, bias_of, epi)
                pg.end()

        TK = 256
        for jb in jobs:
            name, N, nq = jb["name"], jb["N"], jb["nq"]
            sc = S[name]
            with contextlib.ExitStack() as st:
                def sb(nm, shape, dt):
                    return st.enter_context(nc.sbuf_tensor("s3_" + nm + name, list(shape), dt))
                PB = new_ps()
                pg.begin()
                wo = sb("wo", [128, 8, D], BF16)
                Bw = Buf("w3")
                for k in range(8):
                    pg.dma("sp", wo[:, k, :], wb_out[k * 128:(k + 1) * 128, :], writes=[Bw])
                G1 = sb("G1", [128, D], F32)
                A2 = sb("A2", [128, D], F32)
                SH2 = sb("SH2", [128, D], F32)
                Bmodt = Buf("modt")
                for tile_, ri in ((G1, 2), (A2, 3), (SH2, 4)):
                    pg.dma("sp", tile_[:], rows_d[jb["b"], ri:ri + 1, :].broadcast_to([128, D]), writes=[Bmodt])
                xt = [sb("xt%d" % i, [128, 2, D], F32) for i in range(2)]
                Bxt = [Buf("xt%d" % i) for i in range(2)]
                oT = [sb("oT%d" % i, [128, 8, TK], BF16) for i in range(2)]
                BoT = [Buf("oT%d" % i) for i in range(2)]
                junk = sb("junk", [128, D], F32)
                ss = sb("ss", [128, 4], F32)
                rstd = sb("rstd", [128, 4], F32)
                mixs = [sb("mixs%d" % i, [128, D], F32) for i in range(2)]
                tt = [sb("tt%d" % i, [128, D], F32) for i in range(2)]
                hb = [sb("hb%d" % i, [128, D], BF16) for i in range(2)]
                hT = [sb("hT%d" % i, [128, 8, TK], BF16) for i in range(2)]
                Bjunk = Buf("junk")
                Bss = [Buf("ss%d" % i) for i in range(4)]
                Bmixs = [Buf("mixs%d" % i) for i in range(2)]
                Btt = [Buf("tt%d" % i) for i in range(2)]
                Bhb = [Buf("hb%d" % i) for i in range(2)]
                BhT = [Buf("hT%d" % i) for i in range(2)]
                psT = [ps[:, i, :].bitcast(BF16) for i in range(2)]

                def rms_stat(src_ap, src_bufs, col):
                    pg.op("dve", lambda e: e.scalar_tensor_tensor(out=junk[:], in0=src_ap, scalar=1.0 / D, in1=src_ap, op0=ALU.mult, op1=ALU.mult,
                                                                  accum_out=ss[:, col:col + 1]), reads=src_bufs, writes=[Bjunk, Bss[col]])
                    pg.op("dve", lambda e: e.tensor_scalar(out=rstd[:, col:col + 1], in0=ss[:, col:col + 1], scalar1=EPS, scalar2=None, op0=ALU.add),
                          reads=[Bss[col]], writes=[Bss[col]])
                    pg.op("pool", lambda e: e.tensor_tensor(out=rstd[:, col:col + 1], in0=rstd[:, col:col + 1], in1=nhalf[:, 0:1], op=ALU.pow),
                          reads=[Bss[col], Bc], writes=[Bss[col]])

                for t in range(nq // TK):
                    xi = t % 2
                    pg.dma("sp", xt[xi][:], x_in[name][t * TK:(t + 1) * TK, :].rearrange("(s p) d -> p s d", p=128), writes=[Bxt[xi]])
                    pg.dma("sp", oT[xi][:], sc["outT"][:, t * TK:(t + 1) * TK].rearrange("(k p) t -> p k t", p=128), writes=[BoT[xi]])
                    for s in range(2):
                        def mmo(e, s=s, xi=xi):
                            ins = None
                            for hh in range(2):
                                for k in range(8):
                                    ins = e.matmul(psb(2 + 2 * s + hh), lhsT=oT[xi][:, k, s * 128:(s + 1) * 128], rhs=wo[:, k, hh * 512:(hh + 1) * 512],
                                                   start=(k == 0), stop=(k == 7))
                            return ins
                        pg.op("pe", mmo, reads=[BoT[xi], Bw], writes=[PB[2 + 2 * s], PB[3 + 2 * s]])
                        pg.op("act", lambda e, s=s: e.activation(out=mixs[s][:], in_=ps[:, 2 + 2 * s:4 + 2 * s, :].rearrange("p a b -> p (a b)"), func=AF.Copy),
                              reads=[PB[2 + 2 * s], PB[3 + 2 * s]], writes=[Bmixs[s]])
                        rms_stat(mixs[s][:], [Bmixs[s]], s)
                        pg.op("dve", lambda e, s=s: e.scalar_tensor_tensor(out=tt[s][:], in0=mixs[s][:], scalar=rstd[:, s:s + 1], in1=G1[:], op0=ALU.mult, op1=ALU.mult),
                              reads=[Bmixs[s], Bss[s], Bmodt], writes=[Btt[s]])
                        pg.op("pool", lambda e, s=s, xi=xi: e.tensor_tensor(out=xt[xi][:, s, :], in0=xt[xi][:, s, :], in1=tt[s][:], op=ALU.add),
                              reads=[Btt[s], Bxt[xi]], writes=[Bxt[xi]])
                        rms_stat(xt[xi][:, s, :], [Bxt[xi]], 2 + s)
                        pg.op("dve", lambda e, s=s, xi=xi: e.scalar_tensor_tensor(out=tt[s][:], in0=xt[xi][:, s, :], scalar=rstd[:, 2 + s:3 + s], in1=A2[:], op0=ALU.mult, op1=ALU.mult),
                              reads=[Bxt[xi], Bss[2 + s], Bmodt], writes=[Btt[s]])
                        pg.op("pool", lambda e, s=s: e.tensor_tensor(out=hb[s][:], in0=tt[s][:], in1=SH2[:], op=ALU.add), reads=[Btt[s], Bmodt], writes=[Bhb[s]])

                        def tr(e, s=s):
                            ins = None
                            for k in range(8):
                                ins = e.transpose(out=psT[s][:, k * 128:(k + 1) * 128], in_=hb[s][:, k * 128:(k + 1) * 128], identity=ident)
                            return ins
                        pg.op("pe", tr, reads=[Bhb[s], Bc], writes=[PB[s]])
                        pg.op("act", lambda e, s=s, xi=xi: e.activation(out=hT[xi][:, :, s * 128:(s + 1) * 128], in_=psT[s].rearrange("p (k t) -> p k t", k=8), func=AF.Copy),
                              reads=[PB[s]], writes=[BhT[xi]])
                    pg.dma("pool", sc["x1"][t * TK:(t + 1) * TK, :].rearrange("(s p) d -> p s d", p=128), xt[xi][:], reads=[Bxt[xi]])
                    pg.dma("pool", sc["h2T"][:, :, t * TK:(t + 1) * TK].rearrange("k p t -> p k t"), hT[xi][:], reads=[BhT[xi]])
                pg.end()

        for jb in jobs:
            name, N, nq = jb["name"], jb["N"], jb["nq"]
            sc = S[name]
            with contextlib.ExitStack() as st:
                def sb(nm, shape, dt):
                    return st.enter_context(nc.sbuf_tensor("s4_" + nm + name, list(shape), dt))
                PB = new_ps()
                pg.begin()
                wgu = sb("wgu", [128, 8, 2 * DFF], BF16)
                wdn = sb("wdn", [128, 22, D], BF16)
                Bw = Buf("w3")
                for k in range(8):
                    pg.dma("sp", wgu[:, k, :], wb_gu[k * 128:(k + 1) * 128, :], writes=[Bw])
                pg.dma("sp", wdn[:], wb_down.rearrange("(k p) n -> p k n", p=128), writes=[Bw])
                G2 = sb("G2", [128, D], F32)
                Bmodt = Buf("modt")
                pg.dma("sp", G2[:], rows_d[jb["b"], 5:6, :].broadcast_to([128, D]), writes=[Bmodt])
                xt = [sb("xt%d" % i, [128, 2, D], F32) for i in range(2)]
                Bxt = [Buf("xt%d" % i) for i in range(2)]
                hT = [sb("hT%d" % i, [128, 8, TK], BF16) for i in range(2)]
                BhT = [Buf("hT%d" % i) for i in range(2)]
                junk = sb("junk", [128, D], F32)
                ss = sb("ss", [128, 2], F32)
                rstd = sb("rstd", [128, 2], F32)
                act_ = sb("act", [128, 22, TK], BF16)
                sg = [sb("sg%d" % i, [128, TK], F32) for i in range(2)]
                fs = [sb("fs%d" % i, [128, D], F32) for i in range(2)]
                tt = [sb("tt%d" % i, [128, D], F32) for i in range(2)]
                Bjunk = Buf("junk")
                Bss = [Buf("ss%d" % i) for i in range(2)]
                Bfs = [Buf("fs%d" % i) for i in range(2)]
                Btt = [Buf("tt%d" % i) for i in range(2)]
                Bact = [Buf("act%d" % i) for i in range(22)]
                Bsg = [Buf("sg%d" % i) for i in range(2)]
                for t in range(nq // TK):
                    xi = t % 2
                    pg.dma("sp", xt[xi][:], sc["x1"][t * TK:(t + 1) * TK, :].rearrange("(s p) d -> p s d", p=128), writes=[Bxt[xi]])
                    pg.dma("sp", hT[xi][:], sc["h2T"][:, :, t * TK:(t + 1) * TK].rearrange("k p t -> p k t"), writes=[BhT[xi]])
                    for j in range(22):
                        i = j % 2

                        def mmg(e, j=j, i=i, xi=xi):
                            ins = None
                            for k in range(8):
                                ins = e.matmul(ps[:, 2 * i, 0:TK], lhsT=wgu[:, k, j * 128:(j + 1) * 128], rhs=hT[xi][:, k, :], start=(k == 0), stop=(k == 7))
                            for k in range(8):
                                ins = e.matmul(ps[:, 2 * i + 1, 0:TK], lhsT=wgu[:, k, DFF + j * 128:DFF + (j + 1) * 128], rhs=hT[xi][:, k, :], start=(k == 0), stop=(k == 7))
                            return ins
                        pg.op("pe", mmg, reads=[Bw, BhT[xi]], writes=[PB[2 * i], PB[2 * i + 1]])
                        pg.op("act", lambda e, i=i: e.activation(out=sg[i][:], in_=ps[:, 2 * i, 0:TK], func=AF.Silu), reads=[PB[2 * i]], writes=[Bsg[i]])
                        pg.op("dve", lambda e, i=i, j=j: e.tensor_tensor(out=act_[:, j, :], in0=sg[i][:], in1=ps[:, 2 * i + 1, 0:TK], op=ALU.mult),
                              reads=[Bsg[i], PB[2 * i + 1]], writes=[Bact[j]])
                    for s in range(2):
                        def mmd(e, s=s):
                            ins = None
                            for hh in range(2):
                                for j in range(22):
                                    ins = e.matmul(psb(4 + 2 * s + hh), lhsT=act_[:, j, s * 128:(s + 1) * 128], rhs=wdn[:, j, hh * 512:(hh + 1) * 512],
                                                   start=(j == 0), stop=(j == 21))
                            return ins
                        pg.op("pe", mmd, reads=[Bw] + Bact, writes=[PB[4 + 2 * s], PB[5 + 2 * s]])
                        pg.op("act", lambda e, s=s: e.activation(out=fs[s][:], in_=ps[:, 4 + 2 * s:6 + 2 * s, :].rearrange("p a b -> p (a b)"), func=AF.Copy),
                              reads=[PB[4 + 2 * s], PB[5 + 2 * s]], writes=[Bfs[s]])
                        pg.op("dve", lambda e, s=s: e.scalar_tensor_tensor(out=junk[:], in0=fs[s][:], scalar=1.0 / D, in1=fs[s][:], op0=ALU.mult, op1=ALU.mult,
                                                                         accum_out=ss[:, s:s + 1]), reads=[Bfs[s]], writes=[Bjunk, Bss[s]])
                        pg.op("dve", lambda e, s=s: e.tensor_scalar(out=rstd[:, s:s + 1], in0=ss[:, s:s + 1], scalar1=EPS, scalar2=None, op0=ALU.add),
                              reads=[Bss[s]], writes=[Bss[s]])
                        pg.op("pool", lambda e, s=s: e.tensor_tensor(out=rstd[:, s:s + 1], in0=rstd[:, s:s + 1], in1=nhalf[:, 0:1], op=ALU.pow),
                              reads=[Bss[s], Bc], writes=[Bss[s]])
                        pg.op("dve", lambda e, s=s: e.scalar_tensor_tensor(out=tt[s][:], in0=fs[s][:], scalar=rstd[:, s:s + 1], in1=G2[:], op0=ALU.mult, op1=ALU.mult),
                              reads=[Bfs[s], Bss[s], Bmodt], writes=[Btt[s]])
                        pg.op("pool", lambda e, s=s, xi=xi: e.tensor_tensor(out=xt[xi][:, s, :], in0=xt[xi][:, s, :], in1=tt[s][:], op=ALU.add),
                              reads=[Btt[s], Bxt[xi]], writes=[Bxt[xi]])
                    pg.dma("pool", y_out[name][t * TK:(t + 1) * TK, :].rearrange("(s p) d -> p s d", p=128), xt[xi][:], reads=[Bxt[xi]])
                pg.end()
    return nc


def _prep_shared(inp, NP, NS):
    f = lambda a: np.ascontiguousarray(np.asarray(a, dtype=np.float32))
    perm = _perm64()
    w_in = f(inp["w_in"])[0]
    o1, o2, o3, o4, o5 = 512, 640, 768, 1280, 1792
    cols = []
    qa_nat = np.array([[(kv * 4 + g) * 64 + d for kv in range(2) for d in range(64)] for g in range(4)])
    qa_prm = np.array([[(kv * 4 + g) * 64 + perm[d] for kv in range(2) for d in range(64)] for g in range(4)])
    cols += list(qa_nat.reshape(-1)) + list(qa_prm.reshape(-1))
    cols += [o1 + kv * 64 + d for kv in range(2) for d in range(64)]
    cols += [o1 + kv * 64 + perm[d] for kv in range(2) for d in range(64)]
    cols += list(range(o3, o4)) + list(range(o4, o5)) + list(range(o2, o3)) + list(range(o5, 2304))
    cols = np.array(cols)
    assert len(cols) == WIN
    g_q, g_k = f(inp["g_q"])[0], f(inp["g_k"])[0]
    gcols = np.zeros((128, 8), np.float32)
    gcols[:, 0] = np.tile(g_q, 2)
    gcols[:, 1] = np.tile(g_q[perm], 2)
    gcols[:, 2] = np.tile(g_k, 2)
    gcols[:, 3] = np.tile(g_k[perm], 2)
    gcols[:, 4] = f(inp["g_subln"])[0]
    gcols[:, 5] = 1.0 - LAM_INIT
    grow = np.stack([f(inp["g_pre_mix"])[0], f(inp["g_post_mix"])[0], f(inp["g_pre_ffn"])[0], f(inp["g_post_ffn"])[0]])
    grow2 = np.ascontiguousarray(np.stack([grow, grow]))
    lamv = np.stack([f(inp["lam_q1"])[0], f(inp["lam_k1"])[0], f(inp["lam_q2"])[0], f(inp["lam_k2"])[0]])[None]
    b_ada = f(inp["b_ada"])
    cm = np.zeros((128, 5, 128), np.float32)
    cm[:, 0, :] = np.eye(128)
    cm[:, 1, :] = np.eye(128)[::-1]
    cm[:, 2, :] = 1.0
    cm[0:64, 3, 0:64] = 1.0 / 64
    cm[64:128, 3, 64:128] = 1.0 / 64
    cm[:, 4, :] = 1.0 / 128
    m = np.arange(ULEN)
    emain = _onehot(_rel_bucket_np(639 - m))
    sh = dict(
        w_ada=f(inp["w_ada"])[0], b_ada2=np.ascontiguousarray(np.concatenate([b_ada, b_ada], 0)), grow=grow2,
        w_in_p=np.ascontiguousarray(w_in[:, cols]), w_out=f(inp["w_out"])[0], w_gu=f(inp["w_gu"])[0], w_down=f(inp["w_down"])[0],
        gcols=gcols, lamv=np.ascontiguousarray(lamv), relb=f(inp["rel_bias"]),
        cmat=cm.astype(ml_dtypes.bfloat16), emain=emain.astype(ml_dtypes.bfloat16),
    )
    return sh


def _prep_core(inp, sh, c, NP, NS):
    f = lambda a: np.asarray(a, dtype=np.float32)
    pb, pq, sbi, sq = c // 4, c % 4, c // 2, c % 2
    m = dict(sh)
    cp, cs = f(inp["c_prompt"])[pb], f(inp["c_sample"])[sbi]
    cT = np.stack([cp, cs], -1).reshape(8, 128, 2).transpose(1, 0, 2)
    m["cT"] = np.ascontiguousarray(cT)
    for nm, x, N, nq, qi in (("P", f(inp["x_prompt"])[pb], NP, NP // 4, pq), ("S", f(inp["x_sample"])[sbi], NS, NS // 2, sq)):
        qoff = qi * nq
        m["x" + nm] = np.ascontiguousarray(np.roll(x, -qoff, axis=0))
        pos = (np.arange(N) + qoff) % N
        cosT, sinT = _rope_tables(pos)
        m["cos" + nm] = cosT
        m["sin" + nm] = sinT
        NC = N // 128
        mm = np.arange(WLEN)
        if qoff > 0:
            bw = _rel_bucket_np(-1 - mm)
        else:
            bw = np.full(WLEN, NB // 2 + NB // 2 - 1)
        m["ewrap" + nm] = _onehot(bw).astype(ml_dtypes.bfloat16)
        if qoff + nq == N:
            bw2 = np.full(WLEN, NB // 2 - 1)
        else:
            bw2 = _rel_bucket_np(639 - mm)
        m["ewrap2" + nm] = _onehot(bw2).astype(ml_dtypes.bfloat16)
        far = np.zeros(NC + 1, np.int64)
        for ch in range(NC):
            if ch * 128 < nq:
                far[ch] = 31
            else:
                far[ch] = 31 if ch * 128 < N - qoff else 15
        far[NC] = 15
        m["efar" + nm] = _onehot(far).astype(ml_dtypes.bfloat16)
    return m


_CACHE = {}


def run(inputs, NP, NS, debug=False, ncores=8):
    key = (NP, NS, debug)
    if key not in _CACHE:
        _CACHE[key] = build_program(NP, NS, debug)
    nc = _CACHE[key]
    sh = _prep_shared(inputs, NP, NS)
    in_maps = [_prep_core(inputs, sh, c, NP, NS) for c in range(ncores)]
    res = run_bass_kernel_spmd(nc, in_maps, core_ids=list(range(ncores)))
    return res.results


def kernel(**inputs):
    NP = int(np.asarray(inputs["x_prompt"]).shape[1])
    NS = int(np.asarray(inputs["x_sample"]).shape[1])
    r = run(inputs, NP, NS)
    yp = np.zeros((2, NP, D), np.float32)
    ys = np.zeros((4, NS, D), np.float32)
    for c in range(8):
        pb, pq, sbi, sq = c // 4, c % 4, c // 2, c % 2
        nqp, nqs = NP // 4, NS // 2
        yp[pb, pq * nqp:(pq + 1) * nqp] = r[c]["yP"]
        ys[sbi, sq * nqs:(sq + 1) * nqs] = r[c]["yS"]
    return (yp, ys)
```

```python
import contextlib
import math
import numpy as np
import ml_dtypes
import concourse.bass as bass
import concourse.mybir as mybir
from concourse.bass_utils import run_bass_kernel_spmd

F32 = mybir.dt.float32
BF16 = mybir.dt.bfloat16
ALU = mybir.AluOpType
AF = mybir.ActivationFunctionType
AX = mybir.AxisListType

D = 1024
DFF = 2816
HD = 64
EPS = 1e-6
NB = 32
WIN = 2944
LAM_INIT = 0.8 - 0.6 * math.exp(-0.3 * 0)
ULEN = 1279
WLEN = 639

ENGS = ["pe", "act", "dve", "pool", "sp"]
N_DMA_SEMS = 6


class Buf:
    __slots__ = ("name", "w", "r", "excl")

    def __init__(self, name, excl=False):
        self.name = name
        self.excl = excl
        self.w = None
        self.r = []


class Op:
    __slots__ = ("eng", "fn", "waits", "signal", "ev", "is_dma")

    def __init__(self, eng, fn, is_dma):
        self.eng = eng
        self.fn = fn
        self.waits = []
        self.signal = False
        self.ev = None
        self.is_dma = is_dma


class Prog:
    def __init__(self, nc, stack):
        self.nc = nc
        self.sems = {}
        self.cnt = {}
        for e in ENGS:
            self.sems[e] = stack.enter_context(nc.semaphore("s_" + e))
            self.cnt[e] = 0
        self.dma_sems = {}
        for q in ("sp", "act", "pool"):
            lst = []
            for i in range(N_DMA_SEMS):
                nm = "d_%s%d" % (q, i)
                self.sems[nm] = stack.enter_context(nc.semaphore(nm))
                self.cnt[nm] = 0
                lst.append(nm)
            self.dma_sems[q] = lst
        self.dma_rr = {q: 0 for q in self.dma_sems}
        self.waited = {e: {} for e in ENGS}
        self.ops = None
        self.nops = 0

    def begin(self):
        self.ops = {e: [] for e in ENGS}
        self.allops = []
        self.dma_last = {}

    def _dep(self, op, other):
        if other is None or other is op:
            return
        if other.eng == "pe" and op.eng == "pe" and not other.is_dma and not op.is_dma:
            return
        op.waits.append(other)

    def op(self, eng, fn, reads=(), writes=(), dma=False):
        o = Op(eng, fn, dma)
        reads = list(reads)
        writes = list(writes)
        for b in reads:
            if b.excl and b not in writes:
                writes.append(b)
        for b in reads:
            self._dep(o, b.w)
        for b in writes:
            self._dep(o, b.w)
            for r in b.r:
                self._dep(o, r)
        for b in reads:
            b.r.append(o)
        for b in writes:
            b.w = o
            b.r = []
        if dma:
            i = self.dma_rr[eng]
            self.dma_rr[eng] = (i + 1) % N_DMA_SEMS
            nm = self.dma_sems[eng][i]
            prev = self.dma_last.get(nm)
            if prev is not None:
                o.waits.append(prev)
            self.dma_last[nm] = o
            self.cnt[nm] += 16
            o.ev = (nm, self.cnt[nm])
            o.signal = True
        self.ops[eng].append(o)
        self.allops.append(o)
        return o

    def dma(self, q, out, in_, reads=(), writes=()):
        return self.op(q, lambda e: e.dma_start(out=out, in_=in_), reads, writes, dma=True)

    def end(self):
        nc = self.nc
        for o in self.allops:
            for w in o.waits:
                w.signal = True
        for e in ENGS:
            for o in reversed(self.ops[e]):
                if not o.is_dma:
                    o.signal = True
                    break
        for e in ENGS:
            for o in self.ops[e]:
                if not o.is_dma and o.signal:
                    self.cnt[e] += 1
                    o.ev = (e, self.cnt[e])
        final = dict(self.cnt)
        sems = self.sems
        ops = self.ops
        waited_all = self.waited
        self.nops += len(self.allops)

        def emit(ename, eng):
            waited = waited_all[ename]
            for o in ops[ename]:
                need = {}
                for w in o.waits:
                    s, v = w.ev
                    if need.get(s, 0) < v:
                        need[s] = v
                for s, v in need.items():
                    if waited.get(s, 0) >= v:
                        continue
                    waited[s] = v
                    eng.wait_ge(sems[s], v)
                ins = o.fn(eng)
                if o.signal:
                    s, v = o.ev
                    ins.then_inc(sems[s], 16 if o.is_dma else 1)
            for s, v in final.items():
                if v > 0 and waited.get(s, 0) < v:
                    waited[s] = v
                    eng.wait_ge(sems[s], v)

        with nc.Block() as block:
            @block.tensor
            def _(eng):
                emit("pe", eng)

            @block.scalar
            def _(eng):
                emit("act", eng)

            @block.vector
            def _(eng):
                emit("dve", eng)

            @block.gpsimd
            def _(eng):
                emit("pool", eng)

            @block.sync
            def _(eng):
                emit("sp", eng)
        self.ops = None


def _perm64():
    d = np.arange(64)
    return np.where((d % 32) < 16, d + 16, d - 16)


def _rel_bucket_np(rel):
    half = NB // 2
    max_exact = half // 2
    n = np.abs(rel)
    nf = np.maximum(n, max_exact).astype(np.float32)
    large = max_exact + (np.log(nf / np.float32(max_exact)) / np.float32(math.log(128 / max_exact))
                         * (half - max_exact)).astype(np.int32)
    large = np.minimum(large, half - 1)
    return np.where(rel > 0, half, 0) + np.where(n < max_exact, n, large)


def _rope_tables(pos):
    row = (pos // 64).astype(np.float32)
    col = (pos % 64).astype(np.float32)
    half = HD // 2
    inv = (np.float32(10000.0) ** (-np.arange(0, half, 2, dtype=np.float32) / np.float32(half))).astype(np.float32)
    ang_r = row[:, None] * inv[None, :]
    ang_c = col[:, None] * inv[None, :]
    ang = np.concatenate([ang_r, ang_r, ang_c, ang_c], axis=-1).astype(np.float32)
    cos = np.cos(ang).astype(np.float32)
    sin = np.sin(ang).astype(np.float32)
    d = np.arange(64)
    sign = np.where((d % 32) < 16, -1.0, 1.0).astype(np.float32)
    sin_s = sin * sign[None, :]
    cosT = np.ascontiguousarray(np.concatenate([cos.T, cos.T], axis=0))
    sinT = np.ascontiguousarray(np.concatenate([sin_s.T, sin_s.T], axis=0))
    return cosT, sinT


def _onehot(buckets):
    e = np.zeros((NB, len(buckets)), dtype=np.float32)
    e[buckets, np.arange(len(buckets))] = 1.0
    return e


def build_program(NP, NS, debug=False):
    jobs = [dict(name="P", N=NP, nq=NP // 4, b=0), dict(name="S", N=NS, nq=NS // 2, b=1)]
    for jb in jobs:
        assert jb["nq"] % 512 == 0 and jb["N"] % 512 == 0
        jb["NC"] = jb["N"] // 128
    nc = bass.Bass("TRN2", target_bir_lowering=False)

    def din(name, shape, dt=F32):
        return nc.dram_tensor(name, list(shape), dt, kind="ExternalInput").ap()

    def dscr(name, shape, dt):
        if debug and not name.startswith("wb_"):
            return nc.dram_tensor(name, list(shape), dt, kind="ExternalOutput").ap()
        return nc.dram_tensor(name, list(shape), dt).ap()

    x_in = {"P": din("xP", [NP, D]), "S": din("xS", [NS, D])}
    cT_d = din("cT", [128, 8, 2])
    w_ada_d = din("w_ada", [D, 6 * D])
    b_ada2_d = din("b_ada2", [2, 6 * D])
    grow_d = din("grow", [2, 4, D])
    w_in_d = din("w_in_p", [D, WIN])
    w_out_d = din("w_out", [D, D])
    w_gu_d = din("w_gu", [D, 2 * DFF])
    w_down_d = din("w_down", [DFF, D])
    gcols_d = din("gcols", [128, 8])
    lamv_d = din("lamv", [1, 4, 64])
    relb_d = din("relb", [NB, 4])
    rope_d = {jb["name"]: (din("cos" + jb["name"], [128, jb["N"]]), din("sin" + jb["name"], [128, jb["N"]])) for jb in jobs}
    cmat_d = din("cmat", [128, 5, 128], BF16)
    emain_d = din("emain", [NB, ULEN], BF16)
    ewrap_d = {jb["name"]: din("ewrap" + jb["name"], [NB, WLEN], BF16) for jb in jobs}
    ewrap2_d = {jb["name"]: din("ewrap2" + jb["name"], [NB, WLEN], BF16) for jb in jobs}
    efar_d = {jb["name"]: din("efar" + jb["name"], [NB, jb["NC"] + 1], BF16) for jb in jobs}
    y_out = {jb["name"]: nc.dram_tensor("y" + jb["name"], [jb["nq"], D], F32, kind="ExternalOutput").ap() for jb in jobs}

    wb_in = dscr("wb_in", [D, WIN], BF16)
    wb_out = dscr("wb_out", [D, D], BF16)
    wb_gu = dscr("wb_gu", [D, 2 * DFF], BF16)
    wb_down = dscr("wb_down", [DFF, D], BF16)
    rows_d = dscr("rows_d", [2, 6, D], F32)
    lam_d = dscr("lam_d", [1, 2], F32)
    ub_d = dscr("ub_d", [3, 4, ULEN], BF16)
    uw_d = {jb["name"]: dscr("uw_d" + jb["name"], [3, 4, WLEN], BF16) for jb in jobs}
    uw2_d = {jb["name"]: dscr("uw2_d" + jb["name"], [3, 4, WLEN], BF16) for jb in jobs}
    ufar_d = {jb["name"]: dscr("ufar_d" + jb["name"], [1, 4 * (jb["NC"] + 1)], F32) for jb in jobs}
    S = {}
    for jb in jobs:
        n, N, nq = jb["name"], jb["N"], jb["nq"]
        S[n] = dict(
            KTa=dscr("KTa" + n, [128, N], BF16), Va=dscr("Va" + n, [N, 130], BF16),
            KTd=dscr("KTd" + n, [4, 128, N], BF16), Vd=dscr("Vd" + n, [N, 512], BF16),
            QTa=dscr("QTa" + n, [4, 128, nq], BF16), QTd=dscr("QTd" + n, [4, 128, nq], BF16),
            outT=dscr("outT" + n, [D, nq], BF16), zrow=dscr("zrow" + n, [2, 2, 512], F32),
            x1=dscr("x1" + n, [nq, D], F32), h2T=dscr("h2T" + n, [8, 128, nq], BF16),
        )

    with contextlib.ExitStack() as gst:
        pg = Prog(nc, gst)
        def gsb(name, shape, dt):
            return gst.enter_context(nc.sbuf_tensor("g_" + name, list(shape), dt))

        cmat = gsb("cmat", [128, 5, 128], BF16)
        gcols = gsb("gcols", [128, 8], F32)
        negl = gsb("negl", [128, 2], F32)
        nhalf = gsb("nhalf", [128, 512], F32)
        zero1 = gsb("zero1", [128, 1], F32)
        epsc = gsb("epsc", [128, 1], F32)
        farb = {jb["name"]: gsb("farb" + jb["name"], [128, 4, jb["NC"] + 1], F32) for jb in jobs}
        ps = gst.enter_context(nc.psum_tensor("psum_all", [128, 8, 512], F32))
        ident = cmat[:, 0, :]
        antiid = cmat[:, 1, :]
        onesb = cmat[:, 2, :]
        blk64 = cmat[:, 3, :]
        o128 = cmat[:, 4, :]

        def psb(i):
            return ps[:, i, :]

        def new_ps():
            return [Buf("ps%d" % i, excl=True) for i in range(8)]

        with contextlib.ExitStack() as st:
            def sb(name, shape, dt):
                return st.enter_context(nc.sbuf_tensor("s0_" + name, list(shape), dt))

            PB = new_ps()
            pg.begin()
            Bc = Buf("consts")
            pg.dma("sp", cmat[:], cmat_d, writes=[Bc])
            pg.dma("sp", gcols[:], gcols_d, writes=[Bc])
            pg.op("pool", lambda e: e.memset(nhalf[:], -0.5), writes=[Bc])
            pg.op("pool", lambda e: e.memset(zero1[:], 0.0), writes=[Bc])
            pg.op("pool", lambda e: e.memset(epsc[:], EPS), writes=[Bc])
            pg.op("dve", lambda e: e.tensor_scalar(out=gcols[:, 4:5], in0=gcols[:, 4:5], scalar1=1.0 - LAM_INIT, scalar2=None, op0=ALU.mult),
                  reads=[Bc], writes=[Bc])

            cT = sb("cT", [128, 8, 2], F32)
            scT = sb("scT", [128, 8, 2], F32)
            scTb = sb("scTb", [128, 8, 2], BF16)
            BcT, BscT = Buf("cT"), Buf("scT")
            pg.dma("sp", cT[:], cT_d, writes=[BcT])
            pg.op("act", lambda e: e.activation(out=scT[:], in_=cT[:], func=AF.Silu), reads=[BcT], writes=[BscT])
            pg.op("dve", lambda e: e.tensor_copy(scTb[:], scT[:]), reads=[BscT], writes=[BscT])

            mod = sb("mod", [2, 6 * D], F32)
            bada = sb("bada", [2, 6 * D], F32)
            Bmod, Bbada = Buf("mod"), Buf("bada")
            pg.dma("sp", bada[:], b_ada2_d, writes=[Bbada])
            waf = [sb("waf%d" % i, [128, 8, 512], F32) for i in range(2)]
            wab = [sb("wab%d" % i, [128, 8, 512], BF16) for i in range(2)]
            Bwaf = [Buf("waf%d" % i) for i in range(2)]
            Bwab = [Buf("wab%d" % i) for i in range(2)]
            for n in range(12):
                i = n % 2
                pg.dma("sp", waf[i][:], w_ada_d[:, n * 512:(n + 1) * 512].rearrange("(k p) n -> p k n", p=128), writes=[Bwaf[i]])
                ce = "dve" if n % 2 == 0 else "pool"
                pg.op(ce, lambda e, i=i: e.tensor_copy(wab[i][:], waf[i][:]), reads=[Bwaf[i]], writes=[Bwab[i]])

                def mm(e, i=i):
                    ins = None
                    for k in range(8):
                        ins = e.matmul(ps[0:2, i, :], lhsT=scTb[:, k, :], rhs=wab[i][:, k, :], start=(k == 0), stop=(k == 7))
                    return ins
                pg.op("pe", mm, reads=[Bwab[i], BscT], writes=[PB[i]])
                pg.op("dve", lambda e, i=i, n=n: e.tensor_tensor(out=mod[:, n * 512:(n + 1) * 512], in0=ps[0:2, i, :],
                                                               in1=bada[:, n * 512:(n + 1) * 512], op=ALU.add),
                      reads=[PB[i], Bbada], writes=[Bmod])
            grow = sb("grow", [2, 4, D], F32)
            rows = sb("rows", [2, 6, D], F32)
            Bgrow, Brows = Buf("grow"), Buf("rows")
            pg.dma("sp", grow[:], grow_d, writes=[Bgrow])
            pg.op("dve", lambda e: e.scalar_tensor_tensor(out=rows[:, 0, :], in0=mod[:, 1024:2048], scalar=1.0, in1=grow[:, 0, :],
                                                          op0=ALU.add, op1=ALU.mult), reads=[Bmod, Bgrow], writes=[Brows])
            pg.op("dve", lambda e: e.tensor_copy(rows[:, 1, :], mod[:, 0:1024]), reads=[Bmod], writes=[Brows])
            pg.op("dve", lambda e: e.tensor_tensor(out=rows[:, 2, :], in0=mod[:, 2048:3072], in1=grow[:, 1, :], op=ALU.mult),
                  reads=[Bmod, Bgrow], writes=[Brows])
            pg.op("dve", lambda e: e.scalar_tensor_tensor(out=rows[:, 3, :], in0=mod[:, 4096:5120], scalar=1.0, in1=grow[:, 2, :],
                                                          op0=ALU.add, op1=ALU.mult), reads=[Bmod, Bgrow], writes=[Brows])
            pg.op("dve", lambda e: e.tensor_copy(rows[:, 4, :], mod[:, 3072:4096]), reads=[Bmod], writes=[Brows])
            pg.op("dve", lambda e: e.tensor_tensor(out=rows[:, 5, :], in0=mod[:, 5120:6144], in1=grow[:, 3, :], op=ALU.mult),
                  reads=[Bmod, Bgrow], writes=[Brows])
            pg.dma("sp", rows_d, rows[:], reads=[Brows])

            lamv = sb("lamv", [1, 4, 64], F32)
            lj = sb("lj", [1, 64], F32)
            ls = sb("ls", [1, 4], F32)
            Blam = Buf("lam")
            pg.dma("sp", lamv[:], lamv_d, writes=[Blam])
            for t in range(2):
                pg.op("dve", lambda e, t=t: e.scalar_tensor_tensor(out=lj[:], in0=lamv[:, 2 * t, :], scalar=1.0, in1=lamv[:, 2 * t + 1, :],
                                                                 op0=ALU.mult, op1=ALU.mult, accum_out=ls[:, t:t + 1]),
                      reads=[Blam], writes=[Blam])
            pg.op("act", lambda e: e.activation(out=ls[:, 2:4], in_=ls[:, 0:2], func=AF.Exp), reads=[Blam], writes=[Blam])
            pg.op("dve", lambda e: e.scalar_tensor_tensor(out=ls[:, 1:2], in0=ls[:, 2:3], scalar=LAM_INIT, in1=ls[:, 3:4],
                                                          op0=ALU.add, op1=ALU.subtract), reads=[Blam], writes=[Blam])
            pg.op("dve", lambda e: e.tensor_scalar(out=ls[:, 0:1], in0=ls[:, 1:2], scalar1=-1.0, scalar2=None, op0=ALU.mult),
                  reads=[Blam], writes=[Blam])
            pg.dma("sp", lam_d, ls[:, 0:2], reads=[Blam])
            Bnegl = Buf("negl")
            pg.dma("sp", negl[:], lam_d.broadcast_to([128, 2]), reads=[Blam], writes=[Bnegl])

            rb = sb("rb", [NB, 4], F32)
            rbr = sb("rbr", [NB, 4], F32)
            rbp = [sb("rbp%d" % i, [NB, 4], BF16) for i in range(3)]
            Brb = Buf("rb")
            pg.dma("sp", rb[:], relb_d, writes=[Brb])
            pg.op("dve", lambda e: e.tensor_copy(rbp[0][:], rb[:]), reads=[Brb], writes=[Brb])
            pg.op("dve", lambda e: e.tensor_tensor(out=rbr[:], in0=rb[:], in1=rbp[0][:], op=ALU.subtract), reads=[Brb], writes=[Brb])
            pg.op("dve", lambda e: e.tensor_copy(rbp[1][:], rbr[:]), reads=[Brb], writes=[Brb])
            pg.op("dve", lambda e: e.tensor_tensor(out=rbr[:], in0=rbr[:], in1=rbp[1][:], op=ALU.subtract), reads=[Brb], writes=[Brb])
            pg.op("dve", lambda e: e.tensor_copy(rbp[2][:], rbr[:]), reads=[Brb], writes=[Brb])
            emat = sb("emat", [NB, ULEN], BF16)
            uout = sb("uout", [4, 3, ULEN], BF16)
            ufo = sb("ufo", [4, 132], F32)
            Bem, Buo = Buf("emat"), Buf("uout")
            specs = [("main", emain_d, ULEN, ub_d)]
            for jb in jobs:
                specs.append(("wrap", ewrap_d[jb["name"]], WLEN, uw_d[jb["name"]]))
                specs.append(("wrap", ewrap2_d[jb["name"]], WLEN, uw2_d[jb["name"]]))
            for jb in jobs:
                specs.append(("far", efar_d[jb["name"]], jb["NC"] + 1, ufar_d[jb["name"]]))
            pbank = 2
            for kind, esrc, L, dst in specs:
                pg.dma("sp", emat[:, 0:L], esrc, writes=[Bem])
                if kind != "far":
                    for part in range(3):
                        for s0 in range(0, L, 512):
                            w = min(512, L - s0)
                            bk = 2 + (pbank % 2)
                            pbank += 1
                            pg.op("pe", lambda e, bk=bk, part=part, s0=s0, w=w: e.matmul(ps[0:4, bk, 0:w], lhsT=rbp[part][:, :],
                                                                                       rhs=emat[:, s0:s0 + w], start=True, stop=True),
                                  reads=[Bem, Brb], writes=[PB[bk]])
                            pg.op("dve", lambda e, bk=bk, part=part, s0=s0, w=w: e.tensor_copy(uout[:, part, s0:s0 + w], ps[0:4, bk, 0:w]),
                                  reads=[PB[bk]], writes=[Buo])
                    pg.dma("sp", dst.rearrange("t h l -> h t l"), uout[:, :, 0:L], reads=[Buo])
                else:
                    bk = 2 + (pbank % 2)
                    pbank += 1

                    def mmf(e, bk=bk, L=L):
                        ins = None
                        for part in range(3):
                            ins = e.matmul(ps[0:4, bk, 0:L], lhsT=rbp[part][:, :], rhs=emat[:, 0:L], start=(part == 0), stop=(part == 2))
                        return ins
                    pg.op("pe", mmf, reads=[Bem, Brb], writes=[PB[bk]])
                    pg.op("dve", lambda e, bk=bk, L=L: e.tensor_copy(ufo[:, 0:L], ps[0:4, bk, 0:L]), reads=[PB[bk]], writes=[Buo])
                    pg.dma("sp", dst.rearrange("o (h l) -> (o h) l", h=4), ufo[:, 0:L], reads=[Buo])
            for jb in jobs:
                n = jb["name"]
                pg.dma("sp", farb[n][:].rearrange("p h l -> p (h l)"), ufar_d[n].broadcast_to([128, 4 * (jb["NC"] + 1)]),
                       reads=[Buo], writes=[Bc])

            pieces = []
            for k in range(8):
                pieces.append((w_in_d[k * 128:(k + 1) * 128, :], wb_in[k * 128:(k + 1) * 128, :], WIN))
            for k in range(8):
                pieces.append((w_out_d[k * 128:(k + 1) * 128, :], wb_out[k * 128:(k + 1) * 128, :], D))
            for k in range(8):
                for hh in range(2):
                    pieces.append((w_gu_d[k * 128:(k + 1) * 128, hh * DFF:(hh + 1) * DFF], wb_gu[k * 128:(k + 1) * 128, hh * DFF:(hh + 1) * DFF], DFF))
            for k in range(22):
                pieces.append((w_down_d[k * 128:(k + 1) * 128, :], wb_down[k * 128:(k + 1) * 128, :], D))
            NBUF = 2
            wcf = [sb("wcf%d" % i, [128, WIN], F32) for i in range(NBUF)]
            wcb = [sb("wcb%d" % i, [128, WIN], BF16) for i in range(NBUF)]
            Bwcf = [Buf("wcf%d" % i) for i in range(NBUF)]
            Bwcb = [Buf("wcb%d" % i) for i in range(NBUF)]
            for n, (src, dst, w) in enumerate(pieces):
                i = n % NBUF
                pg.dma("sp", wcf[i][:, 0:w], src, writes=[Bwcf[i]])
                ce = ["dve", "pool", "act"][n % 3]
                if ce == "act":
                    pg.op(ce, lambda e, i=i, w=w: e.activation(out=wcb[i][:, 0:w], in_=wcf[i][:, 0:w], func=AF.Copy), reads=[Bwcf[i]], writes=[Bwcb[i]])
                else:
                    pg.op(ce, lambda e, i=i, w=w: e.tensor_copy(wcb[i][:, 0:w], wcf[i][:, 0:w]), reads=[Bwcf[i]], writes=[Bwcb[i]])
                pg.dma("pool", dst, wcb[i][:, 0:w], reads=[Bwcb[i]])
            pg.end()

        for jb in jobs:
            name, N, nq = jb["name"], jb["N"], jb["nq"]
            sc = S[name]
            cos_d, sin_d = rope_d[name]
            with contextlib.ExitStack() as st:
                def sb(nm, shape, dt):
                    return st.enter_context(nc.sbuf_tensor("s1_" + nm + name, list(shape), dt))
                PB = new_ps()
                pg.begin()
                win = sb("win", [128, 8, WIN], BF16)
                Bwin = Buf("win")
                for k in range(8):
                    pg.dma("sp", win[:, k, :], wb_in[k * 128:(k + 1) * 128, :], writes=[Bwin])
                A1 = sb("A1", [128, D], F32)
                SH1 = sb("SH1", [128, D], F32)
                Bmodt = Buf("modt")
                pg.dma("sp", A1[:], rows_d[jb["b"], 0:1, :].broadcast_to([128, D]), writes=[Bmodt])
                pg.dma("sp", SH1[:], rows_d[jb["b"], 1:2, :].broadcast_to([128, D]), writes=[Bmodt])
                NXB = 2
                xt = [sb("xt%d" % i, [128, 4, D], F32) for i in range(NXB)]
                Bxt = [Buf("xt%d" % i) for i in range(NXB)]
                rt = [(sb("cos%d" % i, [128, 512], F32), sb("sin%d" % i, [128, 512], F32)) for i in range(2)]
                Brt = [Buf("rt%d" % i) for i in range(2)]
                junk = sb("junk", [128, D], F32)
                ss = sb("ss", [128, 4], F32)
                rstd = sb("rstd", [128, 4], F32)
                tt = [sb("tt%d" % i, [128, D], F32) for i in range(2)]
                hb = [sb("hb%d" % i, [128, D], BF16) for i in range(2)]
                hT = sb("hT", [128, 8, 512], BF16)
                Bjunk, Bss, Brstd = Buf("junk"), Buf("ss"), Buf("rstd")
                Btt = [Buf("tt%d" % i) for i in range(2)]
                Bhb = [Buf("hb%d" % i) for i in range(2)]
                BhT = [Buf("hT%d" % i) for i in range(4)]
                asb = sb("asb", [128, 512], F32)
                sq = sb("sq", [128, 512], F32)
                sqh = sb("sqh", [128, 512], BF16)
                sqm = sb("sqm", [128, 512], BF16)
                rs = sb("rs", [128, 512], F32)
                t1 = sb("t1", [128, 512], F32)
                t2 = sb("t2", [128, 512], F32)
                Basb, Bsq, Bsqh, Bsqm, Brs, Bt1, Bt2 = [Buf(x) for x in ["asb", "sq", "sqh", "sqm", "rs", "t1", "t2"]]
                kta = [sb("kta%d" % i, [128, 512], BF16) for i in range(2)]
                ktd = [sb("ktd%d" % i, [128, 4, 512], BF16) for i in range(2)]
                qta = [sb("qta%d" % i, [128, 4, 512], BF16) for i in range(2)]
                qtd = [sb("qtd%d" % i, [128, 4, 512], BF16) for i in range(2)]
                va = [sb("va%d" % i, [128, 4, 2, 65], BF16) for i in range(2)]
                vd = [sb("vd%d" % i, [128, 4, 512], BF16) for i in range(2)]
                Bkta = [Buf("kta%d" % i) for i in range(2)]
                Bktd = [Buf("ktd%d" % i) for i in range(2)]
                Bqta = [Buf("qta%d" % i) for i in range(2)]
                Bqtd = [Buf("qtd%d" % i) for i in range(2)]
                Bva = [Buf("va%d" % i) for i in range(2)]
                Bvd = [Buf("vd%d" % i) for i in range(2)]
                for i in range(2):
                    pg.op("pool", lambda e, i=i: e.memset(va[i][:], 1.0), writes=[Bva[i]])
                psT = [ps[:, i, :].bitcast(BF16) for i in range(2)]

                def normrope(pa, pb, gi, gpi, outap, scale, ri):
                    cs, sn = rt[ri]
                    pg.op("act", lambda e: e.activation(out=asb[:], in_=psb(pa), func=AF.Copy), reads=[PB[pa]], writes=[Basb])
                    pg.op("dve", lambda e: e.tensor_tensor(out=sq[:], in0=asb[:], in1=asb[:], op=ALU.mult), reads=[Basb], writes=[Bsq])
                    pg.op("dve", lambda e: e.tensor_copy(sqh[:], sq[:]), reads=[Bsq], writes=[Bsqh])
                    pg.op("pool", lambda e: e.tensor_tensor(out=sq[:], in0=sq[:], in1=sqh[:], op=ALU.subtract), reads=[Bsq, Bsqh], writes=[Bsq])
                    pg.op("pool", lambda e: e.tensor_copy(sqm[:], sq[:]), reads=[Bsq], writes=[Bsqm])

                    def mmss(e):
                        e.matmul(psb(7), lhsT=blk64, rhs=sqh[:], start=True, stop=False)
                        return e.matmul(psb(7), lhsT=blk64, rhs=sqm[:], start=False, stop=True)
                    pg.op("pe", mmss, reads=[Bsqh, Bsqm, Bc], writes=[PB[7]])
                    pg.op("act", lambda e: e.activation(out=rs[:], in_=psb(7), func=AF.Sqrt, bias=epsc[:], scale=1.0), reads=[PB[7], Bc], writes=[Brs])
                    pg.op("dve", lambda e: e.reciprocal(out=rs[:], in_=rs[:]), reads=[Brs], writes=[Brs])
                    pg.op("dve", lambda e: e.scalar_tensor_tensor(out=t1[:], in0=asb[:], scalar=gcols[:, gi:gi + 1], in1=cs[:], op0=ALU.mult, op1=ALU.mult),
                          reads=[Basb, Bc, Brt[ri]], writes=[Bt1])
                    pg.op("dve", lambda e: e.scalar_tensor_tensor(out=t2[:], in0=psb(pb), scalar=gcols[:, gpi:gpi + 1], in1=sn[:], op0=ALU.mult, op1=ALU.mult),
                          reads=[PB[pb], Bc, Brt[ri]], writes=[Bt2])
                    pg.op("pool", lambda e: e.tensor_tensor(out=t1[:], in0=t1[:], in1=t2[:], op=ALU.add), reads=[Bt1, Bt2], writes=[Bt1])
                    return lambda e: e.scalar_tensor_tensor(out=outap, in0=t1[:], scalar=scale, in1=rs[:], op0=ALU.mult, op1=ALU.mult)

                def proj(bank, ch):
                    def f(e):
                        ins = None
                        for k in range(8):
                            ins = e.matmul(psb(bank), lhsT=win[:, k, ch * 128:(ch + 1) * 128], rhs=hT[:, k, :], start=(k == 0), stop=(k == 7))
                        return ins
                    pg.op("pe", f, reads=[Bwin] + BhT, writes=[PB[bank]])

                ntiles = N // 512
                for t in range(ntiles):
                    xi = t % NXB
                    own = (t * 512 < nq)
                    ti = t % 2
                    pg.dma("sp", xt[xi][:], x_in[name][t * 512:(t + 1) * 512, :].rearrange("(s p) d -> p s d", p=128), writes=[Bxt[xi]])
                    pg.dma("sp", rt[ti][0][:], cos_d[:, t * 512:(t + 1) * 512], writes=[Brt[ti]])
                    pg.dma("sp", rt[ti][1][:], sin_d[:, t * 512:(t + 1) * 512], writes=[Brt[ti]])
                    for s in range(4):
                        pg.op("dve", lambda e, s=s, xi=xi: e.scalar_tensor_tensor(out=junk[:], in0=xt[xi][:, s, :], scalar=1.0 / D, in1=xt[xi][:, s, :],
                                                                               op0=ALU.mult, op1=ALU.mult, accum_out=ss[:, s:s + 1]),
                              reads=[Bxt[xi]], writes=[Bjunk, Bss])
                    pg.op("dve", lambda e: e.tensor_scalar(out=rstd[:], in0=ss[:], scalar1=EPS, scalar2=None, op0=ALU.add), reads=[Bss], writes=[Brstd])
                    pg.op("pool", lambda e: e.tensor_tensor(out=rstd[:], in0=rstd[:], in1=nhalf[:, 0:4], op=ALU.pow), reads=[Brstd, Bc], writes=[Brstd])
                    for s in range(4):
                        i = s % 2
                        pg.op("dve", lambda e, s=s, i=i, xi=xi: e.scalar_tensor_tensor(out=tt[i][:], in0=xt[xi][:, s, :], scalar=rstd[:, s:s + 1], in1=A1[:],
                                                                                    op0=ALU.mult, op1=ALU.mult),
                              reads=[Bxt[xi], Brstd, Bmodt], writes=[Btt[i]])
                        pg.op("pool", lambda e, i=i: e.tensor_tensor(out=hb[i][:], in0=tt[i][:], in1=SH1[:], op=ALU.add),
                              reads=[Btt[i], Bmodt], writes=[Bhb[i]])

                        def tr(e, i=i):
                            ins = None
                            for k in range(8):
                                ins = e.transpose(out=psT[i][:, k * 128:(k + 1) * 128], in_=hb[i][:, k * 128:(k + 1) * 128], identity=ident)
                            return ins
                        pg.op("pe", tr, reads=[Bhb[i], Bc], writes=[PB[i]])
                        pg.op("act", lambda e, s=s, i=i: e.activation(out=hT[:, :, s * 128:(s + 1) * 128],
                                                                    in_=psT[i].rearrange("p (k t) -> p k t", k=8), func=AF.Copy),
                              reads=[PB[i]], writes=[BhT[s]])
                    proj(2, 8)
                    proj(3, 9)
                    fin = normrope(2, 3, 2, 3, kta[ti][:], 1.0, ti)
                    pg.op("dve", fin, reads=[Bt1, Brs], writes=[Bkta[ti]])
                    pg.dma("pool", sc["KTa"][:, t * 512:(t + 1) * 512], kta[ti][:], reads=[Bkta[ti]])
                    for h in range(4):
                        bk = 4 + (h % 2)
                        proj(bk, 14 + h)
                        ce = "act" if h % 2 == 0 else "dve"
                        if ce == "act":
                            pg.op("act", lambda e, h=h, bk=bk, ti=ti: e.activation(out=ktd[ti][:, h, :], in_=psb(bk), func=AF.Copy), reads=[PB[bk]], writes=[Bktd[ti]])
                        else:
                            pg.op("dve", lambda e, h=h, bk=bk, ti=ti: e.tensor_copy(ktd[ti][:, h, :], psb(bk)), reads=[PB[bk]], writes=[Bktd[ti]])
                    pg.dma("pool", sc["KTd"][:, :, t * 512:(t + 1) * 512].rearrange("h p t -> p h t"), ktd[ti][:], reads=[Bktd[ti]])
                    for s in range(4):
                        def mmv(e, s=s):
                            ins = None
                            for k in range(8):
                                ins = e.matmul(psb(6), lhsT=hT[:, k, s * 128:(s + 1) * 128], rhs=win[:, k, 2432:2944], start=(k == 0), stop=(k == 7))
                            return ins

                        def mmva(e, s=s):
                            ins = None
                            for k in range(8):
                                ins = e.matmul(ps[:, 5, 0:128], lhsT=hT[:, k, s * 128:(s + 1) * 128], rhs=win[:, k, 2304:2432], start=(k == 0), stop=(k == 7))
                            return ins
                        pg.op("pe", mmv, reads=[Bwin, BhT[s]], writes=[PB[6]])
                        pg.op("act", lambda e, s=s, ti=ti: e.activation(out=vd[ti][:, s, :], in_=psb(6), func=AF.Copy), reads=[PB[6]], writes=[Bvd[ti]])
                        pg.op("pe", mmva, reads=[Bwin, BhT[s]], writes=[PB[5]])
                        pg.op("dve", lambda e, s=s, ti=ti: e.tensor_copy(va[ti][:, s, :, 0:64], ps[:, 5, 0:128].rearrange("p (a b) -> p a b", a=2)),
                              reads=[PB[5]], writes=[Bva[ti]])
                    pg.dma("pool", sc["Vd"][t * 512:(t + 1) * 512, :].rearrange("(s p) c -> p s c", p=128), vd[ti][:], reads=[Bvd[ti]])
                    pg.dma("pool", sc["Va"][t * 512:(t + 1) * 512, :].rearrange("(s p) c -> p s c", p=128),
                           va[ti][:].rearrange("p s a b -> p s (a b)"), reads=[Bva[ti]])
                    if own:
                        for g in range(4):
                            proj(2, g)
                            proj(3, 4 + g)
                            fin = normrope(2, 3, 0, 1, qta[ti][:, g, :], 0.125, ti)
                            pg.op("dve", fin, reads=[Bt1, Brs], writes=[Bqta[ti]])
                        pg.dma("pool", sc["QTa"][:, :, t * 512:(t + 1) * 512].rearrange("g p t -> p g t"), qta[ti][:], reads=[Bqta[ti]])
                        for h in range(4):
                            bk = 4 + (h % 2)
                            proj(bk, 10 + h)
                            if h % 2 == 0:
                                pg.op("act", lambda e, h=h, bk=bk, ti=ti: e.activation(out=qtd[ti][:, h, :], in_=psb(bk), func=AF.Copy, scale=0.125),
                                      reads=[PB[bk]], writes=[Bqtd[ti]])
                            else:
                                pg.op("dve", lambda e, h=h, bk=bk, ti=ti: e.tensor_scalar(out=qtd[ti][:, h, :], in0=psb(bk), scalar1=0.125, scalar2=None, op0=ALU.mult),
                                      reads=[PB[bk]], writes=[Bqtd[ti]])
                        pg.dma("pool", sc["QTd"][:, :, t * 512:(t + 1) * 512].rearrange("h p t -> p h t"), qtd[ti][:], reads=[Bqtd[ti]])
                pg.end()

        for jb in jobs:
            name, N, nq, NC = jb["name"], jb["N"], jb["nq"], jb["NC"]
            sc = S[name]
            with contextlib.ExitStack() as st:
                def sb(nm, shape, dt):
                    return st.enter_context(nc.sbuf_tensor("s2_" + nm + name, list(shape), dt))
                PB = new_ps()
                pg.begin()
                KT = sb("KT", [128, N], BF16)
                VV = sb("VV", [128, NC, 130], BF16)
                QT = sb("QT", [128, 4, nq], BF16)
                NG = 8
                cpg = NC // NG
                BKV = [Buf("kv%d" % i) for i in range(NG)]
                BQ = Buf("QT")
                pT = [sb("pT%d" % i, [128, 1024], BF16) for i in range(3)]
                BpT = [Buf("pT%d" % i) for i in range(3)]
                osb = [sb("osb%d" % i, [128, 512], F32) for i in range(2)]
                Bosb = [Buf("osb%d" % i) for i in range(2)]
                zr = sb("zr", [128, 2, 512], F32)
                Bzr = Buf("zr")
                bcz = sb("bcz", [128, 2, 512], F32)
                Bbcz = Buf("bcz")
                onrm = sb("onrm", [128, 512], BF16)
                Bonrm = Buf("onrm")
                dd = sb("dd", [128, 512], F32)
                dsq = sb("dsq", [128, 512], F32)
                dsqh = sb("dsqh", [128, 512], BF16)
                dsqm = sb("dsqm", [128, 512], BF16)
                drs = sb("drs", [128, 512], F32)
                Bdd, Bdsq, Bdsqh, Bdsqm, Bdrs = [Buf(x) for x in ["dd", "dsq", "dsqh", "dsqm", "drs"]]
                TT = sb("TT", [128, 8, 512], F32)
                BTT = Buf("TT")
                hk = [sb("hk%d" % i, [128, 3, 512], BF16) for i in range(2)]
                Bhk = [Buf("hk%d" % i) for i in range(2)]
                zdram = sc["zrow"]
                Bzd = [Buf("zd0"), Buf("zd1")]

                def load_kv(kt_src, v_src, vw):
                    for gi in range(NG):
                        c0 = gi * cpg
                        pg.dma("sp", KT[:, c0 * 128:(c0 + cpg) * 128], kt_src[:, c0 * 128:(c0 + cpg) * 128], writes=[BKV[gi]])
                        pg.dma("sp", VV[:, c0:c0 + cpg, 0:vw], v_src[c0 * 128:(c0 + cpg) * 128, :].rearrange("(c p) w -> p c w", p=128), writes=[BKV[gi]])

                def attn_tile(units, lhs_v, near, bias_of, epilogue):
                    def qk(c):
                        par = c % 2

                        def f(e):
                            ins = None
                            for u in range(2):
                                ins = e.matmul(psb(2 * par + u), lhsT=KT[64 * u:64 * u + 64, c * 128:(c + 1) * 128], rhs=units[u], start=True, stop=True)
                            return ins
                        pg.op("pe", f, reads=[BKV[c // cpg], BQ], writes=[PB[2 * par], PB[2 * par + 1]])
                        if c in near:
                            ti = near[c]
                            pg.op("dve", lambda e: e.tensor_tensor(out=ps[:, 2 * par:2 * par + 2, :], in0=ps[:, 2 * par:2 * par + 2, :],
                                                                 in1=TT[:, ti:ti + 1, :].to_broadcast([128, 2, 512]), op=ALU.add),
                                  reads=[BTT], writes=[PB[2 * par], PB[2 * par + 1]])

                    def ex(c):
                        par = c % 2
                        p3 = c % 3
                        b = bias_of(c)
                        pg.op("act", lambda e: e.activation(out=pT[p3][:], in_=ps[:, 2 * par:2 * par + 2, :].rearrange("p a b -> p (a b)"),
                                                          func=AF.Exp, bias=(b if b is not None else zero1[:])),
                              reads=[PB[2 * par], PB[2 * par + 1], Bc], writes=[BpT[p3]])

                    def pv(c):
                        par = c % 3
                        dm = diff_mode[0]

                        def f(e):
                            ins = None
                            for u in range(2):
                                lv, m = lhs_v(u, c)
                                ins = e.matmul(ps[0:m, 4 + u, :], lhsT=lv, rhs=pT[par][:, u * 512:(u + 1) * 512], start=(c == 0), stop=(c == NC - 1))
                            if dm:
                                for u in range(2):
                                    ins = e.matmul(ps[32 * u:32 * u + 1, 6, :], lhsT=onesb[:, 0:1], rhs=pT[par][:, u * 512:(u + 1) * 512],
                                                   start=(c == 0), stop=(c == NC - 1), skip_group_check=True)
                            return ins
                        w = [PB[4], PB[5]] + ([PB[6]] if diff_mode[0] else [])
                        pg.op("pe", f, reads=[BKV[c // cpg], BpT[par], Bc], writes=w)
                    qk(0)
                    qk(1)
                    for c in range(NC):
                        ex(c)
                        if c + 2 < NC:
                            qk(c + 2)
                        pv(c)
                    epilogue()

                diff_mode = [False]
                load_kv(sc["KTa"], sc["Va"], 130)
                for g in range(4):
                    pg.dma("sp", QT[:, g, :], sc["QTa"][g], writes=[BQ])
                for qt in range(nq // 128):
                    units = [QT[64 * u:64 * u + 64, :, qt * 128:(qt + 1) * 128] for u in range(2)]

                    def lhs_v(u, c):
                        return VV[:, c, u * 65:(u + 1) * 65], 65

                    def epi(qt=qt):
                        for u in range(2):
                            pg.op("dve", lambda e, u=u: e.tensor_copy(osb[u][0:65, :], ps[0:65, 4 + u, :]), reads=[PB[4 + u]], writes=[Bosb[u]])
                        for u in range(2):
                            pg.op("dve", lambda e, u=u: e.reciprocal(out=zr[64:65, u, :], in_=osb[u][64:65, :]), reads=[Bosb[u]], writes=[Bzr])
                            pg.dma("pool", zdram[u, 0:1, :], zr[64:65, u, :], reads=[Bzr], writes=[Bzd[u]])
                            pg.dma("pool", bcz[0:64, u, :], zdram[u, 0:1, :].broadcast_to([64, 512]), reads=[Bzd[u]], writes=[Bbcz])
                            pg.op("dve", lambda e, u=u: e.tensor_tensor(out=onrm[0:64, :], in0=osb[u][0:64, :], in1=bcz[0:64, u, :], op=ALU.mult),
                                  reads=[Bosb[u], Bbcz], writes=[Bonrm])
                            pg.dma("pool", sc["outT"][u * 256:(u + 1) * 256, qt * 128:(qt + 1) * 128].rearrange("(g d) t -> d g t", g=4),
                                   onrm[0:64, :].rearrange("d (g t) -> d g t", g=4), reads=[Bonrm])
                    attn_tile(units, lhs_v, {}, lambda c: None, epi)
                diff_mode[0] = True
                for h in range(4):
                    load_kv(sc["KTd"][h], sc["Vd"][:, h * 128:(h + 1) * 128], 128)
                    pg.dma("sp", QT[:, 0, :], sc["QTd"][h], writes=[BQ])
                    for ti in range(8):
                        i = ti % 2
                        if ti < 6:
                            dofs = (ti - 1) * 128
                            base = 512 - dofs
                            for part in range(3):
                                src = bass.AP(ub_d.tensor, ub_d[part, h, base:base + 1].offset, [[1, 128], [1, 512]])
                                pg.dma("sp", hk[i][:, part, :], src, writes=[Bhk[i]])
                        else:
                            uw = uw_d[name] if ti == 6 else uw2_d[name]
                            for part in range(3):
                                src = bass.AP(uw.tensor, uw[part, h, 0:1].offset, [[1, 128], [1, 512]])
                                pg.dma("sp", hk[i][:, part, :], src, writes=[Bhk[i]])

                        def mmT(e, i=i):
                            ins = None
                            for part in range(3):
                                ins = e.matmul(psb(7), lhsT=antiid, rhs=hk[i][:, part, :], start=(part == 0), stop=(part == 2))
                            return ins
                        pg.op("pe", mmT, reads=[Bhk[i], Bc], writes=[PB[7]])
                        pg.op("dve", lambda e, ti=ti: e.tensor_copy(TT[:, ti, :], psb(7)), reads=[PB[7]], writes=[BTT])
                    for qt in range(nq // 512):
                        units = [QT[64 * u:64 * u + 64, 0, qt * 512:(qt + 1) * 512] for u in range(2)]
                        c0 = qt * 4
                        near = {}
                        for ti in range(6):
                            c = c0 - 1 + ti
                            if 0 <= c < NC:
                                near[c] = ti
                        if qt == 0:
                            near[NC - 1] = 6
                        if qt == nq // 512 - 1:
                            near[nq // 128] = 7
                        fb = farb[name]

                        def bias_of(c, near=near, c0=c0, h=h, fb=fb):
                            if c in near:
                                return None
                            if c < c0:
                                return fb[:, h, NC:NC + 1]
                            return fb[:, h, c:c + 1]

                        def lhs_v(u, c):
                            return VV[:, c, 0:128], 128

                        def epi(qt=qt, h=h):
                            for u in range(2):
                                pg.op("dve", lambda e, u=u: e.tensor_copy(osb[u][:], psb(4 + u)), reads=[PB[4 + u]], writes=[Bosb[u]])
                            for u in range(2):
                                pg.op("dve", lambda e, u=u: e.tensor_copy(zr[32 * u:32 * u + 1, u, :], ps[32 * u:32 * u + 1, 6, :]), reads=[PB[6]], writes=[Bzr])
                            for u in range(2):
                                pg.op("dve", lambda e, u=u: e.reciprocal(out=zr[32 * u:32 * u + 1, u, :], in_=zr[32 * u:32 * u + 1, u, :]), reads=[Bzr], writes=[Bzr])
                                pg.dma("pool", zdram[u, 1:2, :], zr[32 * u:32 * u + 1, u, :], reads=[Bzr], writes=[Bzd[u]])
                                pg.dma("pool", bcz[:, u, :], zdram[u, 1:2, :].broadcast_to([128, 512]), reads=[Bzd[u]], writes=[Bbcz])
                            pg.op("dve", lambda e: e.tensor_tensor(out=osb[0][:], in0=osb[0][:], in1=bcz[:, 0, :], op=ALU.mult), reads=[Bosb[0], Bbcz], writes=[Bosb[0]])
                            pg.op("pool", lambda e: e.tensor_tensor(out=osb[1][:], in0=osb[1][:], in1=bcz[:, 1, :], op=ALU.mult), reads=[Bosb[1], Bbcz], writes=[Bosb[1]])
                            pg.op("dve", lambda e: e.scalar_tensor_tensor(out=dd[:], in0=osb[1][:], scalar=negl[:, 0:1], in1=osb[0][:], op0=ALU.mult, op1=ALU.add),
                                  reads=[Bosb[0], Bosb[1], Bnegl], writes=[Bdd])
                            pg.op("pool", lambda e: e.tensor_tensor(out=dsq[:], in0=dd[:], in1=dd[:], op=ALU.mult), reads=[Bdd], writes=[Bdsq])
                            pg.op("dve", lambda e: e.tensor_copy(dsqh[:], dsq[:]), reads=[Bdsq], writes=[Bdsqh])
                            pg.op("pool", lambda e: e.tensor_tensor(out=dsq[:], in0=dsq[:], in1=dsqh[:], op=ALU.subtract), reads=[Bdsq, Bdsqh], writes=[Bdsq])
                            pg.op("pool", lambda e: e.tensor_copy(dsqm[:], dsq[:]), reads=[Bdsq], writes=[Bdsqm])

                            def mmss(e):
                                e.matmul(psb(7), lhsT=o128, rhs=dsqh[:], start=True, stop=False)
                                return e.matmul(psb(7), lhsT=o128, rhs=dsqm[:], start=False, stop=True)
                            pg.op("pe", mmss, reads=[Bdsqh, Bdsqm, Bc], writes=[PB[7]])
                            pg.op("act", lambda e: e.activation(out=drs[:], in_=psb(7), func=AF.Sqrt, bias=epsc[:], scale=1.0), reads=[PB[7], Bc], writes=[Bdrs])
                            pg.op("dve", lambda e: e.reciprocal(out=drs[:], in_=drs[:]), reads=[Bdrs], writes=[Bdrs])
                            pg.op("dve", lambda e: e.scalar_tensor_tensor(out=onrm[:], in0=dd[:], scalar=gcols[:, 4:5], in1=drs[:], op0=ALU.mult, op1=ALU.mult),
                                  reads=[Bdd, Bdrs, Bc], writes=[Bonrm])
                            pg.dma("pool", sc["outT"][512 + h * 128:512 + (h + 1) * 128, qt * 512:(qt + 1) * 512], onrm[:], reads=[Bonrm])
                        attn_tile(units, lhs_v, near, bias_of, epi)
                pg.end()

        TK = 256
        for jb in jobs:
            name, N, nq = jb["name"], jb["N"], jb["nq"]
            sc = S[name]
            with contextlib.ExitStack() as st:
                def sb(nm, shape, dt):
                    return st.enter_context(nc.sbuf_tensor("s3_" + nm + name, list(shape), dt))
                PB = new_ps()
                pg.begin()
                wo = sb("wo", [128, 8, D], BF16)
                Bw = Buf("w3")
                for k in range(8):
                    pg.dma("sp", wo[:, k, :], wb_out[k * 128:(k + 1) * 128, :], writes=[Bw])
                G1 = sb("G1", [128, D], F32)
                A2 = sb("A2", [128, D], F32)
                SH2 = sb("SH2", [128, D], F32)
                Bmodt = Buf("modt")
                for tile_, ri in ((G1, 2), (A2, 3), (SH2, 4)):
                    pg.dma("sp", tile_[:], rows_d[jb["b"], ri:ri + 1, :].broadcast_to([128, D]), writes=[Bmodt])
                xt = [sb("xt%d" % i, [128, 2, D], F32) for i in range(2)]
                Bxt = [Buf("xt%d" % i) for i in range(2)]
                oT = [sb("oT%d" % i, [128, 8, TK], BF16) for i in range(2)]
                BoT = [Buf("oT%d" % i) for i in range(2)]
                junk = sb("junk", [128, D], F32)
                ss = sb("ss", [128, 4], F32)
                rstd = sb("rstd", [128, 4], F32)
                mixs = [sb("mixs%d" % i, [128, D], F32) for i in range(2)]
                tt = [sb("tt%d" % i, [128, D], F32) for i in range(2)]
                hb = [sb("hb%d" % i, [128, D], BF16) for i in range(2)]
                hT = [sb("hT%d" % i, [128, 8, TK], BF16) for i in range(2)]
                Bjunk = Buf("junk")
                Bss = [Buf("ss%d" % i) for i in range(4)]
                Bmixs = [Buf("mixs%d" % i) for i in range(2)]
                Btt = [Buf("tt%d" % i) for i in range(2)]
                Bhb = [Buf("hb%d" % i) for i in range(2)]
                BhT = [Buf("hT%d" % i) for i in range(2)]
                psT = [ps[:, i, :].bitcast(BF16) for i in range(2)]

                def rms_stat(src_ap, src_bufs, col):
                    pg.op("dve", lambda e: e.scalar_tensor_tensor(out=junk[:], in0=src_ap, scalar=1.0 / D, in1=src_ap, op0=ALU.mult, op1=ALU.mult,
                                                                  accum_out=ss[:, col:col + 1]), reads=src_bufs, writes=[Bjunk, Bss[col]])
                    pg.op("dve", lambda e: e.tensor_scalar(out=rstd[:, col:col + 1], in0=ss[:, col:col + 1], scalar1=EPS, scalar2=None, op0=ALU.add),
                          reads=[Bss[col]], writes=[Bss[col]])
                    pg.op("pool", lambda e: e.tensor_tensor(out=rstd[:, col:col + 1], in0=rstd[:, col:col + 1], in1=nhalf[:, 0:1], op=ALU.pow),
                          reads=[Bss[col], Bc], writes=[Bss[col]])

                for t in range(nq // TK):
                    xi = t % 2
                    pg.dma("sp", xt[xi][:], x_in[name][t * TK:(t + 1) * TK, :].rearrange("(s p) d -> p s d", p=128), writes=[Bxt[xi]])
                    pg.dma("sp", oT[xi][:], sc["outT"][:, t * TK:(t + 1) * TK].rearrange("(k p) t -> p k t", p=128), writes=[BoT[xi]])
                    for s in range(2):
                        def mmo(e, s=s, xi=xi):
                            ins = None
                            for hh in range(2):
                                for k in range(8):
                                    ins = e.matmul(psb(2 + 2 * s + hh), lhsT=oT[xi][:, k, s * 128:(s + 1) * 128], rhs=wo[:, k, hh * 512:(hh + 1) * 512],
                                                   start=(k == 0), stop=(k == 7))
                            return ins
                        pg.op("pe", mmo, reads=[BoT[xi], Bw], writes=[PB[2 + 2 * s], PB[3 + 2 * s]])
                        pg.op("act", lambda e, s=s: e.activation(out=mixs[s][:], in_=ps[:, 2 + 2 * s:4 + 2 * s, :].rearrange("p a b -> p (a b)"), func=AF.Copy),
                              reads=[PB[2 + 2 * s], PB[3 + 2 * s]], writes=[Bmixs[s]])
                        rms_stat(mixs[s][:], [Bmixs[s]], s)
                        pg.op("dve", lambda e, s=s: e.scalar_tensor_tensor(out=tt[s][:], in0=mixs[s][:], scalar=rstd[:, s:s + 1], in1=G1[:], op0=ALU.mult, op1=ALU.mult),
                              reads=[Bmixs[s], Bss[s], Bmodt], writes=[Btt[s]])
                        pg.op("pool", lambda e, s=s, xi=xi: e.tensor_tensor(out=xt[xi][:, s, :], in0=xt[xi][:, s, :], in1=tt[s][:], op=ALU.add),
                              reads=[Btt[s], Bxt[xi]], writes=[Bxt[xi]])
                        rms_stat(xt[xi][:, s, :], [Bxt[xi]], 2 + s)
                        pg.op("dve", lambda e, s=s, xi=xi: e.scalar_tensor_tensor(out=tt[s][:], in0=xt[xi][:, s, :], scalar=rstd[:, 2 + s:3 + s], in1=A2[:], op0=ALU.mult, op1=ALU.mult),
                              reads=[Bxt[xi], Bss[2 + s], Bmodt], writes=[Btt[s]])
                        pg.op("pool", lambda e, s=s: e.tensor_tensor(out=hb[s][:], in0=tt[s][:], in1=SH2[:], op=ALU.add), reads=[Btt[s], Bmodt], writes=[Bhb[s]])

                        def tr(e, s=s):
                            ins = None
                            for k in range(8):
                                ins = e.transpose(out=psT[s][:, k * 128:(k + 1) * 128], in_=hb[s][:, k * 128:(k + 1) * 128], identity=ident)
                            return ins
                        pg.op("pe", tr, reads=[Bhb[s], Bc], writes=[PB[s]])
                        pg.op("act", lambda e, s=s, xi=xi: e.activation(out=hT[xi][:, :, s * 128:(s + 1) * 128], in_=psT[s].rearrange("p (k t) -> p k t", k=8), func=AF.Copy),
                              reads=[PB[s]], writes=[BhT[xi]])
                    pg.dma("pool", sc["x1"][t * TK:(t + 1) * TK, :].rearrange("(s p) d -> p s d", p=128), xt[xi][:], reads=[Bxt[xi]])
                    pg.dma("pool", sc["h2T"][:, :, t * TK:(t + 1) * TK].rearrange("k p t -> p k t"), hT[xi][:], reads=[BhT[xi]])
                pg.end()

        for jb in jobs:
            name, N, nq = jb["name"], jb["N"], jb["nq"]
            sc = S[name]
            with contextlib.ExitStack() as st:
                def sb(nm, shape, dt):
                    return st.enter_context(nc.sbuf_tensor("s4_" + nm + name, list(shape), dt))
                PB = new_ps()
                pg.begin()
                wgu = sb("wgu", [128, 8, 2 * DFF], BF16)
                wdn = sb("wdn", [128, 22, D], BF16)
                Bw = Buf("w3")
                for k in range(8):
                    pg.dma("sp", wgu[:, k, :], wb_gu[k * 128:(k + 1) * 128, :], writes=[Bw])
                pg.dma("sp", wdn[:], wb_down.rearrange("(k p) n -> p k n", p=128), writes=[Bw])
                G2 = sb("G2", [128, D], F32)
                Bmodt = Buf("modt")
                pg.dma("sp", G2[:], rows_d[jb["b"], 5:6, :].broadcast_to([128, D]), writes=[Bmodt])
                xt = [sb("xt%d" % i, [128, 2, D], F32) for i in range(2)]
                Bxt = [Buf("xt%d" % i) for i in range(2)]
                hT = [sb("hT%d" % i, [128, 8, TK], BF16) for i in range(2)]
                BhT = [Buf("hT%d" % i) for i in range(2)]
                junk = sb("junk", [128, D], F32)
                ss = sb("ss", [128, 2], F32)
                rstd = sb("rstd", [128, 2], F32)
                act_ = sb("act", [128, 22, TK], BF16)
                sg = [sb("sg%d" % i, [128, TK], F32) for i in range(2)]
                fs = [sb("fs%d" % i, [128, D], F32) for i in range(2)]
                tt = [sb("tt%d" % i, [128, D], F32) for i in range(2)]
                Bjunk = Buf("junk")
                Bss = [Buf("ss%d" % i) for i in range(2)]
                Bfs = [Buf("fs%d" % i) for i in range(2)]
                Btt = [Buf("tt%d" % i) for i in range(2)]
                Bact = [Buf("act%d" % i) for i in range(22)]
                Bsg = [Buf("sg%d" % i) for i in range(2)]
                for t in range(nq // TK):
                    xi = t % 2
                    pg.dma("sp", xt[xi][:], sc["x1"][t * TK:(t + 1) * TK, :].rearrange("(s p) d -> p s d", p=128), writes=[Bxt[xi]])
                    pg.dma("sp", hT[xi][:], sc["h2T"][:, :, t * TK:(t + 1) * TK].rearrange("k p t -> p k t"), writes=[BhT[xi]])
                    for j in range(22):
                        i = j % 2

                        def mmg(e, j=j, i=i, xi=xi):
                            ins = None
                            for k in range(8):
                                ins = e.matmul(ps[:, 2 * i, 0:TK], lhsT=wgu[:, k, j * 128:(j + 1) * 128], rhs=hT[xi][:, k, :], start=(k == 0), stop=(k == 7))
                            for k in range(8):
                                ins = e.matmul(ps[:, 2 * i + 1, 0:TK], lhsT=wgu[:, k, DFF + j * 128:DFF + (j + 1) * 128], rhs=hT[xi][:, k, :], start=(k == 0), stop=(k == 7))
                            return ins
                        pg.op("pe", mmg, reads=[Bw, BhT[xi]], writes=[PB[2 * i], PB[2 * i + 1]])
                        pg.op("act", lambda e, i=i: e.activation(out=sg[i][:], in_=ps[:, 2 * i, 0:TK], func=AF.Silu), reads=[PB[2 * i]], writes=[Bsg[i]])
                        pg.op("dve", lambda e, i=i, j=j: e.tensor_tensor(out=act_[:, j, :], in0=sg[i][:], in1=ps[:, 2 * i + 1, 0:TK], op=ALU.mult),
                              reads=[Bsg[i], PB[2 * i + 1]], writes=[Bact[j]])
                    for s in range(2):
                        def mmd(e, s=s):
                            ins = None
                            for hh in range(2):
                                for j in range(22):
                                    ins = e.matmul(psb(4 + 2 * s + hh), lhsT=act_[:, j, s * 128:(s + 1) * 128], rhs=wdn[:, j, hh * 512:(hh + 1) * 512],
                                                   start=(j == 0), stop=(j == 21))
                            return ins
                        pg.op("pe", mmd, reads=[Bw] + Bact, writes=[PB[4 + 2 * s], PB[5 + 2 * s]])
                        pg.op("act", lambda e, s=s: e.activation(out=fs[s][:], in_=ps[:, 4 + 2 * s:6 + 2 * s, :].rearrange("p a b -> p (a b)"), func=AF.Copy),
                              reads=[PB[4 + 2 * s], PB[5 + 2 * s]], writes=[Bfs[s]])
                        pg.op("dve", lambda e, s=s: e.scalar_tensor_tensor(out=junk[:], in0=fs[s][:], scalar=1.0 / D, in1=fs[s][:], op0=ALU.mult, op1=ALU.mult,
                                                                         accum_out=ss[:, s:s + 1]), reads=[Bfs[s]], writes=[Bjunk, Bss[s]])
                        pg.op("dve", lambda e, s=s: e.tensor_scalar(out=rstd[:, s:s + 1], in0=ss[:, s:s + 1], scalar1=EPS, scalar2=None, op0=ALU.add),
                              reads=[Bss[s]], writes=[Bss[s]])
                        pg.op("pool", lambda e, s=s: e.tensor_tensor(out=rstd[:, s:s + 1], in0=rstd[:, s:s + 1], in1=nhalf[:, 0:1], op=ALU.pow),
                              reads=[Bss[s], Bc], writes=[Bss[s]])
                        pg.op("dve", lambda e, s=s: e.scalar_tensor_tensor(out=tt[s][:], in0=fs[s][:], scalar=rstd[:, s:s + 1], in1=G2[:], op0=ALU.mult, op1=ALU.mult),
                              reads=[Bfs[s], Bss[s], Bmodt], writes=[Btt[s]])
                        pg.op("pool", lambda e, s=s, xi=xi: e.tensor_tensor(out=xt[xi][:, s, :], in0=xt[xi][:, s, :], in1=tt[s][:], op=ALU.add),
                              reads=[Btt[s], Bxt[xi]], writes=[Bxt[xi]])
                    pg.dma("pool", y_out[name][t * TK:(t + 1) * TK, :].rearrange("(s p) d -> p s d", p=128), xt[xi][:], reads=[Bxt[xi]])
                pg.end()
    return nc


def _prep_shared(inp, NP, NS):
    f = lambda a: np.ascontiguousarray(np.asarray(a, dtype=np.float32))
    perm = _perm64()
    w_in = f(inp["w_in"])[0]
    o1, o2, o3, o4, o5 = 512, 640, 768, 1280, 1792
    cols = []
    qa_nat = np.array([[(kv * 4 + g) * 64 + d for kv in range(2) for d in range(64)] for g in range(4)])
    qa_prm = np.array([[(kv * 4 + g) * 64 + perm[d] for kv in range(2) for d in range(64)] for g in range(4)])
    cols += list(qa_nat.reshape(-1)) + list(qa_prm.reshape(-1))
    cols += [o1 + kv * 64 + d for kv in range(2) for d in range(64)]
    cols += [o1 + kv * 64 + perm[d] for kv in range(2) for d in range(64)]
    cols += list(range(o3, o4)) + list(range(o4, o5)) + list(range(o2, o3)) + list(range(o5, 2304))
    cols = np.array(cols)
    assert len(cols) == WIN
    g_q, g_k = f(inp["g_q"])[0], f(inp["g_k"])[0]
    gcols = np.zeros((128, 8), np.float32)
    gcols[:, 0] = np.tile(g_q, 2)
    gcols[:, 1] = np.tile(g_q[perm], 2)
    gcols[:, 2] = np.tile(g_k, 2)
    gcols[:, 3] = np.tile(g_k[perm], 2)
    gcols[:, 4] = f(inp["g_subln"])[0]
    gcols[:, 5] = 1.0 - LAM_INIT
    grow = np.stack([f(inp["g_pre_mix"])[0], f(inp["g_post_mix"])[0], f(inp["g_pre_ffn"])[0], f(inp["g_post_ffn"])[0]])
    grow2 = np.ascontiguousarray(np.stack([grow, grow]))
    lamv = np.stack([f(inp["lam_q1"])[0], f(inp["lam_k1"])[0], f(inp["lam_q2"])[0], f(inp["lam_k2"])[0]])[None]
    b_ada = f(inp["b_ada"])
    cm = np.zeros((128, 5, 128), np.float32)
    cm[:, 0, :] = np.eye(128)
    cm[:, 1, :] = np.eye(128)[::-1]
    cm[:, 2, :] = 1.0
    cm[0:64, 3, 0:64] = 1.0 / 64
    cm[64:128, 3, 64:128] = 1.0 / 64
    cm[:, 4, :] = 1.0 / 128
    m = np.arange(ULEN)
    emain = _onehot(_rel_bucket_np(639 - m))
    sh = dict(
        w_ada=f(inp["w_ada"])[0], b_ada2=np.ascontiguousarray(np.concatenate([b_ada, b_ada], 0)), grow=grow2,
        w_in_p=np.ascontiguousarray(w_in[:, cols]), w_out=f(inp["w_out"])[0], w_gu=f(inp["w_gu"])[0], w_down=f(inp["w_down"])[0],
        gcols=gcols, lamv=np.ascontiguousarray(lamv), relb=f(inp["rel_bias"]),
        cmat=cm.astype(ml_dtypes.bfloat16), emain=emain.astype(ml_dtypes.bfloat16),
    )
    return sh


def _prep_core(inp, sh, c, NP, NS):
    f = lambda a: np.asarray(a, dtype=np.float32)
    pb, pq, sbi, sq = c // 4, c % 4, c // 2, c % 2
    m = dict(sh)
    cp, cs = f(inp["c_prompt"])[pb], f(inp["c_sample"])[sbi]
    cT = np.stack([cp, cs], -1).reshape(8, 128, 2).transpose(1, 0, 2)
    m["cT"] = np.ascontiguousarray(cT)
    for nm, x, N, nq, qi in (("P", f(inp["x_prompt"])[pb], NP, NP // 4, pq), ("S", f(inp["x_sample"])[sbi], NS, NS // 2, sq)):
        qoff = qi * nq
        m["x" + nm] = np.ascontiguousarray(np.roll(x, -qoff, axis=0))
        pos = (np.arange(N) + qoff) % N
        cosT, sinT = _rope_tables(pos)
        m["cos" + nm] = cosT
        m["sin" + nm] = sinT
        NC = N // 128
        mm = np.arange(WLEN)
        if qoff > 0:
            bw = _rel_bucket_np(-1 - mm)
        else:
            bw = np.full(WLEN, NB // 2 + NB // 2 - 1)
        m["ewrap" + nm] = _onehot(bw).astype(ml_dtypes.bfloat16)
        if qoff + nq == N:
            bw2 = np.full(WLEN, NB // 2 - 1)
        else:
            bw2 = _rel_bucket_np(639 - mm)
        m["ewrap2" + nm] = _onehot(bw2).astype(ml_dtypes.bfloat16)
        far = np.zeros(NC + 1, np.int64)
        for ch in range(NC):
            if ch * 128 < nq:
                far[ch] = 31
            else:
                far[ch] = 31 if ch * 128 < N - qoff else 15
        far[NC] = 15
        m["efar" + nm] = _onehot(far).astype(ml_dtypes.bfloat16)
    return m


_CACHE = {}


def run(inputs, NP, NS, debug=False, ncores=8):
    key = (NP, NS, debug)
    if key not in _CACHE:
        _CACHE[key] = build_program(NP, NS, debug)
    nc = _CACHE[key]
    sh = _prep_shared(inputs, NP, NS)
    in_maps = [_prep_core(inputs, sh, c, NP, NS) for c in range(ncores)]
    res = run_bass_kernel_spmd(nc, in_maps, core_ids=list(range(ncores)))
    return res.results


def kernel(**inputs):
    NP = int(np.asarray(inputs["x_prompt"]).shape[1])
    NS = int(np.asarray(inputs["x_sample"]).shape[1])
    r = run(inputs, NP, NS)
    yp = np.zeros((2, NP, D), np.float32)
    ys = np.zeros((4, NS, D), np.float32)
    for c in range(8):
        pb, pq, sbi, sq = c // 4, c % 4, c // 2, c % 2
        nqp, nqs = NP // 4, NS // 2
        yp[pb, pq * nqp:(pq + 1) * nqp] = r[c]["yP"]
        ys[sbi, sq * nqs:(sq + 1) * nqs] = r[c]["yS"]
    return (yp, ys)
```

```python
import contextlib
import math
import numpy as np
import ml_dtypes
import concourse.bass as bass
import concourse.mybir as mybir
from concourse.bass_utils import run_bass_kernel_spmd

F32 = mybir.dt.float32
BF16 = mybir.dt.bfloat16
ALU = mybir.AluOpType
AF = mybir.ActivationFunctionType
AX = mybir.AxisListType

D = 1024
DFF = 2816
HD = 64
EPS = 1e-6
NB = 32
WIN = 2944
LAM_INIT = 0.8 - 0.6 * math.exp(-0.3 * 0)
ULEN = 1279
WLEN = 639

ENGS = ["pe", "act", "dve", "pool", "sp"]
N_DMA_SEMS = 6


class Buf:
    __slots__ = ("name", "w", "r", "excl")

    def __init__(self, name, excl=False):
        self.name = name
        self.excl = excl
        self.w = None
        self.r = []


class Op:
    __slots__ = ("eng", "fn", "waits", "signal", "ev", "is_dma")

    def __init__(self, eng, fn, is_dma):
        self.eng = eng
        self.fn = fn
        self.waits = []
        self.signal = False
        self.ev = None
        self.is_dma = is_dma


class Prog:
    def __init__(self, nc, stack):
        self.nc = nc
        self.sems = {}
        self.cnt = {}
        for e in ENGS:
            self.sems[e] = stack.enter_context(nc.semaphore("s_" + e))
            self.cnt[e] = 0
        self.dma_sems = {}
        for q in ("sp", "act", "pool"):
            lst = []
            for i in range(N_DMA_SEMS):
                nm = "d_%s%d" % (q, i)
                self.sems[nm] = stack.enter_context(nc.semaphore(nm))
                self.cnt[nm] = 0
                lst.append(nm)
            self.dma_sems[q] = lst
        self.dma_rr = {q: 0 for q in self.dma_sems}
        self.waited = {e: {} for e in ENGS}
        self.ops = None
        self.nops = 0

    def begin(self):
        self.ops = {e: [] for e in ENGS}
        self.allops = []
        self.dma_last = {}

    def _dep(self, op, other):
        if other is None or other is op:
            return
        if other.eng == "pe" and op.eng == "pe" and not other.is_dma and not op.is_dma:
            return
        op.waits.append(other)

    def op(self, eng, fn, reads=(), writes=(), dma=False):
        o = Op(eng, fn, dma)
        reads = list(reads)
        writes = list(writes)
        for b in reads:
            if b.excl and b not in writes:
                writes.append(b)
        for b in reads:
            self._dep(o, b.w)
        for b in writes:
            self._dep(o, b.w)
            for r in b.r:
                self._dep(o, r)
        for b in reads:
            b.r.append(o)
        for b in writes:
            b.w = o
            b.r = []
        if dma:
            i = self.dma_rr[eng]
            self.dma_rr[eng] = (i + 1) % N_DMA_SEMS
            nm = self.dma_sems[eng][i]
            prev = self.dma_last.get(nm)
            if prev is not None:
                o.waits.append(prev)
            self.dma_last[nm] = o
            self.cnt[nm] += 16
            o.ev = (nm, self.cnt[nm])
            o.signal = True
        self.ops[eng].append(o)
        self.allops.append(o)
        return o

    def dma(self, q, out, in_, reads=(), writes=()):
        return self.op(q, lambda e: e.dma_start(out=out, in_=in_), reads, writes, dma=True)

    def end(self):
        nc = self.nc
        for o in self.allops:
            for w in o.waits:
                w.signal = True
        for e in ENGS:
            for o in reversed(self.ops[e]):
                if not o.is_dma:
                    o.signal = True
                    break
        for e in ENGS:
            for o in self.ops[e]:
                if not o.is_dma and o.signal:
                    self.cnt[e] += 1
                    o.ev = (e, self.cnt[e])
        final = dict(self.cnt)
        sems = self.sems
        ops = self.ops
        waited_all = self.waited
        self.nops += len(self.allops)

        def emit(ename, eng):
            waited = waited_all[ename]
            for o in ops[ename]:
                need = {}
                for w in o.waits:
                    s, v = w.ev
                    if need.get(s, 0) < v:
                        need[s] = v
                for s, v in need.items():
                    if waited.get(s, 0) >= v:
                        continue
                    waited[s] = v
                    eng.wait_ge(sems[s], v)
                ins = o.fn(eng)
                if o.signal:
                    s, v = o.ev
                    ins.then_inc(sems[s], 16 if o.is_dma else 1)
            for s, v in final.items():
                if v > 0 and waited.get(s, 0) < v:
                    waited[s] = v
                    eng.wait_ge(sems[s], v)

        with nc.Block() as block:
            @block.tensor
            def _(eng):
                emit("pe", eng)

            @block.scalar
            def _(eng):
                emit("act", eng)

            @block.vector
            def _(eng):
                emit("dve", eng)

            @block.gpsimd
            def _(eng):
                emit("pool", eng)

            @block.sync
            def _(eng):
                emit("sp", eng)
        self.ops = None


def _perm64():
    d = np.arange(64)
    return np.where((d % 32) < 16, d + 16, d - 16)


def _rel_bucket_np(rel):
    half = NB // 2
    max_exact = half // 2
    n = np.abs(rel)
    nf = np.maximum(n, max_exact).astype(np.float32)
    large = max_exact + (np.log(nf / np.float32(max_exact)) / np.float32(math.log(128 / max_exact))
                         * (half - max_exact)).astype(np.int32)
    large = np.minimum(large, half - 1)
    return np.where(rel > 0, half, 0) + np.where(n < max_exact, n, large)


def _rope_tables(pos):
    row = (pos // 64).astype(np.float32)
    col = (pos % 64).astype(np.float32)
    half = HD // 2
    inv = (np.float32(10000.0) ** (-np.arange(0, half, 2, dtype=np.float32) / np.float32(half))).astype(np.float32)
    ang_r = row[:, None] * inv[None, :]
    ang_c = col[:, None] * inv[None, :]
    ang = np.concatenate([ang_r, ang_r, ang_c, ang_c], axis=-1).astype(np.float32)
    cos = np.cos(ang).astype(np.float32)
    sin = np.sin(ang).astype(np.float32)
    d = np.arange(64)
    sign = np.where((d % 32) < 16, -1.0, 1.0).astype(np.float32)
    sin_s = sin * sign[None, :]
    cosT = np.ascontiguousarray(np.concatenate([cos.T, cos.T], axis=0))
    sinT = np.ascontiguousarray(np.concatenate([sin_s.T, sin_s.T], axis=0))
    return cosT, sinT


def _onehot(buckets):
    e = np.zeros((NB, len(buckets)), dtype=np.float32)
    e[buckets, np.arange(len(buckets))] = 1.0
    return e


def build_program(NP, NS, debug=False):
    jobs = [dict(name="P", N=NP, nq=NP // 4, b=0), dict(name="S", N=NS, nq=NS // 2, b=1)]
    for jb in jobs:
        assert jb["nq"] % 512 == 0 and jb["N"] % 512 == 0
        jb["NC"] = jb["N"] // 128
    nc = bass.Bass("TRN2", target_bir_lowering=False)

    def din(name, shape, dt=F32):
        return nc.dram_tensor(name, list(shape), dt, kind="ExternalInput").ap()

    def dscr(name, shape, dt):
        if debug and not name.startswith("wb_"):
            return nc.dram_tensor(name, list(shape), dt, kind="ExternalOutput").ap()
        return nc.dram_tensor(name, list(shape), dt).ap()

    x_in = {"P": din("xP", [NP, D]), "S": din("xS", [NS, D])}
    cT_d = din("cT", [128, 8, 2])
    w_ada_d = din("w_ada", [D, 6 * D])
    b_ada2_d = din("b_ada2", [2, 6 * D])
    grow_d = din("grow", [2, 4, D])
    w_in_d = din("w_in_p", [D, WIN])
    w_out_d = din("w_out", [D, D])
    w_gu_d = din("w_gu", [D, 2 * DFF])
    w_down_d = din("w_down", [DFF, D])
    gcols_d = din("gcols", [128, 8])
    lamv_d = din("lamv", [1, 4, 64])
    relb_d = din("relb", [NB, 4])
    rope_d = {jb["name"]: (din("cos" + jb["name"], [128, jb["N"]]), din("sin" + jb["name"], [128, jb["N"]])) for jb in jobs}
    cmat_d = din("cmat", [128, 5, 128], BF16)
    emain_d = din("emain", [NB, ULEN], BF16)
    ewrap_d = {jb["name"]: din("ewrap" + jb["name"], [NB, WLEN], BF16) for jb in jobs}
    ewrap2_d = {jb["name"]: din("ewrap2" + jb["name"], [NB, WLEN], BF16) for jb in jobs}
    efar_d = {jb["name"]: din("efar" + jb["name"], [NB, jb["NC"] + 1], BF16) for jb in jobs}
    y_out = {jb["name"]: nc.dram_tensor("y" + jb["name"], [jb["nq"], D], F32, kind="ExternalOutput").ap() for jb in jobs}

    wb_in = dscr("wb_in", [D, WIN], BF16)
    wb_out = dscr("wb_out", [D, D], BF16)
    wb_gu = dscr("wb_gu", [D, 2 * DFF], BF16)
    wb_down = dscr("wb_down", [DFF, D], BF16)
    rows_d = dscr("rows_d", [2, 6, D], F32)
    lam_d = dscr("lam_d", [1, 2], F32)
    ub_d = dscr("ub_d", [3, 4, ULEN], BF16)
    uw_d = {jb["name"]: dscr("uw_d" + jb["name"], [3, 4, WLEN], BF16) for jb in jobs}
    uw2_d = {jb["name"]: dscr("uw2_d" + jb["name"], [3, 4, WLEN], BF16) for jb in jobs}
    ufar_d = {jb["name"]: dscr("ufar_d" + jb["name"], [1, 4 * (jb["NC"] + 1)], F32) for jb in jobs}
    S = {}
    for jb in jobs:
        n, N, nq = jb["name"], jb["N"], jb["nq"]
        S[n] = dict(
            KTa=dscr("KTa" + n, [128, N], BF16), Va=dscr("Va" + n, [N, 130], BF16),
            KTd=dscr("KTd" + n, [4, 128, N], BF16), Vd=dscr("Vd" + n, [N, 512], BF16),
            QTa=dscr("QTa" + n, [4, 128, nq], BF16), QTd=dscr("QTd" + n, [4, 128, nq], BF16),
            outT=dscr("outT" + n, [D, nq], BF16), zrow=dscr("zrow" + n, [2, 2, 512], F32),
            x1=dscr("x1" + n, [nq, D], F32), h2T=dscr("h2T" + n, [8, 128, nq], BF16),
        )

    with contextlib.ExitStack() as gst:
        pg = Prog(nc, gst)
        def gsb(name, shape, dt):
            return gst.enter_context(nc.sbuf_tensor("g_" + name, list(shape), dt))

        cmat = gsb("cmat", [128, 5, 128], BF16)
        gcols = gsb("gcols", [128, 8], F32)
        negl = gsb("negl", [128, 2], F32)
        nhalf = gsb("nhalf", [128, 512], F32)
        zero1 = gsb("zero1", [128, 1], F32)
        epsc = gsb("epsc", [128, 1], F32)
        farb = {jb["name"]: gsb("farb" + jb["name"], [128, 4, jb["NC"] + 1], F32) for jb in jobs}
        ps = gst.enter_context(nc.psum_tensor("psum_all", [128, 8, 512], F32))
        ident = cmat[:, 0, :]
        antiid = cmat[:, 1, :]
        onesb = cmat[:, 2, :]
        blk64 = cmat[:, 3, :]
        o128 = cmat[:, 4, :]

        def psb(i):
            return ps[:, i, :]

        def new_ps():
            return [Buf("ps%d" % i, excl=True) for i in range(8)]

        with contextlib.ExitStack() as st:
            def sb(name, shape, dt):
                return st.enter_context(nc.sbuf_tensor("s0_" + name, list(shape), dt))

            PB = new_ps()
            pg.begin()
            Bc = Buf("consts")
            pg.dma("sp", cmat[:], cmat_d, writes=[Bc])
            pg.dma("sp", gcols[:], gcols_d, writes=[Bc])
            pg.op("pool", lambda e: e.memset(nhalf[:], -0.5), writes=[Bc])
            pg.op("pool", lambda e: e.memset(zero1[:], 0.0), writes=[Bc])
            pg.op("pool", lambda e: e.memset(epsc[:], EPS), writes=[Bc])
            pg.op("dve", lambda e: e.tensor_scalar(out=gcols[:, 4:5], in0=gcols[:, 4:5], scalar1=1.0 - LAM_INIT, scalar2=None, op0=ALU.mult),
                  reads=[Bc], writes=[Bc])

            cT = sb("cT", [128, 8, 2], F32)
            scT = sb("scT", [128, 8, 2], F32)
            scTb = sb("scTb", [128, 8, 2], BF16)
            BcT, BscT = Buf("cT"), Buf("scT")
            pg.dma("sp", cT[:], cT_d, writes=[BcT])
            pg.op("act", lambda e: e.activation(out=scT[:], in_=cT[:], func=AF.Silu), reads=[BcT], writes=[BscT])
            pg.op("dve", lambda e: e.tensor_copy(scTb[:], scT[:]), reads=[BscT], writes=[BscT])

            mod = sb("mod", [2, 6 * D], F32)
            bada = sb("bada", [2, 6 * D], F32)
            Bmod, Bbada = Buf("mod"), Buf("bada")
            pg.dma("sp", bada[:], b_ada2_d, writes=[Bbada])
            waf = [sb("waf%d" % i, [128, 8, 512], F32) for i in range(2)]
            wab = [sb("wab%d" % i, [128, 8, 512], BF16) for i in range(2)]
            Bwaf = [Buf("waf%d" % i) for i in range(2)]
            Bwab = [Buf("wab%d" % i) for i in range(2)]
            for n in range(12):
                i = n % 2
                pg.dma("sp", waf[i][:], w_ada_d[:, n * 512:(n + 1) * 512].rearrange("(k p) n -> p k n", p=128), writes=[Bwaf[i]])
                ce = "dve" if n % 2 == 0 else "pool"
                pg.op(ce, lambda e, i=i: e.tensor_copy(wab[i][:], waf[i][:]), reads=[Bwaf[i]], writes=[Bwab[i]])

                def mm(e, i=i):
                    ins = None
                    for k in range(8):
                        ins = e.matmul(ps[0:2, i, :], lhsT=scTb[:, k, :], rhs=wab[i][:, k, :], start=(k == 0), stop=(k == 7))
                    return ins
                pg.op("pe", mm, reads=[Bwab[i], BscT], writes=[PB[i]])
                pg.op("dve", lambda e, i=i, n=n: e.tensor_tensor(out=mod[:, n * 512:(n + 1) * 512], in0=ps[0:2, i, :],
                                                               in1=bada[:, n * 512:(n + 1) * 512], op=ALU.add),
                      reads=[PB[i], Bbada], writes=[Bmod])
            grow = sb("grow", [2, 4, D], F32)
            rows = sb("rows", [2, 6, D], F32)
            Bgrow, Brows = Buf("grow"), Buf("rows")
            pg.dma("sp", grow[:], grow_d, writes=[Bgrow])
            pg.op("dve", lambda e: e.scalar_tensor_tensor(out=rows[:, 0, :], in0=mod[:, 1024:2048], scalar=1.0, in1=grow[:, 0, :],
                                                          op0=ALU.add, op1=ALU.mult), reads=[Bmod, Bgrow], writes=[Brows])
            pg.op("dve", lambda e: e.tensor_copy(rows[:, 1, :], mod[:, 0:1024]), reads=[Bmod], writes=[Brows])
            pg.op("dve", lambda e: e.tensor_tensor(out=rows[:, 2, :], in0=mod[:, 2048:3072], in1=grow[:, 1, :], op=ALU.mult),
                  reads=[Bmod, Bgrow], writes=[Brows])
            pg.op("dve", lambda e: e.scalar_tensor_tensor(out=rows[:, 3, :], in0=mod[:, 4096:5120], scalar=1.0, in1=grow[:, 2, :],
                                                          op0=ALU.add, op1=ALU.mult), reads=[Bmod, Bgrow], writes=[Brows])
            pg.op("dve", lambda e: e.tensor_copy(rows[:, 4, :], mod[:, 3072:4096]), reads=[Bmod], writes=[Brows])
            pg.op("dve", lambda e: e.tensor_tensor(out=rows[:, 5, :], in0=mod[:, 5120:6144], in1=grow[:, 3, :], op=ALU.mult),
                  reads=[Bmod, Bgrow], writes=[Brows])
            pg.dma("sp", rows_d, rows[:], reads=[Brows])

            lamv = sb("lamv", [1, 4, 64], F32)
            lj = sb("lj", [1, 64], F32)
            ls = sb("ls", [1, 4], F32)
            Blam = Buf("lam")
            pg.dma("sp", lamv[:], lamv_d, writes=[Blam])
            for t in range(2):
                pg.op("dve", lambda e, t=t: e.scalar_tensor_tensor(out=lj[:], in0=lamv[:, 2 * t, :], scalar=1.0, in1=lamv[:, 2 * t + 1, :],
                                                                 op0=ALU.mult, op1=ALU.mult, accum_out=ls[:, t:t + 1]),
                      reads=[Blam], writes=[Blam])
            pg.op("act", lambda e: e.activation(out=ls[:, 2:4], in_=ls[:, 0:2], func=AF.Exp), reads=[Blam], writes=[Blam])
            pg.op("dve", lambda e: e.scalar_tensor_tensor(out=ls[:, 1:2], in0=ls[:, 2:3], scalar=LAM_INIT, in1=ls[:, 3:4],
                                                          op0=ALU.add, op1=ALU.subtract), reads=[Blam], writes=[Blam])
            pg.op("dve", lambda e: e.tensor_scalar(out=ls[:, 0:1], in0=ls[:, 1:2], scalar1=-1.0, scalar2=None, op0=ALU.mult),
                  reads=[Blam], writes=[Blam])
            pg.dma("sp", lam_d, ls[:, 0:2], reads=[Blam])
            Bnegl = Buf("negl")
            pg.dma("sp", negl[:], lam_d.broadcast_to([128, 2]), reads=[Blam], writes=[Bnegl])

            rb = sb("rb", [NB, 4], F32)
            rbr = sb("rbr", [NB, 4], F32)
            rbp = [sb("rbp%d" % i, [NB, 4], BF16) for i in range(3)]
            Brb = Buf("rb")
            pg.dma("sp", rb[:], relb_d, writes=[Brb])
            pg.op("dve", lambda e: e.tensor_copy(rbp[0][:], rb[:]), reads=[Brb], writes=[Brb])
            pg.op("dve", lambda e: e.tensor_tensor(out=rbr[:], in0=rb[:], in1=rbp[0][:], op=ALU.subtract), reads=[Brb], writes=[Brb])
            pg.op("dve", lambda e: e.tensor_copy(rbp[1][:], rbr[:]), reads=[Brb], writes=[Brb])
            pg.op("dve", lambda e: e.tensor_tensor(out=rbr[:], in0=rbr[:], in1=rbp[1][:], op=ALU.subtract), reads=[Brb], writes=[Brb])
            pg.op("dve", lambda e: e.tensor_copy(rbp[2][:], rbr[:]), reads=[Brb], writes=[Brb])
            emat = sb("emat", [NB, ULEN], BF16)
            uout = sb("uout", [4, 3, ULEN], BF16)
            ufo = sb("ufo", [4, 132], F32)
            Bem, Buo = Buf("emat"), Buf("uout")
            specs = [("main", emain_d, ULEN, ub_d)]
            for jb in jobs:
                specs.append(("wrap", ewrap_d[jb["name"]], WLEN, uw_d[jb["name"]]))
                specs.append(("wrap", ewrap2_d[jb["name"]], WLEN, uw2_d[jb["name"]]))
            for jb in jobs:
                specs.append(("far", efar_d[jb["name"]], jb["NC"] + 1, ufar_d[jb["name"]]))
            pbank = 2
            for kind, esrc, L, dst in specs:
                pg.dma("sp", emat[:, 0:L], esrc, writes=[Bem])
                if kind != "far":
                    for part in range(3):
                        for s0 in range(0, L, 512):
                            w = min(512, L - s0)
                            bk = 2 + (pbank % 2)
                            pbank += 1
                            pg.op("pe", lambda e, bk=bk, part=part, s0=s0, w=w: e.matmul(ps[0:4, bk, 0:w], lhsT=rbp[part][:, :],
                                                                                       rhs=emat[:, s0:s0 + w], start=True, stop=True),
                                  reads=[Bem, Brb], writes=[PB[bk]])
                            pg.op("dve", lambda e, bk=bk, part=part, s0=s0, w=w: e.tensor_copy(uout[:, part, s0:s0 + w], ps[0:4, bk, 0:w]),
                                  reads=[PB[bk]], writes=[Buo])
                    pg.dma("sp", dst.rearrange("t h l -> h t l"), uout[:, :, 0:L], reads=[Buo])
                else:
                    bk = 2 + (pbank % 2)
                    pbank += 1

                    def mmf(e, bk=bk, L=L):
                        ins = None
                        for part in range(3):
                            ins = e.matmul(ps[0:4, bk, 0:L], lhsT=rbp[part][:, :], rhs=emat[:, 0:L], start=(part == 0), stop=(part == 2))
                        return ins
                    pg.op("pe", mmf, reads=[Bem, Brb], writes=[PB[bk]])
                    pg.op("dve", lambda e, bk=bk, L=L: e.tensor_copy(ufo[:, 0:L], ps[0:4, bk, 0:L]), reads=[PB[bk]], writes=[Buo])
                    pg.dma("sp", dst.rearrange("o (h l) -> (o h) l", h=4), ufo[:, 0:L], reads=[Buo])
            for jb in jobs:
                n = jb["name"]
                pg.dma("sp", farb[n][:].rearrange("p h l -> p (h l)"), ufar_d[n].broadcast_to([128, 4 * (jb["NC"] + 1)]),
                       reads=[Buo], writes=[Bc])

            pieces = []
            for k in range(8):
                pieces.append((w_in_d[k * 128:(k + 1) * 128, :], wb_in[k * 128:(k + 1) * 128, :], WIN))
            NBUF = 2
            wcf = [sb("wcf%d" % i, [128, WIN], F32) for i in range(NBUF)]
            wcb = [sb("wcb%d" % i, [128, WIN], BF16) for i in range(NBUF)]
            Bwcf = [Buf("wcf%d" % i) for i in range(NBUF)]
            Bwcb = [Buf("wcb%d" % i) for i in range(NBUF)]
            for n, (src, dst, w) in enumerate(pieces):
                i = n % NBUF
                pg.dma("sp", wcf[i][:, 0:w], src, writes=[Bwcf[i]])
                ce = ["dve", "pool", "act"][n % 3]
                if ce == "act":
                    pg.op(ce, lambda e, i=i, w=w: e.activation(out=wcb[i][:, 0:w], in_=wcf[i][:, 0:w], func=AF.Copy), reads=[Bwcf[i]], writes=[Bwcb[i]])
                else:
                    pg.op(ce, lambda e, i=i, w=w: e.tensor_copy(wcb[i][:, 0:w], wcf[i][:, 0:w]), reads=[Bwcf[i]], writes=[Bwcb[i]])
                pg.dma("pool", dst, wcb[i][:, 0:w], reads=[Bwcb[i]])
            pg.end()

        for jb in jobs:
            name, N, nq = jb["name"], jb["N"], jb["nq"]
            sc = S[name]
            cos_d, sin_d = rope_d[name]
            with contextlib.ExitStack() as st:
                def sb(nm, shape, dt):
                    return st.enter_context(nc.sbuf_tensor("s1_" + nm + name, list(shape), dt))
                PB = new_ps()
                pg.begin()
                win = sb("win", [128, 8, WIN], BF16)
                Bwin = Buf("win")
                for k in range(8):
                    pg.dma("sp", win[:, k, :], wb_in[k * 128:(k + 1) * 128, :], writes=[Bwin])
                A1 = sb("A1", [128, D], F32)
                SH1 = sb("SH1", [128, D], F32)
                Bmodt = Buf("modt")
                pg.dma("sp", A1[:], rows_d[jb["b"], 0:1, :].broadcast_to([128, D]), writes=[Bmodt])
                pg.dma("sp", SH1[:], rows_d[jb["b"], 1:2, :].broadcast_to([128, D]), writes=[Bmodt])
                NXB = 2
                xt = [sb("xt%d" % i, [128, 4, D], F32) for i in range(NXB)]
                Bxt = [Buf("xt%d" % i) for i in range(NXB)]
                rt = [(sb("cos%d" % i, [128, 512], F32), sb("sin%d" % i, [128, 512], F32)) for i in range(2)]
                Brt = [Buf("rt%d" % i) for i in range(2)]
                junk = sb("junk", [128, D], F32)
                ss = sb("ss", [128, 4], F32)
                rstd = sb("rstd", [128, 4], F32)
                tt = [sb("tt%d" % i, [128, D], F32) for i in range(2)]
                hb = [sb("hb%d" % i, [128, D], BF16) for i in range(2)]
                hTs = [sb("hT%d" % i, [128, 8, 512], BF16) for i in range(2)]
                Bjunk, Bss, Brstd = Buf("junk"), Buf("ss"), Buf("rstd")
                Btt = [Buf("tt%d" % i) for i in range(2)]
                Bhb = [Buf("hb%d" % i) for i in range(2)]
                BhTs = [[Buf("hT%d_%d" % (j, i)) for i in range(4)] for j in range(2)]
                asb = sb("asb", [128, 512], F32)
                sq = sb("sq", [128, 512], F32)
                sqh = sb("sqh", [128, 512], BF16)
                sqm = sb("sqm", [128, 512], BF16)
                rs = sb("rs", [128, 512], F32)
                t1 = sb("t1", [128, 512], F32)
                t2 = sb("t2", [128, 512], F32)
                Basb, Bsq, Bsqh, Bsqm, Brs, Bt1, Bt2 = [Buf(x) for x in ["asb", "sq", "sqh", "sqm", "rs", "t1", "t2"]]
                kta = [sb("kta%d" % i, [128, 512], BF16) for i in range(2)]
                ktd = [sb("ktd%d" % i, [128, 4, 512], BF16) for i in range(2)]
                qta = [sb("qta%d" % i, [128, 4, 512], BF16) for i in range(2)]
                qtd = [sb("qtd%d" % i, [128, 4, 512], BF16) for i in range(2)]
                va = [sb("va%d" % i, [128, 4, 2, 65], BF16) for i in range(2)]
                vd = [sb("vd%d" % i, [128, 4, 512], BF16) for i in range(2)]
                Bkta = [Buf("kta%d" % i) for i in range(2)]
                Bktd = [Buf("ktd%d" % i) for i in range(2)]
                Bqta = [Buf("qta%d" % i) for i in range(2)]
                Bqtd = [Buf("qtd%d" % i) for i in range(2)]
                Bva = [Buf("va%d" % i) for i in range(2)]
                Bvd = [Buf("vd%d" % i) for i in range(2)]
                for i in range(2):
                    pg.op("pool", lambda e, i=i: e.memset(va[i][:], 1.0), writes=[Bva[i]])
                psT = [ps[:, i, :].bitcast(BF16) for i in range(2)]

                def normrope(pa, pb, gi, gpi, outap, scale, ri):
                    cs, sn = rt[ri]
                    pg.op("act", lambda e: e.activation(out=asb[:], in_=psb(pa), func=AF.Copy), reads=[PB[pa]], writes=[Basb])
                    pg.op("dve", lambda e: e.scalar_tensor_tensor(out=t2[:], in0=psb(pb), scalar=gcols[:, gpi:gpi + 1], in1=sn[:], op0=ALU.mult, op1=ALU.mult),
                          reads=[PB[pb], Bc, Brt[ri]], writes=[Bt2])
                    pg.op("dve", lambda e: e.tensor_tensor(out=sq[:], in0=asb[:], in1=asb[:], op=ALU.mult), reads=[Basb], writes=[Bsq])
                    pg.op("dve", lambda e: e.tensor_copy(sqh[:], sq[:]), reads=[Bsq], writes=[Bsqh])
                    pg.op("pool", lambda e: e.tensor_tensor(out=sq[:], in0=sq[:], in1=sqh[:], op=ALU.subtract), reads=[Bsq, Bsqh], writes=[Bsq])
                    pg.op("pool", lambda e: e.tensor_copy(sqm[:], sq[:]), reads=[Bsq], writes=[Bsqm])

                    def mmss(e):
                        e.matmul(psb(7), lhsT=blk64, rhs=sqh[:], start=True, stop=False)
                        return e.matmul(psb(7), lhsT=blk64, rhs=sqm[:], start=False, stop=True)
                    pg.op("pe", mmss, reads=[Bsqh, Bsqm, Bc], writes=[PB[7]])
                    pg.op("act", lambda e: e.activation(out=rs[:], in_=psb(7), func=AF.Sqrt, bias=epsc[:], scale=1.0), reads=[PB[7], Bc], writes=[Brs])
                    pg.op("dve", lambda e: e.reciprocal(out=rs[:], in_=rs[:]), reads=[Brs], writes=[Brs])
                    pg.op("dve", lambda e: e.scalar_tensor_tensor(out=t1[:], in0=asb[:], scalar=gcols[:, gi:gi + 1], in1=cs[:], op0=ALU.mult, op1=ALU.mult),
                          reads=[Basb, Bc, Brt[ri]], writes=[Bt1])
                    pg.op("pool", lambda e: e.tensor_tensor(out=t1[:], in0=t1[:], in1=t2[:], op=ALU.add), reads=[Bt1, Bt2], writes=[Bt1])
                    return lambda e: e.scalar_tensor_tensor(out=outap, in0=t1[:], scalar=scale, in1=rs[:], op0=ALU.mult, op1=ALU.mult)

                def proj(bank, ch, hT, BhT):
                    def f(e):
                        ins = None
                        for k in range(8):
                            ins = e.matmul(psb(bank), lhsT=win[:, k, ch * 128:(ch + 1) * 128], rhs=hT[:, k, :], start=(k == 0), stop=(k == 7))
                        return ins
                    pg.op("pe", f, reads=[Bwin] + BhT, writes=[PB[bank]])

                ntiles = N // 512
                for t in range(ntiles):
                    xi = t % NXB
                    own = (t * 512 < nq)
                    ti = t % 2
                    hT, BhT = hTs[ti], BhTs[ti]
                    pg.dma("sp", xt[xi][:], x_in[name][t * 512:(t + 1) * 512, :].rearrange("(s p) d -> p s d", p=128), writes=[Bxt[xi]])
                    pg.dma("sp", rt[ti][0][:], cos_d[:, t * 512:(t + 1) * 512], writes=[Brt[ti]])
                    pg.dma("sp", rt[ti][1][:], sin_d[:, t * 512:(t + 1) * 512], writes=[Brt[ti]])
                    for s in range(4):
                        pg.op("dve", lambda e, s=s, xi=xi: e.scalar_tensor_tensor(out=junk[:], in0=xt[xi][:, s, :], scalar=1.0 / D, in1=xt[xi][:, s, :],
                                                                               op0=ALU.mult, op1=ALU.mult, accum_out=ss[:, s:s + 1]),
                              reads=[Bxt[xi]], writes=[Bjunk, Bss])
                    pg.op("dve", lambda e: e.tensor_scalar(out=rstd[:], in0=ss[:], scalar1=EPS, scalar2=None, op0=ALU.add), reads=[Bss], writes=[Brstd])
                    pg.op("pool", lambda e: e.tensor_tensor(out=rstd[:], in0=rstd[:], in1=nhalf[:, 0:4], op=ALU.pow), reads=[Brstd, Bc], writes=[Brstd])
                    for s in range(4):
                        i = s % 2
                        pg.op("dve", lambda e, s=s, i=i, xi=xi: e.scalar_tensor_tensor(out=tt[i][:], in0=xt[xi][:, s, :], scalar=rstd[:, s:s + 1], in1=A1[:],
                                                                                    op0=ALU.mult, op1=ALU.mult),
                              reads=[Bxt[xi], Brstd, Bmodt], writes=[Btt[i]])
                        pg.op("pool", lambda e, i=i: e.tensor_tensor(out=hb[i][:], in0=tt[i][:], in1=SH1[:], op=ALU.add),
                              reads=[Btt[i], Bmodt], writes=[Bhb[i]])

                        def tr(e, i=i):
                            ins = None
                            for k in range(8):
                                ins = e.transpose(out=psT[i][:, k * 128:(k + 1) * 128], in_=hb[i][:, k * 128:(k + 1) * 128], identity=ident)
                            return ins
                        pg.op("pe", tr, reads=[Bhb[i], Bc], writes=[PB[i]])
                        pg.op("act", lambda e, s=s, i=i, hT=hT: e.activation(out=hT[:, :, s * 128:(s + 1) * 128],
                                                                    in_=psT[i].rearrange("p (k t) -> p k t", k=8), func=AF.Copy),
                              reads=[PB[i]], writes=[BhT[s]])
                    proj(2, 8, hT, BhT)
                    proj(3, 9, hT, BhT)
                    fin = normrope(2, 3, 2, 3, kta[ti][:], 1.0, ti)
                    pg.op("dve", fin, reads=[Bt1, Brs], writes=[Bkta[ti]])
                    pg.dma("pool", sc["KTa"][:, t * 512:(t + 1) * 512], kta[ti][:], reads=[Bkta[ti]])
                    for h in range(4):
                        bk = 4 + (h % 2)
                        proj(bk, 14 + h, hT, BhT)
                        ce = "act" if h % 2 == 0 else "dve"
                        if ce == "act":
                            pg.op("act", lambda e, h=h, bk=bk, ti=ti: e.activation(out=ktd[ti][:, h, :], in_=psb(bk), func=AF.Copy), reads=[PB[bk]], writes=[Bktd[ti]])
                        else:
                            pg.op("dve", lambda e, h=h, bk=bk, ti=ti: e.tensor_copy(ktd[ti][:, h, :], psb(bk)), reads=[PB[bk]], writes=[Bktd[ti]])
                    pg.dma("pool", sc["KTd"][:, :, t * 512:(t + 1) * 512].rearrange("h p t -> p h t"), ktd[ti][:], reads=[Bktd[ti]])
                    for s in range(4):
                        def mmv(e, s=s, hT=hT):
                            ins = None
                            for k in range(8):
                                ins = e.matmul(psb(6), lhsT=hT[:, k, s * 128:(s + 1) * 128], rhs=win[:, k, 2432:2944], start=(k == 0), stop=(k == 7))
                            return ins

                        def mmva(e, s=s, hT=hT):
                            ins = None
                            for k in range(8):
                                ins = e.matmul(ps[:, 5, 0:128], lhsT=hT[:, k, s * 128:(s + 1) * 128], rhs=win[:, k, 2304:2432], start=(k == 0), stop=(k == 7))
                            return ins
                        pg.op("pe", mmv, reads=[Bwin, BhT[s]], writes=[PB[6]])
                        pg.op("act", lambda e, s=s, ti=ti: e.activation(out=vd[ti][:, s, :], in_=psb(6), func=AF.Copy), reads=[PB[6]], writes=[Bvd[ti]])
                        pg.op("pe", mmva, reads=[Bwin, BhT[s]], writes=[PB[5]])
                        pg.op("dve", lambda e, s=s, ti=ti: e.tensor_copy(va[ti][:, s, :, 0:64], ps[:, 5, 0:128].rearrange("p (a b) -> p a b", a=2)),
                              reads=[PB[5]], writes=[Bva[ti]])
                    pg.dma("pool", sc["Vd"][t * 512:(t + 1) * 512, :].rearrange("(s p) c -> p s c", p=128), vd[ti][:], reads=[Bvd[ti]])
                    pg.dma("pool", sc["Va"][t * 512:(t + 1) * 512, :].rearrange("(s p) c -> p s c", p=128),
                           va[ti][:].rearrange("p s a b -> p s (a b)"), reads=[Bva[ti]])
                    if own:
                        for g in range(4):
                            proj(2, g, hT, BhT)
                            proj(3, 4 + g, hT, BhT)
                            fin = normrope(2, 3, 0, 1, qta[ti][:, g, :], 0.125, ti)
                            pg.op("dve", fin, reads=[Bt1, Brs], writes=[Bqta[ti]])
                        pg.dma("pool", sc["QTa"][:, :, t * 512:(t + 1) * 512].rearrange("g p t -> p g t"), qta[ti][:], reads=[Bqta[ti]])
                        for h in range(4):
                            bk = 4 + (h % 2)
                            proj(bk, 10 + h, hT, BhT)
                            if h % 2 == 0:
                                pg.op("act", lambda e, h=h, bk=bk, ti=ti: e.activation(out=qtd[ti][:, h, :], in_=psb(bk), func=AF.Copy, scale=0.125),
                                      reads=[PB[bk]], writes=[Bqtd[ti]])
                            else:
                                pg.op("dve", lambda e, h=h, bk=bk, ti=ti: e.tensor_scalar(out=qtd[ti][:, h, :], in0=psb(bk), scalar1=0.125, scalar2=None, op0=ALU.mult),
                                      reads=[PB[bk]], writes=[Bqtd[ti]])
                        pg.dma("pool", sc["QTd"][:, :, t * 512:(t + 1) * 512].rearrange("h p t -> p h t"), qtd[ti][:], reads=[Bqtd[ti]])
                pg.end()

        late_pieces = []
        for k in range(8):
            late_pieces.append((w_out_d[k * 128:(k + 1) * 128, :], wb_out[k * 128:(k + 1) * 128, :], D))
        for k in range(8):
            for c0_ in range(0, 2 * DFF, 1024):
                w_ = min(1024, 2 * DFF - c0_)
                late_pieces.append((w_gu_d[k * 128:(k + 1) * 128, c0_:c0_ + w_], wb_gu[k * 128:(k + 1) * 128, c0_:c0_ + w_], w_))
        for k in range(22):
            late_pieces.append((w_down_d[k * 128:(k + 1) * 128, :], wb_down[k * 128:(k + 1) * 128, :], D))

        for jb in jobs:
            name, N, nq, NC = jb["name"], jb["N"], jb["nq"], jb["NC"]
            sc = S[name]
            with contextlib.ExitStack() as st:
                def sb(nm, shape, dt):
                    return st.enter_context(nc.sbuf_tensor("s2_" + nm + name, list(shape), dt))
                PB = new_ps()
                pg.begin()
                KT = sb("KT", [128, N], BF16)
                VV = sb("VV", [128, NC, 130], BF16)
                QT = sb("QT", [128, 4, nq], BF16)
                NG = 8
                cpg = NC // NG
                BKV = [Buf("kv%d" % i) for i in range(NG)]
                BQ = Buf("QT")
                pT = [sb("pT%d" % i, [128, 1024], BF16) for i in range(3)]
                BpT = [Buf("pT%d" % i) for i in range(3)]
                osb = [sb("osb%d" % i, [128, 512], F32) for i in range(2)]
                Bosb = [Buf("osb%d" % i) for i in range(2)]
                zr = sb("zr", [128, 2, 512], F32)
                Bzr = Buf("zr")
                bcz = sb("bcz", [128, 2, 512], F32)
                Bbcz = Buf("bcz")
                onrm = sb("onrm", [128, 512], BF16)
                Bonrm = Buf("onrm")
                dd = sb("dd", [128, 512], F32)
                dsq = sb("dsq", [128, 512], F32)
                dsqh = sb("dsqh", [128, 512], BF16)
                dsqm = sb("dsqm", [128, 512], BF16)
                drs = sb("drs", [128, 512], F32)
                Bdd, Bdsq, Bdsqh, Bdsqm, Bdrs = [Buf(x) for x in ["dd", "dsq", "dsqh", "dsqm", "drs"]]
                TTs = [sb("TT%d" % i, [128, 8, 512], F32) for i in range(2)]
                BTTs = [Buf("TT%d" % i) for i in range(2)]
                pending = []
                hk = [sb("hk%d" % i, [128, 3, 512], BF16) for i in range(2)]
                Bhk = [Buf("hk%d" % i) for i in range(2)]
                zdram = sc["zrow"]
                Bzd = [Buf("zd0"), Buf("zd1")]

                def load_kv(kt_src, v_src, vw):
                    for gi in range(NG):
                        c0 = gi * cpg
                        pg.dma("sp", KT[:, c0 * 128:(c0 + cpg) * 128], kt_src[:, c0 * 128:(c0 + cpg) * 128], writes=[BKV[gi]])
                        pg.dma("sp", VV[:, c0:c0 + cpg, 0:vw], v_src[c0 * 128:(c0 + cpg) * 128, :].rearrange("(c p) w -> p c w", p=128), writes=[BKV[gi]])

                def attn_tile(units, lhs_v, near, bias_of, epilogue, TT=None, BTT=None):
                    def qk(c):
                        par = c % 2

                        def f(e):
                            ins = None
                            for u in range(2):
                                ins = e.matmul(psb(2 * par + u), lhsT=KT[64 * u:64 * u + 64, c * 128:(c + 1) * 128], rhs=units[u], start=True, stop=True)
                            return ins
                        pg.op("pe", f, reads=[BKV[c // cpg], BQ], writes=[PB[2 * par], PB[2 * par + 1]])
                        if c in near:
                            ti = near[c]
                            pg.op("dve", lambda e: e.tensor_tensor(out=ps[:, 2 * par:2 * par + 2, :], in0=ps[:, 2 * par:2 * par + 2, :],
                                                                 in1=TT[:, ti:ti + 1, :].to_broadcast([128, 2, 512]), op=ALU.add),
                                  reads=[BTT], writes=[PB[2 * par], PB[2 * par + 1]])

                    def ex(c):
                        par = c % 2
                        p3 = c % 3
                        b = bias_of(c)
                        pg.op("act", lambda e: e.activation(out=pT[p3][:], in_=ps[:, 2 * par:2 * par + 2, :].rearrange("p a b -> p (a b)"),
                                                          func=AF.Exp, bias=(b if b is not None else zero1[:])),
                              reads=[PB[2 * par], PB[2 * par + 1], Bc], writes=[BpT[p3]])

                    def pv(c):
                        par = c % 3
                        dm = diff_mode[0]

                        def f(e):
                            ins = None
                            for u in range(2):
                                lv, m = lhs_v(u, c)
                                ins = e.matmul(ps[0:m, 4 + u, :], lhsT=lv, rhs=pT[par][:, u * 512:(u + 1) * 512], start=(c == 0), stop=(c == NC - 1))
                            if dm:
                                for u in range(2):
                                    ins = e.matmul(ps[32 * u:32 * u + 1, 6, :], lhsT=onesb[:, 0:1], rhs=pT[par][:, u * 512:(u + 1) * 512],
                                                   start=(c == 0), stop=(c == NC - 1), skip_group_check=True)
                            return ins
                        w = [PB[4], PB[5]] + ([PB[6]] if diff_mode[0] else [])
                        pg.op("pe", f, reads=[BKV[c // cpg], BpT[par], Bc], writes=w)
                    qk(0)
                    qk(1)
                    cflush = min(20, NC // 2)
                    for c in range(NC):
                        ex(c)
                        if c + 2 < NC:
                            qk(c + 2)
                        pv(c)
                        if c == cflush:
                            while pending:
                                pending.pop(0)()
                    epilogue()

                diff_mode = [False]
                if name == "P":
                    lcf = [sb("lcf%d" % i, [128, 1024], F32) for i in range(2)]
                    lcb = [sb("lcb%d" % i, [128, 1024], BF16) for i in range(2)]
                    Blcf = [Buf("lcf%d" % i) for i in range(2)]
                    Blcb = [Buf("lcb%d" % i) for i in range(2)]
                lp_state = [0]

                def emit_late(n):
                    while n > 0 and name == "P" and lp_state[0] < len(late_pieces):
                        src, dst, w = late_pieces[lp_state[0]]
                        i = lp_state[0] % 2
                        lp_state[0] += 1
                        n -= 1
                        pg.dma("sp", lcf[i][:, 0:w], src, writes=[Blcf[i]])
                        pg.op("pool", lambda e, i=i, w=w: e.tensor_copy(lcb[i][:, 0:w], lcf[i][:, 0:w]), reads=[Blcf[i]], writes=[Blcb[i]])
                        pg.dma("pool", dst, lcb[i][:, 0:w], reads=[Blcb[i]])
                load_kv(sc["KTa"], sc["Va"], 130)
                for g in range(4):
                    pg.dma("sp", QT[:, g, :], sc["QTa"][g], writes=[BQ])
                for qt in range(nq // 128):
                    units = [QT[64 * u:64 * u + 64, :, qt * 128:(qt + 1) * 128] for u in range(2)]

                    def lhs_v(u, c):
                        return VV[:, c, u * 65:(u + 1) * 65], 65

                    def epi(qt=qt):
                        for u in range(2):
                            pg.op("dve", lambda e, u=u: e.tensor_copy(osb[u][0:65, :], ps[0:65, 4 + u, :]), reads=[PB[4 + u]], writes=[Bosb[u]])
                        for u in range(2):
                            pg.op("dve", lambda e, u=u: e.reciprocal(out=zr[64:65, u, :], in_=osb[u][64:65, :]), reads=[Bosb[u]], writes=[Bzr])
                            pg.dma("pool", zdram[u, 0:1, :], zr[64:65, u, :], reads=[Bzr], writes=[Bzd[u]])
                            pg.dma("pool", bcz[0:64, u, :], zdram[u, 0:1, :].broadcast_to([64, 512]), reads=[Bzd[u]], writes=[Bbcz])
                            pg.op("dve", lambda e, u=u: e.tensor_tensor(out=onrm[0:64, :], in0=osb[u][0:64, :], in1=bcz[0:64, u, :], op=ALU.mult),
                                  reads=[Bosb[u], Bbcz], writes=[Bonrm])
                            pg.dma("pool", sc["outT"][u * 256:(u + 1) * 256, qt * 128:(qt + 1) * 128].rearrange("(g d) t -> d g t", g=4),
                                   onrm[0:64, :].rearrange("d (g t) -> d g t", g=4), reads=[Bonrm])
                    attn_tile(units, lhs_v, {}, lambda c: None, epi)
                    emit_late(3)
                emit_late(10 ** 6)
                diff_mode[0] = True

                def build_TT(h):
                    TT, BTT = TTs[h % 2], BTTs[h % 2]
                    for ti in range(8):
                        i = ti % 2
                        if ti < 6:
                            dofs = (ti - 1) * 128
                            base = 512 - dofs
                            for part in range(3):
                                src = bass.AP(ub_d.tensor, ub_d[part, h, base:base + 1].offset, [[1, 128], [1, 512]])
                                pg.dma("sp", hk[i][:, part, :], src, writes=[Bhk[i]])
                        else:
                            uw = uw_d[name] if ti == 6 else uw2_d[name]
                            for part in range(3):
                                src = bass.AP(uw.tensor, uw[part, h, 0:1].offset, [[1, 128], [1, 512]])
                                pg.dma("sp", hk[i][:, part, :], src, writes=[Bhk[i]])

                        def mmT(e, i=i):
                            ins = None
                            for part in range(3):
                                ins = e.matmul(psb(7), lhsT=antiid, rhs=hk[i][:, part, :], start=(part == 0), stop=(part == 2))
                            return ins
                        pg.op("pe", mmT, reads=[Bhk[i], Bc], writes=[PB[7]])
                        pg.op("dve", lambda e, ti=ti, TT=TT: e.tensor_copy(TT[:, ti, :], psb(7)), reads=[PB[7]], writes=[BTT])
                for h in range(4):
                    load_kv(sc["KTd"][h], sc["Vd"][:, h * 128:(h + 1) * 128], 128)
                    pg.dma("sp", QT[:, 0, :], sc["QTd"][h], writes=[BQ])
                    if h == 0:
                        build_TT(0)
                    for qt in range(nq // 512):
                        units = [QT[64 * u:64 * u + 64, 0, qt * 512:(qt + 1) * 512] for u in range(2)]
                        c0 = qt * 4
                        near = {}
                        for ti in range(6):
                            c = c0 - 1 + ti
                            if 0 <= c < NC:
                                near[c] = ti
                        if qt == 0:
                            near[NC - 1] = 6
                        if qt == nq // 512 - 1:
                            near[nq // 128] = 7
                        fb = farb[name]

                        def bias_of(c, near=near, c0=c0, h=h, fb=fb):
                            if c in near:
                                return None
                            if c < c0:
                                return fb[:, h, NC:NC + 1]
                            return fb[:, h, c:c + 1]

                        def lhs_v(u, c):
                            return VV[:, c, 0:128], 128

                        def epi(qt=qt, h=h):
                            for u in range(2):
                                pg.op("dve", lambda e, u=u: e.tensor_copy(osb[u][:], psb(4 + u)), reads=[PB[4 + u]], writes=[Bosb[u]])
                            for u in range(2):
                                pg.op("dve", lambda e, u=u: e.tensor_copy(zr[32 * u:32 * u + 1, u, :], ps[32 * u:32 * u + 1, 6, :]), reads=[PB[6]], writes=[Bzr])
                            pending.append(lambda qt=qt, h=h: epi_tail(qt, h))

                        def epi_tail(qt, h):
                            for u in range(2):
                                pg.op("dve", lambda e, u=u: e.reciprocal(out=zr[32 * u:32 * u + 1, u, :], in_=zr[32 * u:32 * u + 1, u, :]), reads=[Bzr], writes=[Bzr])
                                pg.dma("pool", zdram[u, 1:2, :], zr[32 * u:32 * u + 1, u, :], reads=[Bzr], writes=[Bzd[u]])
                                pg.dma("pool", bcz[:, u, :], zdram[u, 1:2, :].broadcast_to([128, 512]), reads=[Bzd[u]], writes=[Bbcz])
                            pg.op("dve", lambda e: e.tensor_tensor(out=osb[0][:], in0=osb[0][:], in1=bcz[:, 0, :], op=ALU.mult), reads=[Bosb[0], Bbcz], writes=[Bosb[0]])
                            pg.op("pool", lambda e: e.tensor_tensor(out=osb[1][:], in0=osb[1][:], in1=bcz[:, 1, :], op=ALU.mult), reads=[Bosb[1], Bbcz], writes=[Bosb[1]])
                            pg.op("dve", lambda e: e.scalar_tensor_tensor(out=dd[:], in0=osb[1][:], scalar=negl[:, 0:1], in1=osb[0][:], op0=ALU.mult, op1=ALU.add),
                                  reads=[Bosb[0], Bosb[1], Bnegl], writes=[Bdd])
                            pg.op("pool", lambda e: e.tensor_tensor(out=dsq[:], in0=dd[:], in1=dd[:], op=ALU.mult), reads=[Bdd], writes=[Bdsq])
                            pg.op("dve", lambda e: e.tensor_copy(dsqh[:], dsq[:]), reads=[Bdsq], writes=[Bdsqh])
                            pg.op("pool", lambda e: e.tensor_tensor(out=dsq[:], in0=dsq[:], in1=dsqh[:], op=ALU.subtract), reads=[Bdsq, Bdsqh], writes=[Bdsq])
                            pg.op("pool", lambda e: e.tensor_copy(dsqm[:], dsq[:]), reads=[Bdsq], writes=[Bdsqm])

                            def mmss(e):
                                e.matmul(psb(7), lhsT=o128, rhs=dsqh[:], start=True, stop=False)
                                return e.matmul(psb(7), lhsT=o128, rhs=dsqm[:], start=False, stop=True)
                            pg.op("pe", mmss, reads=[Bdsqh, Bdsqm, Bc], writes=[PB[7]])
                            pg.op("act", lambda e: e.activation(out=drs[:], in_=psb(7), func=AF.Sqrt, bias=epsc[:], scale=1.0), reads=[PB[7], Bc], writes=[Bdrs])
                            pg.op("dve", lambda e: e.reciprocal(out=drs[:], in_=drs[:]), reads=[Bdrs], writes=[Bdrs])
                            pg.op("dve", lambda e: e.scalar_tensor_tensor(out=onrm[:], in0=dd[:], scalar=gcols[:, 4:5], in1=drs[:], op0=ALU.mult, op1=ALU.mult),
                                  reads=[Bdd, Bdrs, Bc], writes=[Bonrm])
                            pg.dma("pool", sc["outT"][512 + h * 128:512 + (h + 1) * 128, qt * 512:(qt + 1) * 512], onrm[:], reads=[Bonrm])
                        attn_tile(units, lhs_v, near, bias_of, epi, TT=TTs[h % 2], BTT=BTTs[h % 2])
                        if qt == 0 and h + 1 < 4:
                            build_TT(h + 1)
                while pending:
                    pending.pop(0)()
                pg.end()

        TK = 256
        for jb in jobs:
            name, N, nq = jb["name"], jb["N"], jb["nq"]
            sc = S[name]
            with contextlib.ExitStack() as st:
                def sb(nm, shape, dt):
                    return st.enter_context(nc.sbuf_tensor("s3_" + nm + name, list(shape), dt))
                PB = new_ps()
                pg.begin()
                wo = sb("wo", [128, 8, D], BF16)
                Bw = Buf("w3")
                for k in range(8):
                    pg.dma("sp", wo[:, k, :], wb_out[k * 128:(k + 1) * 128, :], writes=[Bw])
                G1 = sb("G1", [128, D], F32)
                A2 = sb("A2", [128, D], F32)
                SH2 = sb("SH2", [128, D], F32)
                Bmodt = Buf("modt")
                for tile_, ri in ((G1, 2), (A2, 3), (SH2, 4)):
                    pg.dma("sp", tile_[:], rows_d[jb["b"], ri:ri + 1, :].broadcast_to([128, D]), writes=[Bmodt])
                xt = [sb("xt%d" % i, [128, 2, D], F32) for i in range(2)]
                Bxt = [Buf("xt%d" % i) for i in range(2)]
                oT = [sb("oT%d" % i, [128, 8, TK], BF16) for i in range(2)]
                BoT = [Buf("oT%d" % i) for i in range(2)]
                junk = sb("junk", [128, D], F32)
                ss = sb("ss", [128, 4], F32)
                rstd = sb("rstd", [128, 4], F32)
                mixs = [sb("mixs%d" % i, [128, D], F32) for i in range(2)]
                tt = [sb("tt%d" % i, [128, D], F32) for i in range(2)]
                hb = [sb("hb%d" % i, [128, D], BF16) for i in range(2)]
                hT = [sb("hT%d" % i, [128, 8, TK], BF16) for i in range(2)]
                Bjunk = Buf("junk")
                Bss = [Buf("ss%d" % i) for i in range(4)]
                Bmixs = [Buf("mixs%d" % i) for i in range(2)]
                Btt = [Buf("tt%d" % i) for i in range(2)]
                Bhb = [Buf("hb%d" % i) for i in range(2)]
                BhT = [Buf("hT%d" % i) for i in range(2)]
                psT = [ps[:, i, :].bitcast(BF16) for i in range(2)]

                def rms_stat(src_ap, src_bufs, col):
                    pg.op("dve", lambda e: e.scalar_tensor_tensor(out=junk[:], in0=src_ap, scalar=1.0 / D, in1=src_ap, op0=ALU.mult, op1=ALU.mult,
                                                                  accum_out=ss[:, col:col + 1]), reads=src_bufs, writes=[Bjunk, Bss[col]])
                    pg.op("dve", lambda e: e.tensor_scalar(out=rstd[:, col:col + 1], in0=ss[:, col:col + 1], scalar1=EPS, scalar2=None, op0=ALU.add),
                          reads=[Bss[col]], writes=[Bss[col]])
                    pg.op("pool", lambda e: e.tensor_tensor(out=rstd[:, col:col + 1], in0=rstd[:, col:col + 1], in1=nhalf[:, 0:1], op=ALU.pow),
                          reads=[Bss[col], Bc], writes=[Bss[col]])

                for t in range(nq // TK):
                    xi = t % 2
                    pg.dma("sp", xt[xi][:], x_in[name][t * TK:(t + 1) * TK, :].rearrange("(s p) d -> p s d", p=128), writes=[Bxt[xi]])
                    pg.dma("sp", oT[xi][:], sc["outT"][:, t * TK:(t + 1) * TK].rearrange("(k p) t -> p k t", p=128), writes=[BoT[xi]])
                    for s in range(2):
                        def mmo(e, s=s, xi=xi):
                            ins = None
                            for hh in range(2):
                                for k in range(8):
                                    ins = e.matmul(psb(2 + 2 * s + hh), lhsT=oT[xi][:, k, s * 128:(s + 1) * 128], rhs=wo[:, k, hh * 512:(hh + 1) * 512],
                                                   start=(k == 0), stop=(k == 7))
                            return ins
                        pg.op("pe", mmo, reads=[BoT[xi], Bw], writes=[PB[2 + 2 * s], PB[3 + 2 * s]])
                        pg.op("act", lambda e, s=s: e.activation(out=mixs[s][:], in_=ps[:, 2 + 2 * s:4 + 2 * s, :].rearrange("p a b -> p (a b)"), func=AF.Copy),
                              reads=[PB[2 + 2 * s], PB[3 + 2 * s]], writes=[Bmixs[s]])
                        rms_stat(mixs[s][:], [Bmixs[s]], s)
                        pg.op("dve", lambda e, s=s: e.scalar_tensor_tensor(out=tt[s][:], in0=mixs[s][:], scalar=rstd[:, s:s + 1], in1=G1[:], op0=ALU.mult, op1=ALU.mult),
                              reads=[Bmixs[s], Bss[s], Bmodt], writes=[Btt[s]])
                        pg.op("pool", lambda e, s=s, xi=xi: e.tensor_tensor(out=xt[xi][:, s, :], in0=xt[xi][:, s, :], in1=tt[s][:], op=ALU.add),
                              reads=[Btt[s], Bxt[xi]], writes=[Bxt[xi]])
                        rms_stat(xt[xi][:, s, :], [Bxt[xi]], 2 + s)
                        pg.op("dve", lambda e, s=s, xi=xi: e.scalar_tensor_tensor(out=tt[s][:], in0=xt[xi][:, s, :], scalar=rstd[:, 2 + s:3 + s], in1=A2[:], op0=ALU.mult, op1=ALU.mult),
                              reads=[Bxt[xi], Bss[2 + s], Bmodt], writes=[Btt[s]])
                        pg.op("pool", lambda e, s=s: e.tensor_tensor(out=hb[s][:], in0=tt[s][:], in1=SH2[:], op=ALU.add), reads=[Btt[s], Bmodt], writes=[Bhb[s]])

                        def tr(e, s=s):
                            ins = None
                            for k in range(8):
                                ins = e.transpose(out=psT[s][:, k * 128:(k + 1) * 128], in_=hb[s][:, k * 128:(k + 1) * 128], identity=ident)
                            return ins
                        pg.op("pe", tr, reads=[Bhb[s], Bc], writes=[PB[s]])
                        pg.op("act", lambda e, s=s, xi=xi: e.activation(out=hT[xi][:, :, s * 128:(s + 1) * 128], in_=psT[s].rearrange("p (k t) -> p k t", k=8), func=AF.Copy),
                              reads=[PB[s]], writes=[BhT[xi]])
                    pg.dma("pool", sc["x1"][t * TK:(t + 1) * TK, :].rearrange("(s p) d -> p s d", p=128), xt[xi][:], reads=[Bxt[xi]])
                    pg.dma("pool", sc["h2T"][:, :, t * TK:(t + 1) * TK].rearrange("k p t -> p k t"), hT[xi][:], reads=[BhT[xi]])
                pg.end()

        for jb in jobs:
            name, N, nq = jb["name"], jb["N"], jb["nq"]
            sc = S[name]
            with contextlib.ExitStack() as st:
                def sb(nm, shape, dt):
                    return st.enter_context(nc.sbuf_tensor("s4_" + nm + name, list(shape), dt))
                PB = new_ps()
                pg.begin()
                wgu = sb("wgu", [128, 8, 2 * DFF], BF16)
                wdn = sb("wdn", [128, 22, D], BF16)
                Bw = Buf("w3")
                for k in range(8):
                    pg.dma("sp", wgu[:, k, :], wb_gu[k * 128:(k + 1) * 128, :], writes=[Bw])
                pg.dma("sp", wdn[:], wb_down.rearrange("(k p) n -> p k n", p=128), writes=[Bw])
                G2 = sb("G2", [128, D], F32)
                Bmodt = Buf("modt")
                pg.dma("sp", G2[:], rows_d[jb["b"], 5:6, :].broadcast_to([128, D]), writes=[Bmodt])
                xt = [sb("xt%d" % i, [128, 2, D], F32) for i in range(2)]
                Bxt = [Buf("xt%d" % i) for i in range(2)]
                hT = [sb("hT%d" % i, [128, 8, TK], BF16) for i in range(2)]
                BhT = [Buf("hT%d" % i) for i in range(2)]
                junk = sb("junk", [128, D], F32)
                ss = sb("ss", [128, 2], F32)
                rstd = sb("rstd", [128, 2], F32)
                act_ = sb("act", [128, 22, TK], BF16)
                sg = [sb("sg%d" % i, [128, TK], F32) for i in range(2)]
                fs = [sb("fs%d" % i, [128, D], F32) for i in range(2)]
                tt = [sb("tt%d" % i, [128, D], F32) for i in range(2)]
                Bjunk = Buf("junk")
                Bss = [Buf("ss%d" % i) for i in range(2)]
                Bfs = [Buf("fs%d" % i) for i in range(2)]
                Btt = [Buf("tt%d" % i) for i in range(2)]
                Bact = [Buf("act%d" % i) for i in range(22)]
                Bsg = [Buf("sg%d" % i) for i in range(2)]
                for t in range(nq // TK):
                    xi = t % 2
                    pg.dma("sp", xt[xi][:], sc["x1"][t * TK:(t + 1) * TK, :].rearrange("(s p) d -> p s d", p=128), writes=[Bxt[xi]])
                    pg.dma("sp", hT[xi][:], sc["h2T"][:, :, t * TK:(t + 1) * TK].rearrange("k p t -> p k t"), writes=[BhT[xi]])
                    for j in range(22):
                        i = j % 2

                        def mmg(e, j=j, i=i, xi=xi):
                            ins = None
                            for k in range(8):
                                ins = e.matmul(ps[:, 2 * i, 0:TK], lhsT=wgu[:, k, j * 128:(j + 1) * 128], rhs=hT[xi][:, k, :], start=(k == 0), stop=(k == 7))
                            for k in range(8):
                                ins = e.matmul(ps[:, 2 * i + 1, 0:TK], lhsT=wgu[:, k, DFF + j * 128:DFF + (j + 1) * 128], rhs=hT[xi][:, k, :], start=(k == 0), stop=(k == 7))
                            return ins
                        pg.op("pe", mmg, reads=[Bw, BhT[xi]], writes=[PB[2 * i], PB[2 * i + 1]])
                        pg.op("act", lambda e, i=i: e.activation(out=sg[i][:], in_=ps[:, 2 * i, 0:TK], func=AF.Silu), reads=[PB[2 * i]], writes=[Bsg[i]])
                        pg.op("dve", lambda e, i=i, j=j: e.tensor_tensor(out=act_[:, j, :], in0=sg[i][:], in1=ps[:, 2 * i + 1, 0:TK], op=ALU.mult),
                              reads=[Bsg[i], PB[2 * i + 1]], writes=[Bact[j]])
                    for s in range(2):
                        def mmd(e, s=s):
                            ins = None
                            for hh in range(2):
                                for j in range(22):
                                    ins = e.matmul(psb(4 + 2 * s + hh), lhsT=act_[:, j, s * 128:(s + 1) * 128], rhs=wdn[:, j, hh * 512:(hh + 1) * 512],
                                                   start=(j == 0), stop=(j == 21))
                            return ins
                        pg.op("pe", mmd, reads=[Bw] + Bact, writes=[PB[4 + 2 * s], PB[5 + 2 * s]])
                        pg.op("act", lambda e, s=s: e.activation(out=fs[s][:], in_=ps[:, 4 + 2 * s:6 + 2 * s, :].rearrange("p a b -> p (a b)"), func=AF.Copy),
                              reads=[PB[4 + 2 * s], PB[5 + 2 * s]], writes=[Bfs[s]])
                        pg.op("dve", lambda e, s=s: e.scalar_tensor_tensor(out=junk[:], in0=fs[s][:], scalar=1.0 / D, in1=fs[s][:], op0=ALU.mult, op1=ALU.mult,
                                                                         accum_out=ss[:, s:s + 1]), reads=[Bfs[s]], writes=[Bjunk, Bss[s]])
                        pg.op("dve", lambda e, s=s: e.tensor_scalar(out=rstd[:, s:s + 1], in0=ss[:, s:s + 1], scalar1=EPS, scalar2=None, op0=ALU.add),
                              reads=[Bss[s]], writes=[Bss[s]])
                        pg.op("pool", lambda e, s=s: e.tensor_tensor(out=rstd[:, s:s + 1], in0=rstd[:, s:s + 1], in1=nhalf[:, 0:1], op=ALU.pow),
                              reads=[Bss[s], Bc], writes=[Bss[s]])
                        pg.op("dve", lambda e, s=s: e.scalar_tensor_tensor(out=tt[s][:], in0=fs[s][:], scalar=rstd[:, s:s + 1], in1=G2[:], op0=ALU.mult, op1=ALU.mult),
                              reads=[Bfs[s], Bss[s], Bmodt], writes=[Btt[s]])
                        pg.op("pool", lambda e, s=s, xi=xi: e.tensor_tensor(out=xt[xi][:, s, :], in0=xt[xi][:, s, :], in1=tt[s][:], op=ALU.add),
                              reads=[Btt[s], Bxt[xi]], writes=[Bxt[xi]])
                    pg.dma("pool", y_out[name][t * TK:(t + 1) * TK, :].rearrange("(s p) d -> p s d", p=128), xt[xi][:], reads=[Bxt[xi]])
                pg.end()
    return nc


def _prep_shared(inp, NP, NS):
    f = lambda a: np.ascontiguousarray(np.asarray(a, dtype=np.float32))
    perm = _perm64()
    w_in = f(inp["w_in"])[0]
    o1, o2, o3, o4, o5 = 512, 640, 768, 1280, 1792
    cols = []
    qa_nat = np.array([[(kv * 4 + g) * 64 + d for kv in range(2) for d in range(64)] for g in range(4)])
    qa_prm = np.array([[(kv * 4 + g) * 64 + perm[d] for kv in range(2) for d in range(64)] for g in range(4)])
    cols += list(qa_nat.reshape(-1)) + list(qa_prm.reshape(-1))
    cols += [o1 + kv * 64 + d for kv in range(2) for d in range(64)]
    cols += [o1 + kv * 64 + perm[d] for kv in range(2) for d in range(64)]
    cols += list(range(o3, o4)) + list(range(o4, o5)) + list(range(o2, o3)) + list(range(o5, 2304))
    cols = np.array(cols)
    assert len(cols) == WIN
    g_q, g_k = f(inp["g_q"])[0], f(inp["g_k"])[0]
    gcols = np.zeros((128, 8), np.float32)
    gcols[:, 0] = np.tile(g_q, 2)
    gcols[:, 1] = np.tile(g_q[perm], 2)
    gcols[:, 2] = np.tile(g_k, 2)
    gcols[:, 3] = np.tile(g_k[perm], 2)
    gcols[:, 4] = f(inp["g_subln"])[0]
    gcols[:, 5] = 1.0 - LAM_INIT
    grow = np.stack([f(inp["g_pre_mix"])[0], f(inp["g_post_mix"])[0], f(inp["g_pre_ffn"])[0], f(inp["g_post_ffn"])[0]])
    grow2 = np.ascontiguousarray(np.stack([grow, grow]))
    lamv = np.stack([f(inp["lam_q1"])[0], f(inp["lam_k1"])[0], f(inp["lam_q2"])[0], f(inp["lam_k2"])[0]])[None]
    b_ada = f(inp["b_ada"])
    cm = np.zeros((128, 5, 128), np.float32)
    cm[:, 0, :] = np.eye(128)
    cm[:, 1, :] = np.eye(128)[::-1]
    cm[:, 2, :] = 1.0
    cm[0:64, 3, 0:64] = 1.0 / 64
    cm[64:128, 3, 64:128] = 1.0 / 64
    cm[:, 4, :] = 1.0 / 128
    m = np.arange(ULEN)
    emain = _onehot(_rel_bucket_np(639 - m))
    sh = dict(
        w_ada=f(inp["w_ada"])[0], b_ada2=np.ascontiguousarray(np.concatenate([b_ada, b_ada], 0)), grow=grow2,
        w_in_p=np.ascontiguousarray(w_in[:, cols]), w_out=f(inp["w_out"])[0], w_gu=f(inp["w_gu"])[0], w_down=f(inp["w_down"])[0],
        gcols=gcols, lamv=np.ascontiguousarray(lamv), relb=f(inp["rel_bias"]),
        cmat=cm.astype(ml_dtypes.bfloat16), emain=emain.astype(ml_dtypes.bfloat16),
    )
    return sh


def _prep_core(inp, sh, c, NP, NS):
    f = lambda a: np.asarray(a, dtype=np.float32)
    pb, pq, sbi, sq = c // 4, c % 4, c // 2, c % 2
    m = dict(sh)
    cp, cs = f(inp["c_prompt"])[pb], f(inp["c_sample"])[sbi]
    cT = np.stack([cp, cs], -1).reshape(8, 128, 2).transpose(1, 0, 2)
    m["cT"] = np.ascontiguousarray(cT)
    for nm, x, N, nq, qi in (("P", f(inp["x_prompt"])[pb], NP, NP // 4, pq), ("S", f(inp["x_sample"])[sbi], NS, NS // 2, sq)):
        qoff = qi * nq
        m["x" + nm] = np.ascontiguousarray(np.roll(x, -qoff, axis=0))
        pos = (np.arange(N) + qoff) % N
        cosT, sinT = _rope_tables(pos)
        m["cos" + nm] = cosT
        m["sin" + nm] = sinT
        NC = N // 128
        mm = np.arange(WLEN)
        if qoff > 0:
            bw = _rel_bucket_np(-1 - mm)
        else:
            bw = np.full(WLEN, NB // 2 + NB // 2 - 1)
        m["ewrap" + nm] = _onehot(bw).astype(ml_dtypes.bfloat16)
        if qoff + nq == N:
            bw2 = np.full(WLEN, NB // 2 - 1)
        else:
            bw2 = _rel_bucket_np(639 - mm)
        m["ewrap2" + nm] = _onehot(bw2).astype(ml_dtypes.bfloat16)
        far = np.zeros(NC + 1, np.int64)
        for ch in range(NC):
            if ch * 128 < nq:
                far[ch] = 31
            else:
                far[ch] = 31 if ch * 128 < N - qoff else 15
        far[NC] = 15
        m["efar" + nm] = _onehot(far).astype(ml_dtypes.bfloat16)
    return m


_CACHE = {}


def run(inputs, NP, NS, debug=False, ncores=8):
    key = (NP, NS, debug)
    if key not in _CACHE:
        _CACHE[key] = build_program(NP, NS, debug)
    nc = _CACHE[key]
    sh = _prep_shared(inputs, NP, NS)
    in_maps = [_prep_core(inputs, sh, c, NP, NS) for c in range(ncores)]
    res = run_bass_kernel_spmd(nc, in_maps, core_ids=list(range(ncores)))
    return res.results


def kernel(**inputs):
    NP = int(np.asarray(inputs["x_prompt"]).shape[1])
    NS = int(np.asarray(inputs["x_sample"]).shape[1])
    r = run(inputs, NP, NS)
    yp = np.zeros((2, NP, D), np.float32)
    ys = np.zeros((4, NS, D), np.float32)
    for c in range(8):
        pb, pq, sbi, sq = c // 4, c % 4, c // 2, c % 2
        nqp, nqs = NP // 4, NS // 2
        yp[pb, pq * nqp:(pq + 1) * nqp] = r[c]["yP"]
        ys[sbi, sq * nqs:(sq + 1) * nqs] = r[c]["yS"]
    return (yp, ys)
```

```python
import contextlib
import math
import numpy as np
import ml_dtypes
import concourse.bass as bass
import concourse.mybir as mybir
from concourse.bass_utils import run_bass_kernel_spmd

F32 = mybir.dt.float32
BF16 = mybir.dt.bfloat16
ALU = mybir.AluOpType
AF = mybir.ActivationFunctionType
AX = mybir.AxisListType

D = 1024
DFF = 2816
HD = 64
EPS = 1e-6
NB = 32
WIN = 2944
LAM_INIT = 0.8 - 0.6 * math.exp(-0.3 * 0)
ULEN = 1279
WLEN = 639

ENGS = ["pe", "act", "dve", "pool", "sp"]
N_DMA_SEMS = 6


class Buf:
    __slots__ = ("name", "w", "r", "excl")

    def __init__(self, name, excl=False):
        self.name = name
        self.excl = excl
        self.w = None
        self.r = []


class Op:
    __slots__ = ("eng", "fn", "waits", "signal", "ev", "is_dma")

    def __init__(self, eng, fn, is_dma):
        self.eng = eng
        self.fn = fn
        self.waits = []
        self.signal = False
        self.ev = None
        self.is_dma = is_dma


class Prog:
    def __init__(self, nc, stack):
        self.nc = nc
        self.sems = {}
        self.cnt = {}
        for e in ENGS:
            self.sems[e] = stack.enter_context(nc.semaphore("s_" + e))
            self.cnt[e] = 0
        self.dma_sems = {}
        for q in ("sp", "act", "pool"):
            lst = []
            for i in range(N_DMA_SEMS):
                nm = "d_%s%d" % (q, i)
                self.sems[nm] = stack.enter_context(nc.semaphore(nm))
                self.cnt[nm] = 0
                lst.append(nm)
            self.dma_sems[q] = lst
        self.dma_rr = {q: 0 for q in self.dma_sems}
        self.waited = {e: {} for e in ENGS}
        self.ops = None
        self.nops = 0

    def begin(self):
        self.ops = {e: [] for e in ENGS}
        self.allops = []
        self.dma_last = {}

    def _dep(self, op, other):
        if other is None or other is op:
            return
        if other.eng == "pe" and op.eng == "pe" and not other.is_dma and not op.is_dma:
            return
        op.waits.append(other)

    def op(self, eng, fn, reads=(), writes=(), dma=False):
        o = Op(eng, fn, dma)
        reads = list(reads)
        writes = list(writes)
        for b in reads:
            if b.excl and b not in writes:
                writes.append(b)
        for b in reads:
            self._dep(o, b.w)
        for b in writes:
            self._dep(o, b.w)
            for r in b.r:
                self._dep(o, r)
        for b in reads:
            b.r.append(o)
        for b in writes:
            b.w = o
            b.r = []
        if dma:
            i = self.dma_rr[eng]
            self.dma_rr[eng] = (i + 1) % N_DMA_SEMS
            nm = self.dma_sems[eng][i]
            prev = self.dma_last.get(nm)
            if prev is not None:
                o.waits.append(prev)
            self.dma_last[nm] = o
            self.cnt[nm] += 16
            o.ev = (nm, self.cnt[nm])
            o.signal = True
        self.ops[eng].append(o)
        self.allops.append(o)
        return o

    def dma(self, q, out, in_, reads=(), writes=()):
        return self.op(q, lambda e: e.dma_start(out=out, in_=in_), reads, writes, dma=True)

    def end(self):
        nc = self.nc
        for o in self.allops:
            for w in o.waits:
                w.signal = True
        for e in ENGS:
            for o in reversed(self.ops[e]):
                if not o.is_dma:
                    o.signal = True
                    break
        for e in ENGS:
            for o in self.ops[e]:
                if not o.is_dma and o.signal:
                    self.cnt[e] += 1
                    o.ev = (e, self.cnt[e])
        final = dict(self.cnt)
        sems = self.sems
        ops = self.ops
        waited_all = self.waited
        self.nops += len(self.allops)

        def emit(ename, eng):
            waited = waited_all[ename]
            for o in ops[ename]:
                need = {}
                for w in o.waits:
                    s, v = w.ev
                    if need.get(s, 0) < v:
                        need[s] = v
                for s, v in need.items():
                    if waited.get(s, 0) >= v:
                        continue
                    waited[s] = v
                    eng.wait_ge(sems[s], v)
                ins = o.fn(eng)
                if o.signal:
                    s, v = o.ev
                    ins.then_inc(sems[s], 16 if o.is_dma else 1)
            for s, v in final.items():
                if v > 0 and waited.get(s, 0) < v:
                    waited[s] = v
                    eng.wait_ge(sems[s], v)

        with nc.Block() as block:
            @block.tensor
            def _(eng):
                emit("pe", eng)

            @block.scalar
            def _(eng):
                emit("act", eng)

            @block.vector
            def _(eng):
                emit("dve", eng)

            @block.gpsimd
            def _(eng):
                emit("pool", eng)

            @block.sync
            def _(eng):
                emit("sp", eng)
        self.ops = None


def _perm64():
    d = np.arange(64)
    return np.where((d % 32) < 16, d + 16, d - 16)


def _rel_bucket_np(rel):
    half = NB // 2
    max_exact = half // 2
    n = np.abs(rel)
    nf = np.maximum(n, max_exact).astype(np.float32)
    large = max_exact + (np.log(nf / np.float32(max_exact)) / np.float32(math.log(128 / max_exact))
                         * (half - max_exact)).astype(np.int32)
    large = np.minimum(large, half - 1)
    return np.where(rel > 0, half, 0) + np.where(n < max_exact, n, large)


def _rope_tables(pos):
    row = (pos // 64).astype(np.float32)
    col = (pos % 64).astype(np.float32)
    half = HD // 2
    inv = (np.float32(10000.0) ** (-np.arange(0, half, 2, dtype=np.float32) / np.float32(half))).astype(np.float32)
    ang_r = row[:, None] * inv[None, :]
    ang_c = col[:, None] * inv[None, :]
    ang = np.concatenate([ang_r, ang_r, ang_c, ang_c], axis=-1).astype(np.float32)
    cos = np.cos(ang).astype(np.float32)
    sin = np.sin(ang).astype(np.float32)
    d = np.arange(64)
    sign = np.where((d % 32) < 16, -1.0, 1.0).astype(np.float32)
    sin_s = sin * sign[None, :]
    cosT = np.ascontiguousarray(np.concatenate([cos.T, cos.T], axis=0))
    sinT = np.ascontiguousarray(np.concatenate([sin_s.T, sin_s.T], axis=0))
    return cosT, sinT


def _onehot(buckets):
    e = np.zeros((NB, len(buckets)), dtype=np.float32)
    e[buckets, np.arange(len(buckets))] = 1.0
    return e


def build_program(NP, NS, debug=False):
    jobs = [dict(name="P", N=NP, nq=NP // 4, b=0), dict(name="S", N=NS, nq=NS // 2, b=1)]
    for jb in jobs:
        assert jb["nq"] % 512 == 0 and jb["N"] % 512 == 0
        jb["NC"] = jb["N"] // 128
    nc = bass.Bass("TRN2", target_bir_lowering=False)

    def din(name, shape, dt=F32):
        return nc.dram_tensor(name, list(shape), dt, kind="ExternalInput").ap()

    def dscr(name, shape, dt):
        if debug and not name.startswith("wb_"):
            return nc.dram_tensor(name, list(shape), dt, kind="ExternalOutput").ap()
        return nc.dram_tensor(name, list(shape), dt).ap()

    x_in = {"P": din("xP", [NP, D]), "S": din("xS", [NS, D])}
    cT_d = din("cT", [128, 8, 2])
    w_ada_d = din("w_ada", [D, 6 * D])
    b_ada2_d = din("b_ada2", [2, 6 * D])
    grow_d = din("grow", [2, 4, D])
    w_in_d = din("w_in_p", [D, WIN])
    w_out_d = din("w_out", [D, D])
    w_gu_d = din("w_gu", [D, 2 * DFF])
    w_down_d = din("w_down", [DFF, D])
    gcols_d = din("gcols", [128, 8])
    lamv_d = din("lamv", [1, 4, 64])
    relb_d = din("relb", [NB, 4])
    rope_d = {jb["name"]: (din("cos" + jb["name"], [128, jb["N"]]), din("sin" + jb["name"], [128, jb["N"]])) for jb in jobs}
    cmat_d = din("cmat", [128, 5, 128], BF16)
    emain_d = din("emain", [NB, ULEN], BF16)
    ewrap_d = {jb["name"]: din("ewrap" + jb["name"], [NB, WLEN], BF16) for jb in jobs}
    ewrap2_d = {jb["name"]: din("ewrap2" + jb["name"], [NB, WLEN], BF16) for jb in jobs}
    efar_d = {jb["name"]: din("efar" + jb["name"], [NB, jb["NC"] + 1], BF16) for jb in jobs}
    y_out = {jb["name"]: nc.dram_tensor("y" + jb["name"], [jb["nq"], D], F32, kind="ExternalOutput").ap() for jb in jobs}

    wb_in = dscr("wb_in", [D, WIN], BF16)
    wb_out = dscr("wb_out", [D, D], BF16)
    wb_gu = dscr("wb_gu", [D, 2 * DFF], BF16)
    wb_down = dscr("wb_down", [DFF, D], BF16)
    rows_d = dscr("rows_d", [2, 6, D], F32)
    lam_d = dscr("lam_d", [1, 2], F32)
    ub_d = dscr("ub_d", [3, 4, ULEN], BF16)
    uw_d = {jb["name"]: dscr("uw_d" + jb["name"], [3, 4, WLEN], BF16) for jb in jobs}
    uw2_d = {jb["name"]: dscr("uw2_d" + jb["name"], [3, 4, WLEN], BF16) for jb in jobs}
    ufar_d = {jb["name"]: dscr("ufar_d" + jb["name"], [1, 4 * (jb["NC"] + 1)], F32) for jb in jobs}
    S = {}
    for jb in jobs:
        n, N, nq = jb["name"], jb["N"], jb["nq"]
        S[n] = dict(
            KTa=dscr("KTa" + n, [128, N], BF16), Va=dscr("Va" + n, [N, 130], BF16),
            KTd=dscr("KTd" + n, [4, 128, N], BF16), Vd=dscr("Vd" + n, [N, 512], BF16),
            QTa=dscr("QTa" + n, [4, 128, nq], BF16), QTd=dscr("QTd" + n, [4, 128, nq], BF16),
            outT=dscr("outT" + n, [D, nq], BF16), zrow=dscr("zrow" + n, [2, 2, 512], F32),
            x1=dscr("x1" + n, [nq, D], F32), h2T=dscr("h2T" + n, [8, 128, nq], BF16),
        )

    with contextlib.ExitStack() as gst:
        pg = Prog(nc, gst)
        def gsb(name, shape, dt):
            return gst.enter_context(nc.sbuf_tensor("g_" + name, list(shape), dt))

        cmat = gsb("cmat", [128, 5, 128], BF16)
        gcols = gsb("gcols", [128, 8], F32)
        negl = gsb("negl", [128, 2], F32)
        nhalf = gsb("nhalf", [128, 512], F32)
        zero1 = gsb("zero1", [128, 1], F32)
        epsc = gsb("epsc", [128, 1], F32)
        farb = {jb["name"]: gsb("farb" + jb["name"], [128, 4, jb["NC"] + 1], F32) for jb in jobs}
        ps = gst.enter_context(nc.psum_tensor("psum_all", [128, 8, 512], F32))
        ident = cmat[:, 0, :]
        antiid = cmat[:, 1, :]
        onesb = cmat[:, 2, :]
        blk64 = cmat[:, 3, :]
        o128 = cmat[:, 4, :]

        def psb(i):
            return ps[:, i, :]

        def new_ps():
            return [Buf("ps%d" % i, excl=True) for i in range(8)]

        with contextlib.ExitStack() as st:
            def sb(name, shape, dt):
                return st.enter_context(nc.sbuf_tensor("s0_" + name, list(shape), dt))

            PB = new_ps()
            pg.begin()
            Bc = Buf("consts")
            pg.dma("sp", cmat[:], cmat_d, writes=[Bc])
            pg.dma("sp", gcols[:], gcols_d, writes=[Bc])
            pg.op("pool", lambda e: e.memset(nhalf[:], -0.5), writes=[Bc])
            pg.op("pool", lambda e: e.memset(zero1[:], 0.0), writes=[Bc])
            pg.op("pool", lambda e: e.memset(epsc[:], EPS), writes=[Bc])
            pg.op("dve", lambda e: e.tensor_scalar(out=gcols[:, 4:5], in0=gcols[:, 4:5], scalar1=1.0 - LAM_INIT, scalar2=None, op0=ALU.mult),
                  reads=[Bc], writes=[Bc])

            cT = sb("cT", [128, 8, 2], F32)
            scT = sb("scT", [128, 8, 2], F32)
            scTb = sb("scTb", [128, 8, 2], BF16)
            BcT, BscT = Buf("cT"), Buf("scT")
            pg.dma("sp", cT[:], cT_d, writes=[BcT])
            pg.op("act", lambda e: e.activation(out=scT[:], in_=cT[:], func=AF.Silu), reads=[BcT], writes=[BscT])
            pg.op("dve", lambda e: e.tensor_copy(scTb[:], scT[:]), reads=[BscT], writes=[BscT])

            mod = sb("mod", [2, 6 * D], F32)
            bada = sb("bada", [2, 6 * D], F32)
            Bmod, Bbada = Buf("mod"), Buf("bada")
            pg.dma("sp", bada[:], b_ada2_d, writes=[Bbada])
            waf = [sb("waf%d" % i, [128, 8, 512], F32) for i in range(2)]
            wab = [sb("wab%d" % i, [128, 8, 512], BF16) for i in range(2)]
            Bwaf = [Buf("waf%d" % i) for i in range(2)]
            Bwab = [Buf("wab%d" % i) for i in range(2)]
            for n in range(12):
                i = n % 2
                pg.dma("sp", waf[i][:], w_ada_d[:, n * 512:(n + 1) * 512].rearrange("(k p) n -> p k n", p=128), writes=[Bwaf[i]])
                ce = "dve" if n % 2 == 0 else "pool"
                pg.op(ce, lambda e, i=i: e.tensor_copy(wab[i][:], waf[i][:]), reads=[Bwaf[i]], writes=[Bwab[i]])

                def mm(e, i=i):
                    ins = None
                    for k in range(8):
                        ins = e.matmul(ps[0:2, i, :], lhsT=scTb[:, k, :], rhs=wab[i][:, k, :], start=(k == 0), stop=(k == 7))
                    return ins
                pg.op("pe", mm, reads=[Bwab[i], BscT], writes=[PB[i]])
                pg.op("dve", lambda e, i=i, n=n: e.tensor_tensor(out=mod[:, n * 512:(n + 1) * 512], in0=ps[0:2, i, :],
                                                               in1=bada[:, n * 512:(n + 1) * 512], op=ALU.add),
                      reads=[PB[i], Bbada], writes=[Bmod])
            grow = sb("grow", [2, 4, D], F32)
            rows = sb("rows", [2, 6, D], F32)
            Bgrow, Brows = Buf("grow"), Buf("rows")
            pg.dma("sp", grow[:], grow_d, writes=[Bgrow])
            pg.op("dve", lambda e: e.scalar_tensor_tensor(out=rows[:, 0, :], in0=mod[:, 1024:2048], scalar=1.0, in1=grow[:, 0, :],
                                                          op0=ALU.add, op1=ALU.mult), reads=[Bmod, Bgrow], writes=[Brows])
            pg.op("dve", lambda e: e.tensor_copy(rows[:, 1, :], mod[:, 0:1024]), reads=[Bmod], writes=[Brows])
            pg.op("dve", lambda e: e.tensor_tensor(out=rows[:, 2, :], in0=mod[:, 2048:3072], in1=grow[:, 1, :], op=ALU.mult),
                  reads=[Bmod, Bgrow], writes=[Brows])
            pg.op("dve", lambda e: e.scalar_tensor_tensor(out=rows[:, 3, :], in0=mod[:, 4096:5120], scalar=1.0, in1=grow[:, 2, :],
                                                          op0=ALU.add, op1=ALU.mult), reads=[Bmod, Bgrow], writes=[Brows])
            pg.op("dve", lambda e: e.tensor_copy(rows[:, 4, :], mod[:, 3072:4096]), reads=[Bmod], writes=[Brows])
            pg.op("dve", lambda e: e.tensor_tensor(out=rows[:, 5, :], in0=mod[:, 5120:6144], in1=grow[:, 3, :], op=ALU.mult),
                  reads=[Bmod, Bgrow], writes=[Brows])
            pg.dma("sp", rows_d, rows[:], reads=[Brows])

            lamv = sb("lamv", [1, 4, 64], F32)
            lj = sb("lj", [1, 64], F32)
            ls = sb("ls", [1, 4], F32)
            Blam = Buf("lam")
            pg.dma("sp", lamv[:], lamv_d, writes=[Blam])
            for t in range(2):
                pg.op("dve", lambda e, t=t: e.scalar_tensor_tensor(out=lj[:], in0=lamv[:, 2 * t, :], scalar=1.0, in1=lamv[:, 2 * t + 1, :],
                                                                 op0=ALU.mult, op1=ALU.mult, accum_out=ls[:, t:t + 1]),
                      reads=[Blam], writes=[Blam])
            pg.op("act", lambda e: e.activation(out=ls[:, 2:4], in_=ls[:, 0:2], func=AF.Exp), reads=[Blam], writes=[Blam])
            pg.op("dve", lambda e: e.scalar_tensor_tensor(out=ls[:, 1:2], in0=ls[:, 2:3], scalar=LAM_INIT, in1=ls[:, 3:4],
                                                          op0=ALU.add, op1=ALU.subtract), reads=[Blam], writes=[Blam])
            pg.op("dve", lambda e: e.tensor_scalar(out=ls[:, 0:1], in0=ls[:, 1:2], scalar1=-1.0, scalar2=None, op0=ALU.mult),
                  reads=[Blam], writes=[Blam])
            pg.dma("sp", lam_d, ls[:, 0:2], reads=[Blam])
            Bnegl = Buf("negl")
            pg.dma("sp", negl[:], lam_d.broadcast_to([128, 2]), reads=[Blam], writes=[Bnegl])

            rb = sb("rb", [NB, 4], F32)
            rbr = sb("rbr", [NB, 4], F32)
            rbp = [sb("rbp%d" % i, [NB, 4], BF16) for i in range(3)]
            Brb = Buf("rb")
            pg.dma("sp", rb[:], relb_d, writes=[Brb])
            pg.op("dve", lambda e: e.tensor_copy(rbp[0][:], rb[:]), reads=[Brb], writes=[Brb])
            pg.op("dve", lambda e: e.tensor_tensor(out=rbr[:], in0=rb[:], in1=rbp[0][:], op=ALU.subtract), reads=[Brb], writes=[Brb])
            pg.op("dve", lambda e: e.tensor_copy(rbp[1][:], rbr[:]), reads=[Brb], writes=[Brb])
            pg.op("dve", lambda e: e.tensor_tensor(out=rbr[:], in0=rbr[:], in1=rbp[1][:], op=ALU.subtract), reads=[Brb], writes=[Brb])
            pg.op("dve", lambda e: e.tensor_copy(rbp[2][:], rbr[:]), reads=[Brb], writes=[Brb])
            emat = sb("emat", [NB, ULEN], BF16)
            uout = sb("uout", [4, 3, ULEN], BF16)
            ufo = sb("ufo", [4, 132], F32)
            Bem, Buo = Buf("emat"), Buf("uout")
            specs = [("main", emain_d, ULEN, ub_d)]
            for jb in jobs:
                specs.append(("wrap", ewrap_d[jb["name"]], WLEN, uw_d[jb["name"]]))
                specs.append(("wrap", ewrap2_d[jb["name"]], WLEN, uw2_d[jb["name"]]))
            for jb in jobs:
                specs.append(("far", efar_d[jb["name"]], jb["NC"] + 1, ufar_d[jb["name"]]))
            pbank = 2
            for kind, esrc, L, dst in specs:
                pg.dma("sp", emat[:, 0:L], esrc, writes=[Bem])
                if kind != "far":
                    for part in range(3):
                        for s0 in range(0, L, 512):
                            w = min(512, L - s0)
                            bk = 2 + (pbank % 2)
                            pbank += 1
                            pg.op("pe", lambda e, bk=bk, part=part, s0=s0, w=w: e.matmul(ps[0:4, bk, 0:w], lhsT=rbp[part][:, :],
                                                                                       rhs=emat[:, s0:s0 + w], start=True, stop=True),
                                  reads=[Bem, Brb], writes=[PB[bk]])
                            pg.op("dve", lambda e, bk=bk, part=part, s0=s0, w=w: e.tensor_copy(uout[:, part, s0:s0 + w], ps[0:4, bk, 0:w]),
                                  reads=[PB[bk]], writes=[Buo])
                    pg.dma("sp", dst.rearrange("t h l -> h t l"), uout[:, :, 0:L], reads=[Buo])
                else:
                    bk = 2 + (pbank % 2)
                    pbank += 1

                    def mmf(e, bk=bk, L=L):
                        ins = None
                        for part in range(3):
                            ins = e.matmul(ps[0:4, bk, 0:L], lhsT=rbp[part][:, :], rhs=emat[:, 0:L], start=(part == 0), stop=(part == 2))
                        return ins
                    pg.op("pe", mmf, reads=[Bem, Brb], writes=[PB[bk]])
                    pg.op("dve", lambda e, bk=bk, L=L: e.tensor_copy(ufo[:, 0:L], ps[0:4, bk, 0:L]), reads=[PB[bk]], writes=[Buo])
                    pg.dma("sp", dst.rearrange("o (h l) -> (o h) l", h=4), ufo[:, 0:L], reads=[Buo])
            for jb in jobs:
                n = jb["name"]
                pg.dma("sp", farb[n][:].rearrange("p h l -> p (h l)"), ufar_d[n].broadcast_to([128, 4 * (jb["NC"] + 1)]),
                       reads=[Buo], writes=[Bc])

            pieces = []
            for k in range(8):
                pieces.append((w_in_d[k * 128:(k + 1) * 128, :], wb_in[k * 128:(k + 1) * 128, :], WIN))
            NBUF = 2
            wcf = [sb("wcf%d" % i, [128, WIN], F32) for i in range(NBUF)]
            wcb = [sb("wcb%d" % i, [128, WIN], BF16) for i in range(NBUF)]
            Bwcf = [Buf("wcf%d" % i) for i in range(NBUF)]
            Bwcb = [Buf("wcb%d" % i) for i in range(NBUF)]
            for n, (src, dst, w) in enumerate(pieces):
                i = n % NBUF
                pg.dma("sp", wcf[i][:, 0:w], src, writes=[Bwcf[i]])
                ce = ["dve", "pool", "act"][n % 3]
                if ce == "act":
                    pg.op(ce, lambda e, i=i, w=w: e.activation(out=wcb[i][:, 0:w], in_=wcf[i][:, 0:w], func=AF.Copy), reads=[Bwcf[i]], writes=[Bwcb[i]])
                else:
                    pg.op(ce, lambda e, i=i, w=w: e.tensor_copy(wcb[i][:, 0:w], wcf[i][:, 0:w]), reads=[Bwcf[i]], writes=[Bwcb[i]])
                pg.dma("pool", dst, wcb[i][:, 0:w], reads=[Bwcb[i]])
            pg.end()

        for jb in jobs:
            name, N, nq = jb["name"], jb["N"], jb["nq"]
            sc = S[name]
            cos_d, sin_d = rope_d[name]
            with contextlib.ExitStack() as st:
                def sb(nm, shape, dt):
                    return st.enter_context(nc.sbuf_tensor("s1_" + nm + name, list(shape), dt))
                PB = new_ps()
                pg.begin()
                win = sb("win", [128, 8, WIN], BF16)
                Bwin = Buf("win")
                for k in range(8):
                    pg.dma("sp", win[:, k, :], wb_in[k * 128:(k + 1) * 128, :], writes=[Bwin])
                A1 = sb("A1", [128, D], F32)
                SH1 = sb("SH1", [128, D], F32)
                Bmodt = Buf("modt")
                pg.dma("sp", A1[:], rows_d[jb["b"], 0:1, :].broadcast_to([128, D]), writes=[Bmodt])
                pg.dma("sp", SH1[:], rows_d[jb["b"], 1:2, :].broadcast_to([128, D]), writes=[Bmodt])
                NXB = 2
                xt = [sb("xt%d" % i, [128, 4, D], F32) for i in range(NXB)]
                Bxt = [Buf("xt%d" % i) for i in range(NXB)]
                rt = [(sb("cos%d" % i, [128, 512], F32), sb("sin%d" % i, [128, 512], F32)) for i in range(2)]
                Brt = [Buf("rt%d" % i) for i in range(2)]
                junk = sb("junk", [128, D], F32)
                ss = sb("ss", [128, 4], F32)
                rstd = sb("rstd", [128, 4], F32)
                tt = [sb("tt%d" % i, [128, D], F32) for i in range(2)]
                hb = [sb("hb%d" % i, [128, D], BF16) for i in range(2)]
                hTs = [sb("hT%d" % i, [128, 8, 512], BF16) for i in range(2)]
                Bjunk, Bss, Brstd = Buf("junk"), Buf("ss"), Buf("rstd")
                Btt = [Buf("tt%d" % i) for i in range(2)]
                Bhb = [Buf("hb%d" % i) for i in range(2)]
                BhTs = [[Buf("hT%d_%d" % (j, i)) for i in range(4)] for j in range(2)]
                asb = sb("asb", [128, 512], F32)
                sq = sb("sq", [128, 512], F32)
                sqh = sb("sqh", [128, 512], BF16)
                sqm = sb("sqm", [128, 512], BF16)
                rs = sb("rs", [128, 512], F32)
                t1 = sb("t1", [128, 512], F32)
                t2 = sb("t2", [128, 512], F32)
                Basb, Bsq, Bsqh, Bsqm, Brs, Bt1, Bt2 = [Buf(x) for x in ["asb", "sq", "sqh", "sqm", "rs", "t1", "t2"]]
                kta = [sb("kta%d" % i, [128, 512], BF16) for i in range(2)]
                ktd = [sb("ktd%d" % i, [128, 4, 512], BF16) for i in range(2)]
                qta = [sb("qta%d" % i, [128, 4, 512], BF16) for i in range(2)]
                qtd = [sb("qtd%d" % i, [128, 4, 512], BF16) for i in range(2)]
                va = [sb("va%d" % i, [128, 4, 2, 65], BF16) for i in range(2)]
                vd = [sb("vd%d" % i, [128, 4, 512], BF16) for i in range(2)]
                Bkta = [Buf("kta%d" % i) for i in range(2)]
                Bktd = [Buf("ktd%d" % i) for i in range(2)]
                Bqta = [Buf("qta%d" % i) for i in range(2)]
                Bqtd = [Buf("qtd%d" % i) for i in range(2)]
                Bva = [Buf("va%d" % i) for i in range(2)]
                Bvd = [Buf("vd%d" % i) for i in range(2)]
                for i in range(2):
                    pg.op("pool", lambda e, i=i: e.memset(va[i][:], 1.0), writes=[Bva[i]])
                psT = [ps[:, i, :].bitcast(BF16) for i in range(2)]

                def normrope(pa, pb, gi, gpi, outap, scale, ri):
                    cs, sn = rt[ri]
                    pg.op("act", lambda e: e.activation(out=asb[:], in_=psb(pa), func=AF.Copy), reads=[PB[pa]], writes=[Basb])
                    pg.op("dve", lambda e: e.scalar_tensor_tensor(out=t2[:], in0=psb(pb), scalar=gcols[:, gpi:gpi + 1], in1=sn[:], op0=ALU.mult, op1=ALU.mult),
                          reads=[PB[pb], Bc, Brt[ri]], writes=[Bt2])
                    pg.op("dve", lambda e: e.tensor_tensor(out=sq[:], in0=asb[:], in1=asb[:], op=ALU.mult), reads=[Basb], writes=[Bsq])
                    pg.op("dve", lambda e: e.tensor_copy(sqh[:], sq[:]), reads=[Bsq], writes=[Bsqh])
                    pg.op("pool", lambda e: e.tensor_tensor(out=sq[:], in0=sq[:], in1=sqh[:], op=ALU.subtract), reads=[Bsq, Bsqh], writes=[Bsq])
                    pg.op("pool", lambda e: e.tensor_copy(sqm[:], sq[:]), reads=[Bsq], writes=[Bsqm])

                    def mmss(e):
                        e.matmul(psb(7), lhsT=blk64, rhs=sqh[:], start=True, stop=False)
                        return e.matmul(psb(7), lhsT=blk64, rhs=sqm[:], start=False, stop=True)
                    pg.op("pe", mmss, reads=[Bsqh, Bsqm, Bc], writes=[PB[7]])
                    pg.op("act", lambda e: e.activation(out=rs[:], in_=psb(7), func=AF.Sqrt, bias=epsc[:], scale=1.0), reads=[PB[7], Bc], writes=[Brs])
                    pg.op("dve", lambda e: e.reciprocal(out=rs[:], in_=rs[:]), reads=[Brs], writes=[Brs])
                    pg.op("dve", lambda e: e.scalar_tensor_tensor(out=t1[:], in0=asb[:], scalar=gcols[:, gi:gi + 1], in1=cs[:], op0=ALU.mult, op1=ALU.mult),
                          reads=[Basb, Bc, Brt[ri]], writes=[Bt1])
                    pg.op("pool", lambda e: e.tensor_tensor(out=t1[:], in0=t1[:], in1=t2[:], op=ALU.add), reads=[Bt1, Bt2], writes=[Bt1])
                    return lambda e: e.scalar_tensor_tensor(out=outap, in0=t1[:], scalar=scale, in1=rs[:], op0=ALU.mult, op1=ALU.mult)

                def proj(bank, ch, hT, BhT):
                    def f(e):
                        ins = None
                        for k in range(8):
                            ins = e.matmul(psb(bank), lhsT=win[:, k, ch * 128:(ch + 1) * 128], rhs=hT[:, k, :], start=(k == 0), stop=(k == 7))
                        return ins
                    pg.op("pe", f, reads=[Bwin] + BhT, writes=[PB[bank]])

                ntiles = N // 512
                hb4 = [[sb("hbq%d_%d" % (j, i), [128, D], BF16) for i in range(4)] for j in range(1)][0]
                Bhb4 = [Buf("hbq%d" % i) for i in range(4)]
                ssq = [sb("ssq%d" % i, [128, 4], F32) for i in range(2)]
                rsq = [sb("rsq%d" % i, [128, 4], F32) for i in range(2)]
                Bssq = [[Buf("ssq%d_%d" % (j, i)) for i in range(4)] for j in range(2)]

                def xload(t):
                    xi = t % NXB
                    ti = t % 2
                    pg.dma("sp", xt[xi][:], x_in[name][t * 512:(t + 1) * 512, :].rearrange("(s p) d -> p s d", p=128), writes=[Bxt[xi]])
                    pg.dma("sp", rt[ti][0][:], cos_d[:, t * 512:(t + 1) * 512], writes=[Brt[ti]])
                    pg.dma("sp", rt[ti][1][:], sin_d[:, t * 512:(t + 1) * 512], writes=[Brt[ti]])

                def prep_sub(t, s):
                    xi = t % NXB
                    ti = t % 2
                    i = s % 2
                    sq_, rq_, B_ = ssq[ti], rsq[ti], Bssq[ti][s]
                    pg.op("dve", lambda e: e.scalar_tensor_tensor(out=junk[:], in0=xt[xi][:, s, :], scalar=1.0 / D, in1=xt[xi][:, s, :],
                                                                  op0=ALU.mult, op1=ALU.mult, accum_out=sq_[:, s:s + 1]),
                          reads=[Bxt[xi]], writes=[Bjunk, B_])
                    pg.op("dve", lambda e: e.tensor_scalar(out=rq_[:, s:s + 1], in0=sq_[:, s:s + 1], scalar1=EPS, scalar2=None, op0=ALU.add), reads=[B_], writes=[B_])
                    pg.op("pool", lambda e: e.tensor_tensor(out=rq_[:, s:s + 1], in0=rq_[:, s:s + 1], in1=nhalf[:, 0:1], op=ALU.pow), reads=[B_, Bc], writes=[B_])
                    pg.op("dve", lambda e: e.scalar_tensor_tensor(out=tt[i][:], in0=xt[xi][:, s, :], scalar=rq_[:, s:s + 1], in1=A1[:],
                                                                  op0=ALU.mult, op1=ALU.mult),
                          reads=[Bxt[xi], B_, Bmodt], writes=[Btt[i]])
                    pg.op("pool", lambda e: e.tensor_tensor(out=hb4[s][:], in0=tt[i][:], in1=SH1[:], op=ALU.add),
                          reads=[Btt[i], Bmodt], writes=[Bhb4[s]])

                def trans(t, s):
                    ti = t % 2
                    i = s % 2
                    hT, BhT = hTs[ti], BhTs[ti]

                    def tr(e):
                        ins = None
                        for k in range(8):
                            ins = e.transpose(out=psT[i][:, k * 128:(k + 1) * 128], in_=hb4[s][:, k * 128:(k + 1) * 128], identity=ident)
                        return ins
                    pg.op("pe", tr, reads=[Bhb4[s], Bc], writes=[PB[i]])
                    pg.op("act", lambda e: e.activation(out=hT[:, :, s * 128:(s + 1) * 128],
                                                        in_=psT[i].rearrange("p (k t) -> p k t", k=8), func=AF.Copy),
                          reads=[PB[i]], writes=[BhT[s]])

                def vpair(t, s):
                    ti = t % 2
                    hT, BhT = hTs[ti], BhTs[ti]

                    def mmv(e):
                        ins = None
                        for k in range(8):
                            ins = e.matmul(psb(6), lhsT=hT[:, k, s * 128:(s + 1) * 128], rhs=win[:, k, 2432:2944], start=(k == 0), stop=(k == 7))
                        return ins

                    def mmva(e):
                        ins = None
                        for k in range(8):
                            ins = e.matmul(ps[:, 5, 0:128], lhsT=hT[:, k, s * 128:(s + 1) * 128], rhs=win[:, k, 2304:2432], start=(k == 0), stop=(k == 7))
                        return ins
                    pg.op("pe", mmv, reads=[Bwin, BhT[s]], writes=[PB[6]])
                    pg.op("act", lambda e: e.activation(out=vd[ti][:, s, :], in_=psb(6), func=AF.Copy), reads=[PB[6]], writes=[Bvd[ti]])
                    pg.op("pe", mmva, reads=[Bwin, BhT[s]], writes=[PB[5]])
                    pg.op("dve", lambda e: e.tensor_copy(va[ti][:, s, :, 0:64], ps[:, 5, 0:128].rearrange("p (a b) -> p a b", a=2)),
                          reads=[PB[5]], writes=[Bva[ti]])

                def kd_or_qd(t, h, ch0, dst, Bdst, scale):
                    ti = t % 2
                    hT, BhT = hTs[ti], BhTs[ti]
                    bk = 4 + (h % 2)
                    proj(bk, ch0 + h, hT, BhT)
                    if h % 2 == 0:
                        pg.op("act", lambda e: e.activation(out=dst[ti][:, h, :], in_=psb(bk), func=AF.Copy, scale=scale), reads=[PB[bk]], writes=[Bdst[ti]])
                    else:
                        pg.op("dve", lambda e: e.tensor_scalar(out=dst[ti][:, h, :], in0=psb(bk), scalar1=scale, scalar2=None, op0=ALU.mult),
                              reads=[PB[bk]], writes=[Bdst[ti]])

                xload(0)
                for s in range(4):
                    prep_sub(0, s)
                for s in range(4):
                    trans(0, s)
                for t in range(ntiles):
                    own = (t * 512 < nq)
                    ti = t % 2
                    hT, BhT = hTs[ti], BhTs[ti]
                    nxt = t + 1 < ntiles
                    if nxt:
                        xload(t + 1)
                    proj(2, 8, hT, BhT)
                    proj(3, 9, hT, BhT)
                    fin = normrope(2, 3, 2, 3, kta[ti][:], 1.0, ti)
                    pg.op("dve", fin, reads=[Bt1, Brs], writes=[Bkta[ti]])
                    pg.dma("pool", sc["KTa"][:, t * 512:(t + 1) * 512], kta[ti][:], reads=[Bkta[ti]])
                    if nxt:
                        prep_sub(t + 1, 0)
                    for h in range(4):
                        kd_or_qd(t, h, 14, ktd, Bktd, 1.0)
                    pg.dma("pool", sc["KTd"][:, :, t * 512:(t + 1) * 512].rearrange("h p t -> p h t"), ktd[ti][:], reads=[Bktd[ti]])
                    if nxt:
                        prep_sub(t + 1, 1)
                    vpair(t, 0)
                    vpair(t, 1)
                    if nxt:
                        prep_sub(t + 1, 2)
                    vpair(t, 2)
                    vpair(t, 3)
                    pg.dma("pool", sc["Vd"][t * 512:(t + 1) * 512, :].rearrange("(s p) c -> p s c", p=128), vd[ti][:], reads=[Bvd[ti]])
                    pg.dma("pool", sc["Va"][t * 512:(t + 1) * 512, :].rearrange("(s p) c -> p s c", p=128),
                           va[ti][:].rearrange("p s a b -> p s (a b)"), reads=[Bva[ti]])
                    if nxt:
                        prep_sub(t + 1, 3)
                    if own:
                        for g in range(4):
                            proj(2, g, hT, BhT)
                            proj(3, 4 + g, hT, BhT)
                            fin = normrope(2, 3, 0, 1, qta[ti][:, g, :], 0.125, ti)
                            pg.op("dve", fin, reads=[Bt1, Brs], writes=[Bqta[ti]])
                        pg.dma("pool", sc["QTa"][:, :, t * 512:(t + 1) * 512].rearrange("g p t -> p g t"), qta[ti][:], reads=[Bqta[ti]])
                        for h in range(4):
                            kd_or_qd(t, h, 10, qtd, Bqtd, 0.125)
                        pg.dma("pool", sc["QTd"][:, :, t * 512:(t + 1) * 512].rearrange("h p t -> p h t"), qtd[ti][:], reads=[Bqtd[ti]])
                    if nxt:
                        for s in range(4):
                            trans(t + 1, s)
                pg.end()

        late_pieces = []
        for k in range(8):
            late_pieces.append((w_out_d[k * 128:(k + 1) * 128, :], wb_out[k * 128:(k + 1) * 128, :], D))
        for k in range(8):
            for c0_ in range(0, 2 * DFF, 1024):
                w_ = min(1024, 2 * DFF - c0_)
                late_pieces.append((w_gu_d[k * 128:(k + 1) * 128, c0_:c0_ + w_], wb_gu[k * 128:(k + 1) * 128, c0_:c0_ + w_], w_))
        for k in range(22):
            late_pieces.append((w_down_d[k * 128:(k + 1) * 128, :], wb_down[k * 128:(k + 1) * 128, :], D))

        for jb in jobs:
            name, N, nq, NC = jb["name"], jb["N"], jb["nq"], jb["NC"]
            sc = S[name]
            with contextlib.ExitStack() as st:
                def sb(nm, shape, dt):
                    return st.enter_context(nc.sbuf_tensor("s2_" + nm + name, list(shape), dt))
                PB = new_ps()
                pg.begin()
                KT = sb("KT", [128, N], BF16)
                VV = sb("VV", [128, NC, 130], BF16)
                QT = sb("QT", [128, 4, nq], BF16)
                NG = 8
                cpg = NC // NG
                BKV = [Buf("kv%d" % i) for i in range(NG)]
                BQ = Buf("QT")
                pT = [sb("pT%d" % i, [128, 1024], BF16) for i in range(3)]
                BpT = [Buf("pT%d" % i) for i in range(3)]
                osb = [sb("osb%d" % i, [128, 512], F32) for i in range(2)]
                Bosb = [Buf("osb%d" % i) for i in range(2)]
                zr = sb("zr", [128, 2, 512], F32)
                Bzr = Buf("zr")
                bcz = sb("bcz", [128, 2, 512], F32)
                Bbcz = Buf("bcz")
                onrm = sb("onrm", [128, 512], BF16)
                Bonrm = Buf("onrm")
                dd = sb("dd", [128, 512], F32)
                dsq = sb("dsq", [128, 512], F32)
                dsqh = sb("dsqh", [128, 512], BF16)
                dsqm = sb("dsqm", [128, 512], BF16)
                drs = sb("drs", [128, 512], F32)
                Bdd, Bdsq, Bdsqh, Bdsqm, Bdrs = [Buf(x) for x in ["dd", "dsq", "dsqh", "dsqm", "drs"]]
                TTs = [sb("TT%d" % i, [128, 8, 512], F32) for i in range(2)]
                BTTs = [Buf("TT%d" % i) for i in range(2)]
                pending = []
                hk = [sb("hk%d" % i, [128, 3, 512], BF16) for i in range(2)]
                Bhk = [Buf("hk%d" % i) for i in range(2)]
                zdram = sc["zrow"]
                Bzd = [Buf("zd0"), Buf("zd1")]

                def load_kv(kt_src, v_src, vw):
                    for gi in range(NG):
                        c0 = gi * cpg
                        pg.dma("sp", KT[:, c0 * 128:(c0 + cpg) * 128], kt_src[:, c0 * 128:(c0 + cpg) * 128], writes=[BKV[gi]])
                        pg.dma("sp", VV[:, c0:c0 + cpg, 0:vw], v_src[c0 * 128:(c0 + cpg) * 128, :].rearrange("(c p) w -> p c w", p=128), writes=[BKV[gi]])

                def attn_tile(units, lhs_v, near, bias_of, epilogue, TT=None, BTT=None):
                    def qk(c):
                        par = c % 2

                        def f(e):
                            ins = None
                            for u in range(2):
                                ins = e.matmul(psb(2 * par + u), lhsT=KT[64 * u:64 * u + 64, c * 128:(c + 1) * 128], rhs=units[u], start=True, stop=True)
                            return ins
                        pg.op("pe", f, reads=[BKV[c // cpg], BQ], writes=[PB[2 * par], PB[2 * par + 1]])
                        if c in near:
                            ti = near[c]
                            pg.op("dve", lambda e: e.tensor_tensor(out=ps[:, 2 * par:2 * par + 2, :], in0=ps[:, 2 * par:2 * par + 2, :],
                                                                 in1=TT[:, ti:ti + 1, :].to_broadcast([128, 2, 512]), op=ALU.add),
                                  reads=[BTT], writes=[PB[2 * par], PB[2 * par + 1]])

                    def ex(c):
                        par = c % 2
                        p3 = c % 3
                        b = bias_of(c)
                        pg.op("act", lambda e: e.activation(out=pT[p3][:], in_=ps[:, 2 * par:2 * par + 2, :].rearrange("p a b -> p (a b)"),
                                                          func=AF.Exp, bias=(b if b is not None else zero1[:])),
                              reads=[PB[2 * par], PB[2 * par + 1], Bc], writes=[BpT[p3]])

                    def pv(c):
                        par = c % 3
                        dm = diff_mode[0]

                        def f(e):
                            ins = None
                            for u in range(2):
                                lv, m = lhs_v(u, c)
                                ins = e.matmul(ps[0:m, 4 + u, :], lhsT=lv, rhs=pT[par][:, u * 512:(u + 1) * 512], start=(c == 0), stop=(c == NC - 1))
                            if dm:
                                for u in range(2):
                                    ins = e.matmul(ps[32 * u:32 * u + 1, 6, :], lhsT=onesb[:, 0:1], rhs=pT[par][:, u * 512:(u + 1) * 512],
                                                   start=(c == 0), stop=(c == NC - 1), skip_group_check=True)
                            return ins
                        w = [PB[4], PB[5]] + ([PB[6]] if diff_mode[0] else [])
                        pg.op("pe", f, reads=[BKV[c // cpg], BpT[par], Bc], writes=w)
                    qk(0)
                    qk(1)
                    for c in range(NC):
                        ex(c)
                        if c + 2 < NC:
                            qk(c + 2)
                        pv(c)
                        while pending and pending[0][0] * NC // 32 <= c:
                            pending.pop(0)[1]()
                    epilogue()

                diff_mode = [False]
                if name == "P":
                    lcf = [sb("lcf%d" % i, [128, 1024], F32) for i in range(2)]
                    lcb = [sb("lcb%d" % i, [128, 1024], BF16) for i in range(2)]
                    Blcf = [Buf("lcf%d" % i) for i in range(2)]
                    Blcb = [Buf("lcb%d" % i) for i in range(2)]
                lp_state = [0]

                def emit_late(n):
                    while n > 0 and name == "P" and lp_state[0] < len(late_pieces):
                        src, dst, w = late_pieces[lp_state[0]]
                        i = lp_state[0] % 2
                        lp_state[0] += 1
                        n -= 1
                        pg.dma("sp", lcf[i][:, 0:w], src, writes=[Blcf[i]])
                        pg.op("pool", lambda e, i=i, w=w: e.tensor_copy(lcb[i][:, 0:w], lcf[i][:, 0:w]), reads=[Blcf[i]], writes=[Blcb[i]])
                        pg.dma("pool", dst, lcb[i][:, 0:w], reads=[Blcb[i]])
                def tt_dma(h, pair):
                    for ti in (2 * pair, 2 * pair + 1):
                        i = ti % 2
                        if ti < 6:
                            dofs = (ti - 1) * 128
                            base = 512 - dofs
                            for part in range(3):
                                src = bass.AP(ub_d.tensor, ub_d[part, h, base:base + 1].offset, [[1, 128], [1, 512]])
                                pg.dma("sp", hk[i][:, part, :], src, writes=[Bhk[i]])
                        else:
                            uw = uw_d[name] if ti == 6 else uw2_d[name]
                            for part in range(3):
                                src = bass.AP(uw.tensor, uw[part, h, 0:1].offset, [[1, 128], [1, 512]])
                                pg.dma("sp", hk[i][:, part, :], src, writes=[Bhk[i]])

                def tt_pe(h, pair):
                    TT, BTT = TTs[h % 2], BTTs[h % 2]
                    for ti in (2 * pair, 2 * pair + 1):
                        i = ti % 2

                        def mmT(e, i=i):
                            ins = None
                            for part in range(3):
                                ins = e.matmul(psb(7), lhsT=antiid, rhs=hk[i][:, part, :], start=(part == 0), stop=(part == 2))
                            return ins
                        pg.op("pe", mmT, reads=[Bhk[i], Bc], writes=[PB[7]])
                        pg.op("dve", lambda e, ti=ti, TT=TT: e.tensor_copy(TT[:, ti, :], psb(7)), reads=[PB[7]], writes=[BTT])

                def tt_steps(h):
                    st_ = [lambda: tt_dma(h, 0)]
                    for p_ in range(1, 4):
                        st_.append(lambda p_=p_: (tt_pe(h, p_ - 1), tt_dma(h, p_)))
                    st_.append(lambda: tt_pe(h, 3))
                    return st_
                tt_sched = []
                load_kv(sc["KTa"], sc["Va"], 130)
                for g in range(4):
                    pg.dma("sp", QT[:, g, :], sc["QTa"][g], writes=[BQ])
                for st_ in tt_steps(0):
                    st_()
                for qt in range(nq // 128):
                    units = [QT[64 * u:64 * u + 64, :, qt * 128:(qt + 1) * 128] for u in range(2)]

                    def lhs_v(u, c):
                        return VV[:, c, u * 65:(u + 1) * 65], 65

                    def epi(qt=qt):
                        for u in range(2):
                            pg.op("dve", lambda e, u=u: e.tensor_copy(osb[u][0:65, :], ps[0:65, 4 + u, :]), reads=[PB[4 + u]], writes=[Bosb[u]])
                        for u in range(2):
                            pg.op("dve", lambda e, u=u: e.reciprocal(out=zr[64:65, u, :], in_=osb[u][64:65, :]), reads=[Bosb[u]], writes=[Bzr])
                            pg.dma("pool", zdram[u, 0:1, :], zr[64:65, u, :], reads=[Bzr], writes=[Bzd[u]])
                            pg.dma("pool", bcz[0:64, u, :], zdram[u, 0:1, :].broadcast_to([64, 512]), reads=[Bzd[u]], writes=[Bbcz])
                            pg.op("dve", lambda e, u=u: e.tensor_tensor(out=onrm[0:64, :], in0=osb[u][0:64, :], in1=bcz[0:64, u, :], op=ALU.mult),
                                  reads=[Bosb[u], Bbcz], writes=[Bonrm])
                            pg.dma("pool", sc["outT"][u * 256:(u + 1) * 256, qt * 128:(qt + 1) * 128].rearrange("(g d) t -> d g t", g=4),
                                   onrm[0:64, :].rearrange("d (g t) -> d g t", g=4), reads=[Bonrm])
                    attn_tile(units, lhs_v, {}, lambda c: None, epi)
                    emit_late(3)
                emit_late(10 ** 6)
                diff_mode[0] = True

                for h in range(4):
                    load_kv(sc["KTd"][h], sc["Vd"][:, h * 128:(h + 1) * 128], 128)
                    pg.dma("sp", QT[:, 0, :], sc["QTd"][h], writes=[BQ])
                    while tt_sched:
                        tt_sched.pop(0)()
                    for qt in range(nq // 512):
                        units = [QT[64 * u:64 * u + 64, 0, qt * 512:(qt + 1) * 512] for u in range(2)]
                        c0 = qt * 4
                        near = {}
                        for ti in range(6):
                            c = c0 - 1 + ti
                            if 0 <= c < NC:
                                near[c] = ti
                        if qt == 0:
                            near[NC - 1] = 6
                        if qt == nq // 512 - 1:
                            near[nq // 128] = 7
                        fb = farb[name]

                        def bias_of(c, near=near, c0=c0, h=h, fb=fb):
                            if c in near:
                                return None
                            if c < c0:
                                return fb[:, h, NC:NC + 1]
                            return fb[:, h, c:c + 1]

                        def lhs_v(u, c):
                            return VV[:, c, 0:128], 128

                        def epi(qt=qt, h=h):
                            for u in range(2):
                                pg.op("dve", lambda e, u=u: e.tensor_copy(osb[u][:], psb(4 + u)), reads=[PB[4 + u]], writes=[Bosb[u]])
                            for u in range(2):
                                pg.op("dve", lambda e, u=u: e.tensor_copy(zr[32 * u:32 * u + 1, u, :], ps[32 * u:32 * u + 1, 6, :]), reads=[PB[6]], writes=[Bzr])
                            pending.append((5, lambda: epi_tail0()))
                            pending.append((15, lambda: epi_tail1()))
                            pending.append((23, lambda qt=qt, h=h: epi_tail2(qt, h)))

                        def epi_tail0():
                            for u in range(2):
                                pg.op("dve", lambda e, u=u: e.reciprocal(out=zr[32 * u:32 * u + 1, u, :], in_=zr[32 * u:32 * u + 1, u, :]), reads=[Bzr], writes=[Bzr])
                                pg.dma("pool", zdram[u, 1:2, :], zr[32 * u:32 * u + 1, u, :], reads=[Bzr], writes=[Bzd[u]])
                                pg.dma("pool", bcz[:, u, :], zdram[u, 1:2, :].broadcast_to([128, 512]), reads=[Bzd[u]], writes=[Bbcz])
                            pg.op("dve", lambda e: e.tensor_tensor(out=osb[0][:], in0=osb[0][:], in1=bcz[:, 0, :], op=ALU.mult), reads=[Bosb[0], Bbcz], writes=[Bosb[0]])
                            pg.op("pool", lambda e: e.tensor_tensor(out=osb[1][:], in0=osb[1][:], in1=bcz[:, 1, :], op=ALU.mult), reads=[Bosb[1], Bbcz], writes=[Bosb[1]])
                            pg.op("dve", lambda e: e.scalar_tensor_tensor(out=dd[:], in0=osb[1][:], scalar=negl[:, 0:1], in1=osb[0][:], op0=ALU.mult, op1=ALU.add),
                                  reads=[Bosb[0], Bosb[1], Bnegl], writes=[Bdd])
                            pg.op("pool", lambda e: e.tensor_tensor(out=dsq[:], in0=dd[:], in1=dd[:], op=ALU.mult), reads=[Bdd], writes=[Bdsq])
                            pg.op("dve", lambda e: e.tensor_copy(dsqh[:], dsq[:]), reads=[Bdsq], writes=[Bdsqh])
                            pg.op("pool", lambda e: e.tensor_tensor(out=dsq[:], in0=dsq[:], in1=dsqh[:], op=ALU.subtract), reads=[Bdsq, Bdsqh], writes=[Bdsq])
                            pg.op("pool", lambda e: e.tensor_copy(dsqm[:], dsq[:]), reads=[Bdsq], writes=[Bdsqm])

                        def epi_tail1():
                            def mmss(e):
                                e.matmul(psb(7), lhsT=o128, rhs=dsqh[:], start=True, stop=False)
                                return e.matmul(psb(7), lhsT=o128, rhs=dsqm[:], start=False, stop=True)
                            pg.op("pe", mmss, reads=[Bdsqh, Bdsqm, Bc], writes=[PB[7]])
                            pg.op("dve", lambda e: e.tensor_copy(drs[:], psb(7)), reads=[PB[7]], writes=[Bdrs])

                        def epi_tail2(qt, h):
                            pg.op("act", lambda e: e.activation(out=drs[:], in_=drs[:], func=AF.Sqrt, bias=epsc[:], scale=1.0), reads=[Bdrs, Bc], writes=[Bdrs])
                            pg.op("dve", lambda e: e.reciprocal(out=drs[:], in_=drs[:]), reads=[Bdrs], writes=[Bdrs])
                            pg.op("dve", lambda e: e.scalar_tensor_tensor(out=onrm[:], in0=dd[:], scalar=gcols[:, 4:5], in1=drs[:], op0=ALU.mult, op1=ALU.mult),
                                  reads=[Bdd, Bdrs, Bc], writes=[Bonrm])
                            pg.dma("pool", sc["outT"][512 + h * 128:512 + (h + 1) * 128, qt * 512:(qt + 1) * 512], onrm[:], reads=[Bonrm])
                        attn_tile(units, lhs_v, near, bias_of, epi, TT=TTs[h % 2], BTT=BTTs[h % 2])
                        if qt == 0 and h + 1 < 4:
                            tt_sched.extend(tt_steps(h + 1))
                        if tt_sched:
                            tt_sched.pop(0)()
                while pending:
                    pending.pop(0)[1]()
                pg.end()

        TK = 256
        for jb in jobs:
            name, N, nq = jb["name"], jb["N"], jb["nq"]
            sc = S[name]
            with contextlib.ExitStack() as st:
                def sb(nm, shape, dt):
                    return st.enter_context(nc.sbuf_tensor("s3_" + nm + name, list(shape), dt))
                PB = new_ps()
                pg.begin()
                wo = sb("wo", [128, 8, D], BF16)
                Bw = Buf("w3")
                for k in range(8):
                    pg.dma("sp", wo[:, k, :], wb_out[k * 128:(k + 1) * 128, :], writes=[Bw])
                G1 = sb("G1", [128, D], F32)
                A2 = sb("A2", [128, D], F32)
                SH2 = sb("SH2", [128, D], F32)
                Bmodt = Buf("modt")
                for tile_, ri in ((G1, 2), (A2, 3), (SH2, 4)):
                    pg.dma("sp", tile_[:], rows_d[jb["b"], ri:ri + 1, :].broadcast_to([128, D]), writes=[Bmodt])
                xt = [sb("xt%d" % i, [128, 2, D], F32) for i in range(2)]
                Bxt = [Buf("xt%d" % i) for i in range(2)]
                oT = [sb("oT%d" % i, [128, 8, TK], BF16) for i in range(2)]
                BoT = [Buf("oT%d" % i) for i in range(2)]
                junk = sb("junk", [128, D], F32)
                ss = sb("ss", [128, 4], F32)
                rstd = sb("rstd", [128, 4], F32)
                mixs = [sb("mixs%d" % i, [128, D], F32) for i in range(2)]
                tt = [sb("tt%d" % i, [128, D], F32) for i in range(2)]
                hb = [sb("hb%d" % i, [128, D], BF16) for i in range(2)]
                hT = [sb("hT%d" % i, [128, 8, TK], BF16) for i in range(2)]
                Bjunk = Buf("junk")
                Bss = [Buf("ss%d" % i) for i in range(4)]
                Bmixs = [Buf("mixs%d" % i) for i in range(2)]
                Btt = [Buf("tt%d" % i) for i in range(2)]
                Bhb = [Buf("hb%d" % i) for i in range(2)]
                BhT = [Buf("hT%d" % i) for i in range(2)]
                psT = [ps[:, i, :].bitcast(BF16) for i in range(2)]

                def rms_stat(src_ap, src_bufs, col):
                    pg.op("dve", lambda e: e.scalar_tensor_tensor(out=junk[:], in0=src_ap, scalar=1.0 / D, in1=src_ap, op0=ALU.mult, op1=ALU.mult,
                                                                  accum_out=ss[:, col:col + 1]), reads=src_bufs, writes=[Bjunk, Bss[col]])
                    pg.op("dve", lambda e: e.tensor_scalar(out=rstd[:, col:col + 1], in0=ss[:, col:col + 1], scalar1=EPS, scalar2=None, op0=ALU.add),
                          reads=[Bss[col]], writes=[Bss[col]])
                    pg.op("pool", lambda e: e.tensor_tensor(out=rstd[:, col:col + 1], in0=rstd[:, col:col + 1], in1=nhalf[:, 0:1], op=ALU.pow),
                          reads=[Bss[col], Bc], writes=[Bss[col]])

                for t in range(nq // TK):
                    xi = t % 2
                    pg.dma("sp", xt[xi][:], x_in[name][t * TK:(t + 1) * TK, :].rearrange("(s p) d -> p s d", p=128), writes=[Bxt[xi]])
                    pg.dma("sp", oT[xi][:], sc["outT"][:, t * TK:(t + 1) * TK].rearrange("(k p) t -> p k t", p=128), writes=[BoT[xi]])
                    for s in range(2):
                        def mmo(e, s=s, xi=xi):
                            ins = None
                            for hh in range(2):
                                for k in range(8):
                                    ins = e.matmul(psb(2 + 2 * s + hh), lhsT=oT[xi][:, k, s * 128:(s + 1) * 128], rhs=wo[:, k, hh * 512:(hh + 1) * 512],
                                                   start=(k == 0), stop=(k == 7))
                            return ins
                        pg.op("pe", mmo, reads=[BoT[xi], Bw], writes=[PB[2 + 2 * s], PB[3 + 2 * s]])
                        pg.op("act", lambda e, s=s: e.activation(out=mixs[s][:], in_=ps[:, 2 + 2 * s:4 + 2 * s, :].rearrange("p a b -> p (a b)"), func=AF.Copy),
                              reads=[PB[2 + 2 * s], PB[3 + 2 * s]], writes=[Bmixs[s]])
                        rms_stat(mixs[s][:], [Bmixs[s]], s)
                        pg.op("dve", lambda e, s=s: e.scalar_tensor_tensor(out=tt[s][:], in0=mixs[s][:], scalar=rstd[:, s:s + 1], in1=G1[:], op0=ALU.mult, op1=ALU.mult),
                              reads=[Bmixs[s], Bss[s], Bmodt], writes=[Btt[s]])
                        pg.op("pool", lambda e, s=s, xi=xi: e.tensor_tensor(out=xt[xi][:, s, :], in0=xt[xi][:, s, :], in1=tt[s][:], op=ALU.add),
                              reads=[Btt[s], Bxt[xi]], writes=[Bxt[xi]])
                        rms_stat(xt[xi][:, s, :], [Bxt[xi]], 2 + s)
                        pg.op("dve", lambda e, s=s, xi=xi: e.scalar_tensor_tensor(out=tt[s][:], in0=xt[xi][:, s, :], scalar=rstd[:, 2 + s:3 + s], in1=A2[:], op0=ALU.mult, op1=ALU.mult),
                              reads=[Bxt[xi], Bss[2 + s], Bmodt], writes=[Btt[s]])
                        pg.op("pool", lambda e, s=s: e.tensor_tensor(out=hb[s][:], in0=tt[s][:], in1=SH2[:], op=ALU.add), reads=[Btt[s], Bmodt], writes=[Bhb[s]])

                        def tr(e, s=s):
                            ins = None
                            for k in range(8):
                                ins = e.transpose(out=psT[s][:, k * 128:(k + 1) * 128], in_=hb[s][:, k * 128:(k + 1) * 128], identity=ident)
                            return ins
                        pg.op("pe", tr, reads=[Bhb[s], Bc], writes=[PB[s]])
                        pg.op("act", lambda e, s=s, xi=xi: e.activation(out=hT[xi][:, :, s * 128:(s + 1) * 128], in_=psT[s].rearrange("p (k t) -> p k t", k=8), func=AF.Copy),
                              reads=[PB[s]], writes=[BhT[xi]])
                    pg.dma("pool", sc["x1"][t * TK:(t + 1) * TK, :].rearrange("(s p) d -> p s d", p=128), xt[xi][:], reads=[Bxt[xi]])
                    pg.dma("pool", sc["h2T"][:, :, t * TK:(t + 1) * TK].rearrange("k p t -> p k t"), hT[xi][:], reads=[BhT[xi]])
                pg.end()

        for jb in jobs:
            name, N, nq = jb["name"], jb["N"], jb["nq"]
            sc = S[name]
            with contextlib.ExitStack() as st:
                def sb(nm, shape, dt):
                    return st.enter_context(nc.sbuf_tensor("s4_" + nm + name, list(shape), dt))
                PB = new_ps()
                pg.begin()
                wgu = sb("wgu", [128, 8, 2 * DFF], BF16)
                wdn = sb("wdn", [128, 22, D], BF16)
                Bw = Buf("w3")
                for k in range(8):
                    pg.dma("sp", wgu[:, k, :], wb_gu[k * 128:(k + 1) * 128, :], writes=[Bw])
                pg.dma("sp", wdn[:], wb_down.rearrange("(k p) n -> p k n", p=128), writes=[Bw])
                G2 = sb("G2", [128, D], F32)
                Bmodt = Buf("modt")
                pg.dma("sp", G2[:], rows_d[jb["b"], 5:6, :].broadcast_to([128, D]), writes=[Bmodt])
                xt = [sb("xt%d" % i, [128, 2, D], F32) for i in range(2)]
                Bxt = [Buf("xt%d" % i) for i in range(2)]
                hT = [sb("hT%d" % i, [128, 8, TK], BF16) for i in range(2)]
                BhT = [Buf("hT%d" % i) for i in range(2)]
                junk = sb("junk", [128, D], F32)
                ss = sb("ss", [128, 2], F32)
                rstd = sb("rstd", [128, 2], F32)
                act_ = sb("act", [128, 22, TK], BF16)
                sg = [sb("sg%d" % i, [128, TK], F32) for i in range(2)]
                fs = [sb("fs%d" % i, [128, D], F32) for i in range(2)]
                tt = [sb("tt%d" % i, [128, D], F32) for i in range(2)]
                Bjunk = Buf("junk")
                Bss = [Buf("ss%d" % i) for i in range(2)]
                Bfs = [Buf("fs%d" % i) for i in range(2)]
                Btt = [Buf("tt%d" % i) for i in range(2)]
                Bact = [Buf("act%d" % i) for i in range(22)]
                Bsg = [Buf("sg%d" % i) for i in range(2)]
                for t in range(nq // TK):
                    xi = t % 2
                    pg.dma("sp", xt[xi][:], sc["x1"][t * TK:(t + 1) * TK, :].rearrange("(s p) d -> p s d", p=128), writes=[Bxt[xi]])
                    pg.dma("sp", hT[xi][:], sc["h2T"][:, :, t * TK:(t + 1) * TK].rearrange("k p t -> p k t"), writes=[BhT[xi]])
                    for j in range(22):
                        i = j % 2

                        def mmg(e, j=j, i=i, xi=xi):
                            ins = None
                            for k in range(8):
                                ins = e.matmul(ps[:, 2 * i, 0:TK], lhsT=wgu[:, k, j * 128:(j + 1) * 128], rhs=hT[xi][:, k, :], start=(k == 0), stop=(k == 7))
                            for k in range(8):
                                ins = e.matmul(ps[:, 2 * i + 1, 0:TK], lhsT=wgu[:, k, DFF + j * 128:DFF + (j + 1) * 128], rhs=hT[xi][:, k, :], start=(k == 0), stop=(k == 7))
                            return ins
                        pg.op("pe", mmg, reads=[Bw, BhT[xi]], writes=[PB[2 * i], PB[2 * i + 1]])
                        pg.op("act", lambda e, i=i: e.activation(out=sg[i][:], in_=ps[:, 2 * i, 0:TK], func=AF.Silu), reads=[PB[2 * i]], writes=[Bsg[i]])
                        pg.op("dve", lambda e, i=i, j=j: e.tensor_tensor(out=act_[:, j, :], in0=sg[i][:], in1=ps[:, 2 * i + 1, 0:TK], op=ALU.mult),
                              reads=[Bsg[i], PB[2 * i + 1]], writes=[Bact[j]])
                    for s in range(2):
                        def mmd(e, s=s):
                            ins = None
                            for hh in range(2):
                                for j in range(22):
                                    ins = e.matmul(psb(4 + 2 * s + hh), lhsT=act_[:, j, s * 128:(s + 1) * 128], rhs=wdn[:, j, hh * 512:(hh + 1) * 512],
                                                   start=(j == 0), stop=(j == 21))
                            return ins
                        pg.op("pe", mmd, reads=[Bw] + Bact, writes=[PB[4 + 2 * s], PB[5 + 2 * s]])
                        pg.op("act", lambda e, s=s: e.activation(out=fs[s][:], in_=ps[:, 4 + 2 * s:6 + 2 * s, :].rearrange("p a b -> p (a b)"), func=AF.Copy),
                              reads=[PB[4 + 2 * s], PB[5 + 2 * s]], writes=[Bfs[s]])
                        pg.op("dve", lambda e, s=s: e.scalar_tensor_tensor(out=junk[:], in0=fs[s][:], scalar=1.0 / D, in1=fs[s][:], op0=ALU.mult, op1=ALU.mult,
                                                                         accum_out=ss[:, s:s + 1]), reads=[Bfs[s]], writes=[Bjunk, Bss[s]])
                        pg.op("dve", lambda e, s=s: e.tensor_scalar(out=rstd[:, s:s + 1], in0=ss[:, s:s + 1], scalar1=EPS, scalar2=None, op0=ALU.add),
                              reads=[Bss[s]], writes=[Bss[s]])
                        pg.op("pool", lambda e, s=s: e.tensor_tensor(out=rstd[:, s:s + 1], in0=rstd[:, s:s + 1], in1=nhalf[:, 0:1], op=ALU.pow),
                              reads=[Bss[s], Bc], writes=[Bss[s]])
                        pg.op("dve", lambda e, s=s: e.scalar_tensor_tensor(out=tt[s][:], in0=fs[s][:], scalar=rstd[:, s:s + 1], in1=G2[:], op0=ALU.mult, op1=ALU.mult),
                              reads=[Bfs[s], Bss[s], Bmodt], writes=[Btt[s]])
                        pg.op("pool", lambda e, s=s, xi=xi: e.tensor_tensor(out=xt[xi][:, s, :], in0=xt[xi][:, s, :], in1=tt[s][:], op=ALU.add),
                              reads=[Btt[s], Bxt[xi]], writes=[Bxt[xi]])
                    pg.dma("pool", y_out[name][t * TK:(t + 1) * TK, :].rearrange("(s p) d -> p s d", p=128), xt[xi][:], reads=[Bxt[xi]])
                pg.end()
    return nc


def _prep_shared(inp, NP, NS):
    f = lambda a: np.ascontiguousarray(np.asarray(a, dtype=np.float32))
    perm = _perm64()
    w_in = f(inp["w_in"])[0]
    o1, o2, o3, o4, o5 = 512, 640, 768, 1280, 1792
    cols = []
    qa_nat = np.array([[(kv * 4 + g) * 64 + d for kv in range(2) for d in range(64)] for g in range(4)])
    qa_prm = np.array([[(kv * 4 + g) * 64 + perm[d] for kv in range(2) for d in range(64)] for g in range(4)])
    cols += list(qa_nat.reshape(-1)) + list(qa_prm.reshape(-1))
    cols += [o1 + kv * 64 + d for kv in range(2) for d in range(64)]
    cols += [o1 + kv * 64 + perm[d] for kv in range(2) for d in range(64)]
    cols += list(range(o3, o4)) + list(range(o4, o5)) + list(range(o2, o3)) + list(range(o5, 2304))
    cols = np.array(cols)
    assert len(cols) == WIN
    g_q, g_k = f(inp["g_q"])[0], f(inp["g_k"])[0]
    gcols = np.zeros((128, 8), np.float32)
    gcols[:, 0] = np.tile(g_q, 2)
    gcols[:, 1] = np.tile(g_q[perm], 2)
    gcols[:, 2] = np.tile(g_k, 2)
    gcols[:, 3] = np.tile(g_k[perm], 2)
    gcols[:, 4] = f(inp["g_subln"])[0]
    gcols[:, 5] = 1.0 - LAM_INIT
    grow = np.stack([f(inp["g_pre_mix"])[0], f(inp["g_post_mix"])[0], f(inp["g_pre_ffn"])[0], f(inp["g_post_ffn"])[0]])
    grow2 = np.ascontiguousarray(np.stack([grow, grow]))
    lamv = np.stack([f(inp["lam_q1"])[0], f(inp["lam_k1"])[0], f(inp["lam_q2"])[0], f(inp["lam_k2"])[0]])[None]
    b_ada = f(inp["b_ada"])
    cm = np.zeros((128, 5, 128), np.float32)
    cm[:, 0, :] = np.eye(128)
    cm[:, 1, :] = np.eye(128)[::-1]
    cm[:, 2, :] = 1.0
    cm[0:64, 3, 0:64] = 1.0 / 64
    cm[64:128, 3, 64:128] = 1.0 / 64
    cm[:, 4, :] = 1.0 / 128
    m = np.arange(ULEN)
    emain = _onehot(_rel_bucket_np(639 - m))
    sh = dict(
        w_ada=f(inp["w_ada"])[0], b_ada2=np.ascontiguousarray(np.concatenate([b_ada, b_ada], 0)), grow=grow2,
        w_in_p=np.ascontiguousarray(w_in[:, cols]), w_out=f(inp["w_out"])[0], w_gu=f(inp["w_gu"])[0], w_down=f(inp["w_down"])[0],
        gcols=gcols, lamv=np.ascontiguousarray(lamv), relb=f(inp["rel_bias"]),
        cmat=cm.astype(ml_dtypes.bfloat16), emain=emain.astype(ml_dtypes.bfloat16),
    )
    return sh


def _prep_core(inp, sh, c, NP, NS):
    f = lambda a: np.asarray(a, dtype=np.float32)
    pb, pq, sbi, sq = c // 4, c % 4, c // 2, c % 2
    m = dict(sh)
    cp, cs = f(inp["c_prompt"])[pb], f(inp["c_sample"])[sbi]
    cT = np.stack([cp, cs], -1).reshape(8, 128, 2).transpose(1, 0, 2)
    m["cT"] = np.ascontiguousarray(cT)
    for nm, x, N, nq, qi in (("P", f(inp["x_prompt"])[pb], NP, NP // 4, pq), ("S", f(inp["x_sample"])[sbi], NS, NS // 2, sq)):
        qoff = qi * nq
        m["x" + nm] = np.ascontiguousarray(np.roll(x, -qoff, axis=0))
        pos = (np.arange(N) + qoff) % N
        cosT, sinT = _rope_tables(pos)
        m["cos" + nm] = cosT
        m["sin" + nm] = sinT
        NC = N // 128
        mm = np.arange(WLEN)
        if qoff > 0:
            bw = _rel_bucket_np(-1 - mm)
        else:
            bw = np.full(WLEN, NB // 2 + NB // 2 - 1)
        m["ewrap" + nm] = _onehot(bw).astype(ml_dtypes.bfloat16)
        if qoff + nq == N:
            bw2 = np.full(WLEN, NB // 2 - 1)
        else:
            bw2 = _rel_bucket_np(639 - mm)
        m["ewrap2" + nm] = _onehot(bw2).astype(ml_dtypes.bfloat16)
        far = np.zeros(NC + 1, np.int64)
        for ch in range(NC):
            if ch * 128 < nq:
                far[ch] = 31
            else:
                far[ch] = 31 if ch * 128 < N - qoff else 15
        far[NC] = 15
        m["efar" + nm] = _onehot(far).astype(ml_dtypes.bfloat16)
    return m


_CACHE = {}


def run(inputs, NP, NS, debug=False, ncores=8):
    key = (NP, NS, debug)
    if key not in _CACHE:
        _CACHE[key] = build_program(NP, NS, debug)
    nc = _CACHE[key]
    sh = _prep_shared(inputs, NP, NS)
    in_maps = [_prep_core(inputs, sh, c, NP, NS) for c in range(ncores)]
    res = run_bass_kernel_spmd(nc, in_maps, core_ids=list(range(ncores)))
    return res.results


def kernel(**inputs):
    NP = int(np.asarray(inputs["x_prompt"]).shape[1])
    NS = int(np.asarray(inputs["x_sample"]).shape[1])
    r = run(inputs, NP, NS)
    yp = np.zeros((2, NP, D), np.float32)
    ys = np.zeros((4, NS, D), np.float32)
    for c in range(8):
        pb, pq, sbi, sq = c // 4, c % 4, c // 2, c % 2
        nqp, nqs = NP // 4, NS // 2
        yp[pb, pq * nqp:(pq + 1) * nqp] = r[c]["yP"]
        ys[sbi, sq * nqs:(sq + 1) * nqs] = r[c]["yS"]
    return (yp, ys)
```

```python
import contextlib
import math
import numpy as np
import ml_dtypes
import concourse.bass as bass
import concourse.mybir as mybir
from concourse.bass_utils import run_bass_kernel_spmd

F32 = mybir.dt.float32
BF16 = mybir.dt.bfloat16
ALU = mybir.AluOpType
AF = mybir.ActivationFunctionType
AX = mybir.AxisListType

D = 1024
DFF = 2816
HD = 64
EPS = 1e-6
NB = 32
WIN = 2944
LAM_INIT = 0.8 - 0.6 * math.exp(-0.3 * 0)
ULEN = 1279
WLEN = 639

ENGS = ["pe", "act", "dve", "pool", "sp"]
N_DMA_SEMS = 6


class Buf:
    __slots__ = ("name", "w", "r", "excl")

    def __init__(self, name, excl=False):
        self.name = name
        self.excl = excl
        self.w = None
        self.r = []


class Op:
    __slots__ = ("eng", "fn", "waits", "signal", "ev", "is_dma")

    def __init__(self, eng, fn, is_dma):
        self.eng = eng
        self.fn = fn
        self.waits = []
        self.signal = False
        self.ev = None
        self.is_dma = is_dma


class Prog:
    def __init__(self, nc, stack):
        self.nc = nc
        self.sems = {}
        self.cnt = {}
        for e in ENGS:
            self.sems[e] = stack.enter_context(nc.semaphore("s_" + e))
            self.cnt[e] = 0
        self.dma_sems = {}
        for q in ("sp", "act", "pool"):
            lst = []
            for i in range(N_DMA_SEMS):
                nm = "d_%s%d" % (q, i)
                self.sems[nm] = stack.enter_context(nc.semaphore(nm))
                self.cnt[nm] = 0
                lst.append(nm)
            self.dma_sems[q] = lst
        self.dma_rr = {q: 0 for q in self.dma_sems}
        self.waited = {e: {} for e in ENGS}
        self.ops = None
        self.nops = 0

    def begin(self):
        self.ops = {e: [] for e in ENGS}
        self.allops = []
        self.dma_last = {}

    def _dep(self, op, other):
        if other is None or other is op:
            return
        if other.eng == "pe" and op.eng == "pe" and not other.is_dma and not op.is_dma:
            return
        op.waits.append(other)

    def op(self, eng, fn, reads=(), writes=(), dma=False):
        o = Op(eng, fn, dma)
        reads = list(reads)
        writes = list(writes)
        for b in reads:
            if b.excl and b not in writes:
                writes.append(b)
        for b in reads:
            self._dep(o, b.w)
        for b in writes:
            self._dep(o, b.w)
            for r in b.r:
                self._dep(o, r)
        for b in reads:
            b.r.append(o)
        for b in writes:
            b.w = o
            b.r = []
        if dma:
            i = self.dma_rr[eng]
            self.dma_rr[eng] = (i + 1) % N_DMA_SEMS
            nm = self.dma_sems[eng][i]
            prev = self.dma_last.get(nm)
            if prev is not None:
                o.waits.append(prev)
            self.dma_last[nm] = o
            self.cnt[nm] += 16
            o.ev = (nm, self.cnt[nm])
            o.signal = True
        self.ops[eng].append(o)
        self.allops.append(o)
        return o

    def dma(self, q, out, in_, reads=(), writes=()):
        return self.op(q, lambda e: e.dma_start(out=out, in_=in_), reads, writes, dma=True)

    def end(self):
        nc = self.nc
        for o in self.allops:
            for w in o.waits:
                w.signal = True
        for e in ENGS:
            for o in reversed(self.ops[e]):
                if not o.is_dma:
                    o.signal = True
                    break
        for e in ENGS:
            for o in self.ops[e]:
                if not o.is_dma and o.signal:
                    self.cnt[e] += 1
                    o.ev = (e, self.cnt[e])
        final = dict(self.cnt)
        sems = self.sems
        ops = self.ops
        waited_all = self.waited
        self.nops += len(self.allops)

        def emit(ename, eng):
            waited = waited_all[ename]
            for o in ops[ename]:
                need = {}
                for w in o.waits:
                    s, v = w.ev
                    if need.get(s, 0) < v:
                        need[s] = v
                for s, v in need.items():
                    if waited.get(s, 0) >= v:
                        continue
                    waited[s] = v
                    eng.wait_ge(sems[s], v)
                ins = o.fn(eng)
                if o.signal:
                    s, v = o.ev
                    ins.then_inc(sems[s], 16 if o.is_dma else 1)
            for s, v in final.items():
                if v > 0 and waited.get(s, 0) < v:
                    waited[s] = v
                    eng.wait_ge(sems[s], v)

        with nc.Block() as block:
            @block.tensor
            def _(eng):
                emit("pe", eng)

            @block.scalar
            def _(eng):
                emit("act", eng)

            @block.vector
            def _(eng):
                emit("dve", eng)

            @block.gpsimd
            def _(eng):
                emit("pool", eng)

            @block.sync
            def _(eng):
                emit("sp", eng)
        self.ops = None


def _perm64():
    d = np.arange(64)
    return np.where((d % 32) < 16, d + 16, d - 16)


def _rel_bucket_np(rel):
    half = NB // 2
    max_exact = half // 2
    n = np.abs(rel)
    nf = np.maximum(n, max_exact).astype(np.float32)
    large = max_exact + (np.log(nf / np.float32(max_exact)) / np.float32(math.log(128 / max_exact))
                         * (half - max_exact)).astype(np.int32)
    large = np.minimum(large, half - 1)
    return np.where(rel > 0, half, 0) + np.where(n < max_exact, n, large)


def _rope_tables(pos):
    row = (pos // 64).astype(np.float32)
    col = (pos % 64).astype(np.float32)
    half = HD // 2
    inv = (np.float32(10000.0) ** (-np.arange(0, half, 2, dtype=np.float32) / np.float32(half))).astype(np.float32)
    ang_r = row[:, None] * inv[None, :]
    ang_c = col[:, None] * inv[None, :]
    ang = np.concatenate([ang_r, ang_r, ang_c, ang_c], axis=-1).astype(np.float32)
    cos = np.cos(ang).astype(np.float32)
    sin = np.sin(ang).astype(np.float32)
    d = np.arange(64)
    sign = np.where((d % 32) < 16, -1.0, 1.0).astype(np.float32)
    sin_s = sin * sign[None, :]
    cosT = np.ascontiguousarray(np.concatenate([cos.T, cos.T], axis=0))
    sinT = np.ascontiguousarray(np.concatenate([sin_s.T, sin_s.T], axis=0))
    return cosT, sinT


def _onehot(buckets):
    e = np.zeros((NB, len(buckets)), dtype=np.float32)
    e[buckets, np.arange(len(buckets))] = 1.0
    return e


def build_program(NP, NS, debug=False):
    jobs = [dict(name="P", N=NP, nq=NP // 4, b=0), dict(name="S", N=NS, nq=NS // 2, b=1)]
    for jb in jobs:
        assert jb["nq"] % 512 == 0 and jb["N"] % 512 == 0
        jb["NC"] = jb["N"] // 128
    nc = bass.Bass("TRN2", target_bir_lowering=False)

    def din(name, shape, dt=F32):
        return nc.dram_tensor(name, list(shape), dt, kind="ExternalInput").ap()

    def dscr(name, shape, dt):
        if debug and not name.startswith("wb_"):
            return nc.dram_tensor(name, list(shape), dt, kind="ExternalOutput").ap()
        return nc.dram_tensor(name, list(shape), dt).ap()

    x_in = {"P": din("xP", [NP, D]), "S": din("xS", [NS, D])}
    cT_d = din("cT", [128, 8, 2])
    w_ada_d = din("w_ada", [D, 6 * D])
    b_ada2_d = din("b_ada2", [2, 6 * D])
    grow_d = din("grow", [2, 4, D])
    w_in_d = din("w_in_p", [D, WIN])
    w_out_d = din("w_out", [D, D])
    w_gu_d = din("w_gu", [D, 2 * DFF])
    w_down_d = din("w_down", [DFF, D])
    gcols_d = din("gcols", [128, 8])
    lamv_d = din("lamv", [1, 4, 64])
    relb_d = din("relb", [NB, 4])
    rope_d = {jb["name"]: (din("cos" + jb["name"], [128, jb["N"]]), din("sin" + jb["name"], [128, jb["N"]])) for jb in jobs}
    cmat_d = din("cmat", [128, 5, 128], BF16)
    emain_d = din("emain", [NB, ULEN], BF16)
    ewrap_d = {jb["name"]: din("ewrap" + jb["name"], [NB, WLEN], BF16) for jb in jobs}
    ewrap2_d = {jb["name"]: din("ewrap2" + jb["name"], [NB, WLEN], BF16) for jb in jobs}
    efar_d = {jb["name"]: din("efar" + jb["name"], [NB, jb["NC"] + 1], BF16) for jb in jobs}
    y_out = {jb["name"]: nc.dram_tensor("y" + jb["name"], [jb["nq"], D], F32, kind="ExternalOutput").ap() for jb in jobs}

    wb_in = dscr("wb_in", [D, WIN], BF16)
    wb_out = dscr("wb_out", [D, D], BF16)
    wb_gu = dscr("wb_gu", [D, 2 * DFF], BF16)
    wb_down = dscr("wb_down", [DFF, D], BF16)
    rows_d = dscr("rows_d", [2, 6, D], F32)
    lam_d = dscr("lam_d", [1, 2], F32)
    ub_d = dscr("ub_d", [3, 4, ULEN], BF16)
    uw_d = {jb["name"]: dscr("uw_d" + jb["name"], [3, 4, WLEN], BF16) for jb in jobs}
    uw2_d = {jb["name"]: dscr("uw2_d" + jb["name"], [3, 4, WLEN], BF16) for jb in jobs}
    ufar_d = {jb["name"]: dscr("ufar_d" + jb["name"], [1, 4 * (jb["NC"] + 1)], F32) for jb in jobs}
    S = {}
    for jb in jobs:
        n, N, nq = jb["name"], jb["N"], jb["nq"]
        S[n] = dict(
            KTa=dscr("KTa" + n, [128, N], BF16), Va=dscr("Va" + n, [N, 130], BF16),
            KTd=dscr("KTd" + n, [4, 128, N], BF16), Vd=dscr("Vd" + n, [N, 512], BF16),
            QTa=dscr("QTa" + n, [4, 128, nq], BF16), QTd=dscr("QTd" + n, [4, 128, nq], BF16),
            outT=dscr("outT" + n, [D, nq], BF16), zrow=dscr("zrow" + n, [2, 2, 512], F32),
            x1=dscr("x1" + n, [nq, D], F32), h2T=dscr("h2T" + n, [8, 128, nq], BF16),
        )

    with contextlib.ExitStack() as gst:
        pg = Prog(nc, gst)
        def gsb(name, shape, dt):
            return gst.enter_context(nc.sbuf_tensor("g_" + name, list(shape), dt))

        cmat = gsb("cmat", [128, 5, 128], BF16)
        gcols = gsb("gcols", [128, 8], F32)
        negl = gsb("negl", [128, 2], F32)
        nhalf = gsb("nhalf", [128, 512], F32)
        zero1 = gsb("zero1", [128, 1], F32)
        epsc = gsb("epsc", [128, 1], F32)
        farb = {jb["name"]: gsb("farb" + jb["name"], [128, 4, jb["NC"] + 1], F32) for jb in jobs}
        ps = gst.enter_context(nc.psum_tensor("psum_all", [128, 8, 512], F32))
        ident = cmat[:, 0, :]
        antiid = cmat[:, 1, :]
        onesb = cmat[:, 2, :]
        blk64 = cmat[:, 3, :]
        o128 = cmat[:, 4, :]

        def psb(i):
            return ps[:, i, :]

        def new_ps():
            return [Buf("ps%d" % i, excl=True) for i in range(8)]

        with contextlib.ExitStack() as st:
            def sb(name, shape, dt):
                return st.enter_context(nc.sbuf_tensor("s0_" + name, list(shape), dt))

            PB = new_ps()
            pg.begin()
            Bc = Buf("consts")
            pg.dma("sp", cmat[:], cmat_d, writes=[Bc])
            pg.dma("sp", gcols[:], gcols_d, writes=[Bc])
            pg.op("pool", lambda e: e.memset(nhalf[:], -0.5), writes=[Bc])
            pg.op("pool", lambda e: e.memset(zero1[:], 0.0), writes=[Bc])
            pg.op("pool", lambda e: e.memset(epsc[:], EPS), writes=[Bc])
            pg.op("dve", lambda e: e.tensor_scalar(out=gcols[:, 4:5], in0=gcols[:, 4:5], scalar1=1.0 - LAM_INIT, scalar2=None, op0=ALU.mult),
                  reads=[Bc], writes=[Bc])

            cT = sb("cT", [128, 8, 2], F32)
            scT = sb("scT", [128, 8, 2], F32)
            scTb = sb("scTb", [128, 8, 2], BF16)
            BcT, BscT = Buf("cT"), Buf("scT")
            pg.dma("sp", cT[:], cT_d, writes=[BcT])
            pg.op("act", lambda e: e.activation(out=scT[:], in_=cT[:], func=AF.Silu), reads=[BcT], writes=[BscT])
            pg.op("dve", lambda e: e.tensor_copy(scTb[:], scT[:]), reads=[BscT], writes=[BscT])

            mod = sb("mod", [2, 6 * D], F32)
            bada = sb("bada", [2, 6 * D], F32)
            Bmod, Bbada = Buf("mod"), Buf("bada")
            pg.dma("sp", bada[:], b_ada2_d, writes=[Bbada])
            waf = [sb("waf%d" % i, [128, 8, 512], F32) for i in range(2)]
            wab = [sb("wab%d" % i, [128, 8, 512], BF16) for i in range(2)]
            Bwaf = [Buf("waf%d" % i) for i in range(2)]
            Bwab = [Buf("wab%d" % i) for i in range(2)]
            for n in range(12):
                i = n % 2
                pg.dma("sp", waf[i][:], w_ada_d[:, n * 512:(n + 1) * 512].rearrange("(k p) n -> p k n", p=128), writes=[Bwaf[i]])
                ce = "dve" if n % 2 == 0 else "pool"
                pg.op(ce, lambda e, i=i: e.tensor_copy(wab[i][:], waf[i][:]), reads=[Bwaf[i]], writes=[Bwab[i]])

                def mm(e, i=i):
                    ins = None
                    for k in range(8):
                        ins = e.matmul(ps[0:2, i, :], lhsT=scTb[:, k, :], rhs=wab[i][:, k, :], start=(k == 0), stop=(k == 7))
                    return ins
                pg.op("pe", mm, reads=[Bwab[i], BscT], writes=[PB[i]])
                pg.op("dve", lambda e, i=i, n=n: e.tensor_tensor(out=mod[:, n * 512:(n + 1) * 512], in0=ps[0:2, i, :],
                                                               in1=bada[:, n * 512:(n + 1) * 512], op=ALU.add),
                      reads=[PB[i], Bbada], writes=[Bmod])
            grow = sb("grow", [2, 4, D], F32)
            rows = sb("rows", [2, 6, D], F32)
            Bgrow, Brows = Buf("grow"), Buf("rows")
            pg.dma("sp", grow[:], grow_d, writes=[Bgrow])
            pg.op("dve", lambda e: e.scalar_tensor_tensor(out=rows[:, 0, :], in0=mod[:, 1024:2048], scalar=1.0, in1=grow[:, 0, :],
                                                          op0=ALU.add, op1=ALU.mult), reads=[Bmod, Bgrow], writes=[Brows])
            pg.op("dve", lambda e: e.tensor_copy(rows[:, 1, :], mod[:, 0:1024]), reads=[Bmod], writes=[Brows])
            pg.op("dve", lambda e: e.tensor_tensor(out=rows[:, 2, :], in0=mod[:, 2048:3072], in1=grow[:, 1, :], op=ALU.mult),
                  reads=[Bmod, Bgrow], writes=[Brows])
            pg.op("dve", lambda e: e.scalar_tensor_tensor(out=rows[:, 3, :], in0=mod[:, 4096:5120], scalar=1.0, in1=grow[:, 2, :],
                                                          op0=ALU.add, op1=ALU.mult), reads=[Bmod, Bgrow], writes=[Brows])
            pg.op("dve", lambda e: e.tensor_copy(rows[:, 4, :], mod[:, 3072:4096]), reads=[Bmod], writes=[Brows])
            pg.op("dve", lambda e: e.tensor_tensor(out=rows[:, 5, :], in0=mod[:, 5120:6144], in1=grow[:, 3, :], op=ALU.mult),
                  reads=[Bmod, Bgrow], writes=[Brows])
            pg.dma("sp", rows_d, rows[:], reads=[Brows])

            lamv = sb("lamv", [1, 4, 64], F32)
            lj = sb("lj", [1, 64], F32)
            ls = sb("ls", [1, 4], F32)
            Blam = Buf("lam")
            pg.dma("sp", lamv[:], lamv_d, writes=[Blam])
            for t in range(2):
                pg.op("dve", lambda e, t=t: e.scalar_tensor_tensor(out=lj[:], in0=lamv[:, 2 * t, :], scalar=1.0, in1=lamv[:, 2 * t + 1, :],
                                                                 op0=ALU.mult, op1=ALU.mult, accum_out=ls[:, t:t + 1]),
                      reads=[Blam], writes=[Blam])
            pg.op("act", lambda e: e.activation(out=ls[:, 2:4], in_=ls[:, 0:2], func=AF.Exp), reads=[Blam], writes=[Blam])
            pg.op("dve", lambda e: e.scalar_tensor_tensor(out=ls[:, 1:2], in0=ls[:, 2:3], scalar=LAM_INIT, in1=ls[:, 3:4],
                                                          op0=ALU.add, op1=ALU.subtract), reads=[Blam], writes=[Blam])
            pg.op("dve", lambda e: e.tensor_scalar(out=ls[:, 0:1], in0=ls[:, 1:2], scalar1=-1.0, scalar2=None, op0=ALU.mult),
                  reads=[Blam], writes=[Blam])
            pg.dma("sp", lam_d, ls[:, 0:2], reads=[Blam])
            Bnegl = Buf("negl")
            pg.dma("sp", negl[:], lam_d.broadcast_to([128, 2]), reads=[Blam], writes=[Bnegl])

            rb = sb("rb", [NB, 4], F32)
            rbr = sb("rbr", [NB, 4], F32)
            rbp = [sb("rbp%d" % i, [NB, 4], BF16) for i in range(3)]
            Brb = Buf("rb")
            pg.dma("sp", rb[:], relb_d, writes=[Brb])
            pg.op("dve", lambda e: e.tensor_copy(rbp[0][:], rb[:]), reads=[Brb], writes=[Brb])
            pg.op("dve", lambda e: e.tensor_tensor(out=rbr[:], in0=rb[:], in1=rbp[0][:], op=ALU.subtract), reads=[Brb], writes=[Brb])
            pg.op("dve", lambda e: e.tensor_copy(rbp[1][:], rbr[:]), reads=[Brb], writes=[Brb])
            pg.op("dve", lambda e: e.tensor_tensor(out=rbr[:], in0=rbr[:], in1=rbp[1][:], op=ALU.subtract), reads=[Brb], writes=[Brb])
            pg.op("dve", lambda e: e.tensor_copy(rbp[2][:], rbr[:]), reads=[Brb], writes=[Brb])
            emat = sb("emat", [NB, ULEN], BF16)
            uout = sb("uout", [4, 3, ULEN], BF16)
            ufo = sb("ufo", [4, 132], F32)
            Bem, Buo = Buf("emat"), Buf("uout")
            specs = [("main", emain_d, ULEN, ub_d)]
            for jb in jobs:
                specs.append(("wrap", ewrap_d[jb["name"]], WLEN, uw_d[jb["name"]]))
                specs.append(("wrap", ewrap2_d[jb["name"]], WLEN, uw2_d[jb["name"]]))
            for jb in jobs:
                specs.append(("far", efar_d[jb["name"]], jb["NC"] + 1, ufar_d[jb["name"]]))
            pbank = 2
            for kind, esrc, L, dst in specs:
                pg.dma("sp", emat[:, 0:L], esrc, writes=[Bem])
                if kind != "far":
                    for part in range(3):
                        for s0 in range(0, L, 512):
                            w = min(512, L - s0)
                            bk = 2 + (pbank % 2)
                            pbank += 1
                            pg.op("pe", lambda e, bk=bk, part=part, s0=s0, w=w: e.matmul(ps[0:4, bk, 0:w], lhsT=rbp[part][:, :],
                                                                                       rhs=emat[:, s0:s0 + w], start=True, stop=True),
                                  reads=[Bem, Brb], writes=[PB[bk]])
                            pg.op("dve", lambda e, bk=bk, part=part, s0=s0, w=w: e.tensor_copy(uout[:, part, s0:s0 + w], ps[0:4, bk, 0:w]),
                                  reads=[PB[bk]], writes=[Buo])
                    pg.dma("sp", dst.rearrange("t h l -> h t l"), uout[:, :, 0:L], reads=[Buo])
                else:
                    bk = 2 + (pbank % 2)
                    pbank += 1

                    def mmf(e, bk=bk, L=L):
                        ins = None
                        for part in range(3):
                            ins = e.matmul(ps[0:4, bk, 0:L], lhsT=rbp[part][:, :], rhs=emat[:, 0:L], start=(part == 0), stop=(part == 2))
                        return ins
                    pg.op("pe", mmf, reads=[Bem, Brb], writes=[PB[bk]])
                    pg.op("dve", lambda e, bk=bk, L=L: e.tensor_copy(ufo[:, 0:L], ps[0:4, bk, 0:L]), reads=[PB[bk]], writes=[Buo])
                    pg.dma("sp", dst.rearrange("o (h l) -> (o h) l", h=4), ufo[:, 0:L], reads=[Buo])
            for jb in jobs:
                n = jb["name"]
                pg.dma("sp", farb[n][:].rearrange("p h l -> p (h l)"), ufar_d[n].broadcast_to([128, 4 * (jb["NC"] + 1)]),
                       reads=[Buo], writes=[Bc])

            pieces = []
            for k in range(8):
                pieces.append((w_in_d[k * 128:(k + 1) * 128, :], wb_in[k * 128:(k + 1) * 128, :], WIN))
            NBUF = 2
            wcf = [sb("wcf%d" % i, [128, WIN], F32) for i in range(NBUF)]
            wcb = [sb("wcb%d" % i, [128, WIN], BF16) for i in range(NBUF)]
            Bwcf = [Buf("wcf%d" % i) for i in range(NBUF)]
            Bwcb = [Buf("wcb%d" % i) for i in range(NBUF)]
            for n, (src, dst, w) in enumerate(pieces):
                i = n % NBUF
                pg.dma("sp", wcf[i][:, 0:w], src, writes=[Bwcf[i]])
                ce = ["dve", "pool", "act"][n % 3]
                if ce == "act":
                    pg.op(ce, lambda e, i=i, w=w: e.activation(out=wcb[i][:, 0:w], in_=wcf[i][:, 0:w], func=AF.Copy), reads=[Bwcf[i]], writes=[Bwcb[i]])
                else:
                    pg.op(ce, lambda e, i=i, w=w: e.tensor_copy(wcb[i][:, 0:w], wcf[i][:, 0:w]), reads=[Bwcf[i]], writes=[Bwcb[i]])
                pg.dma("pool", dst, wcb[i][:, 0:w], reads=[Bwcb[i]])
            pg.end()

        for jb in jobs:
            name, N, nq = jb["name"], jb["N"], jb["nq"]
            sc = S[name]
            cos_d, sin_d = rope_d[name]
            with contextlib.ExitStack() as st:
                def sb(nm, shape, dt):
                    return st.enter_context(nc.sbuf_tensor("s1_" + nm + name, list(shape), dt))
                PB = new_ps()
                pg.begin()
                win = sb("win", [128, 8, WIN], BF16)
                Bwin = Buf("win")
                for k in range(8):
                    pg.dma("sp", win[:, k, :], wb_in[k * 128:(k + 1) * 128, :], writes=[Bwin])
                A1 = sb("A1", [128, D], F32)
                SH1 = sb("SH1", [128, D], F32)
                Bmodt = Buf("modt")
                pg.dma("sp", A1[:], rows_d[jb["b"], 0:1, :].broadcast_to([128, D]), writes=[Bmodt])
                pg.dma("sp", SH1[:], rows_d[jb["b"], 1:2, :].broadcast_to([128, D]), writes=[Bmodt])
                NXB = 2
                xt = [sb("xt%d" % i, [128, 4, D], F32) for i in range(NXB)]
                Bxt = [Buf("xt%d" % i) for i in range(NXB)]
                rt = [(sb("cos%d" % i, [128, 512], F32), sb("sin%d" % i, [128, 512], F32)) for i in range(2)]
                Brt = [Buf("rt%d" % i) for i in range(2)]
                junk = sb("junk", [128, D], F32)
                tt = [sb("tt%d" % i, [128, D], F32) for i in range(2)]
                hTs = [sb("hT%d" % i, [128, 8, 512], BF16) for i in range(2)]
                Bjunk, Bss, Brstd = Buf("junk"), Buf("ss"), Buf("rstd")
                Btt = [Buf("tt%d" % i) for i in range(2)]
                Bhb = [Buf("hb%d" % i) for i in range(2)]
                BhTs = [[Buf("hT%d_%d" % (j, i)) for i in range(4)] for j in range(2)]
                asb = sb("asb", [128, 512], F32)
                sq = sb("sq", [128, 512], F32)
                sqh = sb("sqh", [128, 512], BF16)
                sqm = sb("sqm", [128, 512], BF16)
                rs = sb("rs", [128, 512], F32)
                t1 = sb("t1", [128, 512], F32)
                t2 = sb("t2", [128, 512], F32)
                Basb, Bsq, Bsqh, Bsqm, Brs, Bt1, Bt2 = [Buf(x) for x in ["asb", "sq", "sqh", "sqm", "rs", "t1", "t2"]]
                kta = [sb("kta%d" % i, [128, 512], BF16) for i in range(2)]
                ktd = [sb("ktd%d" % i, [128, 4, 512], BF16) for i in range(2)]
                qta = [sb("qta%d" % i, [128, 4, 512], BF16) for i in range(2)]
                qtd = [sb("qtd%d" % i, [128, 4, 512], BF16) for i in range(2)]
                va = [sb("va%d" % i, [128, 4, 2, 65], BF16) for i in range(2)]
                vd = [sb("vd%d" % i, [128, 4, 512], BF16) for i in range(2)]
                Bkta = [Buf("kta%d" % i) for i in range(2)]
                Bktd = [Buf("ktd%d" % i) for i in range(2)]
                Bqta = [Buf("qta%d" % i) for i in range(2)]
                Bqtd = [Buf("qtd%d" % i) for i in range(2)]
                Bva = [Buf("va%d" % i) for i in range(2)]
                Bvd = [Buf("vd%d" % i) for i in range(2)]
                for i in range(2):
                    pg.op("pool", lambda e, i=i: e.memset(va[i][:], 1.0), writes=[Bva[i]])
                psT = [ps[:, i, :].bitcast(BF16) for i in range(2)]

                nr_cnt = [0]
                sqh2 = [sqh, sb("sqh_b", [128, 512], BF16)]
                sqm2 = [sqm, sb("sqm_b", [128, 512], BF16)]
                t12 = [t1, sb("t1_b", [128, 512], F32)]
                Bsqh2 = [Bsqh, Buf("sqh_b")]
                Bsqm2 = [Bsqm, Buf("sqm_b")]
                Bt12 = [Bt1, Buf("t1_b")]

                def normrope(pa, pb, gi, gpi, outap, scale, ri, outbuf):
                    cs, sn = rt[ri]
                    j = nr_cnt[0] % 2
                    nr_cnt[0] += 1
                    sqh_, sqm_, t1_ = sqh2[j], sqm2[j], t12[j]
                    Bsqh_, Bsqm_, Bt1_ = Bsqh2[j], Bsqm2[j], Bt12[j]
                    pg.op("act", lambda e: e.activation(out=asb[:], in_=psb(pa), func=AF.Copy), reads=[PB[pa]], writes=[Basb])
                    pg.op("dve", lambda e: e.scalar_tensor_tensor(out=t2[:], in0=psb(pb), scalar=gcols[:, gpi:gpi + 1], in1=sn[:], op0=ALU.mult, op1=ALU.mult),
                          reads=[PB[pb], Bc, Brt[ri]], writes=[Bt2])
                    pg.op("dve", lambda e: e.tensor_tensor(out=sq[:], in0=asb[:], in1=asb[:], op=ALU.mult), reads=[Basb], writes=[Bsq])
                    pg.op("dve", lambda e: e.tensor_copy(sqh_[:], sq[:]), reads=[Bsq], writes=[Bsqh_])
                    pg.op("pool", lambda e: e.tensor_tensor(out=sq[:], in0=sq[:], in1=sqh_[:], op=ALU.subtract), reads=[Bsq, Bsqh_], writes=[Bsq])
                    pg.op("pool", lambda e: e.tensor_copy(sqm_[:], sq[:]), reads=[Bsq], writes=[Bsqm_])
                    pg.op("dve", lambda e: e.scalar_tensor_tensor(out=t1_[:], in0=asb[:], scalar=gcols[:, gi:gi + 1], in1=cs[:], op0=ALU.mult, op1=ALU.mult),
                          reads=[Basb, Bc, Brt[ri]], writes=[Bt1_])
                    pg.op("pool", lambda e: e.tensor_tensor(out=t1_[:], in0=t1_[:], in1=t2[:], op=ALU.add), reads=[Bt1_, Bt2], writes=[Bt1_])

                    def part2():
                        def mmss(e):
                            e.matmul(psb(7), lhsT=blk64, rhs=sqh_[:], start=True, stop=False)
                            return e.matmul(psb(7), lhsT=blk64, rhs=sqm_[:], start=False, stop=True)
                        pg.op("pe", mmss, reads=[Bsqh_, Bsqm_, Bc], writes=[PB[7]])
                        pg.op("act", lambda e: e.activation(out=rs[:], in_=psb(7), func=AF.Sqrt, bias=epsc[:], scale=1.0), reads=[PB[7], Bc], writes=[Brs])
                        pg.op("dve", lambda e: e.reciprocal(out=rs[:], in_=rs[:]), reads=[Brs], writes=[Brs])
                        pg.op("dve", lambda e: e.scalar_tensor_tensor(out=outap, in0=t1_[:], scalar=scale, in1=rs[:], op0=ALU.mult, op1=ALU.mult),
                              reads=[Bt1_, Brs], writes=[outbuf])
                    return part2

                def proj(bank, ch, hT, BhT):
                    def f(e):
                        ins = None
                        for k in range(8):
                            ins = e.matmul(psb(bank), lhsT=win[:, k, ch * 128:(ch + 1) * 128], rhs=hT[:, k, :], start=(k == 0), stop=(k == 7))
                        return ins
                    pg.op("pe", f, reads=[Bwin] + BhT, writes=[PB[bank]])

                ntiles = N // 512
                hb4 = [[sb("hbq%d_%d" % (j, i), [128, D], BF16) for i in range(4)] for j in range(1)][0]
                Bhb4 = [Buf("hbq%d" % i) for i in range(4)]
                ssq = [sb("ssq%d" % i, [128, 4], F32) for i in range(2)]
                rsq = [sb("rsq%d" % i, [128, 4], F32) for i in range(2)]
                Bssq = [[Buf("ssq%d_%d" % (j, i)) for i in range(4)] for j in range(2)]

                def xload(t):
                    xi = t % NXB
                    ti = t % 2
                    pg.dma("sp", xt[xi][:], x_in[name][t * 512:(t + 1) * 512, :].rearrange("(s p) d -> p s d", p=128), writes=[Bxt[xi]])
                    pg.dma("sp", rt[ti][0][:], cos_d[:, t * 512:(t + 1) * 512], writes=[Brt[ti]])
                    pg.dma("sp", rt[ti][1][:], sin_d[:, t * 512:(t + 1) * 512], writes=[Brt[ti]])

                def prep_sub(t, s):
                    xi = t % NXB
                    ti = t % 2
                    i = s % 2
                    sq_, rq_, B_ = ssq[ti], rsq[ti], Bssq[ti][s]
                    pg.op("dve", lambda e: e.scalar_tensor_tensor(out=junk[:], in0=xt[xi][:, s, :], scalar=1.0 / D, in1=xt[xi][:, s, :],
                                                                  op0=ALU.mult, op1=ALU.mult, accum_out=sq_[:, s:s + 1]),
                          reads=[Bxt[xi]], writes=[Bjunk, B_])
                    pg.op("dve", lambda e: e.tensor_scalar(out=rq_[:, s:s + 1], in0=sq_[:, s:s + 1], scalar1=EPS, scalar2=None, op0=ALU.add), reads=[B_], writes=[B_])
                    pg.op("pool", lambda e: e.tensor_tensor(out=rq_[:, s:s + 1], in0=rq_[:, s:s + 1], in1=nhalf[:, 0:1], op=ALU.pow), reads=[B_, Bc], writes=[B_])
                    pg.op("dve", lambda e: e.scalar_tensor_tensor(out=tt[i][:], in0=xt[xi][:, s, :], scalar=rq_[:, s:s + 1], in1=A1[:],
                                                                  op0=ALU.mult, op1=ALU.mult),
                          reads=[Bxt[xi], B_, Bmodt], writes=[Btt[i]])
                    pg.op("pool", lambda e: e.tensor_tensor(out=hb4[s][:], in0=tt[i][:], in1=SH1[:], op=ALU.add),
                          reads=[Btt[i], Bmodt], writes=[Bhb4[s]])

                def trans(t, s):
                    ti = t % 2
                    i = s % 2
                    hT, BhT = hTs[ti], BhTs[ti]

                    def tr(e):
                        ins = None
                        for k in range(8):
                            ins = e.transpose(out=psT[i][:, k * 128:(k + 1) * 128], in_=hb4[s][:, k * 128:(k + 1) * 128], identity=ident)
                        return ins
                    pg.op("pe", tr, reads=[Bhb4[s], Bc], writes=[PB[i]])
                    pg.op("act", lambda e: e.activation(out=hT[:, :, s * 128:(s + 1) * 128],
                                                        in_=psT[i].rearrange("p (k t) -> p k t", k=8), func=AF.Copy),
                          reads=[PB[i]], writes=[BhT[s]])

                def vpair(t, s):
                    ti = t % 2
                    hT, BhT = hTs[ti], BhTs[ti]

                    def mmv(e):
                        ins = None
                        for k in range(8):
                            ins = e.matmul(psb(6), lhsT=hT[:, k, s * 128:(s + 1) * 128], rhs=win[:, k, 2432:2944], start=(k == 0), stop=(k == 7))
                        return ins

                    def mmva(e):
                        ins = None
                        for k in range(8):
                            ins = e.matmul(ps[:, 5, 0:128], lhsT=hT[:, k, s * 128:(s + 1) * 128], rhs=win[:, k, 2304:2432], start=(k == 0), stop=(k == 7))
                        return ins
                    pg.op("pe", mmv, reads=[Bwin, BhT[s]], writes=[PB[6]])
                    pg.op("act", lambda e: e.activation(out=vd[ti][:, s, :], in_=psb(6), func=AF.Copy), reads=[PB[6]], writes=[Bvd[ti]])
                    pg.op("pe", mmva, reads=[Bwin, BhT[s]], writes=[PB[5]])
                    pg.op("dve", lambda e: e.tensor_copy(va[ti][:, s, :, 0:64], ps[:, 5, 0:128].rearrange("p (a b) -> p a b", a=2)),
                          reads=[PB[5]], writes=[Bva[ti]])

                def kd_or_qd(t, h, ch0, dst, Bdst, scale):
                    ti = t % 2
                    hT, BhT = hTs[ti], BhTs[ti]
                    bk = 4 + (h % 2)
                    proj(bk, ch0 + h, hT, BhT)
                    if h % 2 == 0:
                        pg.op("act", lambda e: e.activation(out=dst[ti][:, h, :], in_=psb(bk), func=AF.Copy, scale=scale), reads=[PB[bk]], writes=[Bdst[ti]])
                    else:
                        pg.op("dve", lambda e: e.tensor_scalar(out=dst[ti][:, h, :], in0=psb(bk), scalar1=scale, scalar2=None, op0=ALU.mult),
                              reads=[PB[bk]], writes=[Bdst[ti]])

                xload(0)
                for s in range(4):
                    prep_sub(0, s)
                for s in range(4):
                    trans(0, s)
                for t in range(ntiles):
                    own = (t * 512 < nq)
                    ti = t % 2
                    hT, BhT = hTs[ti], BhTs[ti]
                    nxt = t + 1 < ntiles
                    if nxt:
                        xload(t + 1)
                    proj(2, 8, hT, BhT)
                    proj(3, 9, hT, BhT)
                    kpart2 = normrope(2, 3, 2, 3, kta[ti][:], 1.0, ti, Bkta[ti])
                    if nxt:
                        prep_sub(t + 1, 0)
                    for h in range(4):
                        kd_or_qd(t, h, 14, ktd, Bktd, 1.0)
                        if h == 1:
                            kpart2()
                            pg.dma("pool", sc["KTa"][:, t * 512:(t + 1) * 512], kta[ti][:], reads=[Bkta[ti]])
                    pg.dma("pool", sc["KTd"][:, :, t * 512:(t + 1) * 512].rearrange("h p t -> p h t"), ktd[ti][:], reads=[Bktd[ti]])
                    if nxt:
                        prep_sub(t + 1, 1)
                    vpair(t, 0)
                    vpair(t, 1)
                    if nxt:
                        prep_sub(t + 1, 2)
                    vpair(t, 2)
                    vpair(t, 3)
                    pg.dma("pool", sc["Vd"][t * 512:(t + 1) * 512, :].rearrange("(s p) c -> p s c", p=128), vd[ti][:], reads=[Bvd[ti]])
                    pg.dma("pool", sc["Va"][t * 512:(t + 1) * 512, :].rearrange("(s p) c -> p s c", p=128),
                           va[ti][:].rearrange("p s a b -> p s (a b)"), reads=[Bva[ti]])
                    if nxt:
                        prep_sub(t + 1, 3)
                    if own:
                        prev2 = None
                        for g in range(4):
                            proj(2, g, hT, BhT)
                            proj(3, 4 + g, hT, BhT)
                            p2 = normrope(2, 3, 0, 1, qta[ti][:, g, :], 0.125, ti, Bqta[ti])
                            if prev2 is not None:
                                prev2()
                            prev2 = p2
                        for h in range(4):
                            kd_or_qd(t, h, 10, qtd, Bqtd, 0.125)
                            if h == 1:
                                prev2()
                                pg.dma("pool", sc["QTa"][:, :, t * 512:(t + 1) * 512].rearrange("g p t -> p g t"), qta[ti][:], reads=[Bqta[ti]])
                        pg.dma("pool", sc["QTd"][:, :, t * 512:(t + 1) * 512].rearrange("h p t -> p h t"), qtd[ti][:], reads=[Bqtd[ti]])
                    if nxt:
                        for s in range(4):
                            trans(t + 1, s)
                pg.end()

        late_pieces = []
        for k in range(8):
            late_pieces.append((w_out_d[k * 128:(k + 1) * 128, :], wb_out[k * 128:(k + 1) * 128, :], D))
        for k in range(8):
            for c0_ in range(0, 2 * DFF, 1024):
                w_ = min(1024, 2 * DFF - c0_)
                late_pieces.append((w_gu_d[k * 128:(k + 1) * 128, c0_:c0_ + w_], wb_gu[k * 128:(k + 1) * 128, c0_:c0_ + w_], w_))
        for k in range(22):
            late_pieces.append((w_down_d[k * 128:(k + 1) * 128, :], wb_down[k * 128:(k + 1) * 128, :], D))

        for jb in jobs:
            name, N, nq, NC = jb["name"], jb["N"], jb["nq"], jb["NC"]
            sc = S[name]
            with contextlib.ExitStack() as st:
                def sb(nm, shape, dt):
                    return st.enter_context(nc.sbuf_tensor("s2_" + nm + name, list(shape), dt))
                PB = new_ps()
                pg.begin()
                KT = sb("KT", [128, N], BF16)
                VV = sb("VV", [128, NC, 130], BF16)
                QT = sb("QT", [128, 4, nq], BF16)
                NG = 8
                cpg = NC // NG
                BKV = [Buf("kv%d" % i) for i in range(NG)]
                BQ = Buf("QT")
                pT = [sb("pT%d" % i, [128, 1024], BF16) for i in range(3)]
                BpT = [Buf("pT%d" % i) for i in range(3)]
                osb = [sb("osb%d" % i, [128, 512], F32) for i in range(2)]
                Bosb = [Buf("osb%d" % i) for i in range(2)]
                zr = sb("zr", [128, 2, 512], F32)
                Bzr = Buf("zr")
                bcz = sb("bcz", [128, 2, 512], F32)
                Bbcz = Buf("bcz")
                onrm = sb("onrm", [128, 512], BF16)
                Bonrm = Buf("onrm")
                dd = sb("dd", [128, 512], F32)
                dsq = sb("dsq", [128, 512], F32)
                dsqh = sb("dsqh", [128, 512], BF16)
                dsqm = sb("dsqm", [128, 512], BF16)
                drs = sb("drs", [128, 512], F32)
                Bdd, Bdsq, Bdsqh, Bdsqm, Bdrs = [Buf(x) for x in ["dd", "dsq", "dsqh", "dsqm", "drs"]]
                TTs = [sb("TT%d" % i, [128, 8, 512], F32) for i in range(2)]
                BTTs = [Buf("TT%d" % i) for i in range(2)]
                pending = []
                hk = [sb("hk%d" % i, [128, 3, 512], BF16) for i in range(2)]
                Bhk = [Buf("hk%d" % i) for i in range(2)]
                zdram = sc["zrow"]
                Bzd = [Buf("zd0"), Buf("zd1")]

                def load_kv(kt_src, v_src, vw):
                    for gi in range(NG):
                        c0 = gi * cpg
                        pg.dma("sp", KT[:, c0 * 128:(c0 + cpg) * 128], kt_src[:, c0 * 128:(c0 + cpg) * 128], writes=[BKV[gi]])
                        pg.dma("sp", VV[:, c0:c0 + cpg, 0:vw], v_src[c0 * 128:(c0 + cpg) * 128, :].rearrange("(c p) w -> p c w", p=128), writes=[BKV[gi]])

                def attn_tile(units, lhs_v, near, bias_of, epilogue, TT=None, BTT=None):
                    def qk(c):
                        par = c % 2

                        def f(e):
                            ins = None
                            for u in range(2):
                                ins = e.matmul(psb(2 * par + u), lhsT=KT[64 * u:64 * u + 64, c * 128:(c + 1) * 128], rhs=units[u], start=True, stop=True)
                            return ins
                        pg.op("pe", f, reads=[BKV[c // cpg], BQ], writes=[PB[2 * par], PB[2 * par + 1]])
                        if c in near:
                            ti = near[c]
                            pg.op("dve", lambda e: e.tensor_tensor(out=ps[:, 2 * par:2 * par + 2, :], in0=ps[:, 2 * par:2 * par + 2, :],
                                                                 in1=TT[:, ti:ti + 1, :].to_broadcast([128, 2, 512]), op=ALU.add),
                                  reads=[BTT], writes=[PB[2 * par], PB[2 * par + 1]])

                    def ex(c):
                        par = c % 2
                        p3 = c % 3
                        b = bias_of(c)
                        pg.op("act", lambda e: e.activation(out=pT[p3][:], in_=ps[:, 2 * par:2 * par + 2, :].rearrange("p a b -> p (a b)"),
                                                          func=AF.Exp, bias=(b if b is not None else zero1[:])),
                              reads=[PB[2 * par], PB[2 * par + 1], Bc], writes=[BpT[p3]])

                    def pv(c):
                        par = c % 3
                        dm = diff_mode[0]

                        def f(e):
                            ins = None
                            for u in range(2):
                                lv, m = lhs_v(u, c)
                                ins = e.matmul(ps[0:m, 4 + u, :], lhsT=lv, rhs=pT[par][:, u * 512:(u + 1) * 512], start=(c == 0), stop=(c == NC - 1))
                            if dm:
                                for u in range(2):
                                    ins = e.matmul(ps[32 * u:32 * u + 1, 6, :], lhsT=onesb[:, 0:1], rhs=pT[par][:, u * 512:(u + 1) * 512],
                                                   start=(c == 0), stop=(c == NC - 1), skip_group_check=True)
                            return ins
                        w = [PB[4], PB[5]] + ([PB[6]] if diff_mode[0] else [])
                        pg.op("pe", f, reads=[BKV[c // cpg], BpT[par], Bc], writes=w)
                    qk(0)
                    qk(1)
                    for c in range(NC):
                        ex(c)
                        if c + 2 < NC:
                            qk(c + 2)
                        pv(c)
                        while pending and pending[0][0] * NC // 32 <= c:
                            pending.pop(0)[1]()
                    epilogue()

                diff_mode = [False]
                if name == "P":
                    lcf = [sb("lcf%d" % i, [128, 1024], F32) for i in range(2)]
                    lcb = [sb("lcb%d" % i, [128, 1024], BF16) for i in range(2)]
                    Blcf = [Buf("lcf%d" % i) for i in range(2)]
                    Blcb = [Buf("lcb%d" % i) for i in range(2)]
                lp_state = [0]

                def emit_late(n):
                    while n > 0 and name == "P" and lp_state[0] < len(late_pieces):
                        src, dst, w = late_pieces[lp_state[0]]
                        i = lp_state[0] % 2
                        lp_state[0] += 1
                        n -= 1
                        pg.dma("sp", lcf[i][:, 0:w], src, writes=[Blcf[i]])
                        pg.op("pool", lambda e, i=i, w=w: e.tensor_copy(lcb[i][:, 0:w], lcf[i][:, 0:w]), reads=[Blcf[i]], writes=[Blcb[i]])
                        pg.dma("pool", dst, lcb[i][:, 0:w], reads=[Blcb[i]])
                def tt_dma(h, pair):
                    for ti in (2 * pair, 2 * pair + 1):
                        i = ti % 2
                        if ti < 6:
                            dofs = (ti - 1) * 128
                            base = 512 - dofs
                            for part in range(3):
                                src = bass.AP(ub_d.tensor, ub_d[part, h, base:base + 1].offset, [[1, 128], [1, 512]])
                                pg.dma("sp", hk[i][:, part, :], src, writes=[Bhk[i]])
                        else:
                            uw = uw_d[name] if ti == 6 else uw2_d[name]
                            for part in range(3):
                                src = bass.AP(uw.tensor, uw[part, h, 0:1].offset, [[1, 128], [1, 512]])
                                pg.dma("sp", hk[i][:, part, :], src, writes=[Bhk[i]])

                def tt_pe(h, pair):
                    TT, BTT = TTs[h % 2], BTTs[h % 2]
                    for ti in (2 * pair, 2 * pair + 1):
                        i = ti % 2

                        def mmT(e, i=i):
                            ins = None
                            for part in range(3):
                                ins = e.matmul(psb(7), lhsT=antiid, rhs=hk[i][:, part, :], start=(part == 0), stop=(part == 2))
                            return ins
                        pg.op("pe", mmT, reads=[Bhk[i], Bc], writes=[PB[7]])
                        pg.op("dve", lambda e, ti=ti, TT=TT: e.tensor_copy(TT[:, ti, :], psb(7)), reads=[PB[7]], writes=[BTT])

                def tt_steps(h):
                    st_ = [lambda: tt_dma(h, 0)]
                    for p_ in range(1, 4):
                        st_.append(lambda p_=p_: (tt_pe(h, p_ - 1), tt_dma(h, p_)))
                    st_.append(lambda: tt_pe(h, 3))
                    return st_
                tt_sched = []
                load_kv(sc["KTa"], sc["Va"], 130)
                for g in range(4):
                    pg.dma("sp", QT[:, g, :], sc["QTa"][g], writes=[BQ])
                for st_ in tt_steps(0):
                    st_()
                for qt in range(nq // 128):
                    units = [QT[64 * u:64 * u + 64, :, qt * 128:(qt + 1) * 128] for u in range(2)]

                    def lhs_v(u, c):
                        return VV[:, c, u * 65:(u + 1) * 65], 65

                    def epi(qt=qt):
                        for u in range(2):
                            pg.op("dve", lambda e, u=u: e.tensor_copy(osb[u][0:65, :], ps[0:65, 4 + u, :]), reads=[PB[4 + u]], writes=[Bosb[u]])
                        for u in range(2):
                            pg.op("dve", lambda e, u=u: e.reciprocal(out=zr[64:65, u, :], in_=osb[u][64:65, :]), reads=[Bosb[u]], writes=[Bzr])
                            pg.dma("pool", zdram[u, 0:1, :], zr[64:65, u, :], reads=[Bzr], writes=[Bzd[u]])
                            pg.dma("pool", bcz[0:64, u, :], zdram[u, 0:1, :].broadcast_to([64, 512]), reads=[Bzd[u]], writes=[Bbcz])
                            pg.op("dve", lambda e, u=u: e.tensor_tensor(out=onrm[0:64, :], in0=osb[u][0:64, :], in1=bcz[0:64, u, :], op=ALU.mult),
                                  reads=[Bosb[u], Bbcz], writes=[Bonrm])
                            pg.dma("pool", sc["outT"][u * 256:(u + 1) * 256, qt * 128:(qt + 1) * 128].rearrange("(g d) t -> d g t", g=4),
                                   onrm[0:64, :].rearrange("d (g t) -> d g t", g=4), reads=[Bonrm])
                    attn_tile(units, lhs_v, {}, lambda c: None, epi)
                    emit_late(3)
                emit_late(10 ** 6)
                diff_mode[0] = True

                for h in range(4):
                    load_kv(sc["KTd"][h], sc["Vd"][:, h * 128:(h + 1) * 128], 128)
                    pg.dma("sp", QT[:, 0, :], sc["QTd"][h], writes=[BQ])
                    while tt_sched:
                        tt_sched.pop(0)()
                    for qt in range(nq // 512):
                        units = [QT[64 * u:64 * u + 64, 0, qt * 512:(qt + 1) * 512] for u in range(2)]
                        c0 = qt * 4
                        near = {}
                        for ti in range(6):
                            c = c0 - 1 + ti
                            if 0 <= c < NC:
                                near[c] = ti
                        if qt == 0:
                            near[NC - 1] = 6
                        if qt == nq // 512 - 1:
                            near[nq // 128] = 7
                        fb = farb[name]

                        def bias_of(c, near=near, c0=c0, h=h, fb=fb):
                            if c in near:
                                return None
                            if c < c0:
                                return fb[:, h, NC:NC + 1]
                            return fb[:, h, c:c + 1]

                        def lhs_v(u, c):
                            return VV[:, c, 0:128], 128

                        def epi(qt=qt, h=h):
                            for u in range(2):
                                pg.op("dve", lambda e, u=u: e.tensor_copy(osb[u][:], psb(4 + u)), reads=[PB[4 + u]], writes=[Bosb[u]])
                            for u in range(2):
                                pg.op("dve", lambda e, u=u: e.tensor_copy(zr[32 * u:32 * u + 1, u, :], ps[32 * u:32 * u + 1, 6, :]), reads=[PB[6]], writes=[Bzr])
                            pending.append((5, lambda: epi_tail0()))
                            pending.append((15, lambda: epi_tail1()))
                            pending.append((23, lambda qt=qt, h=h: epi_tail2(qt, h)))

                        def epi_tail0():
                            for u in range(2):
                                pg.op("dve", lambda e, u=u: e.reciprocal(out=zr[32 * u:32 * u + 1, u, :], in_=zr[32 * u:32 * u + 1, u, :]), reads=[Bzr], writes=[Bzr])
                                pg.dma("pool", zdram[u, 1:2, :], zr[32 * u:32 * u + 1, u, :], reads=[Bzr], writes=[Bzd[u]])
                                pg.dma("pool", bcz[:, u, :], zdram[u, 1:2, :].broadcast_to([128, 512]), reads=[Bzd[u]], writes=[Bbcz])
                            pg.op("dve", lambda e: e.tensor_tensor(out=osb[0][:], in0=osb[0][:], in1=bcz[:, 0, :], op=ALU.mult), reads=[Bosb[0], Bbcz], writes=[Bosb[0]])
                            pg.op("pool", lambda e: e.tensor_tensor(out=osb[1][:], in0=osb[1][:], in1=bcz[:, 1, :], op=ALU.mult), reads=[Bosb[1], Bbcz], writes=[Bosb[1]])
                            pg.op("dve", lambda e: e.scalar_tensor_tensor(out=dd[:], in0=osb[1][:], scalar=negl[:, 0:1], in1=osb[0][:], op0=ALU.mult, op1=ALU.add),
                                  reads=[Bosb[0], Bosb[1], Bnegl], writes=[Bdd])
                            pg.op("pool", lambda e: e.tensor_tensor(out=dsq[:], in0=dd[:], in1=dd[:], op=ALU.mult), reads=[Bdd], writes=[Bdsq])
                            pg.op("dve", lambda e: e.tensor_copy(dsqh[:], dsq[:]), reads=[Bdsq], writes=[Bdsqh])
                            pg.op("pool", lambda e: e.tensor_tensor(out=dsq[:], in0=dsq[:], in1=dsqh[:], op=ALU.subtract), reads=[Bdsq, Bdsqh], writes=[Bdsq])
                            pg.op("pool", lambda e: e.tensor_copy(dsqm[:], dsq[:]), reads=[Bdsq], writes=[Bdsqm])

                        def epi_tail1():
                            def mmss(e):
                                e.matmul(psb(7), lhsT=o128, rhs=dsqh[:], start=True, stop=False)
                                return e.matmul(psb(7), lhsT=o128, rhs=dsqm[:], start=False, stop=True)
                            pg.op("pe", mmss, reads=[Bdsqh, Bdsqm, Bc], writes=[PB[7]])
                            pg.op("dve", lambda e: e.tensor_copy(drs[:], psb(7)), reads=[PB[7]], writes=[Bdrs])

                        def epi_tail2(qt, h):
                            pg.op("act", lambda e: e.activation(out=drs[:], in_=drs[:], func=AF.Sqrt, bias=epsc[:], scale=1.0), reads=[Bdrs, Bc], writes=[Bdrs])
                            pg.op("dve", lambda e: e.reciprocal(out=drs[:], in_=drs[:]), reads=[Bdrs], writes=[Bdrs])
                            pg.op("dve", lambda e: e.scalar_tensor_tensor(out=onrm[:], in0=dd[:], scalar=gcols[:, 4:5], in1=drs[:], op0=ALU.mult, op1=ALU.mult),
                                  reads=[Bdd, Bdrs, Bc], writes=[Bonrm])
                            pg.dma("pool", sc["outT"][512 + h * 128:512 + (h + 1) * 128, qt * 512:(qt + 1) * 512], onrm[:], reads=[Bonrm])
                        attn_tile(units, lhs_v, near, bias_of, epi, TT=TTs[h % 2], BTT=BTTs[h % 2])
                        if qt == 0 and h + 1 < 4:
                            tt_sched.extend(tt_steps(h + 1))
                        if tt_sched:
                            tt_sched.pop(0)()
                while pending:
                    pending.pop(0)[1]()
                pg.end()

        TK = 256
        for jb in jobs:
            name, N, nq = jb["name"], jb["N"], jb["nq"]
            sc = S[name]
            with contextlib.ExitStack() as st:
                def sb(nm, shape, dt):
                    return st.enter_context(nc.sbuf_tensor("s3_" + nm + name, list(shape), dt))
                PB = new_ps()
                pg.begin()
                wo = sb("wo", [128, 8, D], BF16)
                Bw = Buf("w3")
                for k in range(8):
                    pg.dma("sp", wo[:, k, :], wb_out[k * 128:(k + 1) * 128, :], writes=[Bw])
                G1 = sb("G1", [128, D], F32)
                A2 = sb("A2", [128, D], F32)
                SH2 = sb("SH2", [128, D], F32)
                Bmodt = Buf("modt")
                for tile_, ri in ((G1, 2), (A2, 3), (SH2, 4)):
                    pg.dma("sp", tile_[:], rows_d[jb["b"], ri:ri + 1, :].broadcast_to([128, D]), writes=[Bmodt])
                xt = [sb("xt%d" % i, [128, 2, D], F32) for i in range(2)]
                Bxts = [[Buf("xt%d_%d" % (i, j)) for j in range(2)] for i in range(2)]
                BhTs = [[Buf("hT%d_%d" % (i, j)) for j in range(2)] for i in range(2)]
                oT = [sb("oT%d" % i, [128, 8, TK], BF16) for i in range(2)]
                BoT = [Buf("oT%d" % i) for i in range(2)]
                junk = sb("junk", [128, D], F32)
                ss = sb("ss", [128, 4], F32)
                rstd = sb("rstd", [128, 4], F32)
                mixs = [sb("mixs%d" % i, [128, D], F32) for i in range(2)]
                tt = [sb("tt%d" % i, [128, D], F32) for i in range(2)]
                hb = [sb("hb%d" % i, [128, D], BF16) for i in range(2)]
                hT = [sb("hT%d" % i, [128, 8, TK], BF16) for i in range(2)]
                Bjunk = Buf("junk")
                Bss = [Buf("ss%d" % i) for i in range(4)]
                Bmixs = [Buf("mixs%d" % i) for i in range(2)]
                Btt = [Buf("tt%d" % i) for i in range(2)]
                Bhb = [Buf("hb%d" % i) for i in range(2)]
                BhT = [Buf("hT%d" % i) for i in range(2)]
                psT = [ps[:, i, :].bitcast(BF16) for i in range(2)]

                def rms_stat(src_ap, src_bufs, col):
                    pg.op("dve", lambda e: e.scalar_tensor_tensor(out=junk[:], in0=src_ap, scalar=1.0 / D, in1=src_ap, op0=ALU.mult, op1=ALU.mult,
                                                                  accum_out=ss[:, col:col + 1]), reads=src_bufs, writes=[Bjunk, Bss[col]])
                    pg.op("dve", lambda e: e.tensor_scalar(out=rstd[:, col:col + 1], in0=ss[:, col:col + 1], scalar1=EPS, scalar2=None, op0=ALU.add),
                          reads=[Bss[col]], writes=[Bss[col]])
                    pg.op("pool", lambda e: e.tensor_tensor(out=rstd[:, col:col + 1], in0=rstd[:, col:col + 1], in1=nhalf[:, 0:1], op=ALU.pow),
                          reads=[Bss[col], Bc], writes=[Bss[col]])

                for t in range(nq // TK):
                    xi = t % 2
                    pg.dma("sp", xt[xi][:], x_in[name][t * TK:(t + 1) * TK, :].rearrange("(s p) d -> p s d", p=128), writes=Bxts[xi])
                    pg.dma("sp", oT[xi][:], sc["outT"][:, t * TK:(t + 1) * TK].rearrange("(k p) t -> p k t", p=128), writes=[BoT[xi]])

                    def chain(s, xi=xi):
                        steps = []

                        def mmo(e):
                            ins = None
                            for hh in range(2):
                                for k in range(8):
                                    ins = e.matmul(psb(2 + 2 * s + hh), lhsT=oT[xi][:, k, s * 128:(s + 1) * 128], rhs=wo[:, k, hh * 512:(hh + 1) * 512],
                                                   start=(k == 0), stop=(k == 7))
                            return ins
                        steps.append(lambda: pg.op("pe", mmo, reads=[BoT[xi], Bw], writes=[PB[2 + 2 * s], PB[3 + 2 * s]]))
                        steps.append(lambda: pg.op("act", lambda e: e.activation(out=mixs[s][:], in_=ps[:, 2 + 2 * s:4 + 2 * s, :].rearrange("p a b -> p (a b)"), func=AF.Copy),
                                                   reads=[PB[2 + 2 * s], PB[3 + 2 * s]], writes=[Bmixs[s]]))
                        steps.append(lambda: rms_stat(mixs[s][:], [Bmixs[s]], s))
                        steps.append(lambda: pg.op("dve", lambda e: e.scalar_tensor_tensor(out=tt[s][:], in0=mixs[s][:], scalar=rstd[:, s:s + 1], in1=G1[:], op0=ALU.mult, op1=ALU.mult),
                                                   reads=[Bmixs[s], Bss[s], Bmodt], writes=[Btt[s]]))
                        steps.append(lambda: pg.op("pool", lambda e: e.tensor_tensor(out=xt[xi][:, s, :], in0=xt[xi][:, s, :], in1=tt[s][:], op=ALU.add),
                                                   reads=[Btt[s], Bxts[xi][s]], writes=[Bxts[xi][s]]))
                        steps.append(lambda: rms_stat(xt[xi][:, s, :], [Bxts[xi][s]], 2 + s))
                        steps.append(lambda: pg.op("dve", lambda e: e.scalar_tensor_tensor(out=tt[s][:], in0=xt[xi][:, s, :], scalar=rstd[:, 2 + s:3 + s], in1=A2[:], op0=ALU.mult, op1=ALU.mult),
                                                   reads=[Bxts[xi][s], Bss[2 + s], Bmodt], writes=[Btt[s]]))
                        steps.append(lambda: pg.op("pool", lambda e: e.tensor_tensor(out=hb[s][:], in0=tt[s][:], in1=SH2[:], op=ALU.add), reads=[Btt[s], Bmodt], writes=[Bhb[s]]))

                        def tr(e):
                            ins = None
                            for k in range(8):
                                ins = e.transpose(out=psT[s][:, k * 128:(k + 1) * 128], in_=hb[s][:, k * 128:(k + 1) * 128], identity=ident)
                            return ins
                        steps.append(lambda: pg.op("pe", tr, reads=[Bhb[s], Bc], writes=[PB[s]]))
                        steps.append(lambda: pg.op("act", lambda e: e.activation(out=hT[xi][:, :, s * 128:(s + 1) * 128], in_=psT[s].rearrange("p (k t) -> p k t", k=8), func=AF.Copy),
                                                   reads=[PB[s]], writes=[BhTs[xi][s]]))
                        return steps
                    c0s, c1s = chain(0), chain(1)
                    for f0, f1 in zip(c0s, c1s):
                        f0()
                        f1()
                    pg.dma("pool", sc["x1"][t * TK:(t + 1) * TK, :].rearrange("(s p) d -> p s d", p=128), xt[xi][:], reads=Bxts[xi])
                    pg.dma("pool", sc["h2T"][:, :, t * TK:(t + 1) * TK].rearrange("k p t -> p k t"), hT[xi][:], reads=BhTs[xi])
                pg.end()

        for jb in jobs:
            name, N, nq = jb["name"], jb["N"], jb["nq"]
            sc = S[name]
            with contextlib.ExitStack() as st:
                def sb(nm, shape, dt):
                    return st.enter_context(nc.sbuf_tensor("s4_" + nm + name, list(shape), dt))
                PB = new_ps()
                pg.begin()
                wgu = sb("wgu", [128, 8, 2 * DFF], BF16)
                wdn = sb("wdn", [128, 22, D], BF16)
                Bw = Buf("w3")
                for k in range(8):
                    pg.dma("sp", wgu[:, k, :], wb_gu[k * 128:(k + 1) * 128, :], writes=[Bw])
                pg.dma("sp", wdn[:], wb_down.rearrange("(k p) n -> p k n", p=128), writes=[Bw])
                G2 = sb("G2", [128, D], F32)
                Bmodt = Buf("modt")
                pg.dma("sp", G2[:], rows_d[jb["b"], 5:6, :].broadcast_to([128, D]), writes=[Bmodt])
                xt = [sb("xt%d" % i, [128, 2, D], F32) for i in range(2)]
                Bxt = [Buf("xt%d" % i) for i in range(2)]
                hT = [sb("hT%d" % i, [128, 8, TK], BF16) for i in range(2)]
                BhT = [Buf("hT%d" % i) for i in range(2)]
                junk = sb("junk", [128, D], F32)
                ss = sb("ss", [128, 2], F32)
                rstd = sb("rstd", [128, 2], F32)
                act_ = sb("act", [128, 22, TK], BF16)
                sg = [sb("sg%d" % i, [128, TK], F32) for i in range(2)]
                fs = [sb("fs%d" % i, [128, D], F32) for i in range(2)]
                tt = [sb("tt%d" % i, [128, D], F32) for i in range(2)]
                Bjunk = Buf("junk")
                Bss = [Buf("ss%d" % i) for i in range(2)]
                Bfs = [Buf("fs%d" % i) for i in range(2)]
                Btt = [Buf("tt%d" % i) for i in range(2)]
                Bact = [Buf("act%d" % i) for i in range(22)]
                Bsg = [Buf("sg%d" % i) for i in range(2)]
                for t in range(nq // TK):
                    xi = t % 2
                    pg.dma("sp", xt[xi][:], sc["x1"][t * TK:(t + 1) * TK, :].rearrange("(s p) d -> p s d", p=128), writes=[Bxt[xi]])
                    pg.dma("sp", hT[xi][:], sc["h2T"][:, :, t * TK:(t + 1) * TK].rearrange("k p t -> p k t"), writes=[BhT[xi]])
                    for j in range(22):
                        i = j % 2

                        def mmg(e, j=j, i=i, xi=xi):
                            ins = None
                            for k in range(8):
                                ins = e.matmul(ps[:, 2 * i, 0:TK], lhsT=wgu[:, k, j * 128:(j + 1) * 128], rhs=hT[xi][:, k, :], start=(k == 0), stop=(k == 7))
                            for k in range(8):
                                ins = e.matmul(ps[:, 2 * i + 1, 0:TK], lhsT=wgu[:, k, DFF + j * 128:DFF + (j + 1) * 128], rhs=hT[xi][:, k, :], start=(k == 0), stop=(k == 7))
                            return ins
                        pg.op("pe", mmg, reads=[Bw, BhT[xi]], writes=[PB[2 * i], PB[2 * i + 1]])
                        pg.op("act", lambda e, i=i: e.activation(out=sg[i][:], in_=ps[:, 2 * i, 0:TK], func=AF.Silu), reads=[PB[2 * i]], writes=[Bsg[i]])
                        pg.op("dve", lambda e, i=i, j=j: e.tensor_tensor(out=act_[:, j, :], in0=sg[i][:], in1=ps[:, 2 * i + 1, 0:TK], op=ALU.mult),
                              reads=[Bsg[i], PB[2 * i + 1]], writes=[Bact[j]])
                    for s in range(2):
                        def mmd(e, s=s):
                            ins = None
                            for hh in range(2):
                                for j in range(22):
                                    ins = e.matmul(psb(4 + 2 * s + hh), lhsT=act_[:, j, s * 128:(s + 1) * 128], rhs=wdn[:, j, hh * 512:(hh + 1) * 512],
                                                   start=(j == 0), stop=(j == 21))
                            return ins
                        pg.op("pe", mmd, reads=[Bw] + Bact, writes=[PB[4 + 2 * s], PB[5 + 2 * s]])
                        pg.op("act", lambda e, s=s: e.activation(out=fs[s][:], in_=ps[:, 4 + 2 * s:6 + 2 * s, :].rearrange("p a b -> p (a b)"), func=AF.Copy),
                              reads=[PB[4 + 2 * s], PB[5 + 2 * s]], writes=[Bfs[s]])
                        pg.op("dve", lambda e, s=s: e.scalar_tensor_tensor(out=junk[:], in0=fs[s][:], scalar=1.0 / D, in1=fs[s][:], op0=ALU.mult, op1=ALU.mult,
                                                                         accum_out=ss[:, s:s + 1]), reads=[Bfs[s]], writes=[Bjunk, Bss[s]])
                        pg.op("dve", lambda e, s=s: e.tensor_scalar(out=rstd[:, s:s + 1], in0=ss[:, s:s + 1], scalar1=EPS, scalar2=None, op0=ALU.add),
                              reads=[Bss[s]], writes=[Bss[s]])
                        pg.op("pool", lambda e, s=s: e.tensor_tensor(out=rstd[:, s:s + 1], in0=rstd[:, s:s + 1], in1=nhalf[:, 0:1], op=ALU.pow),
                              reads=[Bss[s], Bc], writes=[Bss[s]])
                        pg.op("dve", lambda e, s=s: e.scalar_tensor_tensor(out=tt[s][:], in0=fs[s][:], scalar=rstd[:, s:s + 1], in1=G2[:], op0=ALU.mult, op1=ALU.mult),
                              reads=[Bfs[s], Bss[s], Bmodt], writes=[Btt[s]])
                        pg.op("pool", lambda e, s=s, xi=xi: e.tensor_tensor(out=xt[xi][:, s, :], in0=xt[xi][:, s, :], in1=tt[s][:], op=ALU.add),
                              reads=[Btt[s], Bxt[xi]], writes=[Bxt[xi]])
                    pg.dma("pool", y_out[name][t * TK:(t + 1) * TK, :].rearrange("(s p) d -> p s d", p=128), xt[xi][:], reads=[Bxt[xi]])
                pg.end()
    return nc


def _prep_shared(inp, NP, NS):
    f = lambda a: np.ascontiguousarray(np.asarray(a, dtype=np.float32))
    perm = _perm64()
    w_in = f(inp["w_in"])[0]
    o1, o2, o3, o4, o5 = 512, 640, 768, 1280, 1792
    cols = []
    qa_nat = np.array([[(kv * 4 + g) * 64 + d for kv in range(2) for d in range(64)] for g in range(4)])
    qa_prm = np.array([[(kv * 4 + g) * 64 + perm[d] for kv in range(2) for d in range(64)] for g in range(4)])
    cols += list(qa_nat.reshape(-1)) + list(qa_prm.reshape(-1))
    cols += [o1 + kv * 64 + d for kv in range(2) for d in range(64)]
    cols += [o1 + kv * 64 + perm[d] for kv in range(2) for d in range(64)]
    cols += list(range(o3, o4)) + list(range(o4, o5)) + list(range(o2, o3)) + list(range(o5, 2304))
    cols = np.array(cols)
    assert len(cols) == WIN
    g_q, g_k = f(inp["g_q"])[0], f(inp["g_k"])[0]
    gcols = np.zeros((128, 8), np.float32)
    gcols[:, 0] = np.tile(g_q, 2)
    gcols[:, 1] = np.tile(g_q[perm], 2)
    gcols[:, 2] = np.tile(g_k, 2)
    gcols[:, 3] = np.tile(g_k[perm], 2)
    gcols[:, 4] = f(inp["g_subln"])[0]
    gcols[:, 5] = 1.0 - LAM_INIT
    grow = np.stack([f(inp["g_pre_mix"])[0], f(inp["g_post_mix"])[0], f(inp["g_pre_ffn"])[0], f(inp["g_post_ffn"])[0]])
    grow2 = np.ascontiguousarray(np.stack([grow, grow]))
    lamv = np.stack([f(inp["lam_q1"])[0], f(inp["lam_k1"])[0], f(inp["lam_q2"])[0], f(inp["lam_k2"])[0]])[None]
    b_ada = f(inp["b_ada"])
    cm = np.zeros((128, 5, 128), np.float32)
    cm[:, 0, :] = np.eye(128)
    cm[:, 1, :] = np.eye(128)[::-1]
    cm[:, 2, :] = 1.0
    cm[0:64, 3, 0:64] = 1.0 / 64
    cm[64:128, 3, 64:128] = 1.0 / 64
    cm[:, 4, :] = 1.0 / 128
    m = np.arange(ULEN)
    emain = _onehot(_rel_bucket_np(639 - m))
    sh = dict(
        w_ada=f(inp["w_ada"])[0], b_ada2=np.ascontiguousarray(np.concatenate([b_ada, b_ada], 0)), grow=grow2,
        w_in_p=np.ascontiguousarray(w_in[:, cols]), w_out=f(inp["w_out"])[0], w_gu=f(inp["w_gu"])[0], w_down=f(inp["w_down"])[0],
        gcols=gcols, lamv=np.ascontiguousarray(lamv), relb=f(inp["rel_bias"]),
        cmat=cm.astype(ml_dtypes.bfloat16), emain=emain.astype(ml_dtypes.bfloat16),
    )
    return sh


def _prep_core(inp, sh, c, NP, NS):
    f = lambda a: np.asarray(a, dtype=np.float32)
    pb, pq, sbi, sq = c // 4, c % 4, c // 2, c % 2
    m = dict(sh)
    cp, cs = f(inp["c_prompt"])[pb], f(inp["c_sample"])[sbi]
    cT = np.stack([cp, cs], -1).reshape(8, 128, 2).transpose(1, 0, 2)
    m["cT"] = np.ascontiguousarray(cT)
    for nm, x, N, nq, qi in (("P", f(inp["x_prompt"])[pb], NP, NP // 4, pq), ("S", f(inp["x_sample"])[sbi], NS, NS // 2, sq)):
        qoff = qi * nq
        m["x" + nm] = np.ascontiguousarray(np.roll(x, -qoff, axis=0))
        pos = (np.arange(N) + qoff) % N
        cosT, sinT = _rope_tables(pos)
        m["cos" + nm] = cosT
        m["sin" + nm] = sinT
        NC = N // 128
        mm = np.arange(WLEN)
        if qoff > 0:
            bw = _rel_bucket_np(-1 - mm)
        else:
            bw = np.full(WLEN, NB // 2 + NB // 2 - 1)
        m["ewrap" + nm] = _onehot(bw).astype(ml_dtypes.bfloat16)
        if qoff + nq == N:
            bw2 = np.full(WLEN, NB // 2 - 1)
        else:
            bw2 = _rel_bucket_np(639 - mm)
        m["ewrap2" + nm] = _onehot(bw2).astype(ml_dtypes.bfloat16)
        far = np.zeros(NC + 1, np.int64)
        for ch in range(NC):
            if ch * 128 < nq:
                far[ch] = 31
            else:
                far[ch] = 31 if ch * 128 < N - qoff else 15
        far[NC] = 15
        m["efar" + nm] = _onehot(far).astype(ml_dtypes.bfloat16)
    return m


_CACHE = {}


def run(inputs, NP, NS, debug=False, ncores=8):
    key = (NP, NS, debug)
    if key not in _CACHE:
        _CACHE[key] = build_program(NP, NS, debug)
    nc = _CACHE[key]
    sh = _prep_shared(inputs, NP, NS)
    in_maps = [_prep_core(inputs, sh, c, NP, NS) for c in range(ncores)]
    res = run_bass_kernel_spmd(nc, in_maps, core_ids=list(range(ncores)))
    return res.results


def kernel(**inputs):
    NP = int(np.asarray(inputs["x_prompt"]).shape[1])
    NS = int(np.asarray(inputs["x_sample"]).shape[1])
    r = run(inputs, NP, NS)
    yp = np.zeros((2, NP, D), np.float32)
    ys = np.zeros((4, NS, D), np.float32)
    for c in range(8):
        pb, pq, sbi, sq = c // 4, c % 4, c // 2, c % 2
        nqp, nqs = NP // 4, NS // 2
        yp[pb, pq * nqp:(pq + 1) * nqp] = r[c]["yP"]
        ys[sbi, sq * nqs:(sq + 1) * nqs] = r[c]["yS"]
    return (yp, ys)
```

```python
import contextlib
import math
import numpy as np
import ml_dtypes
import concourse.bass as bass
import concourse.mybir as mybir
from concourse.bass_utils import run_bass_kernel_spmd

F32 = mybir.dt.float32
BF16 = mybir.dt.bfloat16
ALU = mybir.AluOpType
AF = mybir.ActivationFunctionType
AX = mybir.AxisListType

D = 1024
DFF = 2816
HD = 64
EPS = 1e-6
NB = 32
WIN = 2944
LAM_INIT = 0.8 - 0.6 * math.exp(-0.3 * 0)
ULEN = 1279
WLEN = 639

ENGS = ["pe", "act", "dve", "pool", "sp"]
N_DMA_SEMS = 6


class Buf:
    __slots__ = ("name", "w", "r", "excl")

    def __init__(self, name, excl=False):
        self.name = name
        self.excl = excl
        self.w = None
        self.r = []


class Op:
    __slots__ = ("eng", "fn", "waits", "signal", "ev", "is_dma")

    def __init__(self, eng, fn, is_dma):
        self.eng = eng
        self.fn = fn
        self.waits = []
        self.signal = False
        self.ev = None
        self.is_dma = is_dma


class Prog:
    def __init__(self, nc, stack):
        self.nc = nc
        self.sems = {}
        self.cnt = {}
        for e in ENGS:
            self.sems[e] = stack.enter_context(nc.semaphore("s_" + e))
            self.cnt[e] = 0
        self.dma_sems = {}
        for q in ("sp", "act", "pool"):
            lst = []
            for i in range(N_DMA_SEMS):
                nm = "d_%s%d" % (q, i)
                self.sems[nm] = stack.enter_context(nc.semaphore(nm))
                self.cnt[nm] = 0
                lst.append(nm)
            self.dma_sems[q] = lst
        self.dma_rr = {q: 0 for q in self.dma_sems}
        self.waited = {e: {} for e in ENGS}
        self.ops = None
        self.nops = 0

    def begin(self):
        self.ops = {e: [] for e in ENGS}
        self.allops = []
        self.dma_last = {}

    def _dep(self, op, other):
        if other is None or other is op:
            return
        if other.eng == "pe" and op.eng == "pe" and not other.is_dma and not op.is_dma:
            return
        op.waits.append(other)

    def op(self, eng, fn, reads=(), writes=(), dma=False):
        o = Op(eng, fn, dma)
        reads = list(reads)
        writes = list(writes)
        for b in reads:
            if b.excl and b not in writes:
                writes.append(b)
        for b in reads:
            self._dep(o, b.w)
        for b in writes:
            self._dep(o, b.w)
            for r in b.r:
                self._dep(o, r)
        for b in reads:
            b.r.append(o)
        for b in writes:
            b.w = o
            b.r = []
        if dma:
            i = self.dma_rr[eng]
            self.dma_rr[eng] = (i + 1) % N_DMA_SEMS
            nm = self.dma_sems[eng][i]
            prev = self.dma_last.get(nm)
            if prev is not None:
                o.waits.append(prev)
            self.dma_last[nm] = o
            self.cnt[nm] += 16
            o.ev = (nm, self.cnt[nm])
            o.signal = True
        self.ops[eng].append(o)
        self.allops.append(o)
        return o

    def dma(self, q, out, in_, reads=(), writes=()):
        return self.op(q, lambda e: e.dma_start(out=out, in_=in_), reads, writes, dma=True)

    def end(self):
        nc = self.nc
        for o in self.allops:
            for w in o.waits:
                w.signal = True
        for e in ENGS:
            for o in reversed(self.ops[e]):
                if not o.is_dma:
                    o.signal = True
                    break
        for e in ENGS:
            for o in self.ops[e]:
                if not o.is_dma and o.signal:
                    self.cnt[e] += 1
                    o.ev = (e, self.cnt[e])
        final = dict(self.cnt)
        sems = self.sems
        ops = self.ops
        waited_all = self.waited
        self.nops += len(self.allops)

        def emit(ename, eng):
            waited = waited_all[ename]
            for o in ops[ename]:
                need = {}
                for w in o.waits:
                    s, v = w.ev
                    if need.get(s, 0) < v:
                        need[s] = v
                for s, v in need.items():
                    if waited.get(s, 0) >= v:
                        continue
                    waited[s] = v
                    eng.wait_ge(sems[s], v)
                ins = o.fn(eng)
                if o.signal:
                    s, v = o.ev
                    ins.then_inc(sems[s], 16 if o.is_dma else 1)
            for s, v in final.items():
                if v > 0 and waited.get(s, 0) < v:
                    waited[s] = v
                    eng.wait_ge(sems[s], v)

        with nc.Block() as block:
            @block.tensor
            def _(eng):
                emit("pe", eng)

            @block.scalar
            def _(eng):
                emit("act", eng)

            @block.vector
            def _(eng):
                emit("dve", eng)

            @block.gpsimd
            def _(eng):
                emit("pool", eng)

            @block.sync
            def _(eng):
                emit("sp", eng)
        self.ops = None


def _perm64():
    d = np.arange(64)
    return np.where((d % 32) < 16, d + 16, d - 16)


def _rel_bucket_np(rel):
    half = NB // 2
    max_exact = half // 2
    n = np.abs(rel)
    nf = np.maximum(n, max_exact).astype(np.float32)
    large = max_exact + (np.log(nf / np.float32(max_exact)) / np.float32(math.log(128 / max_exact))
                         * (half - max_exact)).astype(np.int32)
    large = np.minimum(large, half - 1)
    return np.where(rel > 0, half, 0) + np.where(n < max_exact, n, large)


def _rope_tables(pos):
    row = (pos // 64).astype(np.float32)
    col = (pos % 64).astype(np.float32)
    half = HD // 2
    inv = (np.float32(10000.0) ** (-np.arange(0, half, 2, dtype=np.float32) / np.float32(half))).astype(np.float32)
    ang_r = row[:, None] * inv[None, :]
    ang_c = col[:, None] * inv[None, :]
    ang = np.concatenate([ang_r, ang_r, ang_c, ang_c], axis=-1).astype(np.float32)
    cos = np.cos(ang).astype(np.float32)
    sin = np.sin(ang).astype(np.float32)
    d = np.arange(64)
    sign = np.where((d % 32) < 16, -1.0, 1.0).astype(np.float32)
    sin_s = sin * sign[None, :]
    cosT = np.ascontiguousarray(np.concatenate([cos.T, cos.T], axis=0))
    sinT = np.ascontiguousarray(np.concatenate([sin_s.T, sin_s.T], axis=0))
    return cosT, sinT


def _onehot(buckets):
    e = np.zeros((NB, len(buckets)), dtype=np.float32)
    e[buckets, np.arange(len(buckets))] = 1.0
    return e


def build_program(NP, NS, debug=False):
    jobs = [dict(name="P", N=NP, nq=NP // 4, b=0), dict(name="S", N=NS, nq=NS // 2, b=1)]
    for jb in jobs:
        assert jb["nq"] % 512 == 0 and jb["N"] % 512 == 0
        jb["NC"] = jb["N"] // 128
    nc = bass.Bass("TRN2", target_bir_lowering=False)

    def din(name, shape, dt=F32):
        return nc.dram_tensor(name, list(shape), dt, kind="ExternalInput").ap()

    def dscr(name, shape, dt):
        if debug and not name.startswith("wb_"):
            return nc.dram_tensor(name, list(shape), dt, kind="ExternalOutput").ap()
        return nc.dram_tensor(name, list(shape), dt).ap()

    x_in = {"P": din("xP", [NP, D]), "S": din("xS", [NS, D])}
    cT_d = din("cT", [128, 8, 2])
    w_ada_d = din("w_ada", [D, 6 * D])
    b_ada2_d = din("b_ada2", [2, 6 * D])
    grow_d = din("grow", [2, 4, D])
    w_in_d = din("w_in_p", [D, WIN])
    w_out_d = din("w_out", [D, D])
    w_gu_d = din("w_gu", [D, 2 * DFF])
    w_down_d = din("w_down", [DFF, D])
    gcols_d = din("gcols", [128, 8])
    lamv_d = din("lamv", [1, 4, 64])
    relb_d = din("relb", [NB, 4])
    rope_d = {jb["name"]: (din("cos" + jb["name"], [128, jb["N"]]), din("sin" + jb["name"], [128, jb["N"]])) for jb in jobs}
    cmat_d = din("cmat", [128, 5, 128], BF16)
    emain_d = din("emain", [NB, ULEN], BF16)
    ewrap_d = {jb["name"]: din("ewrap" + jb["name"], [NB, WLEN], BF16) for jb in jobs}
    ewrap2_d = {jb["name"]: din("ewrap2" + jb["name"], [NB, WLEN], BF16) for jb in jobs}
    efar_d = {jb["name"]: din("efar" + jb["name"], [NB, jb["NC"] + 1], BF16) for jb in jobs}
    y_out = {jb["name"]: nc.dram_tensor("y" + jb["name"], [jb["nq"], D], F32, kind="ExternalOutput").ap() for jb in jobs}

    wb_in = dscr("wb_in", [D, WIN], BF16)
    wb_out = dscr("wb_out", [D, D], BF16)
    wb_gu = dscr("wb_gu", [D, 2 * DFF], BF16)
    wb_down = dscr("wb_down", [DFF, D], BF16)
    rows_d = dscr("rows_d", [2, 6, D], F32)
    lam_d = dscr("lam_d", [1, 2], F32)
    ub_d = dscr("ub_d", [3, 4, ULEN], BF16)
    uw_d = {jb["name"]: dscr("uw_d" + jb["name"], [3, 4, WLEN], BF16) for jb in jobs}
    uw2_d = {jb["name"]: dscr("uw2_d" + jb["name"], [3, 4, WLEN], BF16) for jb in jobs}
    ufar_d = {jb["name"]: dscr("ufar_d" + jb["name"], [1, 4 * (jb["NC"] + 1)], F32) for jb in jobs}
    S = {}
    for jb in jobs:
        n, N, nq = jb["name"], jb["N"], jb["nq"]
        S[n] = dict(
            KTa=dscr("KTa" + n, [128, N], BF16), Va=dscr("Va" + n, [N, 130], BF16),
            KTd=dscr("KTd" + n, [4, 128, N], BF16), Vd=dscr("Vd" + n, [N, 512], BF16),
            QTa=dscr("QTa" + n, [4, 128, nq], BF16), QTd=dscr("QTd" + n, [4, 128, nq], BF16),
            outT=dscr("outT" + n, [D, nq], BF16), zrow=dscr("zrow" + n, [2, 2, 512], F32),
            x1=dscr("x1" + n, [nq, D], F32), h2T=dscr("h2T" + n, [8, 128, nq], BF16),
        )

    with contextlib.ExitStack() as gst:
        pg = Prog(nc, gst)
        def gsb(name, shape, dt):
            return gst.enter_context(nc.sbuf_tensor("g_" + name, list(shape), dt))

        cmat = gsb("cmat", [128, 5, 128], BF16)
        gcols = gsb("gcols", [128, 8], F32)
        negl = gsb("negl", [128, 2], F32)
        nhalf = gsb("nhalf", [128, 512], F32)
        zero1 = gsb("zero1", [128, 1], F32)
        epsc = gsb("epsc", [128, 1], F32)
        farb = {jb["name"]: gsb("farb" + jb["name"], [128, 4, jb["NC"] + 1], F32) for jb in jobs}
        ps = gst.enter_context(nc.psum_tensor("psum_all", [128, 8, 512], F32))
        ident = cmat[:, 0, :]
        antiid = cmat[:, 1, :]
        onesb = cmat[:, 2, :]
        blk64 = cmat[:, 3, :]
        o128 = cmat[:, 4, :]

        def psb(i):
            return ps[:, i, :]

        def new_ps():
            return [Buf("ps%d" % i, excl=True) for i in range(8)]

        with contextlib.ExitStack() as st:
            def sb(name, shape, dt):
                return st.enter_context(nc.sbuf_tensor("s0_" + name, list(shape), dt))

            PB = new_ps()
            pg.begin()
            Bc = Buf("consts")
            pg.dma("sp", cmat[:], cmat_d, writes=[Bc])
            pg.dma("sp", gcols[:], gcols_d, writes=[Bc])
            pg.op("pool", lambda e: e.memset(nhalf[:], -0.5), writes=[Bc])
            pg.op("pool", lambda e: e.memset(zero1[:], 0.0), writes=[Bc])
            pg.op("pool", lambda e: e.memset(epsc[:], EPS), writes=[Bc])
            pg.op("dve", lambda e: e.tensor_scalar(out=gcols[:, 4:5], in0=gcols[:, 4:5], scalar1=1.0 - LAM_INIT, scalar2=None, op0=ALU.mult),
                  reads=[Bc], writes=[Bc])

            cT = sb("cT", [128, 8, 2], F32)
            scT = sb("scT", [128, 8, 2], F32)
            scTb = sb("scTb", [128, 8, 2], BF16)
            BcT, BscT = Buf("cT"), Buf("scT")
            pg.dma("sp", cT[:], cT_d, writes=[BcT])
            pg.op("act", lambda e: e.activation(out=scT[:], in_=cT[:], func=AF.Silu), reads=[BcT], writes=[BscT])
            pg.op("dve", lambda e: e.tensor_copy(scTb[:], scT[:]), reads=[BscT], writes=[BscT])

            mod = sb("mod", [2, 6 * D], F32)
            bada = sb("bada", [2, 6 * D], F32)
            Bmod, Bbada = Buf("mod"), Buf("bada")
            pg.dma("sp", bada[:], b_ada2_d, writes=[Bbada])
            waf = [sb("waf%d" % i, [128, 8, 512], F32) for i in range(2)]
            wab = [sb("wab%d" % i, [128, 8, 512], BF16) for i in range(2)]
            Bwaf = [Buf("waf%d" % i) for i in range(2)]
            Bwab = [Buf("wab%d" % i) for i in range(2)]
            for n in range(12):
                i = n % 2
                pg.dma("sp", waf[i][:], w_ada_d[:, n * 512:(n + 1) * 512].rearrange("(k p) n -> p k n", p=128), writes=[Bwaf[i]])
                ce = "dve" if n % 2 == 0 else "pool"
                pg.op(ce, lambda e, i=i: e.tensor_copy(wab[i][:], waf[i][:]), reads=[Bwaf[i]], writes=[Bwab[i]])

                def mm(e, i=i):
                    ins = None
                    for k in range(8):
                        ins = e.matmul(ps[0:2, i, :], lhsT=scTb[:, k, :], rhs=wab[i][:, k, :], start=(k == 0), stop=(k == 7))
                    return ins
                pg.op("pe", mm, reads=[Bwab[i], BscT], writes=[PB[i]])
                pg.op("dve", lambda e, i=i, n=n: e.tensor_tensor(out=mod[:, n * 512:(n + 1) * 512], in0=ps[0:2, i, :],
                                                               in1=bada[:, n * 512:(n + 1) * 512], op=ALU.add),
                      reads=[PB[i], Bbada], writes=[Bmod])
            grow = sb("grow", [2, 4, D], F32)
            rows = sb("rows", [2, 6, D], F32)
            Bgrow, Brows = Buf("grow"), Buf("rows")
            pg.dma("sp", grow[:], grow_d, writes=[Bgrow])
            pg.op("dve", lambda e: e.scalar_tensor_tensor(out=rows[:, 0, :], in0=mod[:, 1024:2048], scalar=1.0, in1=grow[:, 0, :],
                                                          op0=ALU.add, op1=ALU.mult), reads=[Bmod, Bgrow], writes=[Brows])
            pg.op("dve", lambda e: e.tensor_copy(rows[:, 1, :], mod[:, 0:1024]), reads=[Bmod], writes=[Brows])
            pg.op("dve", lambda e: e.tensor_tensor(out=rows[:, 2, :], in0=mod[:, 2048:3072], in1=grow[:, 1, :], op=ALU.mult),
                  reads=[Bmod, Bgrow], writes=[Brows])
            pg.op("dve", lambda e: e.scalar_tensor_tensor(out=rows[:, 3, :], in0=mod[:, 4096:5120], scalar=1.0, in1=grow[:, 2, :],
                                                          op0=ALU.add, op1=ALU.mult), reads=[Bmod, Bgrow], writes=[Brows])
            pg.op("dve", lambda e: e.tensor_copy(rows[:, 4, :], mod[:, 3072:4096]), reads=[Bmod], writes=[Brows])
            pg.op("dve", lambda e: e.tensor_tensor(out=rows[:, 5, :], in0=mod[:, 5120:6144], in1=grow[:, 3, :], op=ALU.mult),
                  reads=[Bmod, Bgrow], writes=[Brows])
            pg.dma("sp", rows_d, rows[:], reads=[Brows])

            lamv = sb("lamv", [1, 4, 64], F32)
            lj = sb("lj", [1, 64], F32)
            ls = sb("ls", [1, 4], F32)
            Blam = Buf("lam")
            pg.dma("sp", lamv[:], lamv_d, writes=[Blam])
            for t in range(2):
                pg.op("dve", lambda e, t=t: e.scalar_tensor_tensor(out=lj[:], in0=lamv[:, 2 * t, :], scalar=1.0, in1=lamv[:, 2 * t + 1, :],
                                                                 op0=ALU.mult, op1=ALU.mult, accum_out=ls[:, t:t + 1]),
                      reads=[Blam], writes=[Blam])
            pg.op("act", lambda e: e.activation(out=ls[:, 2:4], in_=ls[:, 0:2], func=AF.Exp), reads=[Blam], writes=[Blam])
            pg.op("dve", lambda e: e.scalar_tensor_tensor(out=ls[:, 1:2], in0=ls[:, 2:3], scalar=LAM_INIT, in1=ls[:, 3:4],
                                                          op0=ALU.add, op1=ALU.subtract), reads=[Blam], writes=[Blam])
            pg.op("dve", lambda e: e.tensor_scalar(out=ls[:, 0:1], in0=ls[:, 1:2], scalar1=-1.0, scalar2=None, op0=ALU.mult),
                  reads=[Blam], writes=[Blam])
            pg.dma("sp", lam_d, ls[:, 0:2], reads=[Blam])
            Bnegl = Buf("negl")
            pg.dma("sp", negl[:], lam_d.broadcast_to([128, 2]), reads=[Blam], writes=[Bnegl])

            rb = sb("rb", [NB, 4], F32)
            rbr = sb("rbr", [NB, 4], F32)
            rbp = [sb("rbp%d" % i, [NB, 4], BF16) for i in range(3)]
            Brb = Buf("rb")
            pg.dma("sp", rb[:], relb_d, writes=[Brb])
            pg.op("dve", lambda e: e.tensor_copy(rbp[0][:], rb[:]), reads=[Brb], writes=[Brb])
            pg.op("dve", lambda e: e.tensor_tensor(out=rbr[:], in0=rb[:], in1=rbp[0][:], op=ALU.subtract), reads=[Brb], writes=[Brb])
            pg.op("dve", lambda e: e.tensor_copy(rbp[1][:], rbr[:]), reads=[Brb], writes=[Brb])
            pg.op("dve", lambda e: e.tensor_tensor(out=rbr[:], in0=rbr[:], in1=rbp[1][:], op=ALU.subtract), reads=[Brb], writes=[Brb])
            pg.op("dve", lambda e: e.tensor_copy(rbp[2][:], rbr[:]), reads=[Brb], writes=[Brb])
            emat = sb("emat", [NB, ULEN], BF16)
            uout = sb("uout", [4, 3, ULEN], BF16)
            ufo = sb("ufo", [4, 132], F32)
            Bem, Buo = Buf("emat"), Buf("uout")
            specs = [("main", emain_d, ULEN, ub_d)]
            for jb in jobs:
                specs.append(("wrap", ewrap_d[jb["name"]], WLEN, uw_d[jb["name"]]))
                specs.append(("wrap", ewrap2_d[jb["name"]], WLEN, uw2_d[jb["name"]]))
            for jb in jobs:
                specs.append(("far", efar_d[jb["name"]], jb["NC"] + 1, ufar_d[jb["name"]]))
            pbank = 2
            for kind, esrc, L, dst in specs:
                pg.dma("sp", emat[:, 0:L], esrc, writes=[Bem])
                if kind != "far":
                    for part in range(3):
                        for s0 in range(0, L, 512):
                            w = min(512, L - s0)
                            bk = 2 + (pbank % 2)
                            pbank += 1
                            pg.op("pe", lambda e, bk=bk, part=part, s0=s0, w=w: e.matmul(ps[0:4, bk, 0:w], lhsT=rbp[part][:, :],
                                                                                       rhs=emat[:, s0:s0 + w], start=True, stop=True),
                                  reads=[Bem, Brb], writes=[PB[bk]])
                            pg.op("dve", lambda e, bk=bk, part=part, s0=s0, w=w: e.tensor_copy(uout[:, part, s0:s0 + w], ps[0:4, bk, 0:w]),
                                  reads=[PB[bk]], writes=[Buo])
                    pg.dma("sp", dst.rearrange("t h l -> h t l"), uout[:, :, 0:L], reads=[Buo])
                else:
                    bk = 2 + (pbank % 2)
                    pbank += 1

                    def mmf(e, bk=bk, L=L):
                        ins = None
                        for part in range(3):
                            ins = e.matmul(ps[0:4, bk, 0:L], lhsT=rbp[part][:, :], rhs=emat[:, 0:L], start=(part == 0), stop=(part == 2))
                        return ins
                    pg.op("pe", mmf, reads=[Bem, Brb], writes=[PB[bk]])
                    pg.op("dve", lambda e, bk=bk, L=L: e.tensor_copy(ufo[:, 0:L], ps[0:4, bk, 0:L]), reads=[PB[bk]], writes=[Buo])
                    pg.dma("sp", dst.rearrange("o (h l) -> (o h) l", h=4), ufo[:, 0:L], reads=[Buo])
            for jb in jobs:
                n = jb["name"]
                pg.dma("sp", farb[n][:].rearrange("p h l -> p (h l)"), ufar_d[n].broadcast_to([128, 4 * (jb["NC"] + 1)]),
                       reads=[Buo], writes=[Bc])

            pieces = []
            for k in range(8):
                pieces.append((w_in_d[k * 128:(k + 1) * 128, :], wb_in[k * 128:(k + 1) * 128, :], WIN))
            NBUF = 2
            wcf = [sb("wcf%d" % i, [128, WIN], F32) for i in range(NBUF)]
            wcb = [sb("wcb%d" % i, [128, WIN], BF16) for i in range(NBUF)]
            Bwcf = [Buf("wcf%d" % i) for i in range(NBUF)]
            Bwcb = [Buf("wcb%d" % i) for i in range(NBUF)]
            for n, (src, dst, w) in enumerate(pieces):
                i = n % NBUF
                pg.dma("sp", wcf[i][:, 0:w], src, writes=[Bwcf[i]])
                ce = ["dve", "pool", "act"][n % 3]
                if ce == "act":
                    pg.op(ce, lambda e, i=i, w=w: e.activation(out=wcb[i][:, 0:w], in_=wcf[i][:, 0:w], func=AF.Copy), reads=[Bwcf[i]], writes=[Bwcb[i]])
                else:
                    pg.op(ce, lambda e, i=i, w=w: e.tensor_copy(wcb[i][:, 0:w], wcf[i][:, 0:w]), reads=[Bwcf[i]], writes=[Bwcb[i]])
                pg.dma("pool", dst, wcb[i][:, 0:w], reads=[Bwcb[i]])
            pg.end()

        for jb in jobs:
            name, N, nq = jb["name"], jb["N"], jb["nq"]
            sc = S[name]
            cos_d, sin_d = rope_d[name]
            with contextlib.ExitStack() as st:
                def sb(nm, shape, dt):
                    return st.enter_context(nc.sbuf_tensor("s1_" + nm + name, list(shape), dt))
                PB = new_ps()
                pg.begin()
                win = sb("win", [128, 8, WIN], BF16)
                Bwin = Buf("win")
                for k in range(8):
                    pg.dma("sp", win[:, k, :], wb_in[k * 128:(k + 1) * 128, :], writes=[Bwin])
                A1 = sb("A1", [128, D], F32)
                SH1 = sb("SH1", [128, D], F32)
                Bmodt = Buf("modt")
                pg.dma("sp", A1[:], rows_d[jb["b"], 0:1, :].broadcast_to([128, D]), writes=[Bmodt])
                pg.dma("sp", SH1[:], rows_d[jb["b"], 1:2, :].broadcast_to([128, D]), writes=[Bmodt])
                NXB = 2
                xt = [sb("xt%d" % i, [128, 4, D], F32) for i in range(NXB)]
                Bxt = [Buf("xt%d" % i) for i in range(NXB)]
                rt = [(sb("cos%d" % i, [128, 512], F32), sb("sin%d" % i, [128, 512], F32)) for i in range(2)]
                Brt = [Buf("rt%d" % i) for i in range(2)]
                junk = sb("junk", [128, D], F32)
                tt = [sb("tt%d" % i, [128, D], F32) for i in range(2)]
                hTs = [sb("hT%d" % i, [128, 8, 512], BF16) for i in range(2)]
                Bjunk, Bss, Brstd = Buf("junk"), Buf("ss"), Buf("rstd")
                Btt = [Buf("tt%d" % i) for i in range(2)]
                Bhb = [Buf("hb%d" % i) for i in range(2)]
                BhTs = [[Buf("hT%d_%d" % (j, i)) for i in range(4)] for j in range(2)]
                asb = sb("asb", [128, 512], F32)
                sq = sb("sq", [128, 512], F32)
                sqh = sb("sqh", [128, 512], BF16)
                sqm = sb("sqm", [128, 512], BF16)
                rs = sb("rs", [128, 512], F32)
                t1 = sb("t1", [128, 512], F32)
                t2 = sb("t2", [128, 512], F32)
                Basb, Bsq, Bsqh, Bsqm, Brs, Bt1, Bt2 = [Buf(x) for x in ["asb", "sq", "sqh", "sqm", "rs", "t1", "t2"]]
                kta = [sb("kta%d" % i, [128, 512], BF16) for i in range(2)]
                ktd = [sb("ktd%d" % i, [128, 4, 512], BF16) for i in range(2)]
                qta = [sb("qta%d" % i, [128, 4, 512], BF16) for i in range(2)]
                qtd = [sb("qtd%d" % i, [128, 4, 512], BF16) for i in range(2)]
                va = [sb("va%d" % i, [128, 4, 2, 65], BF16) for i in range(2)]
                vd = [sb("vd%d" % i, [128, 4, 512], BF16) for i in range(2)]
                Bkta = [Buf("kta%d" % i) for i in range(2)]
                Bktd = [Buf("ktd%d" % i) for i in range(2)]
                Bqta = [Buf("qta%d" % i) for i in range(2)]
                Bqtd = [Buf("qtd%d" % i) for i in range(2)]
                Bva = [Buf("va%d" % i) for i in range(2)]
                Bvd = [Buf("vd%d" % i) for i in range(2)]
                for i in range(2):
                    pg.op("pool", lambda e, i=i: e.memset(va[i][:], 1.0), writes=[Bva[i]])
                psT = [ps[:, i, :].bitcast(BF16) for i in range(2)]

                nr_cnt = [0]
                sqh2 = [sqh, sb("sqh_b", [128, 512], BF16)]
                sqm2 = [sqm, sb("sqm_b", [128, 512], BF16)]
                t12 = [t1, sb("t1_b", [128, 512], F32)]
                Bsqh2 = [Bsqh, Buf("sqh_b")]
                Bsqm2 = [Bsqm, Buf("sqm_b")]
                Bt12 = [Bt1, Buf("t1_b")]

                def normrope(pa, pb, gi, gpi, outap, scale, ri, outbuf):
                    cs, sn = rt[ri]
                    j = nr_cnt[0] % 2
                    nr_cnt[0] += 1
                    sqh_, sqm_, t1_ = sqh2[j], sqm2[j], t12[j]
                    Bsqh_, Bsqm_, Bt1_ = Bsqh2[j], Bsqm2[j], Bt12[j]
                    pg.op("act", lambda e: e.activation(out=asb[:], in_=psb(pa), func=AF.Copy), reads=[PB[pa]], writes=[Basb])
                    pg.op("dve", lambda e: e.scalar_tensor_tensor(out=t2[:], in0=psb(pb), scalar=gcols[:, gpi:gpi + 1], in1=sn[:], op0=ALU.mult, op1=ALU.mult),
                          reads=[PB[pb], Bc, Brt[ri]], writes=[Bt2])
                    pg.op("dve", lambda e: e.tensor_tensor(out=sq[:], in0=asb[:], in1=asb[:], op=ALU.mult), reads=[Basb], writes=[Bsq])
                    pg.op("dve", lambda e: e.tensor_copy(sqh_[:], sq[:]), reads=[Bsq], writes=[Bsqh_])
                    pg.op("pool", lambda e: e.tensor_tensor(out=sq[:], in0=sq[:], in1=sqh_[:], op=ALU.subtract), reads=[Bsq, Bsqh_], writes=[Bsq])
                    pg.op("pool", lambda e: e.tensor_copy(sqm_[:], sq[:]), reads=[Bsq], writes=[Bsqm_])
                    pg.op("dve", lambda e: e.scalar_tensor_tensor(out=t1_[:], in0=asb[:], scalar=gcols[:, gi:gi + 1], in1=cs[:], op0=ALU.mult, op1=ALU.mult),
                          reads=[Basb, Bc, Brt[ri]], writes=[Bt1_])
                    pg.op("pool", lambda e: e.tensor_tensor(out=t1_[:], in0=t1_[:], in1=t2[:], op=ALU.add), reads=[Bt1_, Bt2], writes=[Bt1_])

                    def part2():
                        def mmss(e):
                            e.matmul(psb(7), lhsT=blk64, rhs=sqh_[:], start=True, stop=False)
                            return e.matmul(psb(7), lhsT=blk64, rhs=sqm_[:], start=False, stop=True)
                        pg.op("pe", mmss, reads=[Bsqh_, Bsqm_, Bc], writes=[PB[7]])
                        pg.op("act", lambda e: e.activation(out=rs[:], in_=psb(7), func=AF.Sqrt, bias=epsc[:], scale=1.0), reads=[PB[7], Bc], writes=[Brs])
                        pg.op("dve", lambda e: e.reciprocal(out=rs[:], in_=rs[:]), reads=[Brs], writes=[Brs])
                        pg.op("dve", lambda e: e.scalar_tensor_tensor(out=outap, in0=t1_[:], scalar=scale, in1=rs[:], op0=ALU.mult, op1=ALU.mult),
                              reads=[Bt1_, Brs], writes=[outbuf])
                    return part2

                def proj(bank, ch, hT, BhT):
                    def f(e):
                        ins = None
                        for k in range(8):
                            ins = e.matmul(psb(bank), lhsT=win[:, k, ch * 128:(ch + 1) * 128], rhs=hT[:, k, :], start=(k == 0), stop=(k == 7))
                        return ins
                    pg.op("pe", f, reads=[Bwin] + BhT, writes=[PB[bank]])

                ntiles = N // 512
                hb4 = [[sb("hbq%d_%d" % (j, i), [128, D], BF16) for i in range(4)] for j in range(1)][0]
                Bhb4 = [Buf("hbq%d" % i) for i in range(4)]
                ssq = [sb("ssq%d" % i, [128, 4], F32) for i in range(2)]
                rsq = [sb("rsq%d" % i, [128, 4], F32) for i in range(2)]
                Bssq = [[Buf("ssq%d_%d" % (j, i)) for i in range(4)] for j in range(2)]

                def xload(t):
                    xi = t % NXB
                    ti = t % 2
                    pg.dma("sp", xt[xi][:], x_in[name][t * 512:(t + 1) * 512, :].rearrange("(s p) d -> p s d", p=128), writes=[Bxt[xi]])
                    pg.dma("sp", rt[ti][0][:], cos_d[:, t * 512:(t + 1) * 512], writes=[Brt[ti]])
                    pg.dma("sp", rt[ti][1][:], sin_d[:, t * 512:(t + 1) * 512], writes=[Brt[ti]])

                def prep_sub(t, s):
                    xi = t % NXB
                    ti = t % 2
                    i = s % 2
                    sq_, rq_, B_ = ssq[ti], rsq[ti], Bssq[ti][s]
                    pg.op("dve", lambda e: e.scalar_tensor_tensor(out=junk[:], in0=xt[xi][:, s, :], scalar=1.0 / D, in1=xt[xi][:, s, :],
                                                                  op0=ALU.mult, op1=ALU.mult, accum_out=sq_[:, s:s + 1]),
                          reads=[Bxt[xi]], writes=[Bjunk, B_])
                    pg.op("dve", lambda e: e.tensor_scalar(out=rq_[:, s:s + 1], in0=sq_[:, s:s + 1], scalar1=EPS, scalar2=None, op0=ALU.add), reads=[B_], writes=[B_])
                    pg.op("pool", lambda e: e.tensor_tensor(out=rq_[:, s:s + 1], in0=rq_[:, s:s + 1], in1=nhalf[:, 0:1], op=ALU.pow), reads=[B_, Bc], writes=[B_])
                    pg.op("dve", lambda e: e.scalar_tensor_tensor(out=tt[i][:], in0=xt[xi][:, s, :], scalar=rq_[:, s:s + 1], in1=A1[:],
                                                                  op0=ALU.mult, op1=ALU.mult),
                          reads=[Bxt[xi], B_, Bmodt], writes=[Btt[i]])
                    pg.op("pool", lambda e: e.tensor_tensor(out=hb4[s][:], in0=tt[i][:], in1=SH1[:], op=ALU.add),
                          reads=[Btt[i], Bmodt], writes=[Bhb4[s]])

                def trans(t, s):
                    ti = t % 2
                    i = s % 2
                    hT, BhT = hTs[ti], BhTs[ti]

                    def tr(e):
                        ins = None
                        for k in range(8):
                            ins = e.transpose(out=psT[i][:, k * 128:(k + 1) * 128], in_=hb4[s][:, k * 128:(k + 1) * 128], identity=ident)
                        return ins
                    pg.op("pe", tr, reads=[Bhb4[s], Bc], writes=[PB[i]])
                    pg.op("act", lambda e: e.activation(out=hT[:, :, s * 128:(s + 1) * 128],
                                                        in_=psT[i].rearrange("p (k t) -> p k t", k=8), func=AF.Copy),
                          reads=[PB[i]], writes=[BhT[s]])

                def vpair(t, s):
                    ti = t % 2
                    hT, BhT = hTs[ti], BhTs[ti]

                    def mmv(e):
                        ins = None
                        for k in range(8):
                            ins = e.matmul(psb(6), lhsT=hT[:, k, s * 128:(s + 1) * 128], rhs=win[:, k, 2432:2944], start=(k == 0), stop=(k == 7))
                        return ins

                    def mmva(e):
                        ins = None
                        for k in range(8):
                            ins = e.matmul(ps[:, 5, 0:128], lhsT=hT[:, k, s * 128:(s + 1) * 128], rhs=win[:, k, 2304:2432], start=(k == 0), stop=(k == 7))
                        return ins
                    pg.op("pe", mmv, reads=[Bwin, BhT[s]], writes=[PB[6]])
                    pg.op("act", lambda e: e.activation(out=vd[ti][:, s, :], in_=psb(6), func=AF.Copy), reads=[PB[6]], writes=[Bvd[ti]])
                    pg.op("pe", mmva, reads=[Bwin, BhT[s]], writes=[PB[5]])
                    pg.op("dve", lambda e: e.tensor_copy(va[ti][:, s, :, 0:64], ps[:, 5, 0:128].rearrange("p (a b) -> p a b", a=2)),
                          reads=[PB[5]], writes=[Bva[ti]])

                def kd_or_qd(t, h, ch0, dst, Bdst, scale):
                    ti = t % 2
                    hT, BhT = hTs[ti], BhTs[ti]
                    bk = 4 + (h % 2)
                    proj(bk, ch0 + h, hT, BhT)
                    if h % 2 == 0:
                        pg.op("act", lambda e: e.activation(out=dst[ti][:, h, :], in_=psb(bk), func=AF.Copy, scale=scale), reads=[PB[bk]], writes=[Bdst[ti]])
                    else:
                        pg.op("dve", lambda e: e.tensor_scalar(out=dst[ti][:, h, :], in0=psb(bk), scalar1=scale, scalar2=None, op0=ALU.mult),
                              reads=[PB[bk]], writes=[Bdst[ti]])

                xload(0)
                for s in range(4):
                    prep_sub(0, s)
                for s in range(4):
                    trans(0, s)
                for t in range(ntiles):
                    own = (t * 512 < nq)
                    ti = t % 2
                    hT, BhT = hTs[ti], BhTs[ti]
                    nxt = t + 1 < ntiles
                    if nxt:
                        xload(t + 1)
                    proj(2, 8, hT, BhT)
                    proj(3, 9, hT, BhT)
                    kpart2 = normrope(2, 3, 2, 3, kta[ti][:], 1.0, ti, Bkta[ti])
                    if nxt:
                        prep_sub(t + 1, 0)
                    for h in range(4):
                        kd_or_qd(t, h, 14, ktd, Bktd, 1.0)
                        if h == 1:
                            kpart2()
                            pg.dma("pool", sc["KTa"][:, t * 512:(t + 1) * 512], kta[ti][:], reads=[Bkta[ti]])
                    pg.dma("pool", sc["KTd"][:, :, t * 512:(t + 1) * 512].rearrange("h p t -> p h t"), ktd[ti][:], reads=[Bktd[ti]])
                    if nxt:
                        prep_sub(t + 1, 1)
                    vpair(t, 0)
                    vpair(t, 1)
                    if nxt:
                        prep_sub(t + 1, 2)
                    vpair(t, 2)
                    vpair(t, 3)
                    pg.dma("pool", sc["Vd"][t * 512:(t + 1) * 512, :].rearrange("(s p) c -> p s c", p=128), vd[ti][:], reads=[Bvd[ti]])
                    pg.dma("pool", sc["Va"][t * 512:(t + 1) * 512, :].rearrange("(s p) c -> p s c", p=128),
                           va[ti][:].rearrange("p s a b -> p s (a b)"), reads=[Bva[ti]])
                    if nxt:
                        prep_sub(t + 1, 3)
                    if own:
                        prev2 = None
                        for g in range(4):
                            proj(2, g, hT, BhT)
                            proj(3, 4 + g, hT, BhT)
                            p2 = normrope(2, 3, 0, 1, qta[ti][:, g, :], 0.125, ti, Bqta[ti])
                            if prev2 is not None:
                                prev2()
                            prev2 = p2
                        for h in range(4):
                            kd_or_qd(t, h, 10, qtd, Bqtd, 0.125)
                            if h == 1:
                                prev2()
                                pg.dma("pool", sc["QTa"][:, :, t * 512:(t + 1) * 512].rearrange("g p t -> p g t"), qta[ti][:], reads=[Bqta[ti]])
                        pg.dma("pool", sc["QTd"][:, :, t * 512:(t + 1) * 512].rearrange("h p t -> p h t"), qtd[ti][:], reads=[Bqtd[ti]])
                    if nxt:
                        for s in range(4):
                            trans(t + 1, s)
                pg.end()

        late_pieces = []
        for k in range(8):
            late_pieces.append((w_out_d[k * 128:(k + 1) * 128, :], wb_out[k * 128:(k + 1) * 128, :], D))
        for k in range(8):
            for c0_ in range(0, 2 * DFF, 1024):
                w_ = min(1024, 2 * DFF - c0_)
                late_pieces.append((w_gu_d[k * 128:(k + 1) * 128, c0_:c0_ + w_], wb_gu[k * 128:(k + 1) * 128, c0_:c0_ + w_], w_))
        for k in range(22):
            late_pieces.append((w_down_d[k * 128:(k + 1) * 128, :], wb_down[k * 128:(k + 1) * 128, :], D))

        for jb in jobs:
            name, N, nq, NC = jb["name"], jb["N"], jb["nq"], jb["NC"]
            sc = S[name]
            with contextlib.ExitStack() as st:
                def sb(nm, shape, dt):
                    return st.enter_context(nc.sbuf_tensor("s2_" + nm + name, list(shape), dt))
                PB = new_ps()
                pg.begin()
                KT = sb("KT", [128, N], BF16)
                VV = sb("VV", [128, NC, 130], BF16)
                QT = sb("QT", [128, 4, nq], BF16)
                NG = 8
                cpg = NC // NG
                BKV = [Buf("kv%d" % i) for i in range(NG)]
                BQ = Buf("QT")
                pT = [sb("pT%d" % i, [128, 1024], BF16) for i in range(3)]
                BpT = [Buf("pT%d" % i) for i in range(3)]
                osb = [sb("osb%d" % i, [128, 512], F32) for i in range(2)]
                Bosb = [Buf("osb%d" % i) for i in range(2)]
                zr = sb("zr", [128, 2, 512], F32)
                Bzr = Buf("zr")
                bcz = sb("bcz", [128, 2, 512], F32)
                Bbcz = Buf("bcz")
                onrm = sb("onrm", [128, 512], BF16)
                Bonrm = Buf("onrm")
                dd = sb("dd", [128, 512], F32)
                dsq = sb("dsq", [128, 512], F32)
                dsqh = sb("dsqh", [128, 512], BF16)
                dsqm = sb("dsqm", [128, 512], BF16)
                drs = sb("drs", [128, 512], F32)
                Bdd, Bdsq, Bdsqh, Bdsqm, Bdrs = [Buf(x) for x in ["dd", "dsq", "dsqh", "dsqm", "drs"]]
                TTs = [sb("TT%d" % i, [128, 8, 512], F32) for i in range(2)]
                BTTs = [Buf("TT%d" % i) for i in range(2)]
                pending = []
                gcn = [0]
                dtile = [0]
                dd2 = [dd, sb("dd_b", [128, 512], F32)]
                dsqh2 = [dsqh, sb("dsqh_b", [128, 512], BF16)]
                dsqm2 = [dsqm, sb("dsqm_b", [128, 512], BF16)]
                drs2 = [drs, sb("drs_b", [128, 512], F32)]
                Bdd2 = [Bdd, Buf("dd_b")]
                Bdsqh2 = [Bdsqh, Buf("dsqh_b")]
                Bdsqm2 = [Bdsqm, Buf("dsqm_b")]
                Bdrs2 = [Bdrs, Buf("drs_b")]
                hk = [sb("hk%d" % i, [128, 3, 512], BF16) for i in range(2)]
                Bhk = [Buf("hk%d" % i) for i in range(2)]
                zdram = sc["zrow"]
                Bzd = [Buf("zd0"), Buf("zd1")]

                def load_kv(kt_src, v_src, vw):
                    for gi in range(NG):
                        c0 = gi * cpg
                        pg.dma("sp", KT[:, c0 * 128:(c0 + cpg) * 128], kt_src[:, c0 * 128:(c0 + cpg) * 128], writes=[BKV[gi]])
                        pg.dma("sp", VV[:, c0:c0 + cpg, 0:vw], v_src[c0 * 128:(c0 + cpg) * 128, :].rearrange("(c p) w -> p c w", p=128), writes=[BKV[gi]])

                def attn_tile(units, lhs_v, near, bias_of, epilogue, TT=None, BTT=None):
                    def qk(c):
                        par = c % 2

                        def f(e):
                            ins = None
                            for u in range(2):
                                ins = e.matmul(psb(2 * par + u), lhsT=KT[64 * u:64 * u + 64, c * 128:(c + 1) * 128], rhs=units[u], start=True, stop=True)
                            return ins
                        pg.op("pe", f, reads=[BKV[c // cpg], BQ], writes=[PB[2 * par], PB[2 * par + 1]])
                        if c in near:
                            ti = near[c]
                            pg.op("dve", lambda e: e.tensor_tensor(out=ps[:, 2 * par:2 * par + 2, :], in0=ps[:, 2 * par:2 * par + 2, :],
                                                                 in1=TT[:, ti:ti + 1, :].to_broadcast([128, 2, 512]), op=ALU.add),
                                  reads=[BTT], writes=[PB[2 * par], PB[2 * par + 1]])

                    def ex(c):
                        par = c % 2
                        p3 = c % 3
                        b = bias_of(c)
                        pg.op("act", lambda e: e.activation(out=pT[p3][:], in_=ps[:, 2 * par:2 * par + 2, :].rearrange("p a b -> p (a b)"),
                                                          func=AF.Exp, bias=(b if b is not None else zero1[:])),
                              reads=[PB[2 * par], PB[2 * par + 1], Bc], writes=[BpT[p3]])

                    def pv(c):
                        par = c % 3
                        dm = diff_mode[0]

                        def f(e):
                            ins = None
                            for u in range(2):
                                lv, m = lhs_v(u, c)
                                ins = e.matmul(ps[0:m, 4 + u, :], lhsT=lv, rhs=pT[par][:, u * 512:(u + 1) * 512], start=(c == 0), stop=(c == NC - 1))
                            if dm:
                                for u in range(2):
                                    ins = e.matmul(ps[32 * u:32 * u + 1, 6, :], lhsT=onesb[:, 0:1], rhs=pT[par][:, u * 512:(u + 1) * 512],
                                                   start=(c == 0), stop=(c == NC - 1), skip_group_check=True)
                            return ins
                        w = [PB[4], PB[5]] + ([PB[6]] if diff_mode[0] else [])
                        pg.op("pe", f, reads=[BKV[c // cpg], BpT[par], Bc], writes=w)
                    qk(0)
                    qk(1)
                    for c in range(NC):
                        ex(c)
                        if c + 2 < NC:
                            qk(c + 2)
                        pv(c)
                        gcn[0] += 1
                        pending.sort(key=lambda p: (p[0], p[1]))
                        while pending and pending[0][0] <= gcn[0]:
                            pending.pop(0)[2]()
                    epilogue()

                diff_mode = [False]
                if name == "P":
                    lcf = [sb("lcf%d" % i, [128, 1024], F32) for i in range(2)]
                    lcb = [sb("lcb%d" % i, [128, 1024], BF16) for i in range(2)]
                    Blcf = [Buf("lcf%d" % i) for i in range(2)]
                    Blcb = [Buf("lcb%d" % i) for i in range(2)]
                lp_state = [0]

                def emit_late(n):
                    while n > 0 and name == "P" and lp_state[0] < len(late_pieces):
                        src, dst, w = late_pieces[lp_state[0]]
                        i = lp_state[0] % 2
                        lp_state[0] += 1
                        n -= 1
                        pg.dma("sp", lcf[i][:, 0:w], src, writes=[Blcf[i]])
                        pg.op("pool", lambda e, i=i, w=w: e.tensor_copy(lcb[i][:, 0:w], lcf[i][:, 0:w]), reads=[Blcf[i]], writes=[Blcb[i]])
                        pg.dma("pool", dst, lcb[i][:, 0:w], reads=[Blcb[i]])
                def tt_dma(h, pair):
                    for ti in (2 * pair, 2 * pair + 1):
                        i = ti % 2
                        if ti < 6:
                            dofs = (ti - 1) * 128
                            base = 512 - dofs
                            for part in range(3):
                                src = bass.AP(ub_d.tensor, ub_d[part, h, base:base + 1].offset, [[1, 128], [1, 512]])
                                pg.dma("sp", hk[i][:, part, :], src, writes=[Bhk[i]])
                        else:
                            uw = uw_d[name] if ti == 6 else uw2_d[name]
                            for part in range(3):
                                src = bass.AP(uw.tensor, uw[part, h, 0:1].offset, [[1, 128], [1, 512]])
                                pg.dma("sp", hk[i][:, part, :], src, writes=[Bhk[i]])

                def tt_pe(h, pair):
                    TT, BTT = TTs[h % 2], BTTs[h % 2]
                    for ti in (2 * pair, 2 * pair + 1):
                        i = ti % 2

                        def mmT(e, i=i):
                            ins = None
                            for part in range(3):
                                ins = e.matmul(psb(7), lhsT=antiid, rhs=hk[i][:, part, :], start=(part == 0), stop=(part == 2))
                            return ins
                        pg.op("pe", mmT, reads=[Bhk[i], Bc], writes=[PB[7]])
                        pg.op("dve", lambda e, ti=ti, TT=TT: e.tensor_copy(TT[:, ti, :], psb(7)), reads=[PB[7]], writes=[BTT])

                def tt_steps(h):
                    st_ = [lambda: tt_dma(h, 0)]
                    for p_ in range(1, 4):
                        st_.append(lambda p_=p_: (tt_pe(h, p_ - 1), tt_dma(h, p_)))
                    st_.append(lambda: tt_pe(h, 3))
                    return st_
                tt_sched = []
                load_kv(sc["KTa"], sc["Va"], 130)
                for g in range(4):
                    pg.dma("sp", QT[:, g, :], sc["QTa"][g], writes=[BQ])
                for st_ in tt_steps(0):
                    st_()
                for qt in range(nq // 128):
                    units = [QT[64 * u:64 * u + 64, :, qt * 128:(qt + 1) * 128] for u in range(2)]

                    def lhs_v(u, c):
                        return VV[:, c, u * 65:(u + 1) * 65], 65

                    def epi(qt=qt):
                        for u in range(2):
                            pg.op("dve", lambda e, u=u: e.tensor_copy(osb[u][0:65, :], ps[0:65, 4 + u, :]), reads=[PB[4 + u]], writes=[Bosb[u]])
                        for u in range(2):
                            pg.op("dve", lambda e, u=u: e.reciprocal(out=zr[64:65, u, :], in_=osb[u][64:65, :]), reads=[Bosb[u]], writes=[Bzr])
                            pg.dma("pool", zdram[u, 0:1, :], zr[64:65, u, :], reads=[Bzr], writes=[Bzd[u]])
                            pg.dma("pool", bcz[0:64, u, :], zdram[u, 0:1, :].broadcast_to([64, 512]), reads=[Bzd[u]], writes=[Bbcz])
                            pg.op("dve", lambda e, u=u: e.tensor_tensor(out=onrm[0:64, :], in0=osb[u][0:64, :], in1=bcz[0:64, u, :], op=ALU.mult),
                                  reads=[Bosb[u], Bbcz], writes=[Bonrm])
                            pg.dma("pool", sc["outT"][u * 256:(u + 1) * 256, qt * 128:(qt + 1) * 128].rearrange("(g d) t -> d g t", g=4),
                                   onrm[0:64, :].rearrange("d (g t) -> d g t", g=4), reads=[Bonrm])
                    attn_tile(units, lhs_v, {}, lambda c: None, epi)
                    emit_late(3)
                emit_late(10 ** 6)
                diff_mode[0] = True

                for h in range(4):
                    load_kv(sc["KTd"][h], sc["Vd"][:, h * 128:(h + 1) * 128], 128)
                    pg.dma("sp", QT[:, 0, :], sc["QTd"][h], writes=[BQ])
                    while tt_sched:
                        tt_sched.pop(0)()
                    for qt in range(nq // 512):
                        units = [QT[64 * u:64 * u + 64, 0, qt * 512:(qt + 1) * 512] for u in range(2)]
                        c0 = qt * 4
                        near = {}
                        for ti in range(6):
                            c = c0 - 1 + ti
                            if 0 <= c < NC:
                                near[c] = ti
                        if qt == 0:
                            near[NC - 1] = 6
                        if qt == nq // 512 - 1:
                            near[nq // 128] = 7
                        fb = farb[name]

                        def bias_of(c, near=near, c0=c0, h=h, fb=fb):
                            if c in near:
                                return None
                            if c < c0:
                                return fb[:, h, NC:NC + 1]
                            return fb[:, h, c:c + 1]

                        def lhs_v(u, c):
                            return VV[:, c, 0:128], 128

                        def epi(qt=qt, h=h):
                            for u in range(2):
                                pg.op("dve", lambda e, u=u: e.tensor_copy(osb[u][:], psb(4 + u)), reads=[PB[4 + u]], writes=[Bosb[u]])
                            for u in range(2):
                                pg.op("dve", lambda e, u=u: e.tensor_copy(zr[32 * u:32 * u + 1, u, :], ps[32 * u:32 * u + 1, 6, :]), reads=[PB[6]], writes=[Bzr])
                            tp = dtile[0] % 2
                            dtile[0] += 1
                            o0 = max(1, NC * 5 // 32)
                            gap = min(40, NC - 1)
                            base = gcn[0]
                            pending.append((base + o0, 3 * dtile[0], lambda: epi_tail0(tp)))
                            pending.append((base + o0 + gap, 3 * dtile[0] + 1, lambda: epi_tail1(tp)))
                            pending.append((base + o0 + 2 * gap, 3 * dtile[0] + 2, lambda qt=qt, h=h: epi_tail2(qt, h, tp)))

                        def epi_tail0(tp):
                            dd_, dsqh_, dsqm_ = dd2[tp], dsqh2[tp], dsqm2[tp]
                            Bdd_, Bdsqh_, Bdsqm_ = Bdd2[tp], Bdsqh2[tp], Bdsqm2[tp]
                            for u in range(2):
                                pg.op("dve", lambda e, u=u: e.reciprocal(out=zr[32 * u:32 * u + 1, u, :], in_=zr[32 * u:32 * u + 1, u, :]), reads=[Bzr], writes=[Bzr])
                                pg.dma("pool", zdram[u, 1:2, :], zr[32 * u:32 * u + 1, u, :], reads=[Bzr], writes=[Bzd[u]])
                                pg.dma("pool", bcz[:, u, :], zdram[u, 1:2, :].broadcast_to([128, 512]), reads=[Bzd[u]], writes=[Bbcz])
                            pg.op("dve", lambda e: e.tensor_tensor(out=osb[0][:], in0=osb[0][:], in1=bcz[:, 0, :], op=ALU.mult), reads=[Bosb[0], Bbcz], writes=[Bosb[0]])
                            pg.op("pool", lambda e: e.tensor_tensor(out=osb[1][:], in0=osb[1][:], in1=bcz[:, 1, :], op=ALU.mult), reads=[Bosb[1], Bbcz], writes=[Bosb[1]])
                            pg.op("dve", lambda e: e.scalar_tensor_tensor(out=dd_[:], in0=osb[1][:], scalar=negl[:, 0:1], in1=osb[0][:], op0=ALU.mult, op1=ALU.add),
                                  reads=[Bosb[0], Bosb[1], Bnegl], writes=[Bdd_])
                            pg.op("pool", lambda e: e.tensor_tensor(out=dsq[:], in0=dd_[:], in1=dd_[:], op=ALU.mult), reads=[Bdd_], writes=[Bdsq])
                            pg.op("dve", lambda e: e.tensor_copy(dsqh_[:], dsq[:]), reads=[Bdsq], writes=[Bdsqh_])
                            pg.op("pool", lambda e: e.tensor_tensor(out=dsq[:], in0=dsq[:], in1=dsqh_[:], op=ALU.subtract), reads=[Bdsq, Bdsqh_], writes=[Bdsq])
                            pg.op("pool", lambda e: e.tensor_copy(dsqm_[:], dsq[:]), reads=[Bdsq], writes=[Bdsqm_])

                        def epi_tail1(tp):
                            dsqh_, dsqm_, drs_ = dsqh2[tp], dsqm2[tp], drs2[tp]

                            def mmss(e):
                                e.matmul(psb(7), lhsT=o128, rhs=dsqh_[:], start=True, stop=False)
                                return e.matmul(psb(7), lhsT=o128, rhs=dsqm_[:], start=False, stop=True)
                            pg.op("pe", mmss, reads=[Bdsqh2[tp], Bdsqm2[tp], Bc], writes=[PB[7]])
                            pg.op("dve", lambda e: e.tensor_copy(drs_[:], psb(7)), reads=[PB[7]], writes=[Bdrs2[tp]])

                        def epi_tail2(qt, h, tp):
                            dd_, drs_ = dd2[tp], drs2[tp]
                            pg.op("act", lambda e: e.activation(out=drs_[:], in_=drs_[:], func=AF.Sqrt, bias=epsc[:], scale=1.0), reads=[Bdrs2[tp], Bc], writes=[Bdrs2[tp]])
                            pg.op("dve", lambda e: e.reciprocal(out=drs_[:], in_=drs_[:]), reads=[Bdrs2[tp]], writes=[Bdrs2[tp]])
                            pg.op("dve", lambda e: e.scalar_tensor_tensor(out=onrm[:], in0=dd_[:], scalar=gcols[:, 4:5], in1=drs_[:], op0=ALU.mult, op1=ALU.mult),
                                  reads=[Bdd2[tp], Bdrs2[tp], Bc], writes=[Bonrm])
                            pg.dma("pool", sc["outT"][512 + h * 128:512 + (h + 1) * 128, qt * 512:(qt + 1) * 512], onrm[:], reads=[Bonrm])
                        attn_tile(units, lhs_v, near, bias_of, epi, TT=TTs[h % 2], BTT=BTTs[h % 2])
                        if qt == 0 and h + 1 < 4:
                            tt_sched.extend(tt_steps(h + 1))
                        if tt_sched:
                            tt_sched.pop(0)()
                pending.sort(key=lambda p: (p[0], p[1]))
                while pending:
                    pending.pop(0)[2]()
                pg.end()

        TK = 256
        for jb in jobs:
            name, N, nq = jb["name"], jb["N"], jb["nq"]
            sc = S[name]
            with contextlib.ExitStack() as st:
                def sb(nm, shape, dt):
                    return st.enter_context(nc.sbuf_tensor("s3_" + nm + name, list(shape), dt))
                PB = new_ps()
                pg.begin()
                wo = sb("wo", [128, 8, D], BF16)
                Bw = Buf("w3")
                for k in range(8):
                    pg.dma("sp", wo[:, k, :], wb_out[k * 128:(k + 1) * 128, :], writes=[Bw])
                G1 = sb("G1", [128, D], F32)
                A2 = sb("A2", [128, D], F32)
                SH2 = sb("SH2", [128, D], F32)
                Bmodt = Buf("modt")
                for tile_, ri in ((G1, 2), (A2, 3), (SH2, 4)):
                    pg.dma("sp", tile_[:], rows_d[jb["b"], ri:ri + 1, :].broadcast_to([128, D]), writes=[Bmodt])
                xt = [sb("xt%d" % i, [128, 2, D], F32) for i in range(2)]
                Bxts = [[Buf("xt%d_%d" % (i, j)) for j in range(2)] for i in range(2)]
                BhTs = [[Buf("hT%d_%d" % (i, j)) for j in range(2)] for i in range(2)]
                oT = [sb("oT%d" % i, [128, 8, TK], BF16) for i in range(2)]
                BoT = [Buf("oT%d" % i) for i in range(2)]
                junk = sb("junk", [128, D], F32)
                ss = sb("ss", [128, 4], F32)
                rstd = sb("rstd", [128, 4], F32)
                mixs = [sb("mixs%d" % i, [128, D], F32) for i in range(2)]
                tt = [sb("tt%d" % i, [128, D], F32) for i in range(2)]
                hb = [sb("hb%d" % i, [128, D], BF16) for i in range(2)]
                hT = [sb("hT%d" % i, [128, 8, TK], BF16) for i in range(2)]
                Bjunk = Buf("junk")
                Bss = [Buf("ss%d" % i) for i in range(4)]
                Bmixs = [Buf("mixs%d" % i) for i in range(2)]
                Btt = [Buf("tt%d" % i) for i in range(2)]
                Bhb = [Buf("hb%d" % i) for i in range(2)]
                BhT = [Buf("hT%d" % i) for i in range(2)]
                psT = [ps[:, i, :].bitcast(BF16) for i in range(2)]

                def rms_stat(src_ap, src_bufs, col):
                    pg.op("dve", lambda e: e.scalar_tensor_tensor(out=junk[:], in0=src_ap, scalar=1.0 / D, in1=src_ap, op0=ALU.mult, op1=ALU.mult,
                                                                  accum_out=ss[:, col:col + 1]), reads=src_bufs, writes=[Bjunk, Bss[col]])
                    pg.op("dve", lambda e: e.tensor_scalar(out=rstd[:, col:col + 1], in0=ss[:, col:col + 1], scalar1=EPS, scalar2=None, op0=ALU.add),
                          reads=[Bss[col]], writes=[Bss[col]])
                    pg.op("pool", lambda e: e.tensor_tensor(out=rstd[:, col:col + 1], in0=rstd[:, col:col + 1], in1=nhalf[:, 0:1], op=ALU.pow),
                          reads=[Bss[col], Bc], writes=[Bss[col]])

                for t in range(nq // TK):
                    xi = t % 2
                    pg.dma("sp", xt[xi][:], x_in[name][t * TK:(t + 1) * TK, :].rearrange("(s p) d -> p s d", p=128), writes=Bxts[xi])
                    pg.dma("sp", oT[xi][:], sc["outT"][:, t * TK:(t + 1) * TK].rearrange("(k p) t -> p k t", p=128), writes=[BoT[xi]])

                    def chain(s, xi=xi):
                        steps = []

                        def mmo(e):
                            ins = None
                            for hh in range(2):
                                for k in range(8):
                                    ins = e.matmul(psb(2 + 2 * s + hh), lhsT=oT[xi][:, k, s * 128:(s + 1) * 128], rhs=wo[:, k, hh * 512:(hh + 1) * 512],
                                                   start=(k == 0), stop=(k == 7))
                            return ins
                        steps.append(lambda: pg.op("pe", mmo, reads=[BoT[xi], Bw], writes=[PB[2 + 2 * s], PB[3 + 2 * s]]))
                        steps.append(lambda: pg.op("act", lambda e: e.activation(out=mixs[s][:], in_=ps[:, 2 + 2 * s:4 + 2 * s, :].rearrange("p a b -> p (a b)"), func=AF.Copy),
                                                   reads=[PB[2 + 2 * s], PB[3 + 2 * s]], writes=[Bmixs[s]]))
                        steps.append(lambda: rms_stat(mixs[s][:], [Bmixs[s]], s))
                        steps.append(lambda: pg.op("dve", lambda e: e.scalar_tensor_tensor(out=tt[s][:], in0=mixs[s][:], scalar=rstd[:, s:s + 1], in1=G1[:], op0=ALU.mult, op1=ALU.mult),
                                                   reads=[Bmixs[s], Bss[s], Bmodt], writes=[Btt[s]]))
                        steps.append(lambda: pg.op("pool", lambda e: e.tensor_tensor(out=xt[xi][:, s, :], in0=xt[xi][:, s, :], in1=tt[s][:], op=ALU.add),
                                                   reads=[Btt[s], Bxts[xi][s]], writes=[Bxts[xi][s]]))
                        steps.append(lambda: rms_stat(xt[xi][:, s, :], [Bxts[xi][s]], 2 + s))
                        steps.append(lambda: pg.op("dve", lambda e: e.scalar_tensor_tensor(out=tt[s][:], in0=xt[xi][:, s, :], scalar=rstd[:, 2 + s:3 + s], in1=A2[:], op0=ALU.mult, op1=ALU.mult),
                                                   reads=[Bxts[xi][s], Bss[2 + s], Bmodt], writes=[Btt[s]]))
                        steps.append(lambda: pg.op("pool", lambda e: e.tensor_tensor(out=hb[s][:], in0=tt[s][:], in1=SH2[:], op=ALU.add), reads=[Btt[s], Bmodt], writes=[Bhb[s]]))

                        def tr(e):
                            ins = None
                            for k in range(8):
                                ins = e.transpose(out=psT[s][:, k * 128:(k + 1) * 128], in_=hb[s][:, k * 128:(k + 1) * 128], identity=ident)
                            return ins
                        steps.append(lambda: pg.op("pe", tr, reads=[Bhb[s], Bc], writes=[PB[s]]))
                        steps.append(lambda: pg.op("act", lambda e: e.activation(out=hT[xi][:, :, s * 128:(s + 1) * 128], in_=psT[s].rearrange("p (k t) -> p k t", k=8), func=AF.Copy),
                                                   reads=[PB[s]], writes=[BhTs[xi][s]]))
                        return steps
                    c0s, c1s = chain(0), chain(1)
                    for f0, f1 in zip(c0s, c1s):
                        f0()
                        f1()
                    pg.dma("pool", sc["x1"][t * TK:(t + 1) * TK, :].rearrange("(s p) d -> p s d", p=128), xt[xi][:], reads=Bxts[xi])
                    pg.dma("pool", sc["h2T"][:, :, t * TK:(t + 1) * TK].rearrange("k p t -> p k t"), hT[xi][:], reads=BhTs[xi])
                pg.end()

        for jb in jobs:
            name, N, nq = jb["name"], jb["N"], jb["nq"]
            sc = S[name]
            with contextlib.ExitStack() as st:
                def sb(nm, shape, dt):
                    return st.enter_context(nc.sbuf_tensor("s4_" + nm + name, list(shape), dt))
                PB = new_ps()
                pg.begin()
                wgu = sb("wgu", [128, 8, 2 * DFF], BF16)
                wdn = sb("wdn", [128, 22, D], BF16)
                Bw = Buf("w3")
                for k in range(8):
                    pg.dma("sp", wgu[:, k, :], wb_gu[k * 128:(k + 1) * 128, :], writes=[Bw])
                pg.dma("sp", wdn[:], wb_down.rearrange("(k p) n -> p k n", p=128), writes=[Bw])
                G2 = sb("G2", [128, D], F32)
                Bmodt = Buf("modt")
                pg.dma("sp", G2[:], rows_d[jb["b"], 5:6, :].broadcast_to([128, D]), writes=[Bmodt])
                xt = [sb("xt%d" % i, [128, 2, D], F32) for i in range(2)]
                Bxt = [Buf("xt%d" % i) for i in range(2)]
                hT = [sb("hT%d" % i, [128, 8, TK], BF16) for i in range(2)]
                BhT = [Buf("hT%d" % i) for i in range(2)]
                junk = sb("junk", [128, D], F32)
                ss = sb("ss", [128, 2], F32)
                rstd = sb("rstd", [128, 2], F32)
                act_ = sb("act", [128, 22, TK], BF16)
                sg = [sb("sg%d" % i, [128, TK], F32) for i in range(2)]
                fs = [sb("fs%d" % i, [128, D], F32) for i in range(2)]
                tt = [sb("tt%d" % i, [128, D], F32) for i in range(2)]
                Bjunk = Buf("junk")
                Bss = [Buf("ss%d" % i) for i in range(2)]
                Bfs = [Buf("fs%d" % i) for i in range(2)]
                Btt = [Buf("tt%d" % i) for i in range(2)]
                Bact = [Buf("act%d" % i) for i in range(22)]
                Bsg = [Buf("sg%d" % i) for i in range(2)]
                for t in range(nq // TK):
                    xi = t % 2
                    pg.dma("sp", xt[xi][:], sc["x1"][t * TK:(t + 1) * TK, :].rearrange("(s p) d -> p s d", p=128), writes=[Bxt[xi]])
                    pg.dma("sp", hT[xi][:], sc["h2T"][:, :, t * TK:(t + 1) * TK].rearrange("k p t -> p k t"), writes=[BhT[xi]])
                    for j in range(22):
                        i = j % 2

                        def mmg(e, j=j, i=i, xi=xi):
                            ins = None
                            for k in range(8):
                                ins = e.matmul(ps[:, 2 * i, 0:TK], lhsT=wgu[:, k, j * 128:(j + 1) * 128], rhs=hT[xi][:, k, :], start=(k == 0), stop=(k == 7))
                            for k in range(8):
                                ins = e.matmul(ps[:, 2 * i + 1, 0:TK], lhsT=wgu[:, k, DFF + j * 128:DFF + (j + 1) * 128], rhs=hT[xi][:, k, :], start=(k == 0), stop=(k == 7))
                            return ins
                        pg.op("pe", mmg, reads=[Bw, BhT[xi]], writes=[PB[2 * i], PB[2 * i + 1]])
                        pg.op("act", lambda e, i=i: e.activation(out=sg[i][:], in_=ps[:, 2 * i, 0:TK], func=AF.Silu), reads=[PB[2 * i]], writes=[Bsg[i]])
                        pg.op("dve", lambda e, i=i, j=j: e.tensor_tensor(out=act_[:, j, :], in0=sg[i][:], in1=ps[:, 2 * i + 1, 0:TK], op=ALU.mult),
                              reads=[Bsg[i], PB[2 * i + 1]], writes=[Bact[j]])
                    for s in range(2):
                        def mmd(e, s=s):
                            ins = None
                            for hh in range(2):
                                for j in range(22):
                                    ins = e.matmul(psb(4 + 2 * s + hh), lhsT=act_[:, j, s * 128:(s + 1) * 128], rhs=wdn[:, j, hh * 512:(hh + 1) * 512],
                                                   start=(j == 0), stop=(j == 21))
                            return ins
                        pg.op("pe", mmd, reads=[Bw] + Bact, writes=[PB[4 + 2 * s], PB[5 + 2 * s]])
                        pg.op("act", lambda e, s=s: e.activation(out=fs[s][:], in_=ps[:, 4 + 2 * s:6 + 2 * s, :].rearrange("p a b -> p (a b)"), func=AF.Copy),
                              reads=[PB[4 + 2 * s], PB[5 + 2 * s]], writes=[Bfs[s]])
                        pg.op("dve", lambda e, s=s: e.scalar_tensor_tensor(out=junk[:], in0=fs[s][:], scalar=1.0 / D, in1=fs[s][:], op0=ALU.mult, op1=ALU.mult,
                                                                         accum_out=ss[:, s:s + 1]), reads=[Bfs[s]], writes=[Bjunk, Bss[s]])
                        pg.op("dve", lambda e, s=s: e.tensor_scalar(out=rstd[:, s:s + 1], in0=ss[:, s:s + 1], scalar1=EPS, scalar2=None, op0=ALU.add),
                              reads=[Bss[s]], writes=[Bss[s]])
                        pg.op("pool", lambda e, s=s: e.tensor_tensor(out=rstd[:, s:s + 1], in0=rstd[:, s:s + 1], in1=nhalf[:, 0:1], op=ALU.pow),
                              reads=[Bss[s], Bc], writes=[Bss[s]])
                        pg.op("dve", lambda e, s=s: e.scalar_tensor_tensor(out=tt[s][:], in0=fs[s][:], scalar=rstd[:, s:s + 1], in1=G2[:], op0=ALU.mult, op1=ALU.mult),
                              reads=[Bfs[s], Bss[s], Bmodt], writes=[Btt[s]])
                        pg.op("pool", lambda e, s=s, xi=xi: e.tensor_tensor(out=xt[xi][:, s, :], in0=xt[xi][:, s, :], in1=tt[s][:], op=ALU.add),
                              reads=[Btt[s], Bxt[xi]], writes=[Bxt[xi]])
                    pg.dma("pool", y_out[name][t * TK:(t + 1) * TK, :].rearrange("(s p) d -> p s d", p=128), xt[xi][:], reads=[Bxt[xi]])
                pg.end()
    return nc


def _prep_shared(inp, NP, NS):
    f = lambda a: np.ascontiguousarray(np.asarray(a, dtype=np.float32))
    perm = _perm64()
    w_in = f(inp["w_in"])[0]
    o1, o2, o3, o4, o5 = 512, 640, 768, 1280, 1792
    cols = []
    qa_nat = np.array([[(kv * 4 + g) * 64 + d for kv in range(2) for d in range(64)] for g in range(4)])
    qa_prm = np.array([[(kv * 4 + g) * 64 + perm[d] for kv in range(2) for d in range(64)] for g in range(4)])
    cols += list(qa_nat.reshape(-1)) + list(qa_prm.reshape(-1))
    cols += [o1 + kv * 64 + d for kv in range(2) for d in range(64)]
    cols += [o1 + kv * 64 + perm[d] for kv in range(2) for d in range(64)]
    cols += list(range(o3, o4)) + list(range(o4, o5)) + list(range(o2, o3)) + list(range(o5, 2304))
    cols = np.array(cols)
    assert len(cols) == WIN
    g_q, g_k = f(inp["g_q"])[0], f(inp["g_k"])[0]
    gcols = np.zeros((128, 8), np.float32)
    gcols[:, 0] = np.tile(g_q, 2)
    gcols[:, 1] = np.tile(g_q[perm], 2)
    gcols[:, 2] = np.tile(g_k, 2)
    gcols[:, 3] = np.tile(g_k[perm], 2)
    gcols[:, 4] = f(inp["g_subln"])[0]
    gcols[:, 5] = 1.0 - LAM_INIT
    grow = np.stack([f(inp["g_pre_mix"])[0], f(inp["g_post_mix"])[0], f(inp["g_pre_ffn"])[0], f(inp["g_post_ffn"])[0]])
    grow2 = np.ascontiguousarray(np.stack([grow, grow]))
    lamv = np.stack([f(inp["lam_q1"])[0], f(inp["lam_k1"])[0], f(inp["lam_q2"])[0], f(inp["lam_k2"])[0]])[None]
    b_ada = f(inp["b_ada"])
    cm = np.zeros((128, 5, 128), np.float32)
    cm[:, 0, :] = np.eye(128)
    cm[:, 1, :] = np.eye(128)[::-1]
    cm[:, 2, :] = 1.0
    cm[0:64, 3, 0:64] = 1.0 / 64
    cm[64:128, 3, 64:128] = 1.0 / 64
    cm[:, 4, :] = 1.0 / 128
    m = np.arange(ULEN)
    emain = _onehot(_rel_bucket_np(639 - m))
    sh = dict(
        w_ada=f(inp["w_ada"])[0], b_ada2=np.ascontiguousarray(np.concatenate([b_ada, b_ada], 0)), grow=grow2,
        w_in_p=np.ascontiguousarray(w_in[:, cols]), w_out=f(inp["w_out"])[0], w_gu=f(inp["w_gu"])[0], w_down=f(inp["w_down"])[0],
        gcols=gcols, lamv=np.ascontiguousarray(lamv), relb=f(inp["rel_bias"]),
        cmat=cm.astype(ml_dtypes.bfloat16), emain=emain.astype(ml_dtypes.bfloat16),
    )
    return sh


def _prep_core(inp, sh, c, NP, NS):
    f = lambda a: np.asarray(a, dtype=np.float32)
    pb, pq, sbi, sq = c // 4, c % 4, c // 2, c % 2
    m = dict(sh)
    cp, cs = f(inp["c_prompt"])[pb], f(inp["c_sample"])[sbi]
    cT = np.stack([cp, cs], -1).reshape(8, 128, 2).transpose(1, 0, 2)
    m["cT"] = np.ascontiguousarray(cT)
    for nm, x, N, nq, qi in (("P", f(inp["x_prompt"])[pb], NP, NP // 4, pq), ("S", f(inp["x_sample"])[sbi], NS, NS // 2, sq)):
        qoff = qi * nq
        m["x" + nm] = np.ascontiguousarray(np.roll(x, -qoff, axis=0))
        pos = (np.arange(N) + qoff) % N
        cosT, sinT = _rope_tables(pos)
        m["cos" + nm] = cosT
        m["sin" + nm] = sinT
        NC = N // 128
        mm = np.arange(WLEN)
        if qoff > 0:
            bw = _rel_bucket_np(-1 - mm)
        else:
            bw = np.full(WLEN, NB // 2 + NB // 2 - 1)
        m["ewrap" + nm] = _onehot(bw).astype(ml_dtypes.bfloat16)
        if qoff + nq == N:
            bw2 = np.full(WLEN, NB // 2 - 1)
        else:
            bw2 = _rel_bucket_np(639 - mm)
        m["ewrap2" + nm] = _onehot(bw2).astype(ml_dtypes.bfloat16)
        far = np.zeros(NC + 1, np.int64)
        for ch in range(NC):
            if ch * 128 < nq:
                far[ch] = 31
            else:
                far[ch] = 31 if ch * 128 < N - qoff else 15
        far[NC] = 15
        m["efar" + nm] = _onehot(far).astype(ml_dtypes.bfloat16)
    return m


_CACHE = {}


def run(inputs, NP, NS, debug=False, ncores=8):
    key = (NP, NS, debug)
    if key not in _CACHE:
        _CACHE[key] = build_program(NP, NS, debug)
    nc = _CACHE[key]
    sh = _prep_shared(inputs, NP, NS)
    in_maps = [_prep_core(inputs, sh, c, NP, NS) for c in range(ncores)]
    res = run_bass_kernel_spmd(nc, in_maps, core_ids=list(range(ncores)))
    return res.results


def kernel(**inputs):
    NP = int(np.asarray(inputs["x_prompt"]).shape[1])
    NS = int(np.asarray(inputs["x_sample"]).shape[1])
    r = run(inputs, NP, NS)
    yp = np.zeros((2, NP, D), np.float32)
    ys = np.zeros((4, NS, D), np.float32)
    for c in range(8):
        pb, pq, sbi, sq = c // 4, c % 4, c // 2, c % 2
        nqp, nqs = NP // 4, NS // 2
        yp[pb, pq * nqp:(pq + 1) * nqp] = r[c]["yP"]
        ys[sbi, sq * nqs:(sq + 1) * nqs] = r[c]["yS"]
    return (yp, ys)
```
